# Optimizing a Trainium2 kernel written in Bass

```python
import jax
import jax.numpy as jnp
from jax import lax
import numpy as np

D_MODEL = 1024
BATCH = 16
SEQ = 256
DEPTH = 4
DEC_BATCH = 4
DEC_SEQ = 1024
PAST_LEN = 512

GRID_W = 64
N_EVEN = (DEPTH + 1) // 2
N_ODD = DEPTH // 2
N_MOD = 6
D_FF = 4 * D_MODEL
EPS = 1e-6

SSD_HEADS = 16
SSD_HEADDIM = 64
D_SSD = SSD_HEADS * SSD_HEADDIM
SSD_GROUPS = 4
D_STATE = 128
SSD_CONV_CH = D_SSD + 2 * SSD_GROUPS * D_STATE
IN_SSD = D_SSD + SSD_CONV_CH + 2 * SSD_HEADS
SSD_CHUNK = 64

RWKV_HEADS = 16
RWKV_HEAD = 64
D_RWKV = RWKV_HEADS * RWKV_HEAD
DECAY_LORA = 64
AAA_LORA = 64
GATE_LORA = 128
IN_RWKV = 3 * D_RWKV + 2 * DECAY_LORA + 2 * AAA_LORA + GATE_LORA
RWKV_LN_EPS = 64e-5

GLA_HEADS = 4
GLA_DK = 128
GLA_DV = 256
GLA_GATE_RANK = 16
GLA_TAU = 16.0
GLA_CHUNK = 16
IN_GLA = 2 * GLA_HEADS * GLA_DK + 2 * GLA_HEADS * GLA_DV + 2 * GLA_GATE_RANK

ML_HEADS = 4
ML_DK = 128
ML_DV = 256
ML_CHUNK = 64
IN_ML = 2 * ML_HEADS * ML_DK + 2 * ML_HEADS * ML_DV + 4 * ML_HEADS

IN_AB = IN_SSD + IN_RWKV
IN_CD = IN_GLA + IN_ML
D_MIX_AB = D_SSD + D_RWKV
D_MIX_CD = GLA_HEADS * GLA_DV + ML_HEADS * ML_DV

kernel_name = 'hybrid_diffusion_ssd_rwkv_gla_mlstm_step'


def rmsnorm(x, g):
    xf = x.astype(jnp.float32)
    y = xf * lax.rsqrt(jnp.mean(xf * xf, axis=-1, keepdims=True) + EPS)
    return y.astype(x.dtype) * g


def head_rmsnorm(x, g, n_heads):
    shp = x.shape
    y = rmsnorm(x.reshape(shp[:-1] + (n_heads, shp[-1] // n_heads)), g.reshape(n_heads, -1))
    return y.reshape(shp)


def head_layernorm(x, w, b, n_heads):
    shp = x.shape
    xf = x.reshape(shp[:-1] + (n_heads, shp[-1] // n_heads)).astype(jnp.float32)
    mu = jnp.mean(xf, axis=-1, keepdims=True)
    var = jnp.mean(jnp.square(xf - mu), axis=-1, keepdims=True)
    y = ((xf - mu) * lax.rsqrt(var + RWKV_LN_EPS)).reshape(shp).astype(x.dtype)
    return y * w + b


def short_conv(x, w, b, grid_rows):
    bsz, t, ch = x.shape
    w = w.astype(x.dtype)
    if grid_rows is None:
        y = lax.conv_general_dilated(x, w[1][:, None, :], (1,), 'SAME',
                                     dimension_numbers=('NWC', 'WIO', 'NWC'), feature_group_count=ch)
    else:
        y = lax.conv_general_dilated(x.reshape(bsz, grid_rows, GRID_W, ch), w[:, :, None, :], (1, 1), 'SAME',
                                     dimension_numbers=('NHWC', 'HWIO', 'NHWC'), feature_group_count=ch)
        y = y.reshape(bsz, t, ch)
    return y + b


def token_shift(x):
    xp = jnp.pad(x, ((0, 0), (1, 1), (0, 0)))
    return 0.5 * (xp[:, :-2] + xp[:, 2:])


def chunked_linear_scan(q, k, v, log_a, s0, chunk):
    out_dtype = v.dtype
    f32 = jnp.float32
    bsz, nh, t, _ = q.shape
    dv = v.shape[-1]
    nc = t // chunk
    q, k, v, la = (u.astype(f32).reshape(bsz, nh, nc, chunk, -1) for u in (q, k, v, log_a))
    b = jnp.cumsum(la, axis=3)
    b_last = b[:, :, :, -1:]
    causal = jnp.tril(jnp.ones((chunk, chunk), dtype=bool))
    if la.shape[-1] == 1:
        b0 = b[..., 0]
        seg = b0[..., :, None] - b0[..., None, :]
        dec = jnp.where(causal, jnp.exp(jnp.where(causal, seg, 0.0)), 0.0)
        scores = jnp.einsum('bhntk,bhnsk->bhnts', q, k) * dec
    else:
        cm = causal[:, :, None]
        seg = b[:, :, :, :, None, :] - b[:, :, :, None, :, :]
        dec = jnp.where(cm, jnp.exp(jnp.where(cm, seg, 0.0)), 0.0)
        scores = jnp.einsum('bhntk,bhnsk,bhntsk->bhnts', q, k, dec)
    o_intra = jnp.einsum('bhnts,bhnsv->bhntv', scores, v)
    ds = jnp.einsum('bhnsk,bhnsv->bhnkv', k * jnp.exp(b_last - b), v)
    g = jnp.exp(b_last[:, :, :, 0])

    def step(s, inp):
        g_c, ds_c = inp
        return g_c[..., None] * s + ds_c, s

    s_fin, s_start = lax.scan(step, s0.astype(f32), (jnp.moveaxis(g, 2, 0), jnp.moveaxis(ds, 2, 0)))
    s_start = jnp.moveaxis(s_start, 0, 2)
    o_inter = jnp.einsum('bhntk,bhnkv->bhntv', q * jnp.exp(b), s_start)
    o = (o_intra + o_inter).reshape(bsz, nh, t, dv)
    return o.astype(out_dtype), s_fin.astype(out_dtype)


def chunked_mlstm(q, k, v, i_pre, log_f, c0, n0, m0, chunk):
    out_dtype = v.dtype
    f32 = jnp.float32
    bsz, nh, t, _ = q.shape
    dv = v.shape[-1]
    nc = t // chunk
    q, k, v = (u.astype(f32).reshape(bsz, nh, nc, chunk, -1) for u in (q, k, v))
    ig, lf = (u.astype(f32).reshape(bsz, nh, nc, chunk) for u in (i_pre, log_f))
    b = jnp.cumsum(lf, axis=-1)
    causal = jnp.tril(jnp.ones((chunk, chunk), dtype=bool))
    dmat = jnp.where(causal, b[..., :, None] - b[..., None, :] + ig[..., None, :], -jnp.inf)
    lw_tail = b[..., -1:] - b + ig
    m_loc = jnp.max(lw_tail, axis=-1)
    w_tail = jnp.exp(lw_tail - m_loc[..., None])
    dc = jnp.einsum('bhns,bhnsk,bhnsv->bhnkv', w_tail, k, v)
    dn = jnp.einsum('bhns,bhnsk->bhnk', w_tail, k)
    g = b[..., -1]

    def step(carry, inp):
        c_s, n_s, m_s = carry
        g_c, ml_c, dc_c, dn_c = inp
        m_new = jnp.maximum(g_c + m_s, ml_c)
        a_old = jnp.exp(g_c + m_s - m_new)
        a_new = jnp.exp(ml_c - m_new)
        c_n = a_old[..., None, None] * c_s + a_new[..., None, None] * dc_c
        n_n = a_old[..., None] * n_s + a_new[..., None] * dn_c
        return (c_n, n_n, m_new), (c_s, n_s, m_s)

    xs = tuple(jnp.moveaxis(u, 2, 0) for u in (g, m_loc, dc, dn))
    (c_f, n_f, m_f), (c_st, n_st, m_st) = lax.scan(step, (c0.astype(f32), n0.astype(f32), m0.astype(f32)), xs)
    c_st, n_st, m_st = (jnp.moveaxis(u, 0, 2) for u in (c_st, n_st, m_st))
    m_inter = b + m_st[..., None]
    m_t = jnp.maximum(m_inter, jnp.max(dmat, axis=-1))
    w_inter = jnp.exp(m_inter - m_t)
    s_qk = jnp.einsum('bhntk,bhnsk->bhnts', q, k) * jnp.exp(dmat - m_t[..., None])
    num = jnp.einsum('bhnts,bhnsv->bhntv', s_qk, v) + w_inter[..., None] * jnp.einsum('bhntk,bhnkv->bhntv', q, c_st)
    den = jnp.sum(s_qk, axis=-1) + w_inter * jnp.einsum('bhntk,bhnk->bhnt', q, n_st)
    h = num / jnp.maximum(jnp.abs(den), jnp.exp(-m_t))[..., None]
    return h.reshape(bsz, nh, t, dv).astype(out_dtype), (c_f.astype(out_dtype), n_f.astype(out_dtype), m_f.astype(out_dtype))


def rwkv7_scan(r, log_w, k, v, kk, a, s0):
    out_dtype = v.dtype
    xs = tuple(jnp.moveaxis(u.astype(jnp.float32), 1, 0) for u in (r, log_w, k, v, kk, a))

    def step(s, inp):
        r_t, lw_t, k_t, v_t, kk_t, a_t = inp
        s_kk = jnp.einsum('bhvk,bhk->bhv', s, kk_t)
        s = (s * jnp.exp(lw_t)[:, :, None, :] - s_kk[..., None] * (kk_t * a_t)[:, :, None, :]
             + v_t[..., None] * k_t[:, :, None, :])
        return s, jnp.einsum('bhvk,bhk->bhv', s, r_t)

    s_fin, ys = lax.scan(step, s0.astype(jnp.float32), xs)
    return jnp.moveaxis(ys, 0, 1).astype(out_dtype), s_fin.astype(out_dtype)


def mixer_ab(h, s_ssd, s_rwkv, grid_rows, p):
    bsz, t, _ = h.shape
    proj = h @ p['w_in']
    ssd_in, rw_in = proj[..., :IN_SSD], proj[..., IN_SSD:]

    z = ssd_in[..., :D_SSD]
    xbc = jax.nn.silu(short_conv(ssd_in[..., D_SSD:D_SSD + SSD_CONV_CH], p['ssd_conv_w'], p['ssd_conv_b'], grid_rows))
    dt_raw = ssd_in[..., D_SSD + SSD_CONV_CH:].reshape(bsz, t, 2, SSD_HEADS)
    xs = xbc[..., :D_SSD].reshape(bsz, t, SSD_HEADS, SSD_HEADDIM)
    bc = xbc[..., D_SSD:].reshape(bsz, t, 2, SSD_GROUPS, D_STATE)
    rep = SSD_HEADS // SSD_GROUPS
    b_h = jnp.repeat(bc[:, :, 0], rep, axis=2)
    c_h = jnp.repeat(bc[:, :, 1], rep, axis=2)
    dt = jax.nn.softplus(dt_raw.astype(jnp.float32) + p['ssd_dt_bias'])
    log_a = dt * -jnp.exp(p['ssd_a_log'].astype(jnp.float32))
    q = c_h.transpose(0, 2, 1, 3)
    v = xs.transpose(0, 2, 1, 3)
    ys, ss = [], []
    for d in range(2):
        k = (b_h * dt[:, :, d, :, None]).transpose(0, 2, 1, 3)
        la = log_a[:, :, d].transpose(0, 2, 1)[..., None]
        args = (q, k, v, la) if d == 0 else tuple(jnp.flip(u, 2) for u in (q, k, v, la))
        y, s = chunked_linear_scan(*args, s_ssd[:, d], SSD_CHUNK)
        ys.append(y if d == 0 else jnp.flip(y, 2))
        ss.append(s)
    y_ssd = (ys[0] + ys[1]).transpose(0, 2, 1, 3) + xs * p['ssd_d'][:, None]
    y_ssd = rmsnorm(y_ssd.reshape(bsz, t, D_SSD) * jax.nn.silu(z), p['ssd_norm'])

    rw = rw_in + p['rwkv_mu'] * (token_shift(rw_in) - rw_in)
    o4 = 3 * D_RWKV + 2 * DECAY_LORA
    o5 = o4 + 2 * AAA_LORA
    r, k, v, wd, ad, gd = jnp.split(rw, [D_RWKV, 2 * D_RWKV, 3 * D_RWKV, o4, o5], axis=-1)
    wd = wd.reshape(bsz, t, 2, DECAY_LORA)
    ad = ad.reshape(bsz, t, 2, AAA_LORA)
    w_pre = (p['rwkv_w0'] + jnp.einsum('btdr,drc->btdc', jnp.tanh(wd), p['rwkv_w2'])).astype(jnp.float32)
    log_w = -jnp.exp(-jax.nn.softplus(-w_pre) - 0.5)
    a = jax.nn.sigmoid(p['rwkv_a0'] + jnp.einsum('btdr,drc->btdc', ad, p['rwkv_a2']))
    g = jax.nn.sigmoid(gd) @ p['rwkv_g2']

    def hsplit(u):
        return u.reshape(u.shape[:-1] + (RWKV_HEADS, RWKV_HEAD))

    rh, kh, vh = hsplit(r), hsplit(k), hsplit(v)
    kk = hsplit(k * p['rwkv_k_k']).astype(jnp.float32)
    kk = kk * lax.rsqrt(jnp.sum(kk * kk, axis=-1, keepdims=True) + 1e-12)
    lw_h, a_h = hsplit(log_w), hsplit(a)
    k_a = hsplit(p['rwkv_k_a'])
    ys_rw, ss_rw = [], []
    for d in range(2):
        kd = kh * (1 + (a_h[:, :, d] - 1) * k_a)
        args = (rh, lw_h[:, :, d], kd, vh, kk, a_h[:, :, d])
        if d == 1:
            args = tuple(jnp.flip(u, 1) for u in args)
        y, s = rwkv7_scan(*args, s_rwkv[:, d])
        ys_rw.append(y if d == 0 else jnp.flip(y, 1))
        ss_rw.append(s)
    y_rw = head_layernorm((ys_rw[0] + ys_rw[1]).reshape(bsz, t, D_RWKV), p['rwkv_ln_w'], p['rwkv_ln_b'], RWKV_HEADS)
    bonus = jnp.sum(rh * kh * p['rwkv_r_k'], axis=-1, keepdims=True) * vh
    y_rw = (y_rw + bonus.reshape(bsz, t, D_RWKV)) * g

    out = jnp.concatenate([y_ssd, y_rw], axis=-1) @ p['w_out']
    return out, (jnp.stack(ss, axis=1), jnp.stack(ss_rw, axis=1))


def mixer_cd(h, s_gla, s_mc, s_mn, s_mm, grid_rows, p):
    bsz, t, _ = h.shape
    proj = h @ p['w_in']
    gla_in, ml_in = proj[..., :IN_GLA], proj[..., IN_GLA:]

    def heads(u, n_heads):
        return u.reshape(bsz, t, n_heads, -1).transpose(0, 2, 1, 3)

    dkg, dvg = GLA_HEADS * GLA_DK, GLA_HEADS * GLA_DV
    gq, gk, gv, gg, gd = jnp.split(gla_in, [dkg, 2 * dkg, 2 * dkg + dvg, 2 * dkg + 2 * dvg], axis=-1)
    gd = gd.reshape(bsz, t, 2, GLA_GATE_RANK)
    gate_pre = jnp.einsum('btdr,drc->btdc', gd, p['gla_gate_w']) + p['gla_gate_b']
    log_alpha = jax.nn.log_sigmoid(gate_pre.astype(jnp.float32)) / GLA_TAU
    q, k, v = heads(gq, GLA_HEADS) * GLA_DK ** -0.5, heads(gk, GLA_HEADS), heads(gv, GLA_HEADS)
    os_g, ss_g = [], []
    for d in range(2):
        la = heads(log_alpha[:, :, d], GLA_HEADS)
        args = (q, k, v, la) if d == 0 else tuple(jnp.flip(u, 2) for u in (q, k, v, la))
        o, s = chunked_linear_scan(*args, s_gla[:, d], GLA_CHUNK)
        os_g.append(o if d == 0 else jnp.flip(o, 2))
        ss_g.append(s)
    o_g = (os_g[0] + os_g[1]).transpose(0, 2, 1, 3).reshape(bsz, t, dvg)
    y_gla = head_rmsnorm(o_g, p['gla_norm'], GLA_HEADS) * jax.nn.silu(gg)

    dkm, dvm = ML_HEADS * ML_DK, ML_HEADS * ML_DV
    mqk, mv, mo, mif = jnp.split(ml_in, [2 * dkm, 2 * dkm + dvm, 2 * dkm + 2 * dvm], axis=-1)
    mqk = jax.nn.silu(short_conv(mqk, p['ml_conv_w'], p['ml_conv_b'], grid_rows))
    q = heads(mqk[..., :dkm], ML_HEADS)
    k = heads(mqk[..., dkm:], ML_HEADS) * ML_DK ** -0.5
    v = heads(mv, ML_HEADS)
    mif = mif.reshape(bsz, t, 2, 2, ML_HEADS).astype(jnp.float32)
    i_pre = mif[:, :, 0] + p['ml_i_b']
    log_f = jax.nn.log_sigmoid(mif[:, :, 1] + p['ml_f_b'])
    hs, cs, ns, ms = [], [], [], []
    for d in range(2):
        ig = i_pre[:, :, d].transpose(0, 2, 1)
        lf = log_f[:, :, d].transpose(0, 2, 1)
        args = (q, k, v, ig, lf) if d == 0 else tuple(jnp.flip(u, 2) for u in (q, k, v, ig, lf))
        hh, (sc, sn, sm) = chunked_mlstm(*args, s_mc[:, d], s_mn[:, d], s_mm[:, d], ML_CHUNK)
        hs.append(hh if d == 0 else jnp.flip(hh, 2))
        cs.append(sc)
        ns.append(sn)
        ms.append(sm)
    h_m = (hs[0] + hs[1]).transpose(0, 2, 1, 3).reshape(bsz, t, dvm)
    y_ml = jax.nn.sigmoid(mo) * head_rmsnorm(h_m, p['ml_norm'], ML_HEADS)

    out = jnp.concatenate([y_gla, y_ml], axis=-1) @ p['w_out']
    return out, (jnp.stack(ss_g, axis=1), jnp.stack(cs, axis=1), jnp.stack(ns, axis=1), jnp.stack(ms, axis=1))


def sandwich_block(x, mod, mix_fn, g, w_up, w_down):
    shift1, scale1, gate1, shift2, scale2, gate2 = jnp.split(mod, N_MOD, axis=-1)
    m, st = mix_fn(rmsnorm(x, g[0]) * (1 + scale1) + shift1)
    x = x + gate1 * rmsnorm(m, g[1])
    f = jnp.square(jax.nn.relu((rmsnorm(x, g[2]) * (1 + scale2) + shift2) @ w_up)) @ w_down
    x = x + gate2 * rmsnorm(f, g[3])
    return x, st


def setup_inputs(seed: int = 0) -> dict:
    key = jax.random.key(seed)
    ks = iter(jax.random.split(key, 64))

    def nrm(shape, scale=1.0):
        return scale * jax.random.normal(next(ks), shape, jnp.float32)

    def unif(shape, lo, hi):
        return jax.random.uniform(next(ks), shape, jnp.float32, lo, hi)

    dt0 = jnp.exp(unif((N_EVEN, 2, SSD_HEADS), float(np.log(1e-3)), float(np.log(1e-1))))
    return {
        'x_prompt': nrm((BATCH, SEQ, D_MODEL)),
        'x_sample': nrm((DEC_BATCH, DEC_SEQ, D_MODEL)),
        'state_ssd': nrm((DEC_BATCH, N_EVEN, 2, SSD_HEADS, D_STATE, SSD_HEADDIM), 0.3),
        'state_rwkv': nrm((DEC_BATCH, N_EVEN, 2, RWKV_HEADS, RWKV_HEAD, RWKV_HEAD), 0.3),
        'state_gla': nrm((DEC_BATCH, N_ODD, 2, GLA_HEADS, GLA_DK, GLA_DV), 0.3),
        'state_mlstm_c': nrm((DEC_BATCH, N_ODD, 2, ML_HEADS, ML_DK, ML_DV), 0.3),
        'state_mlstm_n': nrm((DEC_BATCH, N_ODD, 2, ML_HEADS, ML_DK), 0.3),
        'state_mlstm_m': nrm((DEC_BATCH, N_ODD, 2, ML_HEADS)),
        'c': nrm((DEC_BATCH, D_MODEL)),
        'c_ctx': nrm((D_MODEL,)),
        'w_mod': nrm((DEPTH, D_MODEL, N_MOD * D_MODEL), 0.5 * D_MODEL ** -0.5),
        'b_mod': nrm((DEPTH, N_MOD * D_MODEL), 0.02),
        'norm_g': 1.0 + nrm((DEPTH, 4, D_MODEL), 0.05),
        'w_mlp_up': nrm((DEPTH, D_MODEL, D_FF), D_MODEL ** -0.5),
        'w_mlp_down': nrm((DEPTH, D_FF, D_MODEL), D_FF ** -0.5),
        'w_in_ab': nrm((N_EVEN, D_MODEL, IN_AB), D_MODEL ** -0.5),
        'ssd_conv_w': nrm((N_EVEN, 3, 3, SSD_CONV_CH), 1.0 / 3.0),
        'ssd_conv_b': nrm((N_EVEN, SSD_CONV_CH), 0.02),
        'ssd_dt_bias': dt0 + jnp.log(-jnp.expm1(-dt0)),
        'ssd_a_log': jnp.log(unif((N_EVEN, 2, SSD_HEADS), 1.0, 16.0)),
        'ssd_d': 1.0 + nrm((N_EVEN, SSD_HEADS), 0.1),
        'ssd_norm': 1.0 + nrm((N_EVEN, D_SSD), 0.05),
        'rwkv_mu': unif((N_EVEN, IN_RWKV), 0.0, 1.0),
        'rwkv_w0': unif((N_EVEN, 2, D_RWKV), -6.0, -1.0),
        'rwkv_w2': nrm((N_EVEN, 2, DECAY_LORA, D_RWKV), 0.5 * DECAY_LORA ** -0.5),
        'rwkv_a0': nrm((N_EVEN, 2, D_RWKV), 0.5),
        'rwkv_a2': nrm((N_EVEN, 2, AAA_LORA, D_RWKV), 0.5 * AAA_LORA ** -0.5),
        'rwkv_g2': nrm((N_EVEN, GATE_LORA, D_RWKV), GATE_LORA ** -0.5),
        'rwkv_k_k': 0.85 + nrm((N_EVEN, D_RWKV), 0.05),
        'rwkv_k_a': 1.0 + nrm((N_EVEN, D_RWKV), 0.05),
        'rwkv_r_k': nrm((N_EVEN, RWKV_HEADS, RWKV_HEAD), 0.1),
        'rwkv_ln_w': 1.0 + nrm((N_EVEN, D_RWKV), 0.05),
        'rwkv_ln_b': nrm((N_EVEN, D_RWKV), 0.02),
        'w_out_ab': nrm((N_EVEN, D_MIX_AB, D_MODEL), D_MIX_AB ** -0.5),
        'w_in_cd': nrm((N_ODD, D_MODEL, IN_CD), D_MODEL ** -0.5),
        'gla_gate_w': nrm((N_ODD, 2, GLA_GATE_RANK, GLA_HEADS * GLA_DK), GLA_GATE_RANK ** -0.5),
        'gla_gate_b': nrm((N_ODD, 2, GLA_HEADS * GLA_DK), 0.5),
        'gla_norm': 1.0 + nrm((N_ODD, GLA_HEADS * GLA_DV), 0.05),
        'mlstm_conv_w': nrm((N_ODD, 3, 3, 2 * ML_HEADS * ML_DK), 1.0 / 3.0),
        'mlstm_conv_b': nrm((N_ODD, 2 * ML_HEADS * ML_DK), 0.02),
        'mlstm_i_b': nrm((N_ODD, 2, ML_HEADS), 0.1),
        'mlstm_f_b': unif((N_ODD, 2, ML_HEADS), 3.0, 6.0),
        'mlstm_norm': 1.0 + nrm((N_ODD, ML_HEADS * ML_DV), 0.05),
        'w_out_cd': nrm((N_ODD, D_MIX_CD, D_MODEL), D_MIX_CD ** -0.5),
    }


def reference(x_prompt, x_sample, state_ssd, state_rwkv, state_gla, state_mlstm_c, state_mlstm_n, state_mlstm_m, c,
              c_ctx, w_mod, b_mod, norm_g, w_mlp_up, w_mlp_down,
              w_in_ab, ssd_conv_w, ssd_conv_b, ssd_dt_bias, ssd_a_log, ssd_d, ssd_norm,
              rwkv_mu, rwkv_w0, rwkv_w2, rwkv_a0, rwkv_a2, rwkv_g2, rwkv_k_k, rwkv_k_a, rwkv_r_k, rwkv_ln_w, rwkv_ln_b,
              w_out_ab, w_in_cd, gla_gate_w, gla_gate_b, gla_norm, mlstm_conv_w, mlstm_conv_b, mlstm_i_b, mlstm_f_b,
              mlstm_norm, w_out_cd):
    bp = x_prompt.shape[0]
    rows = x_sample.shape[1] // GRID_W
    dtp = x_prompt.dtype
    zero_ssd = jnp.zeros((bp, 2, SSD_HEADS, D_STATE, SSD_HEADDIM), dtp)
    zero_rwkv = jnp.zeros((bp, 2, RWKV_HEADS, RWKV_HEAD, RWKV_HEAD), dtp)
    zero_gla = jnp.zeros((bp, 2, GLA_HEADS, GLA_DK, GLA_DV), dtp)
    zero_mc = jnp.zeros((bp, 2, ML_HEADS, ML_DK, ML_DV), dtp)
    zero_mn = jnp.zeros((bp, 2, ML_HEADS, ML_DK), dtp)
    zero_mm = jnp.zeros((bp, 2, ML_HEADS), dtp)
    cond_ctx = jax.nn.silu(c_ctx)[None, :]
    cond_lat = jax.nn.silu(c)
    y_prompt, y_sample = x_prompt, x_sample
    new_ssd, new_rwkv, new_gla, new_mc, new_mn, new_mm = [], [], [], [], [], []
    for l in range(DEPTH):
        j = l // 2
        mod_ctx = (cond_ctx @ w_mod[l] + b_mod[l])[:, None, :]
        mod_lat = (cond_lat @ w_mod[l] + b_mod[l])[:, None, :]
        if l % 2 == 0:
            p = {'w_in': w_in_ab[j], 'ssd_conv_w': ssd_conv_w[j], 'ssd_conv_b': ssd_conv_b[j],
                 'ssd_dt_bias': ssd_dt_bias[j], 'ssd_a_log': ssd_a_log[j], 'ssd_d': ssd_d[j], 'ssd_norm': ssd_norm[j],
                 'rwkv_mu': rwkv_mu[j], 'rwkv_w0': rwkv_w0[j], 'rwkv_w2': rwkv_w2[j], 'rwkv_a0': rwkv_a0[j],
                 'rwkv_a2': rwkv_a2[j], 'rwkv_g2': rwkv_g2[j], 'rwkv_k_k': rwkv_k_k[j], 'rwkv_k_a': rwkv_k_a[j],
                 'rwkv_r_k': rwkv_r_k[j], 'rwkv_ln_w': rwkv_ln_w[j], 'rwkv_ln_b': rwkv_ln_b[j], 'w_out': w_out_ab[j]}
            y_prompt, (s_ssd, s_rw) = sandwich_block(
                y_prompt, mod_ctx, lambda h: mixer_ab(h, zero_ssd, zero_rwkv, None, p),
                norm_g[l], w_mlp_up[l], w_mlp_down[l])
            y_sample, _ = sandwich_block(
                y_sample, mod_lat, lambda h: mixer_ab(h, state_ssd[:, j], state_rwkv[:, j], rows, p),
                norm_g[l], w_mlp_up[l], w_mlp_down[l])
            new_ssd.append(s_ssd)
            new_rwkv.append(s_rw)
        else:
            p = {'w_in': w_in_cd[j], 'gla_gate_w': gla_gate_w[j], 'gla_gate_b': gla_gate_b[j], 'gla_norm': gla_norm[j],
                 'ml_conv_w': mlstm_conv_w[j], 'ml_conv_b': mlstm_conv_b[j], 'ml_i_b': mlstm_i_b[j],
                 'ml_f_b': mlstm_f_b[j], 'ml_norm': mlstm_norm[j], 'w_out': w_out_cd[j]}
            y_prompt, (s_gla, s_mc, s_mn, s_mm) = sandwich_block(
                y_prompt, mod_ctx, lambda h: mixer_cd(h, zero_gla, zero_mc, zero_mn, zero_mm, None, p),
                norm_g[l], w_mlp_up[l], w_mlp_down[l])
            y_sample, _ = sandwich_block(
                y_sample, mod_lat,
                lambda h: mixer_cd(h, state_gla[:, j], state_mlstm_c[:, j], state_mlstm_n[:, j], state_mlstm_m[:, j], rows, p),
                norm_g[l], w_mlp_up[l], w_mlp_down[l])
            new_gla.append(s_gla)
            new_mc.append(s_mc)
            new_mn.append(s_mn)
            new_mm.append(s_mm)
    return (y_prompt, y_sample, jnp.stack(new_ssd, axis=1), jnp.stack(new_rwkv, axis=1), jnp.stack(new_gla, axis=1),
            jnp.stack(new_mc, axis=1), jnp.stack(new_mn, axis=1), jnp.stack(new_mm, axis=1))
```

```python
import contextlib
import numpy as np
import concourse.bass as bass
import concourse.mybir as mybir
from concourse.bass_utils import run_bass_kernel_spmd

F32 = mybir.dt.float32
BF16 = mybir.dt.bfloat16
AF = mybir.ActivationFunctionType
ALU = mybir.AluOpType
AX = mybir.AxisListType

T = 1024
D = 1024
NT = 8
DFF = 4096
EPS = 1e-6


class Res:
    __slots__ = ("name", "w", "r")

    def __init__(self, name=""):
        self.name = name
        self.w = None
        self.r = {}


class KB:
    ENG = ("pe", "act", "dve", "pool", "sp")

    def __init__(self, nc, es):
        self.nc = nc
        self.es = es
        self.e = {"pe": nc.tensor, "act": nc.scalar, "dve": nc.vector, "pool": nc.gpsimd, "sp": nc.sync}
        self.sem = {k: es.enter_context(nc.semaphore("sem_" + k)) for k in self.ENG}
        self.cnt = {k: 0 for k in self.ENG}
        self.seen = {k: {} for k in self.ENG}
        self.chan = {}
        self.ninst = 0
        self.nwait = 0

    def sb(self, name, shape, dt=F32):
        nb = int(np.prod(shape[1:])) * (4 if dt == F32 else 2)
        self.sbtot = getattr(self, "sbtot", 0) + nb
        return self.es.enter_context(self.nc.sbuf_tensor(name, shape, dt))

    def ps(self, name, shape, dt=F32):
        return self.es.enter_context(self.nc.psum_tensor(name, shape, dt))

    def channel(self, name):
        if name not in self.chan:
            s = self.es.enter_context(self.nc.semaphore("ch_" + name))
            self.chan[name] = [s, 0]
        return name

    def _deps(self, reads, writes):
        deps = {}

        def add(tok):
            if tok is None:
                return
            k, v = tok
            if deps.get(k, 0) < v:
                deps[k] = v
        for r in reads:
            add(r.w)
        for w in writes:
            add(w.w)
            for k, v in w.r.items():
                add((k, v))
        return deps

    def _emit_waits(self, eng, deps):
        seen = self.seen[eng]
        E = self.e[eng]
        for k, v in deps.items():
            if seen.get(k, 0) >= v:
                continue
            seen[k] = v
            s = self.sem[k] if k in self.sem else self.chan[k][0]
            E.wait_ge(s, v)
            self.nwait += 1

    def _mark(self, tok, reads, writes):
        k, v = tok
        for r in reads:
            if r.r.get(k, 0) < v:
                r.r[k] = v
        for w in writes:
            w.w = tok
            w.r = {}

    def op(self, eng, fn, reads=(), writes=()):
        self._emit_waits(eng, self._deps(reads, writes))
        ins = fn(self.e[eng])
        self.cnt[eng] += 1
        ins.then_inc(self.sem[eng], 1)
        tok = (eng, self.cnt[eng])
        self._mark(tok, reads, writes)
        self.ninst += 1
        return tok

    def dma(self, q, ch, out, in_, reads=(), writes=(), **kw):
        self.channel(ch)
        self._emit_waits(q, self._deps(reads, writes))
        ins = self.e[q].dma_start(out=out, in_=in_, **kw)
        c = self.chan[ch]
        c[1] += 16
        ins.then_inc(c[0], 16)
        tok = (ch, c[1])
        self._mark(tok, reads, writes)
        self.ninst += 1
        return tok

    def barrier(self):
        for eng in self.ENG:
            deps = {k: v[1] for k, v in self.chan.items() if v[1] > 0}
            for k in self.ENG:
                if k != eng and self.cnt[k] > 0:
                    deps[k] = self.cnt[k]
            self._emit_waits(eng, deps)

    def finish(self, eng="sp"):
        deps = {k: v[1] for k, v in self.chan.items() if v[1] > 0}
        for k in self.ENG:
            if k != eng and self.cnt[k] > 0:
                deps[k] = self.cnt[k]
        self._emit_waits(eng, deps)


class Prog:
    def __init__(self, nl=4, do_mix=True, layers=None, parts=("gla", "ml", "ssd", "rw")):
        self.nl = nl
        self.layers = list(range(nl)) if layers is None else layers
        self.parts = parts
        self.do_mix = do_mix
        self.nc = bass.Bass("TRN2", target_bir_lowering=False)
        self.es = contextlib.ExitStack()
        self.din = {}
        self.dout = {}

    def inp(self, name, shape, dt=F32):
        ap = self.nc.dram_tensor(name, list(shape), dt, kind="ExternalInput").ap()
        self.din[name] = ap
        return ap

    def outp(self, name, shape):
        ap = self.nc.dram_tensor(name, list(shape), F32, kind="ExternalOutput").ap()
        self.dout[name] = ap
        return ap

    def build(self):
        nc = self.nc
        with self.es as es:
            k = self.k = KB(nc, es)
            self.declare_io()
            self.alloc()
            self.load_consts()
            for l in self.layers:
                self.layer(l)
            self.store_y()
            k.finish()
            print("ninst", k.ninst, "nwait", k.nwait, k.cnt, flush=True)
        return nc

    def declare_io(self):
        self.x_d = self.inp("x", [T, D])
        self.cond_d = self.inp("cond", [128, 8])
        self.ident_d = self.inp("ident", [128, 128])
        self.w_mod_d = self.inp("w_mod", [4, D, 6 * D])
        self.b_mod_d = self.inp("b_mod", [4, 6 * D])
        self.norm_g_d = self.inp("norm_g", [4, 4, D])
        self.w_up_d = self.inp("w_mlp_up", [4, D, DFF])
        self.w_dn_d = self.inp("w_mlp_down", [4, DFF, D])
        self.y_d = self.outp("y", [T, D])
        if self.do_mix:
            self.inp("masks", [128, 9, 128])
            self.inp("flags", [128, 4])
            self.inp("w_in_cd", [2, D, 6192])
            self.inp("w_out_cd", [2, 2048, D])
            self.inp("gla_gate_w", [2, 2, 16, 512])
            self.inp("gla_gate_b", [2, 2, 512])
            self.inp("gla_norm", [2, 1024])
            self.inp("st_gla", [2, 2, 4, 128, 256])
            self.outp("o_gla", [2, 4, 2, 4, 128, 256])
            self.inp("w_in_ab", [2, D, 6560])
            self.inp("w_out_ab", [2, 2048, D])
            self.inp("ssd_dt_bias", [2, 2, 16])
            self.inp("ssd_a_log", [2, 2, 16])
            self.inp("ssd_d", [2, 16])
            self.inp("ssd_nw", [2, 128, 8])
            self.inp("ssd_cw", [2, 16, 128, 9])
            self.inp("ssd_cb", [2, 16, 128, 1])
            self.inp("st_ssd", [2, 2, 16, 128, 64])
            self.outp("o_ssd", [2, 4, 2, 16, 128, 64])
            self.inp("rw_vec", [2, 128, 72])
            self.inp("rw_mu", [2, 128, 27])
            self.inp("tsm", [2, T])
            self.inp("rwkv_w2", [2, 2, 64, 1024])
            self.inp("rwkv_a2", [2, 2, 64, 1024])
            self.inp("rwkv_g2", [2, 128, 1024])
            self.inp("st_rw", [2, 2, 16, 64, 64])
            self.outp("o_rwkv", [2, 4, 2, 16, 64, 64])
            self.inp("convm", [2, T])
            self.inp("tapflag", [128, 9])
            self.inp("ml_cw", [2, 8, 128, 9])
            self.inp("ml_cb", [2, 8, 128, 1])
            self.inp("mlstm_i_b", [2, 2, 4])
            self.inp("mlstm_f_b", [2, 2, 4])
            self.inp("mlstm_norm", [2, 1024])
            self.inp("st_mc", [2, 2, 4, 128, 256])
            self.inp("st_mn", [2, 2, 4, 128])
            self.inp("st_mm", [2, 2, 4])
            self.outp("o_mc", [2, 4, 2, 4, 128, 256])
            self.outp("o_mn", [2, 4, 2, 4, 128])
            self.outp("o_mm", [2, 4, 2, 4])

    def alloc(self):
        k = self.k
        self.X = k.sb("X", [128, NT, D])
        self.RX = [Res("X%d" % i) for i in range(NT)]
        self.hT = k.sb("hT", [128, 8, T], BF16)
        self.RhT = [Res("hT%d" % i) for i in range(NT)]
        self.modb = k.sb("modb", [128, 6 * D])
        self.Rmod = [Res("mod%d" % i) for i in range(6)]
        self.NSLOT = 2
        self.WA = k.sb("WA", [128, self.NSLOT * 4096], BF16)
        self.RW = [Res("W%d" % i) for i in range(self.NSLOT)]
        self.wslot = 0
        self.PS = [k.ps("ps%d" % i, [128, 512]) for i in range(6)]
        self.RPS = [Res("ps%d" % i) for i in range(6)]
        self.psi = 0
        self.PB = [k.ps("pb%d" % i, [128, 1024], BF16) for i in range(2)]
        self.RPB = [Res("pb%d" % i) for i in range(2)]
        self.pbi = 0
        self.identf = k.sb("identf", [128, 128])
        self.identb = k.sb("identb", [128, 128], BF16)
        self.Rid = Res("ident")
        self.cs = k.sb("cs", [128, 8])
        self.Rcond = Res("cond")
        self.ss = k.sb("ss", [128, 16])
        self.Rss = [Res("ss%d" % i) for i in range(16)]
        self.epsb = k.sb("epsb", [128, 1])
        self.Rjunk = Res("junk")
        self.tmpf = [k.sb("tmpf%d" % i, [128, D]) for i in range(2)]
        self.Rtmpf = [Res() for _ in range(2)]
        self.hb = [k.sb("hb%d" % i, [128, D], BF16) for i in range(2)]
        self.Rhb = [Res() for _ in range(2)]
        self.MA = k.sb("MA", [128, 36864], BF16)
        self.RMA = Res("MA")
        self.junk = k.sb("junk", [128, D])
        fa = self.MA[:, 16384:16384 + 4 * 2048].bitcast(F32)
        self.Ff = [fa[:, i * D:(i + 1) * D] for i in range(4)]
        self.RFf = [Res() for _ in range(4)]
        ga = self.MA[:, 2048:2048 + 2 * 2048].bitcast(F32)
        self.gsc = [ga[:, i * D:(i + 1) * D] for i in range(2)]
        self.RWD = Res("wdown")
        self.Rgsc = [Res() for _ in range(2)]

    def scr(self, name, shape, dt=F32):
        if not hasattr(self, "_scr"):
            self._scr = {}
        if name not in self._scr:
            self._scr[name] = self.k.sb(name, shape, dt)
        return self._scr[name]

    def next_ps(self, pin=False):
        if not hasattr(self, "pinned"):
            self.pinned = set()
        while True:
            i = self.psi
            self.psi = (self.psi + 1) % len(self.PS)
            if i not in self.pinned:
                break
        if pin:
            self.pinned.add(i)
        return self.PS[i], self.RPS[i]

    def unpin(self, *rs):
        for r in rs:
            self.pinned.discard(self.RPS.index(r))

    def next_pb(self):
        i = self.pbi
        self.pbi = (self.pbi + 1) % len(self.PB)
        return self.PB[i], self.RPB[i]

    def wslots(self, n):
        if self.wslot + n > self.NSLOT:
            self.wslot = 0
        s = self.wslot
        self.wslot += n
        self.wch = "ws%d" % s
        return self.WA[:, s * 4096:(s + n) * 4096], self.RW[s:s + n]

    def load_consts(self):
        k = self.k
        k.op("dve", lambda e: e.memset(self.epsb[:], EPS), writes=[self.Rid])
        k.dma("sp", "c0", self.identf[:], self.ident_d[:, :], writes=[self.Rid])
        k.op("dve", lambda e: e.tensor_copy(out=self.identb[:], in_=self.identf[:]), reads=[self.Rid], writes=[self.Rid])
        k.dma("sp", "c1", self.cs[:], self.cond_d[:, :], writes=[self.Rcond])
        k.op("act", lambda e: e.activation(out=self.cs[:], in_=self.cs[:], func=AF.Silu), reads=[self.Rcond], writes=[self.Rcond])
        for i in range(NT):
            k.dma("sp", "x%d" % i, self.X[:, i, :], self.x_d[i * 128:(i + 1) * 128, :], writes=[self.RX[i]])

    def store_y(self):
        k = self.k
        for i in range(NT):
            k.dma("sp", "y%d" % i, self.y_d[i * 128:(i + 1) * 128, :], self.X[:, i, :], reads=[self.RX[i]])

    def adaln(self, l):
        k = self.k
        modb = self.modb
        self.condB = self.MA[:, 0:2048].bitcast(F32).rearrange("p (kc m) -> p kc m", kc=8)
        k.op("dve", lambda e: e.tensor_copy(out=self.condB, in_=self.cs[:].unsqueeze(2).to_broadcast([128, 8, 128])),
             reads=[self.Rcond], writes=[self.Rcond])
        k.dma("sp", "bmod", modb[:], self.b_mod_d[l:l + 1, :].partition_broadcast(128), writes=self.Rmod)
        wv = self.w_mod_d[l].rearrange("(kc p) n -> p kc n", p=128)
        NB = 256
        for j in range(6 * D // NB):
            wap, wres = self.wslots(1)
            wf = wap.bitcast(F32).rearrange("p (kc n) -> p kc n", kc=8)
            k.dma("sp", self.wch, wf, wv[:, :, j * NB:(j + 1) * NB], writes=wres)
            ps, rps = self.next_ps()
            for kc in range(8):
                k.op("pe", lambda e, kc=kc: e.matmul(ps[:, 0:NB], lhsT=self.condB[:, kc, :], rhs=wf[:, kc, :],
                                                     start=(kc == 0), stop=(kc == 7)),
                     reads=[self.Rcond] + wres, writes=[rps])
            r = self.Rmod[j * NB // D]
            sl = modb[:, j * NB:(j + 1) * NB]
            k.op("dve", lambda e: e.tensor_tensor(out=sl, in0=ps[:, 0:NB], in1=sl, op=ALU.add), reads=[rps, r], writes=[r])
        for gi, (mi, isscale) in enumerate([(1, True), (2, False), (4, True), (5, False)]):
            g, rg = self.gsc[gi % 2], self.Rgsc[gi % 2]
            k.dma("sp", "g%d" % (gi % 2), g, self.norm_g_d[l, gi:gi + 1, :].partition_broadcast(128), writes=[rg])
            sl = modb[:, mi * D:(mi + 1) * D]
            if isscale:
                k.op("dve", lambda e: e.scalar_tensor_tensor(out=sl, in0=sl, scalar=1.0, in1=g, op0=ALU.add, op1=ALU.mult),
                     reads=[rg, self.Rmod[mi]], writes=[self.Rmod[mi]])
            else:
                k.op("dve", lambda e: e.tensor_tensor(out=sl, in0=sl, in1=g, op=ALU.mult),
                     reads=[rg, self.Rmod[mi]], writes=[self.Rmod[mi]])

    def rstd_of(self, src_ap, rsrc, col, n, junk=None, rjunk=None):
        k = self.k
        ssl = self.ss[:, col:col + 1]
        rss = self.Rss[col]
        k.op("dve", lambda e: e.memset(ssl, 0.0), writes=[rss])
        jk = self.junk[:, 0:n] if junk is None else junk
        rjk = self.Rjunk if rjunk is None else rjunk
        k.op("act", lambda e: e.activation(out=jk, in_=src_ap, func=AF.Square, accum_out=ssl),
             reads=[rsrc], writes=[rss, rjk])
        k.op("act", lambda e: e.activation(out=ssl, in_=ssl, func=AF.Ln, scale=1.0 / n, bias=self.epsb[:, 0:1]),
             reads=[rss, self.Rid], writes=[rss])
        k.op("act", lambda e: e.activation(out=ssl, in_=ssl, func=AF.Exp, scale=-0.5), reads=[rss], writes=[rss])
        return ssl

    def norm_mod_T(self, ai, si):
        k = self.k
        A = self.modb[:, ai * D:(ai + 1) * D]
        S = self.modb[:, si * D:(si + 1) * D]
        for i in range(NT):
            rs = self.rstd_of(self.X[:, i, :], self.RX[i], i, D)
            tf, rtf = self.tmpf[i % 2], self.Rtmpf[i % 2]
            hb, rhb = self.hb[i % 2], self.Rhb[i % 2]
            k.op("dve", lambda e: e.scalar_tensor_tensor(out=tf[:], in0=self.X[:, i, :], scalar=rs, in1=A, op0=ALU.mult, op1=ALU.mult),
                 reads=[self.RX[i], self.Rss[i], self.Rmod[ai]], writes=[rtf])
            k.op("dve", lambda e: e.tensor_tensor(out=hb[:], in0=tf[:], in1=S, op=ALU.add), reads=[rtf, self.Rmod[si]], writes=[rhb])
            pb, rpb = self.next_pb()
            for kc in range(8):
                k.op("pe", lambda e, kc=kc: e.transpose(pb[:, kc * 128:(kc + 1) * 128], hb[:, kc * 128:(kc + 1) * 128], self.identb[:]),
                     reads=[rhb, self.Rid], writes=[rpb])
            k.op("act", lambda e: e.activation(out=self.hT[:, :, i * 128:(i + 1) * 128],
                                               in_=pb[:].rearrange("p (kc t) -> p kc t", kc=8), func=AF.Identity),
                 reads=[rpb], writes=[self.RhT[i]])

    def resid_add(self, i, F, rF, gi):
        k = self.k
        G = self.modb[:, gi * D:(gi + 1) * D]
        rs = self.rstd_of(F, rF, 8 + i, D)
        k.op("dve", lambda e: e.scalar_tensor_tensor(out=F, in0=F, scalar=rs, in1=G, op0=ALU.mult, op1=ALU.mult),
             reads=[rF, self.Rss[8 + i], self.Rmod[gi]], writes=[rF])
        k.op("dve", lambda e: e.tensor_tensor(out=self.X[:, i, :], in0=self.X[:, i, :], in1=F, op=ALU.add),
             reads=[rF, self.RX[i]], writes=[self.RX[i]])

    def mlp(self, l):
        k = self.k
        self.norm_mod_T(4, 3)
        wu = self.w_up_d[l].rearrange("(kc p) n -> p kc n", p=128)
        wd = self.w_dn_d[l].rearrange("(fc p) n -> p fc n", p=128)
        uT = self.MA[:, 0:32 * 512].rearrange("p (fc t) -> p fc t", fc=32)
        Ru = [Res("u%d" % i) for i in range(8)]
        for half in range(2):
            t0 = half * 512
            rh = self.RhT[half * 4:(half + 1) * 4]
            for fb in range(8):
                wap, wres = self.wslots(1)
                w = wap.rearrange("p (kc n) -> p kc n", kc=8)
                k.dma("pool", self.wch, w, wu[:, :, fb * 512:(fb + 1) * 512], writes=wres)
                for fc in range(4):
                    ps, rps = self.next_ps()
                    for kc in range(8):
                        k.op("pe", lambda e, kc=kc: e.matmul(ps[:, :], lhsT=w[:, kc, fc * 128:(fc + 1) * 128], rhs=self.hT[:, kc, t0:t0 + 512],
                                                             start=(kc == 0), stop=(kc == 7)), reads=wres + rh, writes=[rps])
                    tf, rtf = self.tmpf[fc % 2], self.Rtmpf[fc % 2]
                    k.op("act", lambda e: e.activation(out=tf[:, 0:512], in_=ps[:, :], func=AF.Relu), reads=[rps], writes=[rtf])
                    k.op("dve", lambda e: e.tensor_tensor(out=uT[:, fb * 4 + fc, :], in0=tf[:, 0:512], in1=tf[:, 0:512], op=ALU.mult),
                         reads=[rtf], writes=[Ru[fb]])
            Fs = {}
            for nh in range(4):
                wres = [self.RWD]
                w = self.MA[:, 24576:32768].rearrange("p (fc n) -> p fc n", fc=32)
                for q in range(4):
                    k.dma("pool", "wdn", w[:, q * 8:(q + 1) * 8, :], wd[:, q * 8:(q + 1) * 8, nh * 256:(nh + 1) * 256], writes=wres)
                for ti in range(4):
                    i = half * 4 + ti
                    ps, rps = self.next_ps()
                    for fc in range(32):
                        k.op("pe", lambda e, fc=fc: e.matmul(ps[:, 0:256], lhsT=uT[:, fc, ti * 128:(ti + 1) * 128], rhs=w[:, fc, :],
                                                             start=(fc == 0), stop=(fc == 31)), reads=wres + [Ru[fc // 4]], writes=[rps])
                    if nh == 0:
                        Fs[ti] = (self.Ff[ti], self.RFf[ti])
                    F, rF = Fs[ti]
                    k.op("act", lambda e: e.activation(out=F[:, nh * 256:(nh + 1) * 256], in_=ps[:, 0:256], func=AF.Identity),
                         reads=[rps], writes=[rF])
            for ti in range(4):
                F, rF = Fs[ti]
                self.resid_add(half * 4 + ti, F, rF, 5)

    def layer(self, l):
        self.adaln(l)
        self.k.barrier()
        if self.do_mix:
            self.norm_mod_T(1, 0)
            self.mixer(l)
            self.k.barrier()
        self.mlp(l)
        self.k.barrier()


def _prep_core_inputs(inputs, core):
    if core < 4:
        x = np.ascontiguousarray(inputs["x_sample"][core])
        cond = inputs["c"][core]
    else:
        j = core - 4
        x = np.ascontiguousarray(inputs["x_prompt"][4 * j:4 * j + 4].reshape(T, D))
        cond = inputs["c_ctx"]
    m = {"x": x, "cond": np.ascontiguousarray(cond.reshape(8, 128).T), "ident": np.eye(128, dtype=np.float32)}
    for n in ("w_mod", "b_mod", "norm_g", "w_mlp_up", "w_mlp_down", "w_in_cd", "w_out_cd", "gla_gate_w", "gla_gate_b", "gla_norm"):
        m[n] = inputs[n]
    r = np.arange(128)
    bdm = lambda n: (r[:, None] // n) == (r[None, :] // n)
    m["masks"] = np.ascontiguousarray(np.stack([r[:, None] <= r[None, :], r[:, None] >= r[None, :], r[:, None] > r[None, :], r[:, None] < r[None, :],
                                                (r[:, None] // 64) == (r[None, :] // 64),
                                                bdm(16), bdm(32) & ~bdm(16), bdm(64) & ~bdm(32), ~bdm(64)], axis=1).astype(np.float32))
    fl = np.zeros((128, 4), np.float32)
    fl[:, 0] = 1.0 if core < 4 else 0.0
    m["flags"] = fl
    for n in ("mlstm_i_b", "mlstm_f_b", "mlstm_norm"):
        m[n] = inputs[n]
    t = np.arange(T)
    per = 64 if core < 4 else 256
    m["convm"] = np.stack([(t % per) != 0, (t % per) != per - 1]).astype(np.float32)
    tf = np.ones((128, 9), np.float32)
    if core >= 4:
        tf[:, 0:3] = 0.0
        tf[:, 6:9] = 0.0
    m["tapflag"] = tf
    cw = inputs["mlstm_conv_w"]
    m["ml_cw"] = np.ascontiguousarray(cw.reshape(2, 9, 8, 128).transpose(0, 2, 3, 1))
    m["ml_cb"] = np.ascontiguousarray(inputs["mlstm_conv_b"].reshape(2, 8, 128, 1))
    for n in ("w_in_ab", "w_out_ab", "ssd_dt_bias", "ssd_a_log", "ssd_d"):
        m[n] = inputs[n]
    m["ssd_nw"] = np.ascontiguousarray(inputs["ssd_norm"].reshape(2, 8, 128).transpose(0, 2, 1))
    m["ssd_cw"] = np.ascontiguousarray(inputs["ssd_conv_w"].reshape(2, 9, 16, 128).transpose(0, 2, 3, 1))
    m["ssd_cb"] = np.ascontiguousarray(inputs["ssd_conv_b"].reshape(2, 16, 128, 1))
    for n in ("rwkv_w2", "rwkv_a2", "rwkv_g2"):
        m[n] = inputs[n]
    pc = lambda a: a.reshape(2, 8, 128).transpose(0, 2, 1)
    kinds = [inputs["rwkv_w0"][:, 0], inputs["rwkv_w0"][:, 1], inputs["rwkv_a0"][:, 0], inputs["rwkv_a0"][:, 1], inputs["rwkv_k_k"], inputs["rwkv_k_a"],
             inputs["rwkv_r_k"].reshape(2, 1024), inputs["rwkv_ln_w"], inputs["rwkv_ln_b"]]
    m["rw_vec"] = np.ascontiguousarray(np.stack([pc(a) for a in kinds], axis=2).reshape(2, 128, 72))
    m["rw_mu"] = np.ascontiguousarray(inputs["rwkv_mu"].reshape(2, 27, 128).transpose(0, 2, 1))
    sl = 1024 if core < 4 else 256
    m["tsm"] = np.stack([(t % sl) != 0, (t % sl) != sl - 1]).astype(np.float32)
    names = {"st_rw": "state_rwkv", "st_ssd": "state_ssd", "st_gla": "state_gla", "st_mc": "state_mlstm_c", "st_mn": "state_mlstm_n", "st_mm": "state_mlstm_m"}
    for kk, src in names.items():
        a = inputs[src]
        m[kk] = np.ascontiguousarray(a[core]) if core < 4 else np.zeros(a.shape[1:], np.float32)
    return m


def _load_w(self, Wv, c0, n):
    wap, wres = self.wslots(1)
    w = wap[:, 0:8 * n].rearrange("p (kc n) -> p kc n", kc=8)
    self.k.dma("pool", self.wch, w, Wv[:, :, c0:c0 + n], writes=wres)
    return w, wres


def _proj_tok(self, Wv, c0, n, evac, tiles=None):
    k = self.k
    w, wres = self.load_w(Wv, c0, n)
    for i in (range(NT) if tiles is None else tiles):
        ps, rps = self.next_ps()
        for kc in range(8):
            k.op("pe", lambda e, kc=kc: e.matmul(ps[:, 0:n], lhsT=self.hT[:, kc, i * 128:(i + 1) * 128], rhs=w[:, kc, :],
                                                 start=(kc == 0), stop=(kc == 7)), reads=wres + [self.RhT[i]], writes=[rps])
        evac(i, ps[:, 0:n], rps)


def _proj_feat(self, Wv, c0, n, evac):
    k = self.k
    w, wres = self.load_w(Wv, c0, n)
    for c in range((n + 127) // 128):
        m = min(128, n - c * 128)
        for half in range(2):
            ps, rps = self.next_ps()
            for kc in range(8):
                k.op("pe", lambda e, kc=kc: e.matmul(ps[0:m, :], lhsT=w[:, kc, c * 128:c * 128 + m], rhs=self.hT[:, kc, half * 512:(half + 1) * 512],
                                                     start=(kc == 0), stop=(kc == 7)), reads=wres + self.RhT[half * 4:half * 4 + 4], writes=[rps])
            evac(c, half, ps[0:m, :], rps, m)


def _transpose_to(self, src_bf, rsrc, dst3, rdst, ncol=8):
    k = self.k
    pb, rpb = self.next_pb()
    for c in range(ncol):
        k.op("pe", lambda e, c=c: e.transpose(pb[:, c * 128:(c + 1) * 128], src_bf[:, c * 128:(c + 1) * 128], self.identb[:]),
             reads=[rsrc, self.Rid], writes=[rpb])
    k.op("act", lambda e: e.activation(out=dst3, in_=pb[:, 0:ncol * 128].rearrange("p (c t) -> p c t", c=ncol), func=AF.Identity),
         reads=[rpb], writes=[rdst])


def _out_proj(self, Wd, yT, RyT, gi=2):
    k = self.k
    wv = Wd.rearrange("(kc p) n -> p kc n", p=128)
    Fall = self.MA[:, 16384:32768].bitcast(F32).rearrange("p (i n) -> p i n", i=NT)
    RF = [Res() for _ in range(NT)]
    for nh in range(2):
        wap, wres = self.wslots(2)
        w = wap.rearrange("p (kc n) -> p kc n", kc=16)
        for q in range(2):
            k.dma("pool", self.wch, w[:, q * 8:(q + 1) * 8, :], wv[:, q * 8:(q + 1) * 8, nh * 512:(nh + 1) * 512], writes=wres)
        for i in range(NT):
            ps, rps = self.next_ps()
            for kc in range(16):
                k.op("pe", lambda e, kc=kc: e.matmul(ps[:, :], lhsT=yT[:, kc, i * 128:(i + 1) * 128], rhs=w[:, kc, :],
                                                     start=(kc == 0), stop=(kc == 15)), reads=wres + [RyT[i]], writes=[rps])
            k.op("act", lambda e: e.activation(out=Fall[:, i, nh * 512:(nh + 1) * 512], in_=ps[:, :], func=AF.Identity),
                 reads=[rps], writes=[RF[i]])
    for i in range(NT):
        self.resid_add(i, Fall[:, i, :], RF[i], gi)


Prog.load_w = _load_w
Prog.proj_tok = _proj_tok
Prog.proj_feat = _proj_feat
Prog.transpose_to = _transpose_to
Prog.out_proj = _out_proj


def _mix_consts(self):
    k = self.k
    if hasattr(self, "masks"):
        return
    self.masks = k.sb("masks_sb", [128, 9, 128])
    self.Rmask = Res("masks")
    k.dma("sp", "c2", self.masks[:], self.din["masks"][:, :, :], writes=[self.Rmask])
    self.onesc = k.sb("onesc", [128, 1])
    self.flags = k.sb("flags_sb", [128, 4])
    k.op("dve", lambda e: e.memset(self.onesc[:], 1.0), writes=[self.Rmask])
    k.dma("sp", "c3", self.flags[:], self.din["flags"][:, :], writes=[self.Rmask])


def _mixer_cd(self, l):
    k = self.k
    j = l // 2
    self.mix_consts()
    Wv = self.din["w_in_cd"][j].rearrange("(kc p) n -> p kc n", p=128)
    yT = self.MA[:, 0:16384].rearrange("p (c t) -> p c t", c=16)
    RyT = [Res() for _ in range(NT)]
    if "gla" in self.parts:
        self.gla(l, j, Wv, yT, RyT)
    else:
        k.op("dve", lambda e: e.memset(yT[:, 0:8, :], 0.0), writes=RyT)
    k.barrier()
    if "ml" in self.parts:
        self.mlstm(l, j, Wv, yT, RyT)
    else:
        k.op("dve", lambda e: e.memset(yT[:, 8:16, :], 0.0), writes=RyT)
    k.barrier()
    self.out_proj(self.din["w_out_cd"][j], yT, RyT)


def _gla(self, l, j, Wv, yT, RyT):
    k = self.k
    MA = self.MA
    qT = MA[:, 16384:18432].rearrange("p (h t) -> p h t", h=2)
    kT = MA[:, 18432:20480].rearrange("p (h t) -> p h t", h=2)
    ktok = MA[:, 20480:22528].rearrange("p (i c) -> p i c", i=NT)
    vtok = MA[:, 22528:26624].rearrange("p (i c) -> p i c", i=NT)
    Oacc = MA[:, 26624:30720].rearrange("p (i c) -> p i c", i=NT)
    sc = 128 ** -0.5
    gdT = [MA[0:17, 30720 + d * 2048:30720 + (d + 1) * 2048].bitcast(F32) for d in range(2)]
    gw = [MA[0:17, 34816 + d * 1024:34816 + (d + 1) * 1024].bitcast(F32) for d in range(2)]
    Rgd = [Res(), Res()]
    for d in range(2):
        k.op("dve", lambda e: e.memset(gdT[d], 1.0), writes=[Rgd[d]])
        self.proj_feat(Wv, 3072 + 16 * d, 16, lambda c, half, ps, rps, m: k.op(
            "act", lambda e: e.activation(out=gdT[d][0:16, half * 512:(half + 1) * 512], in_=ps, func=AF.Identity), reads=[rps], writes=[Rgd[d]]))
        k.dma("sp", "gw", gw[d][0:16, :], self.din["gla_gate_w"][j, d], writes=[Rgd[d]])
        k.dma("sp", "gw", gw[d][16:17, :], self.din["gla_gate_b"][j, d:d + 1, :], writes=[Rgd[d]])
    gnb = self.scr("nrmb", [128, 1024])
    Rgn = Res()
    k.dma("sp", "gnb", gnb[:], self.din["gla_norm"][j:j + 1, :].partition_broadcast(128), writes=[Rgn])
    S = [self.scr("S%d" % h, [128, 256]) for h in range(2)]
    Sbf = [self.scr("Sb%d" % h, [128, 256], BF16) for h in range(2)]
    la = self.tmpf[0][:, 0:512]; Rla = self.Rtmpf[0]
    te = self.tmpf[0][:, 512:1024]
    ebuf = self.tmpf[1]; Reb = self.Rtmpf[1]
    khat = self.scr("khat", [128, 256], BF16)
    qTt = self.scr("qTt", [128, 2, 128], BF16); kTt = self.scr("kTt", [128, 2, 128], BF16)
    Gt = self.scr("Gt", [128, 4])
    scT = [self.scr("scT%d" % i, [128, 128], BF16) for i in range(2)]
    Ofin = self.junk; ROf = self.Rjunk
    yb = self.hb[0]; Ryb = self.Rhb[0]
    seg_first = {0: [0, 2, 4, 6], 1: [7, 5, 3, 1]}
    seg_last = {0: [1, 3, 5, 7], 1: [6, 4, 2, 0]}
    for hh in range(2):
        RqT, RkT = Res(), Res()
        Rkt = [Res() for _ in range(NT)]
        Rvt = [Res() for _ in range(NT)]
        ROa = [Res() for _ in range(NT)]
        RS = [Res() for _ in range(2)]
        Rkh = Res(); Rqk = [Res() for _ in range(2)]; RG = [Res() for _ in range(2)]; Rsc = [Res(), Res()]
        self.proj_feat(Wv, hh * 256, 256, lambda c, half, ps, rps, m: k.op(
            "act", lambda e: e.activation(out=qT[:, c, half * 512:(half + 1) * 512], in_=ps, func=AF.Identity, scale=sc), reads=[rps], writes=[RqT]))
        self.proj_feat(Wv, 512 + hh * 256, 256, lambda c, half, ps, rps, m: k.op(
            "act", lambda e: e.activation(out=kT[:, c, half * 512:(half + 1) * 512], in_=ps, func=AF.Identity), reads=[rps], writes=[RkT]))
        self.proj_tok(Wv, 512 + hh * 256, 256, lambda i, ps, rps: k.op(
            "act", lambda e: e.activation(out=ktok[:, i, :], in_=ps, func=AF.Identity), reads=[rps], writes=[Rkt[i]]))
        self.proj_tok(Wv, 1024 + hh * 512, 512, lambda i, ps, rps: k.op(
            "act", lambda e: e.activation(out=vtok[:, i, :], in_=ps, func=AF.Identity), reads=[rps], writes=[Rvt[i]]))
        for d in range(2):
            order = list(range(NT)) if d == 0 else list(range(NT - 1, -1, -1))
            mcum = self.masks[:, d, :]
            mend = self.masks[:, 2 + d, :]
            endcol = 127 if d == 0 else 0
            if d == 1:
                ggw = self.load_w(Wv, 2048 + hh * 512, 512)
            for ci, i in enumerate(order):
                seg = i // 2
                if i in seg_first[d]:
                    for hl in range(2):
                        if ci == 0:
                            k.dma("sp", "gst%d" % hl, S[hl][:], self.din["st_gla"][j, d, 2 * hh + hl], writes=[RS[hl]])
                        else:
                            k.op("dve", lambda e: e.tensor_scalar(out=S[hl][:], in0=S[hl][:], scalar1=self.flags[:, 0:1], scalar2=None, op0=ALU.mult),
                                 reads=[RS[hl], self.Rmask], writes=[RS[hl]])
                        k.op("act", lambda e: e.activation(out=Sbf[hl][:], in_=S[hl][:], func=AF.Identity), reads=[RS[hl]], writes=[RS[hl]])
                ps, rps = self.next_ps()
                k.op("pe", lambda e: e.matmul(ps[:, 0:256], lhsT=gdT[d][:, i * 128:(i + 1) * 128], rhs=gw[d][:, hh * 256:(hh + 1) * 256], start=True, stop=True),
                     reads=[Rgd[d]], writes=[rps])
                k.op("act", lambda e: e.activation(out=la[:, 0:256], in_=ps[:, 0:256], func=AF.Exp, scale=-1.0), reads=[rps], writes=[Rla])
                k.op("act", lambda e: e.activation(out=la[:, 0:256], in_=la[:, 0:256], func=AF.Ln, bias=self.onesc[:, 0:1]), reads=[Rla, self.Rmask], writes=[Rla])
                k.op("dve", lambda e: e.tensor_scalar(out=la[:, 0:256], in0=la[:, 0:256], scalar1=-1.0 / 16.0, scalar2=None, op0=ALU.mult), reads=[Rla], writes=[Rla])
                ps, rps = self.next_ps()
                k.op("pe", lambda e: e.matmul(ps[:, 0:256], lhsT=mend, rhs=la[:, 0:256], start=True, stop=True), reads=[Rla, self.Rmask], writes=[rps])
                k.op("act", lambda e: e.activation(out=te[:, 0:256], in_=ps[:, 0:256], func=AF.Exp), reads=[rps], writes=[Rla])
                k.op("dve", lambda e: e.tensor_tensor(out=khat[:], in0=ktok[:, i, :], in1=te[:, 0:256], op=ALU.mult), reads=[Rla, Rkt[i]], writes=[Rkh])
                ps, rps = self.next_ps()
                for hl in range(2):
                    k.op("pe", lambda e: e.matmul(ps[:, hl * 128:(hl + 1) * 128], lhsT=la[:, hl * 128:(hl + 1) * 128], rhs=mcum, start=True, stop=True),
                         reads=[Rla, self.Rmask], writes=[rps])
                k.op("act", lambda e: e.activation(out=ebuf[:, 0:256], in_=ps[:, 0:256], func=AF.Exp), reads=[rps], writes=[Reb])
                k.op("act", lambda e: e.activation(out=ebuf[:, 256:512], in_=ps[:, 0:256], func=AF.Exp, scale=-1.0), reads=[rps], writes=[Reb])
                for hl in range(2):
                    k.op("dve", lambda e: e.tensor_tensor(out=qTt[:, hl, :], in0=qT[:, hl, i * 128:(i + 1) * 128], in1=ebuf[:, hl * 128:(hl + 1) * 128], op=ALU.mult),
                         reads=[Reb, RqT], writes=[Rqk[hl]])
                    k.op("dve", lambda e: e.tensor_tensor(out=kTt[:, hl, :], in0=kT[:, hl, i * 128:(i + 1) * 128], in1=ebuf[:, 256 + hl * 128:256 + (hl + 1) * 128], op=ALU.mult),
                         reads=[Reb, RkT], writes=[Rqk[hl]])
                    k.op("act", lambda e: e.activation(out=Gt[:, hl:hl + 1], in_=ebuf[:, hl * 128 + endcol:hl * 128 + endcol + 1], func=AF.Identity),
                         reads=[Reb], writes=[RG[hl]])
                for hl in range(2):
                    vs = vtok[:, i, hl * 256:(hl + 1) * 256]
                    ps, rps = self.next_ps()
                    k.op("pe", lambda e: e.matmul(ps[:, 0:128], lhsT=kTt[:, hl, :], rhs=qTt[:, hl, :], start=True, stop=True), reads=[Rqk[hl]], writes=[rps])
                    s_, rs_ = scT[hl], Rsc[hl]
                    k.op("dve", lambda e: e.tensor_tensor(out=s_[:], in0=ps[:, 0:128], in1=mcum, op=ALU.mult), reads=[rps, self.Rmask], writes=[rs_])
                    po, rpo = self.next_ps()
                    k.op("pe", lambda e: e.matmul(po[:, 0:256], lhsT=s_[:], rhs=vs, start=True, stop=False), reads=[rs_, Rvt[i]], writes=[rpo])
                    k.op("pe", lambda e: e.matmul(po[:, 0:256], lhsT=qTt[:, hl, :], rhs=Sbf[hl][:], start=False, stop=True), reads=[Rqk[hl], RS[hl]], writes=[rpo])
                    if d == 0:
                        k.op("act", lambda e: e.activation(out=Oacc[:, i, hl * 256:(hl + 1) * 256], in_=po[:, 0:256], func=AF.Identity), reads=[rpo], writes=[ROa[i]])
                    else:
                        k.op("dve", lambda e: e.tensor_tensor(out=Ofin[:, hl * 256:(hl + 1) * 256], in0=po[:, 0:256], in1=Oacc[:, i, hl * 256:(hl + 1) * 256], op=ALU.add),
                             reads=[rpo, ROa[i]], writes=[ROf])
                    pd, rpd = self.next_ps()
                    k.op("pe", lambda e: e.matmul(pd[:, 0:256], lhsT=khat[:, hl * 128:(hl + 1) * 128], rhs=vs, start=True, stop=True), reads=[Rkh, Rvt[i]], writes=[rpd])
                    k.op("dve", lambda e: e.scalar_tensor_tensor(out=S[hl][:], in0=S[hl][:], scalar=Gt[:, hl:hl + 1], in1=pd[:, 0:256], op0=ALU.mult, op1=ALU.add),
                         reads=[RS[hl], RG[hl], rpd], writes=[RS[hl]])
                    k.op("act", lambda e: e.activation(out=Sbf[hl][:], in_=S[hl][:], func=AF.Identity), reads=[RS[hl]], writes=[RS[hl]])
                    if i in seg_last[d]:
                        k.dma("sp", "gso%d" % hl, self.dout["o_gla"][j, seg, d, 2 * hh + hl], S[hl][:], reads=[RS[hl]])
                if d == 1:
                    for hl in range(2):
                        gcol = (2 * hh + hl) * 256
                        rs = self.rstd_of(Ofin[:, hl * 256:(hl + 1) * 256], ROf, 8 + hl, 256, junk=te[:, 0:256], rjunk=Rla)
                        k.op("dve", lambda e: e.scalar_tensor_tensor(out=Ofin[:, hl * 256:(hl + 1) * 256], in0=Ofin[:, hl * 256:(hl + 1) * 256], scalar=rs,
                                                                     in1=gnb[:, gcol:gcol + 256], op0=ALU.mult, op1=ALU.mult),
                             reads=[ROf, self.Rss[8 + hl], Rgn], writes=[ROf])
                    w, wres = ggw
                    ps, rps = self.next_ps()
                    for kc in range(8):
                        k.op("pe", lambda e, kc=kc: e.matmul(ps[:, :], lhsT=self.hT[:, kc, i * 128:(i + 1) * 128], rhs=w[:, kc, :], start=(kc == 0), stop=(kc == 7)),
                             reads=wres + [self.RhT[i]], writes=[rps])
                    k.op("act", lambda e: e.activation(out=te, in_=ps[:, :], func=AF.Silu), reads=[rps], writes=[Rla])
                    k.op("dve", lambda e: e.tensor_tensor(out=yb[:, 0:512], in0=Ofin[:, 0:512], in1=te, op=ALU.mult), reads=[ROf, Rla], writes=[Ryb])
                    self.transpose_to(yb, Ryb, yT[:, hh * 4:(hh + 1) * 4, i * 128:(i + 1) * 128], RyT[i], ncol=4)
        k.barrier()


Prog.mix_consts = _mix_consts
Prog.mixer = lambda self, l: (self.mixer_cd(l) if l % 2 == 1 else self.mixer_ab(l))
Prog.mixer_cd = _mixer_cd
Prog.gla = _gla


def _conv_setup(self, base):
    k = self.k
    MA = self.MA
    cin = MA[:, base:base + 2308].bitcast(F32)
    mLR = MA[:, base + 2308:base + 2308 + 4096].bitcast(F32).rearrange("p (a t) -> p a t", a=2)
    R = {"cin": cin, "mLR": mLR, "Rcin": Res(), "Rm": Res()}
    k.op("dve", lambda e: e.memset(cin[:, 0:65], 0.0), writes=[R["Rcin"]])
    k.op("dve", lambda e: e.memset(cin[:, 1089:1154], 0.0), writes=[R["Rcin"]])
    for a in range(2):
        k.dma("sp", "cm%d" % a, mLR[:, a, :], self.din["convm"][a:a + 1, :].partition_broadcast(128), writes=[R["Rm"]])
    if not hasattr(self, "tapf"):
        self.tapf = self.scr("tapf", [128, 9])
        self.Rtapf = Res()
        k.dma("sp", "tapf", self.tapf[:], self.din["tapflag"][:, :], writes=[self.Rtapf])
    return R


def _conv_chunk(self, C, Wv, col0, cw_ap, cb_ap, out_ap, rout, oscale=1.0):
    k = self.k
    cin, mLR, Rcin, Rm = C["cin"], C["mLR"], C["Rcin"], C["Rm"]
    wc = self.scr("convw", [128, 10])
    Rwc = C.setdefault("Rwc", Res())
    k.dma("sp", "cw", wc[:, 0:9], cw_ap, writes=[Rwc])
    k.dma("sp", "cw", wc[:, 9:10], cb_ap, writes=[Rwc])
    k.op("dve", lambda e: e.tensor_tensor(out=wc[:, 0:9], in0=wc[:, 0:9], in1=self.tapf[:], op=ALU.mult), reads=[self.Rtapf, Rwc], writes=[Rwc])
    self.proj_feat(Wv, col0, 128, lambda c, half, ps, rps, m: k.op(
        "act", lambda e: e.activation(out=cin[:, 65 + half * 512:65 + (half + 1) * 512], in_=ps, func=AF.Identity), reads=[rps], writes=[Rcin]))
    accs = [(self.tmpf[0], self.Rtmpf[0]), (self.tmpf[1], self.Rtmpf[1]), (self.junk, self.Rjunk)]
    for dc, (acc, racc) in zip((-1, 0, 1), accs):
        eng = "dve"
        for n, dr in enumerate((0, -1, 1)):
            tap = (dr + 1) * 3 + (dc + 1)
            off = 65 + 64 * dr + dc
            if n == 0:
                k.op(eng, lambda e: e.tensor_scalar(out=acc[:], in0=cin[:, off:off + T], scalar1=wc[:, tap:tap + 1], scalar2=None, op0=ALU.mult),
                     reads=[Rcin, Rwc], writes=[racc])
            else:
                k.op(eng, lambda e: e.scalar_tensor_tensor(out=acc[:], in0=cin[:, off:off + T], scalar=wc[:, tap:tap + 1], in1=acc[:], op0=ALU.mult, op1=ALU.add),
                     reads=[Rcin, Rwc, racc], writes=[racc])
    (aL, rL), (aC, rC), (aR, rR) = accs
    k.op("pool", lambda e: e.tensor_tensor(out=aL[:], in0=aL[:], in1=mLR[:, 0, :], op=ALU.mult), reads=[rL, Rm], writes=[rL])
    k.op("dve", lambda e: e.tensor_tensor(out=aR[:], in0=aR[:], in1=mLR[:, 1, :], op=ALU.mult), reads=[rR, Rm], writes=[rR])
    k.op("dve", lambda e: e.tensor_tensor(out=aC[:], in0=aC[:], in1=aL[:], op=ALU.add), reads=[rC, rL], writes=[rC])
    k.op("dve", lambda e: e.tensor_tensor(out=aC[:], in0=aC[:], in1=aR[:], op=ALU.add), reads=[rC, rR], writes=[rC])
    k.op("act", lambda e: e.activation(out=aC[:], in_=aC[:], func=AF.Silu, bias=wc[:, 9:10]), reads=[rC, Rwc], writes=[rC])
    k.op("dve", lambda e: e.tensor_scalar(out=out_ap, in0=aC[:], scalar1=float(oscale), scalar2=None, op0=ALU.mult), reads=[rC], writes=[rout])


Prog.conv_setup = _conv_setup
Prog.conv_chunk = _conv_chunk


def _mlstm(self, l, j, Wv, yT, RyT):
    k = self.k
    MA = self.MA
    B0 = 3104
    qT = MA[:, 16384:18432].rearrange("p (h t) -> p h t", h=2)
    kT = MA[:, 18432:20480].rearrange("p (h t) -> p h t", h=2)
    ktok = MA[:, 20480:22528].rearrange("p (i c) -> p i c", i=NT)
    vtok = MA[:, 22528:26624].rearrange("p (i c) -> p i c", i=NT)
    Oacc = MA[:, 26624:30720].rearrange("p (i c) -> p i c", i=NT)
    sc = 128 ** -0.5
    gat = self.scr("mgat", [128, NT, 16])
    lf = self.scr("mlf", [128, NT, 8])
    gbias = self.scr("mgb", [128, 16])
    Rg = Res()
    k.dma("sp", "mgb", gbias[:, 0:8], self.din["mlstm_i_b"][j:j + 1].rearrange("o d h -> o (d h)").partition_broadcast(128), writes=[Rg])
    k.dma("sp", "mgb", gbias[:, 8:16], self.din["mlstm_f_b"][j:j + 1].rearrange("o d h -> o (d h)").partition_broadcast(128), writes=[Rg])
    self.proj_tok(Wv, B0 + 3072, 16, lambda i, ps, rps: k.op(
        "dve", lambda e: e.tensor_tensor(out=gat[:, i, :], in0=ps, in1=gbias[:], op=ALU.add), reads=[rps, Rg], writes=[Rg]))
    k.op("act", lambda e: e.activation(out=lf[:], in_=gat[:, :, 8:16], func=AF.Exp, scale=-1.0), reads=[Rg], writes=[Rg])
    k.op("act", lambda e: e.activation(out=lf[:], in_=lf[:], func=AF.Ln, bias=self.onesc[:, 0:1]), reads=[Rg, self.Rmask], writes=[Rg])
    k.op("dve", lambda e: e.tensor_scalar(out=lf[:], in0=lf[:], scalar1=-1.0, scalar2=None, op0=ALU.mult), reads=[Rg], writes=[Rg])
    gnb = self.scr("nrmb", [128, 1024])
    Rgn = Res()
    k.dma("sp", "gnb", gnb[:], self.din["mlstm_norm"][j:j + 1, :].partition_broadcast(128), writes=[Rgn])
    onesb = self.scr("onesb", [128, 1], BF16)
    ones128 = self.scr("ones128", [128, 128])
    k.op("dve", lambda e: e.memset(onesb[:], 1.0), writes=[Rgn])
    k.op("dve", lambda e: e.memset(ones128[:], 1.0), writes=[Rgn])
    S = [self.scr("mS%d" % h, [128, 260]) for h in range(2)]
    Sbf = [self.scr("mSb%d" % h, [128, 260], BF16) for h in range(2)]
    khat = self.scr("khat", [128, 256], BF16)
    scT = [self.scr("scT%d" % i, [128, 128], BF16) for i in range(2)]
    sm = self.scr("msm", [128, 32])
    mcur = self.scr("mcur", [4, 4])
    dg = self.scr("mdg", [4, 4])
    stage = self.tmpf[0]; Rstage = self.Rtmpf[0]
    Ofin = self.junk; ROf = self.Rjunk
    yb = self.hb[0]; Ryb = self.Rhb[0]
    seg_first = {0: [0, 2, 4, 6], 1: [7, 5, 3, 1]}
    seg_last = {0: [1, 3, 5, 7], 1: [6, 4, 2, 0]}
    for hh in range(2):
        RqT, RkT = Res(), Res()
        Rkt = [Res() for _ in range(NT)]
        Rvt = [Res() for _ in range(NT)]
        ROa = [Res() for _ in range(NT)]
        RS = [Res() for _ in range(2)]
        Rkh = Res(); Rsc = [Res(), Res()]; Rsm = Res(); Rm = Res()
        C = self.conv_setup(22528)
        for hl in range(2):
            h = 2 * hh + hl
            self.conv_chunk(C, Wv, B0 + h * 128, self.din["ml_cw"][j, h], self.din["ml_cb"][j, h], qT[:, hl, :], RqT, 1.0)
            self.conv_chunk(C, Wv, B0 + 512 + h * 128, self.din["ml_cw"][j, 4 + h], self.din["ml_cb"][j, 4 + h], kT[:, hl, :], RkT, sc)
        k.barrier()
        for i in range(NT):
            pb, rpb = self.next_pb()
            for hl in range(2):
                k.op("pe", lambda e: e.transpose(pb[:, hl * 128:(hl + 1) * 128], kT[:, hl, i * 128:(i + 1) * 128], self.identb[:]), reads=[RkT, self.Rid], writes=[rpb])
            k.op("act", lambda e: e.activation(out=ktok[:, i, :], in_=pb[:, 0:256], func=AF.Identity), reads=[rpb], writes=[Rkt[i]])
        self.proj_tok(Wv, B0 + 1024 + hh * 512, 512, lambda i, ps, rps: k.op(
            "act", lambda e: e.activation(out=vtok[:, i, :], in_=ps, func=AF.Identity), reads=[rps], writes=[Rvt[i]]))
        for d in range(2):
            order = list(range(NT)) if d == 0 else list(range(NT - 1, -1, -1))
            mcum = self.masks[:, d, :]
            if d == 1:
                ggw = self.load_w(Wv, B0 + 2048 + hh * 512, 512)
            for ci, i in enumerate(order):
                seg = i // 2
                if i in seg_first[d]:
                    if ci == 0:
                        k.dma("sp", "mm0", mcur[:, 0:1], self.din["st_mm"][j, d:d + 1, :].rearrange("o h -> h o"), writes=[Rm])
                        for hl in range(2):
                            h = 2 * hh + hl
                            k.dma("sp", "mst%d" % hl, S[hl][:, 0:256], self.din["st_mc"][j, d, h], writes=[RS[hl]])
                            k.dma("sp", "mst%d" % hl, S[hl][:, 256:257], self.din["st_mn"][j, d, h].rearrange("(p o) -> p o", o=1), writes=[RS[hl]])
                            k.dma("sp", "mem0", sm[:, 24:25], self.din["st_mm"][j, d:d + 1, h:h + 1].partition_broadcast(128), writes=[Rsm])
                            k.op("act", lambda e: e.activation(out=sm[:, 24:25], in_=sm[:, 24:25], func=AF.Exp), reads=[Rsm], writes=[Rsm])
                            k.op("dve", lambda e: e.tensor_scalar(out=S[hl][:, 0:257], in0=S[hl][:, 0:257], scalar1=sm[:, 24:25], scalar2=None, op0=ALU.mult),
                                 reads=[RS[hl], Rsm], writes=[RS[hl]])
                    else:
                        k.op("dve", lambda e: e.tensor_scalar(out=mcur[:, 0:1], in0=mcur[:, 0:1], scalar1=self.flags[0:4, 0:1], scalar2=None, op0=ALU.mult),
                             reads=[Rm, self.Rmask], writes=[Rm])
                        for hl in range(2):
                            k.op("dve", lambda e: e.tensor_scalar(out=S[hl][:, 0:257], in0=S[hl][:, 0:257], scalar1=self.flags[:, 0:1], scalar2=None, op0=ALU.mult),
                                 reads=[RS[hl], self.Rmask], writes=[RS[hl]])
                    for hl in range(2):
                        k.op("act", lambda e: e.activation(out=Sbf[hl][:, 0:257], in_=S[hl][:, 0:257], func=AF.Identity), reads=[RS[hl]], writes=[RS[hl]])
                lfd = lf[:, i, d * 4:(d + 1) * 4]
                igd = gat[:, i, d * 4:(d + 1) * 4]
                ps, rps = self.next_ps()
                k.op("pe", lambda e: e.matmul(ps[:, 0:4], lhsT=mcum, rhs=lfd, start=True, stop=True), reads=[Rg, self.Rmask], writes=[rps])
                k.op("pe", lambda e: e.matmul(ps[:, 4:8], lhsT=ones128[:], rhs=lfd, start=True, stop=True), reads=[Rg, Rgn], writes=[rps])
                k.op("pe", lambda e: e.matmul(ps[0:4, 8:9], lhsT=lfd, rhs=self.onesc[:, 0:1], start=True, stop=True), reads=[Rg, self.Rmask], writes=[rps])
                k.op("dve", lambda e: e.tensor_tensor(out=sm[:, 0:4], in0=igd, in1=ps[:, 0:4], op=ALU.subtract), reads=[Rg, rps], writes=[Rsm])
                k.op("act", lambda e: e.activation(out=sm[:, 4:12], in_=ps[:, 0:8], func=AF.Exp), reads=[rps], writes=[Rsm])
                k.op("act", lambda e: e.activation(out=mcur[:, 2:3], in_=ps[0:4, 8:9], func=AF.Identity), reads=[rps], writes=[Rm])
                pt, rpt = self.next_ps()
                k.op("pe", lambda e: e.transpose(pt[0:4, 0:128], sm[:, 0:4], self.identf[:]), reads=[Rsm, self.Rid], writes=[rpt])
                k.op("dve", lambda e: e.tensor_reduce(out=mcur[:, 1:2], in_=pt[0:4, 0:128], axis=AX.X, op=ALU.max), reads=[rpt], writes=[Rm])
                k.op("dve", lambda e: e.tensor_scalar(out=mcur[:, 0:1], in0=mcur[:, 0:1], scalar1=mcur[:, 1:2], scalar2=mcur[:, 2:3], op0=ALU.max, op1=ALU.add),
                     reads=[Rm], writes=[Rm])
                k.op("act", lambda e: e.activation(out=sm[:, 0:4], in_=sm[:, 0:4], func=AF.Exp), reads=[Rsm], writes=[Rsm])
                k.op("dve", lambda e: e.tensor_tensor(out=sm[:, 12:16], in0=sm[:, 0:4], in1=sm[:, 8:12], op=ALU.mult), reads=[Rsm], writes=[Rsm])
                for hl in range(2):
                    h = 2 * hh + hl
                    k.op("dve", lambda e: e.tensor_scalar(out=khat[:, hl * 128:(hl + 1) * 128], in0=ktok[:, i, hl * 128:(hl + 1) * 128],
                                                          scalar1=sm[:, 12 + h:13 + h], scalar2=None, op0=ALU.mult), reads=[Rkt[i], Rsm], writes=[Rkh])
                for hl in range(2):
                    h = 2 * hh + hl
                    vs = vtok[:, i, hl * 256:(hl + 1) * 256]
                    qt = qT[:, hl, i * 128:(i + 1) * 128]
                    ps, rps = self.next_ps()
                    k.op("pe", lambda e: e.matmul(ps[:, 0:128], lhsT=kT[:, hl, i * 128:(i + 1) * 128], rhs=qt, start=True, stop=True), reads=[RqT, RkT], writes=[rps])
                    s_, rs_ = scT[hl], Rsc[hl]
                    k.op("dve", lambda e: e.scalar_tensor_tensor(out=s_[:], in0=ps[:, 0:128], scalar=sm[:, h:h + 1], in1=mcum, op0=ALU.mult, op1=ALU.mult),
                         reads=[rps, Rsm, self.Rmask], writes=[rs_])
                    po, rpo = self.next_ps()
                    k.op("pe", lambda e: e.matmul(po[:, 0:256], lhsT=s_[:], rhs=vs, start=True, stop=False), reads=[rs_, Rvt[i]], writes=[rpo])
                    k.op("pe", lambda e: e.matmul(po[:, 0:256], lhsT=qt, rhs=Sbf[hl][:, 0:256], start=False, stop=True), reads=[RqT, RS[hl]], writes=[rpo])
                    k.op("pe", lambda e: e.matmul(po[:, 256:257], lhsT=s_[:], rhs=onesb[:], start=True, stop=False), reads=[rs_, Rgn], writes=[rpo])
                    k.op("pe", lambda e: e.matmul(po[:, 256:257], lhsT=qt, rhs=Sbf[hl][:, 256:257], start=False, stop=True), reads=[RqT, RS[hl]], writes=[rpo])
                    dn = sm[:, 16 + hl:17 + hl]
                    k.op("act", lambda e: e.activation(out=dn, in_=po[:, 256:257], func=AF.Abs, scale=sm[:, 4 + h:5 + h]), reads=[rpo, Rsm], writes=[Rsm])
                    k.op("dve", lambda e: e.tensor_scalar(out=dn, in0=dn, scalar1=1.0, scalar2=None, op0=ALU.max), reads=[Rsm], writes=[Rsm])
                    k.op("dve", lambda e: e.reciprocal(out=dn, in_=dn), reads=[Rsm], writes=[Rsm])
                    k.op("dve", lambda e: e.tensor_tensor(out=dn, in0=dn, in1=sm[:, 4 + h:5 + h], op=ALU.mult), reads=[Rsm], writes=[Rsm])
                    if d == 0:
                        k.op("act", lambda e: e.activation(out=Oacc[:, i, hl * 256:(hl + 1) * 256], in_=po[:, 0:256], func=AF.Identity, scale=dn), reads=[rpo, Rsm], writes=[ROa[i]])
                    else:
                        k.op("dve", lambda e: e.scalar_tensor_tensor(out=Ofin[:, hl * 256:(hl + 1) * 256], in0=po[:, 0:256], scalar=dn, in1=Oacc[:, i, hl * 256:(hl + 1) * 256],
                                                                     op0=ALU.mult, op1=ALU.add), reads=[rpo, Rsm, ROa[i]], writes=[ROf])
                    pd, rpd = self.next_ps()
                    k.op("pe", lambda e: e.matmul(pd[:, 0:256], lhsT=khat[:, hl * 128:(hl + 1) * 128], rhs=vs, start=True, stop=True), reads=[Rkh, Rvt[i]], writes=[rpd])
                    k.op("pe", lambda e: e.matmul(pd[:, 256:257], lhsT=khat[:, hl * 128:(hl + 1) * 128], rhs=onesb[:], start=True, stop=True), reads=[Rkh, Rgn], writes=[rpd])
                    k.op("dve", lambda e: e.scalar_tensor_tensor(out=S[hl][:, 0:257], in0=S[hl][:, 0:257], scalar=sm[:, 8 + h:9 + h], in1=pd[:, 0:257], op0=ALU.mult, op1=ALU.add),
                         reads=[RS[hl], Rsm, rpd], writes=[RS[hl]])
                    k.op("act", lambda e: e.activation(out=Sbf[hl][:, 0:257], in_=S[hl][:, 0:257], func=AF.Identity), reads=[RS[hl]], writes=[RS[hl]])
                if i in seg_last[d]:
                    k.op("dve", lambda e: e.tensor_scalar(out=dg[:], in0=self.identf[0:4, 0:4], scalar1=mcur[:, 0:1], scalar2=None, op0=ALU.mult), reads=[Rm, self.Rid], writes=[Rm])
                    pm, rpm = self.next_ps()
                    k.op("pe", lambda e: e.matmul(pm[:, 0:4], lhsT=ones128[0:4, :], rhs=dg[:], start=True, stop=True), reads=[Rm, Rgn], writes=[rpm])
                    k.op("act", lambda e: e.activation(out=sm[:, 20:24], in_=pm[:, 0:4], func=AF.Exp, scale=-1.0), reads=[rpm], writes=[Rsm])
                    if hh == 0:
                        k.dma("sp", "mmo", self.dout["o_mm"][j, seg, d:d + 1, :].rearrange("o h -> h o"), mcur[:, 0:1], reads=[Rm])
                    for hl in range(2):
                        h = 2 * hh + hl
                        k.op("dve", lambda e: e.tensor_scalar(out=stage[:, hl * 260:hl * 260 + 257], in0=S[hl][:, 0:257], scalar1=sm[:, 20 + h:21 + h], scalar2=None, op0=ALU.mult),
                             reads=[RS[hl], Rsm], writes=[Rstage])
                        k.dma("sp", "mco%d" % hl, self.dout["o_mc"][j, seg, d, h], stage[:, hl * 260:hl * 260 + 256], reads=[Rstage])
                        k.dma("sp", "mno%d" % hl, self.dout["o_mn"][j, seg, d, h].rearrange("(p o) -> p o", o=1), stage[:, hl * 260 + 256:hl * 260 + 257], reads=[Rstage])
                if d == 1:
                    for hl in range(2):
                        gcol = (2 * hh + hl) * 256
                        rs = self.rstd_of(Ofin[:, hl * 256:(hl + 1) * 256], ROf, 8 + hl, 256, junk=self.tmpf[1][:, 0:256], rjunk=self.Rtmpf[1])
                        k.op("dve", lambda e: e.scalar_tensor_tensor(out=Ofin[:, hl * 256:(hl + 1) * 256], in0=Ofin[:, hl * 256:(hl + 1) * 256], scalar=rs,
                                                                     in1=gnb[:, gcol:gcol + 256], op0=ALU.mult, op1=ALU.mult),
                             reads=[ROf, self.Rss[8 + hl], Rgn], writes=[ROf])
                    w, wres = ggw
                    ps, rps = self.next_ps()
                    for kc in range(8):
                        k.op("pe", lambda e, kc=kc: e.matmul(ps[:, :], lhsT=self.hT[:, kc, i * 128:(i + 1) * 128], rhs=w[:, kc, :], start=(kc == 0), stop=(kc == 7)),
                             reads=wres + [self.RhT[i]], writes=[rps])
                    te = self.tmpf[1][:, 512:1024]
                    k.op("act", lambda e: e.activation(out=te, in_=ps[:, :], func=AF.Sigmoid), reads=[rps], writes=[self.Rtmpf[1]])
                    k.op("dve", lambda e: e.tensor_tensor(out=yb[:, 0:512], in0=Ofin[:, 0:512], in1=te, op=ALU.mult), reads=[ROf, self.Rtmpf[1]], writes=[Ryb])
                    self.transpose_to(yb, Ryb, yT[:, 8 + hh * 4:8 + (hh + 1) * 4, i * 128:(i + 1) * 128], RyT[i], ncol=4)
        k.barrier()


Prog.mlstm = _mlstm


def _mixer_ab(self, l):
    k = self.k
    j = l // 2
    self.mix_consts()
    Wv = self.din["w_in_ab"][j].rearrange("(kc p) n -> p kc n", p=128)
    yT = self.MA[:, 0:16384].rearrange("p (c t) -> p c t", c=16)
    RyT = [Res() for _ in range(NT)]
    self.rs_ssd = self.scr("rs_ssd", [128, NT])
    self.Rrs = Res()
    if "ssd" in self.parts:
        self.ssd(l, j, Wv, yT, RyT)
    else:
        k.op("dve", lambda e: e.memset(yT[:, 0:8, :], 0.0), writes=RyT)
        k.op("dve", lambda e: e.memset(self.rs_ssd[:], 1.0), writes=[self.Rrs])
    k.barrier()
    if "rw" in self.parts:
        self.rwkv(l, j, Wv, yT, RyT)
    else:
        k.op("dve", lambda e: e.memset(yT[:, 8:16, :], 0.0), writes=RyT)
    k.barrier()
    self.out_proj(self.din["w_out_ab"][j], yT, RyT, row_scale=(self.rs_ssd, self.Rrs))


def _out_proj(self, Wd, yT, RyT, gi=2, row_scale=None):
    k = self.k
    wv = Wd.rearrange("(kc p) n -> p kc n", p=128)
    Fall = self.MA[:, 16384:32768].bitcast(F32).rearrange("p (i n) -> p i n", i=NT)
    RF = [Res() for _ in range(NT)]
    for nh in range(2):
        wap, wres = self.wslots(2)
        w = wap.rearrange("p (kc n) -> p kc n", kc=16)
        for q in range(2):
            k.dma("pool", self.wch, w[:, q * 8:(q + 1) * 8, :], wv[:, q * 8:(q + 1) * 8, nh * 512:(nh + 1) * 512], writes=wres)
        for i in range(NT):
            Fi = Fall[:, i, nh * 512:(nh + 1) * 512]
            if row_scale is None:
                ps, rps = self.next_ps()
                for kc in range(16):
                    k.op("pe", lambda e, kc=kc: e.matmul(ps[:, :], lhsT=yT[:, kc, i * 128:(i + 1) * 128], rhs=w[:, kc, :],
                                                         start=(kc == 0), stop=(kc == 15)), reads=wres + [RyT[i]], writes=[rps])
                k.op("act", lambda e: e.activation(out=Fi, in_=ps[:, :], func=AF.Identity), reads=[rps], writes=[RF[i]])
            else:
                rsc, rrs = row_scale
                ps2, rps2 = self.next_ps()
                for kc in range(8, 16):
                    k.op("pe", lambda e, kc=kc: e.matmul(ps2[:, :], lhsT=yT[:, kc, i * 128:(i + 1) * 128], rhs=w[:, kc, :],
                                                         start=(kc == 8), stop=(kc == 15)), reads=wres + [RyT[i]], writes=[rps2])
                k.op("act", lambda e: e.activation(out=Fi, in_=ps2[:, :], func=AF.Identity), reads=[rps2], writes=[RF[i]])
                ps1, rps1 = self.next_ps()
                for kc in range(8):
                    k.op("pe", lambda e, kc=kc: e.matmul(ps1[:, :], lhsT=yT[:, kc, i * 128:(i + 1) * 128], rhs=w[:, kc, :],
                                                         start=(kc == 0), stop=(kc == 7)), reads=wres + [RyT[i]], writes=[rps1])
                k.op("dve", lambda e: e.scalar_tensor_tensor(out=Fi, in0=ps1[:, :], scalar=rsc[:, i:i + 1], in1=Fi, op0=ALU.mult, op1=ALU.add),
                     reads=[rps1, rrs, RF[i]], writes=[RF[i]])
    for i in range(NT):
        self.resid_add(i, Fall[:, i, :], RF[i], gi)


def _ssd(self, l, j, Wv, yT, RyT):
    k = self.k
    MA = self.MA
    xT = MA[:, 16384:18432].rearrange("p (c t) -> p c t", c=2)
    BT = MA[:, 18432:19456]
    CT = MA[:, 19456:20480]
    xs = MA[:, 20480:22528].rearrange("p (i c) -> p i c", i=NT)
    Btok = MA[:, 22528:23552].rearrange("p (i c) -> p i c", i=NT)
    Oacc = MA[:, 23552:25600].rearrange("p (i c) -> p i c", i=NT)
    dt = self.scr("sdt", [128, NT, 32])
    la = self.scr("sla", [128, NT, 32])
    cst = self.scr("scst", [128, 96])
    Rdt = Res()
    k.dma("sp", "sc0", cst[:, 0:32], self.din["ssd_dt_bias"][j:j + 1].rearrange("o d h -> o (d h)").partition_broadcast(128), writes=[Rdt])
    k.dma("sp", "sc0", cst[:, 32:64], self.din["ssd_a_log"][j:j + 1].rearrange("o d h -> o (d h)").partition_broadcast(128), writes=[Rdt])
    k.dma("sp", "sc0", cst[:, 64:80], self.din["ssd_d"][j:j + 1, :].partition_broadcast(128), writes=[Rdt])
    k.op("act", lambda e: e.activation(out=cst[:, 32:64], in_=cst[:, 32:64], func=AF.Exp), reads=[Rdt], writes=[Rdt])
    self.proj_tok(Wv, 3072, 32, lambda i, ps, rps: k.op(
        "dve", lambda e: e.tensor_tensor(out=dt[:, i, :], in0=ps, in1=cst[:, 0:32], op=ALU.add), reads=[rps, Rdt], writes=[Rdt]))
    k.op("act", lambda e: e.activation(out=dt[:], in_=dt[:], func=AF.Exp), reads=[Rdt], writes=[Rdt])
    k.op("act", lambda e: e.activation(out=dt[:], in_=dt[:], func=AF.Ln, bias=self.onesc[:, 0:1]), reads=[Rdt, self.Rmask], writes=[Rdt])
    k.op("dve", lambda e: e.scalar_tensor_tensor(out=la[:], in0=dt[:], scalar=-1.0, in1=cst[:, 32:64].unsqueeze(1).to_broadcast([128, NT, 32]),
                                                 op0=ALU.mult, op1=ALU.mult), reads=[Rdt], writes=[Rdt])
    nw = self.scr("snw", [128, 8])
    k.dma("sp", "sc1", nw[:], self.din["ssd_nw"][j], writes=[Rdt])
    ones128 = self.scr("ones128", [128, 128])
    k.op("dve", lambda e: e.memset(ones128[:], 1.0), writes=[Rdt])
    ssq = self.scr("sssq", [128, NT, 4])
    k.op("dve", lambda e: e.memset(ssq[:], 0.0), writes=[self.Rrs])
    S = self.scr("S0", [128, 256]); Sbf = self.scr("Sb0", [128, 256], BF16)
    st = self.scr("sst", [128, 80])
    CBm = self.scr("sCBm", [128, 128])
    seg_ = [self.scr("sseg%d" % i, [128, 128]) for i in range(2)]
    scT = [self.scr("scT%d" % i, [128, 128], BF16) for i in range(2)]
    khat = self.scr("khat", [128, 256], BF16)
    Ofin = self.junk; ROf = self.Rjunk
    tin = self.tmpf[0]; Rtin = self.Rtmpf[0]
    zs = self.tmpf[1]; Rzs = self.Rtmpf[1]
    yb = self.hb[0]; Ryb = self.Rhb[0]
    seg_first = {0: [0, 2, 4, 6], 1: [7, 5, 3, 1]}
    seg_last = {0: [1, 3, 5, 7], 1: [6, 4, 2, 0]}
    for g in range(4):
        RxT, RBT, RCT = Res(), Res(), Res()
        Rxs = [Res() for _ in range(NT)]
        RBt = [Res() for _ in range(NT)]
        ROa = [Res() for _ in range(NT)]
        RS = Res(); Rst = Res(); RCB = Res(); Rseg = [Res(), Res()]; Rsc = [Res(), Res()]; Rkh = Res()
        C = self.conv_setup(25600)
        cw, cb = self.din["ssd_cw"], self.din["ssd_cb"]
        for c in range(2):
            ch = 2 * g + c
            self.conv_chunk(C, Wv, 1024 + ch * 128, cw[j, ch], cb[j, ch], xT[:, c, :], RxT)
        self.conv_chunk(C, Wv, 1024 + (8 + g) * 128, cw[j, 8 + g], cb[j, 8 + g], BT, RBT)
        self.conv_chunk(C, Wv, 1024 + (12 + g) * 128, cw[j, 12 + g], cb[j, 12 + g], CT, RCT)
        k.barrier()
        for i in range(NT):
            pb, rpb = self.next_pb()
            for c in range(2):
                k.op("pe", lambda e: e.transpose(pb[:, c * 128:(c + 1) * 128], xT[:, c, i * 128:(i + 1) * 128], self.identb[:]), reads=[RxT, self.Rid], writes=[rpb])
            k.op("pe", lambda e: e.transpose(pb[:, 256:384], BT[:, i * 128:(i + 1) * 128], self.identb[:]), reads=[RBT, self.Rid], writes=[rpb])
            k.op("act", lambda e: e.activation(out=xs[:, i, :], in_=pb[:, 0:256], func=AF.Identity), reads=[rpb], writes=[Rxs[i]])
            k.op("act", lambda e: e.activation(out=Btok[:, i, :], in_=pb[:, 256:384], func=AF.Identity), reads=[rpb], writes=[RBt[i]])
        for d in range(2):
            order = list(range(NT)) if d == 0 else list(range(NT - 1, -1, -1))
            mcum = self.masks[:, d, :]
            if d == 1:
                zw = self.load_w(Wv, g * 256, 256)
            for ci, i in enumerate(order):
                sgi = i // 2
                if i in seg_first[d]:
                    if ci == 0:
                        for hl in range(4):
                            k.dma("sp", "sst%d" % hl, S[:, hl * 64:(hl + 1) * 64], self.din["st_ssd"][j, d, 4 * g + hl], writes=[RS])
                    else:
                        k.op("dve", lambda e: e.tensor_scalar(out=S[:], in0=S[:], scalar1=self.flags[:, 0:1], scalar2=None, op0=ALU.mult), reads=[RS, self.Rmask], writes=[RS])
                    k.op("act", lambda e: e.activation(out=Sbf[:], in_=S[:], func=AF.Identity), reads=[RS], writes=[RS])
                lad = la[:, i, d * 16:(d + 1) * 16]
                dtd = dt[:, i, d * 16:(d + 1) * 16]
                ps, rps = self.next_ps()
                k.op("pe", lambda e: e.matmul(ps[:, 0:16], lhsT=mcum, rhs=lad, start=True, stop=True), reads=[Rdt, self.Rmask], writes=[rps])
                k.op("pe", lambda e: e.matmul(ps[:, 16:32], lhsT=ones128[:], rhs=lad, start=True, stop=True), reads=[Rdt], writes=[rps])
                k.op("act", lambda e: e.activation(out=st[:, 0:16], in_=ps[:, 0:16], func=AF.Identity), reads=[rps], writes=[Rst])
                k.op("dve", lambda e: e.tensor_tensor(out=st[:, 16:32], in0=ps[:, 16:32], in1=st[:, 0:16], op=ALU.subtract), reads=[rps, Rst], writes=[Rst])
                k.op("act", lambda e: e.activation(out=st[:, 16:32], in_=st[:, 16:32], func=AF.Exp), reads=[Rst], writes=[Rst])
                k.op("dve", lambda e: e.tensor_tensor(out=st[:, 16:32], in0=st[:, 16:32], in1=dtd, op=ALU.mult), reads=[Rst, Rdt], writes=[Rst])
                k.op("act", lambda e: e.activation(out=st[:, 32:48], in_=ps[:, 16:32], func=AF.Exp), reads=[rps], writes=[Rst])
                k.op("act", lambda e: e.activation(out=st[:, 48:64], in_=st[:, 0:16], func=AF.Exp), reads=[Rst], writes=[Rst])
                ps, rps = self.next_ps()
                k.op("pe", lambda e: e.matmul(ps[:, 0:128], lhsT=BT[:, i * 128:(i + 1) * 128], rhs=CT[:, i * 128:(i + 1) * 128], start=True, stop=True), reads=[RBT, RCT], writes=[rps])
                k.op("dve", lambda e: e.tensor_tensor(out=CBm[:], in0=ps[:, 0:128], in1=mcum, op=ALU.mult), reads=[rps, self.Rmask], writes=[RCB])
                po, rpo = self.next_ps(pin=True)
                pd, rpd = self.next_ps(pin=True)
                for hl in range(4):
                    h = 4 * g + hl
                    pbt, rpbt = self.next_ps()
                    k.op("pe", lambda e: e.matmul(pbt[:, 0:128], lhsT=la[:, i, d * 16 + h:d * 16 + h + 1].to_broadcast([128, 128]), rhs=mcum, start=True, stop=True),
                         reads=[Rdt, self.Rmask], writes=[rpbt])
                    sg, rsg = seg_[hl % 2], Rseg[hl % 2]
                    k.op("dve", lambda e: e.tensor_scalar(out=sg[:], in0=pbt[:, 0:128], scalar1=st[:, h:h + 1], scalar2=0.0, op0=ALU.subtract, op1=ALU.min),
                         reads=[rpbt, Rst], writes=[rsg])
                    k.op("act", lambda e: e.activation(out=sg[:], in_=sg[:], func=AF.Exp), reads=[rsg], writes=[rsg])
                    s_, rs_ = scT[hl % 2], Rsc[hl % 2]
                    k.op("dve", lambda e: e.scalar_tensor_tensor(out=s_[:], in0=sg[:], scalar=dt[:, i, d * 16 + h:d * 16 + h + 1], in1=CBm[:], op0=ALU.mult, op1=ALU.mult),
                         reads=[rsg, Rdt, RCB], writes=[rs_])
                    k.op("pe", lambda e: e.matmul(po[:, hl * 64:(hl + 1) * 64], lhsT=s_[:], rhs=xs[:, i, hl * 64:(hl + 1) * 64], start=True, stop=True), reads=[rs_, Rxs[i]], writes=[rpo])
                    k.op("dve", lambda e: e.tensor_scalar(out=khat[:, (hl % 2) * 128:(hl % 2 + 1) * 128], in0=Btok[:, i, :], scalar1=st[:, 16 + h:17 + h], scalar2=None, op0=ALU.mult),
                         reads=[RBt[i], Rst], writes=[Rkh])
                    k.op("pe", lambda e: e.matmul(pd[:, hl * 64:(hl + 1) * 64], lhsT=khat[:, (hl % 2) * 128:(hl % 2 + 1) * 128], rhs=xs[:, i, hl * 64:(hl + 1) * 64], start=True, stop=True),
                         reads=[Rkh, Rxs[i]], writes=[rpd])
                pi_, rpi = self.next_ps()
                k.op("pe", lambda e: e.matmul(pi_[:, 0:256], lhsT=CT[:, i * 128:(i + 1) * 128], rhs=Sbf[:], start=True, stop=True), reads=[RCT, RS], writes=[rpi])
                ebx = st[:, 48 + 4 * g:52 + 4 * g].unsqueeze(2).to_broadcast([128, 4, 64])
                k.op("dve", lambda e: e.tensor_tensor(out=tin[:, 0:256].rearrange("p (h v) -> p h v", h=4), in0=pi_[:, 0:256].rearrange("p (h v) -> p h v", h=4), in1=ebx, op=ALU.mult),
                     reads=[rpi, Rst], writes=[Rtin])
                if d == 0:
                    k.op("dve", lambda e: e.tensor_tensor(out=Oacc[:, i, :], in0=po[:, 0:256], in1=tin[:, 0:256], op=ALU.add), reads=[rpo, Rtin], writes=[ROa[i]])
                else:
                    k.op("dve", lambda e: e.tensor_tensor(out=Ofin[:, 0:256], in0=po[:, 0:256], in1=tin[:, 0:256], op=ALU.add), reads=[rpo, Rtin], writes=[ROf])
                    k.op("dve", lambda e: e.tensor_tensor(out=Ofin[:, 0:256], in0=Ofin[:, 0:256], in1=Oacc[:, i, :], op=ALU.add), reads=[ROf, ROa[i]], writes=[ROf])
                Gx = st[:, 32 + 4 * g:36 + 4 * g].unsqueeze(2).to_broadcast([128, 4, 64])
                k.op("dve", lambda e: e.tensor_tensor(out=S[:].rearrange("p (h v) -> p h v", h=4), in0=S[:].rearrange("p (h v) -> p h v", h=4), in1=Gx, op=ALU.mult), reads=[RS, Rst], writes=[RS])
                k.op("dve", lambda e: e.tensor_tensor(out=S[:], in0=S[:], in1=pd[:, 0:256], op=ALU.add), reads=[RS, rpd], writes=[RS])
                k.op("act", lambda e: e.activation(out=Sbf[:], in_=S[:], func=AF.Identity), reads=[RS], writes=[RS])
                self.unpin(rpo, rpd)
                if i in seg_last[d]:
                    for hl in range(4):
                        k.dma("sp", "sso%d" % hl, self.dout["o_ssd"][j, sgi, d, 4 * g + hl], S[:, hl * 64:(hl + 1) * 64], reads=[RS])
                if d == 1:
                    Dx = cst[:, 64 + 4 * g:68 + 4 * g].unsqueeze(2).to_broadcast([128, 4, 64])
                    k.op("dve", lambda e: e.tensor_tensor(out=tin[:, 256:512].rearrange("p (h v) -> p h v", h=4), in0=xs[:, i, :].rearrange("p (h v) -> p h v", h=4), in1=Dx, op=ALU.mult),
                         reads=[Rxs[i], Rdt], writes=[Rtin])
                    k.op("dve", lambda e: e.tensor_tensor(out=Ofin[:, 0:256], in0=Ofin[:, 0:256], in1=tin[:, 256:512], op=ALU.add), reads=[ROf, Rtin], writes=[ROf])
                    w, wres = zw
                    ps, rps = self.next_ps()
                    for kc in range(8):
                        k.op("pe", lambda e, kc=kc: e.matmul(ps[:, 0:256], lhsT=self.hT[:, kc, i * 128:(i + 1) * 128], rhs=w[:, kc, :], start=(kc == 0), stop=(kc == 7)),
                             reads=wres + [self.RhT[i]], writes=[rps])
                    k.op("act", lambda e: e.activation(out=zs[:, 0:256], in_=ps[:, 0:256], func=AF.Silu), reads=[rps], writes=[Rzs])
                    k.op("dve", lambda e: e.tensor_tensor(out=Ofin[:, 0:256], in0=Ofin[:, 0:256], in1=zs[:, 0:256], op=ALU.mult), reads=[ROf, Rzs], writes=[ROf])
                    k.op("act", lambda e: e.activation(out=zs[:, 256:512], in_=Ofin[:, 0:256], func=AF.Square, accum_out=ssq[:, i, g:g + 1]), reads=[ROf, self.Rrs], writes=[Rzs, self.Rrs])
                    k.op("act", lambda e: e.activation(out=yb[:, 0:256], in_=Ofin[:, 0:256], func=AF.Identity), reads=[ROf], writes=[Ryb])
                    pb, rpb = self.next_pb()
                    for c in range(2):
                        k.op("pe", lambda e: e.transpose(pb[:, c * 128:(c + 1) * 128], yb[:, c * 128:(c + 1) * 128], self.identb[:]), reads=[Ryb, self.Rid], writes=[rpb])
                    for c in range(2):
                        kc = 2 * g + c
                        k.op("act", lambda e: e.activation(out=yT[:, kc, i * 128:(i + 1) * 128], in_=pb[:, c * 128:(c + 1) * 128], func=AF.Identity, scale=nw[:, kc:kc + 1]),
                             reads=[rpb, Rdt], writes=[RyT[i]])
        k.barrier()
    rs = self.rs_ssd
    k.op("dve", lambda e: e.tensor_reduce(out=rs[:], in_=ssq[:], axis=AX.X, op=ALU.add), reads=[self.Rrs], writes=[self.Rrs])
    k.op("act", lambda e: e.activation(out=rs[:], in_=rs[:], func=AF.Ln, scale=1.0 / 1024, bias=self.epsb[:, 0:1]), reads=[self.Rrs, self.Rid], writes=[self.Rrs])
    k.op("act", lambda e: e.activation(out=rs[:], in_=rs[:], func=AF.Exp, scale=-0.5), reads=[self.Rrs], writes=[self.Rrs])


Prog.mixer_ab = _mixer_ab
Prog.out_proj = _out_proj
Prog.ssd = _ssd


def _rwkv(self, l, j, Wv, yT, RyT):
    k = self.k
    MA = self.MA
    B0 = 3104
    f32v = lambda a, b: MA[:, a:b].bitcast(F32)
    twT = MA[:, 16384:17408]; adT = MA[:, 17408:18432]; sgT = MA[:, 18432:19456]
    rT = f32v(19456, 21504); kT = f32v(21504, 23552); vT = f32v(23552, 25600); kkT = f32v(25600, 27648)
    Vtok = f32v(27648, 29696).rearrange("p (i c) -> p i c", i=NT)
    Yacc = MA[:, 29696:30720].rearrange("p (i c) -> p i c", i=NT)
    bonT = MA[:, 30720:31744]
    raw = f32v(31744, 33796)
    tsm = MA[:, 33796:35844].rearrange("p (a t) -> p a t", a=2)
    t1, Rt1 = self.tmpf[0], self.Rtmpf[0]
    t2, Rt2 = self.tmpf[1], self.Rtmpf[1]
    Rsh = Res(); Rraw = Res(); Rvec = Res()
    vec = self.scr("rwvec", [128, 72])
    mu = self.scr("rwmu", [128, 56])
    cst = self.scr("rwc", [128, 4])
    k.dma("sp", "rv0", vec[:], self.din["rw_vec"][j], writes=[Rvec])
    k.dma("sp", "rv0", mu[:, 28:55], self.din["rw_mu"][j], writes=[Rvec])
    k.op("dve", lambda e: e.tensor_scalar(out=mu[:, 0:27], in0=mu[:, 28:55], scalar1=-1.0, scalar2=1.0, op0=ALU.mult, op1=ALU.add), reads=[Rvec], writes=[Rvec])
    k.op("dve", lambda e: e.tensor_scalar(out=mu[:, 28:55], in0=mu[:, 28:55], scalar1=0.5, scalar2=None, op0=ALU.mult), reads=[Rvec], writes=[Rvec])
    k.op("dve", lambda e: e.memset(cst[:, 0:1], 1e-12), writes=[Rvec])
    k.op("dve", lambda e: e.memset(cst[:, 1:2], -0.5), writes=[Rvec])
    k.op("dve", lambda e: e.memset(cst[:, 2:3], 64e-5), writes=[Rvec])
    nvec = self.scr("rwnvec", [128, 24])
    k.op("dve", lambda e: e.tensor_scalar(out=nvec[:, 0:16], in0=vec[:, 0:16], scalar1=-1.0, scalar2=None, op0=ALU.mult), reads=[Rvec], writes=[Rvec])
    k.op("dve", lambda e: e.tensor_scalar(out=nvec[:, 16:24], in0=vec[:, 40:48], scalar1=-1.0, scalar2=1.0, op0=ALU.mult, op1=ALU.add), reads=[Rvec], writes=[Rvec])
    k.op("dve", lambda e: e.memset(raw[:, 0:1], 0.0), writes=[Rraw])
    k.op("dve", lambda e: e.memset(raw[:, 1025:1026], 0.0), writes=[Rraw])
    for a in range(2):
        k.dma("pool", "tsm%d" % a, tsm[:, a, :], self.din["tsm"][a:a + 1, :].partition_broadcast(128), writes=[Rsh])
    ones128 = self.scr("ones128", [128, 128])
    k.op("dve", lambda e: e.memset(ones128[:], 1.0), writes=[Rvec])
    BD = self.masks[:, 4, :]

    def rw_block(cb, post):
        self.proj_feat(Wv, B0 + cb * 128, 128, lambda c, half, ps, rps, m: k.op(
            "act", lambda e: e.activation(out=raw[:, 1 + half * 512:1 + (half + 1) * 512], in_=ps, func=AF.Identity), reads=[rps], writes=[Rraw]))
        k.op("dve", lambda e: e.tensor_tensor(out=t1[:], in0=raw[:, 0:1024], in1=tsm[:, 0, :], op=ALU.mult), reads=[Rraw, Rsh], writes=[Rt1])
        k.op("dve", lambda e: e.tensor_tensor(out=t2[:], in0=raw[:, 2:1026], in1=tsm[:, 1, :], op=ALU.mult), reads=[Rraw, Rsh], writes=[Rt2])
        k.op("dve", lambda e: e.tensor_tensor(out=t1[:], in0=t1[:], in1=t2[:], op=ALU.add), reads=[Rt1, Rt2], writes=[Rt1])
        k.op("dve", lambda e: e.tensor_scalar(out=t2[:], in0=raw[:, 1:1025], scalar1=mu[:, cb:cb + 1], scalar2=None, op0=ALU.mult), reads=[Rraw, Rvec], writes=[Rt2])
        k.op("dve", lambda e: e.scalar_tensor_tensor(out=t1[:], in0=t1[:], scalar=mu[:, 28 + cb:29 + cb], in1=t2[:], op0=ALU.mult, op1=ALU.add), reads=[Rt1, Rt2, Rvec], writes=[Rt1])
        post()

    Rlo = Res()
    rw_block(24, lambda: k.op("act", lambda e: e.activation(out=twT, in_=t1[:], func=AF.Tanh), reads=[Rt1], writes=[Rlo]))
    rw_block(25, lambda: k.op("act", lambda e: e.activation(out=adT, in_=t1[:], func=AF.Identity), reads=[Rt1], writes=[Rlo]))
    rw_block(26, lambda: k.op("act", lambda e: e.activation(out=sgT, in_=t1[:], func=AF.Sigmoid), reads=[Rt1], writes=[Rlo]))
    w2v = self.din["rwkv_w2"][j].rearrange("d r c -> (d r) c")
    a2v = self.din["rwkv_a2"][j].rearrange("d r c -> (d r) c")
    g2v = self.din["rwkv_g2"][j]
    lw3 = self.scr("rwlw3", [128, 3, 128], BF16)
    P = self.scr("rwP", [128, 64]); Z = self.scr("rwZ", [64, 128])
    U = self.scr("rwU", [128, 64]); RH = self.scr("rwRH", [128, 64])
    sm = self.scr("rwsm", [128, 16])
    slot = lambda n: (self.tmpf[0], self.tmpf[1], self.junk)[n // 8][:, (n % 8) * 128:(n % 8 + 1) * 128]
    QR = self.tmpf[0][:, 0:256]
    KT_ = slot(2); CT_ = slot(3); aT = slot(4); e2 = slot(5); cs = slot(6); csx = slot(7)
    Ep = slot(8); Em = slot(9); Ex = slot(10); kd = slot(11); cc = slot(12); Khat = slot(13); Chat = slot(14); KhT = slot(15)
    AkT = self.junk[:, 0:256]; AcT = self.junk[:, 256:512]; MT = slot(20); X = [slot(21), slot(22)]; ChT = slot(23)
    PP = [self.scr("rwPP%d" % i, [128, 256]) for i in range(2)]
    Ys = self.scr("rwYs", [128, 128]); yn = self.scr("rwyn", [128, 128])
    seg_first = {0: [0, 2, 4, 6], 1: [7, 5, 3, 1]}
    seg_last = {0: [1, 3, 5, 7], 1: [6, 4, 2, 0]}
    k.barrier()
    for hp in range(8):
        cp = slice(hp * 128, (hp + 1) * 128)
        Rr, Rk, Rv, Rkk, Rbon = Res(), Res(), Res(), Res(), Res()
        RVt = [Res() for _ in range(NT)]; RYa = [Res() for _ in range(NT)]
        Rlw = Res(); RP = Res(); RU = Res(); RRH = Res(); Rsm = Res(); RYs = Res(); Ryn = Res()
        Rs = {n: Res() for n in ("QR", "KT", "CT", "aT", "e2", "cs", "csx", "Ep", "Em", "Ex", "kd", "cc", "Khat", "Chat", "KhT", "ChT", "AkT", "AcT", "MT", "X0", "X1", "PP0", "PP1")}
        rw_block(hp, lambda: k.op("act", lambda e: e.activation(out=rT, in_=t1[:], func=AF.Identity), reads=[Rt1], writes=[Rr]))
        rw_block(8 + hp, lambda: k.op("act", lambda e: e.activation(out=kT, in_=t1[:], func=AF.Identity), reads=[Rt1], writes=[Rk]))
        rw_block(16 + hp, lambda: k.op("act", lambda e: e.activation(out=vT, in_=t1[:], func=AF.Identity), reads=[Rt1], writes=[Rv]))
        k.dma("pool", "lw3a", lw3[:, 0, :], w2v[:, cp], writes=[Rlw])
        k.dma("pool", "lw3b", lw3[:, 1, :], a2v[:, cp], writes=[Rlw])
        k.dma("pool", "lw3c", lw3[:, 2, :], g2v[:, cp], writes=[Rlw])
        k.op("dve", lambda e: e.tensor_scalar(out=kkT, in0=kT, scalar1=vec[:, 32 + hp:33 + hp], scalar2=None, op0=ALU.mult), reads=[Rk, Rvec], writes=[Rkk])
        k.op("dve", lambda e: e.tensor_tensor(out=t1[:], in0=kkT, in1=kkT, op=ALU.mult), reads=[Rkk], writes=[Rt1])
        for half in range(2):
            hs = slice(half * 512, (half + 1) * 512)
            ps, rps = self.next_ps()
            k.op("pe", lambda e: e.matmul(ps[:, :], lhsT=BD, rhs=t1[:, hs], start=True, stop=True), reads=[Rt1, self.Rmask], writes=[rps])
            k.op("act", lambda e: e.activation(out=t2[:, hs], in_=ps[:, :], func=AF.Ln, bias=cst[:, 0:1]), reads=[rps, Rvec], writes=[Rt2])
        k.op("act", lambda e: e.activation(out=t2[:], in_=t2[:], func=AF.Exp, scale=-0.5), reads=[Rt2], writes=[Rt2])
        k.op("dve", lambda e: e.tensor_tensor(out=kkT, in0=kkT, in1=t2[:], op=ALU.mult), reads=[Rkk, Rt2], writes=[Rkk])
        k.op("dve", lambda e: e.scalar_tensor_tensor(out=t1[:], in0=rT, scalar=vec[:, 48 + hp:49 + hp], in1=kT, op0=ALU.mult, op1=ALU.mult), reads=[Rr, Rk, Rvec], writes=[Rt1])
        for half in range(2):
            hs = slice(half * 512, (half + 1) * 512)
            ps, rps = self.next_ps()
            k.op("pe", lambda e: e.matmul(ps[:, :], lhsT=BD, rhs=t1[:, hs], start=True, stop=True), reads=[Rt1, self.Rmask], writes=[rps])
            k.op("dve", lambda e: e.tensor_tensor(out=bonT[:, hs], in0=ps[:, :], in1=vT[:, hs], op=ALU.mult), reads=[rps, Rv], writes=[Rbon])
        for i in range(NT):
            ps, rps = self.next_ps()
            k.op("pe", lambda e: e.transpose(ps[:, 0:128], vT[:, i * 128:(i + 1) * 128], self.identf[:]), reads=[Rv, self.Rid], writes=[rps])
            k.op("act", lambda e: e.activation(out=Vtok[:, i, :], in_=ps[:, 0:128], func=AF.Identity), reads=[rps], writes=[RVt[i]])
        k.barrier()
        for d in range(2):
            order = list(range(NT)) if d == 0 else list(range(NT - 1, -1, -1))
            ds_ = slice(d * 64, (d + 1) * 64)
            m_inc = self.masks[:, d, :]
            m_str = self.masks[:, 3 - d, :]
            m_strT = self.masks[:, 2 + d, :]
            endcol = 127 if d == 0 else 0
            for ci, i in enumerate(order):
                ts_ = slice(i * 128, (i + 1) * 128)
                sgi = i // 2
                if i in seg_first[d]:
                    if ci == 0:
                        for hl in range(2):
                            k.dma("sp", "rst%d" % hl, Z[:, hl * 64:(hl + 1) * 64], self.din["st_rw"][j, d, 2 * hp + hl], writes=[RP])
                        ps, rps = self.next_ps()
                        k.op("pe", lambda e: e.transpose(ps[:, 0:64], Z[:], self.identf[0:64, 0:64]), reads=[RP, self.Rid], writes=[rps])
                        k.op("act", lambda e: e.activation(out=P[:], in_=ps[:, 0:64], func=AF.Identity), reads=[rps], writes=[RP])
                    else:
                        k.op("dve", lambda e: e.tensor_scalar(out=P[:], in0=P[:], scalar1=self.flags[:, 0:1], scalar2=None, op0=ALU.mult), reads=[RP, self.Rmask], writes=[RP])
                ps, rps = self.next_ps()
                k.op("pe", lambda e: e.matmul(ps[:, 0:128], lhsT=lw3[ds_, 1, :], rhs=adT[ds_, ts_], start=True, stop=True), reads=[Rlw, Rlo], writes=[rps])
                k.op("pe", lambda e: e.matmul(ps[:, 128:256], lhsT=lw3[ds_, 0, :], rhs=twT[ds_, ts_], start=True, stop=True), reads=[Rlw, Rlo], writes=[rps])
                k.op("act", lambda e: e.activation(out=aT, in_=ps[:, 0:128], func=AF.Sigmoid, bias=vec[:, 16 + 8 * d + hp:17 + 8 * d + hp]), reads=[rps, Rvec], writes=[Rs["aT"]])
                k.op("act", lambda e: e.activation(out=e2, in_=ps[:, 128:256], func=AF.Exp, scale=-1.0, bias=nvec[:, 8 * d + hp:8 * d + hp + 1]), reads=[rps, Rvec], writes=[Rs["e2"]])
                k.op("act", lambda e: e.activation(out=e2, in_=e2, func=AF.Ln, bias=self.onesc[:, 0:1]), reads=[Rs["e2"], self.Rmask], writes=[Rs["e2"]])
                k.op("act", lambda e: e.activation(out=e2, in_=e2, func=AF.Exp, scale=-1.0, bias=cst[:, 1:2]), reads=[Rs["e2"], Rvec], writes=[Rs["e2"]])
                k.op("dve", lambda e: e.tensor_tensor_scan(out=cs, data0=ones128[:], data1=e2, initial=0.0, op0=ALU.mult, op1=ALU.add), reads=[Rs["e2"], Rvec], writes=[Rs["cs"]])
                if d == 1:
                    k.op("dve", lambda e: e.tensor_copy(out=sm[:, 0:1], in_=cs[:, 127:128]), reads=[Rs["cs"]], writes=[Rsm])
                    k.op("dve", lambda e: e.scalar_tensor_tensor(out=cs, in0=e2, scalar=sm[:, 0:1], in1=cs, op0=ALU.add, op1=ALU.subtract), reads=[Rs["e2"], Rs["cs"], Rsm], writes=[Rs["cs"]])
                k.op("dve", lambda e: e.tensor_tensor(out=csx, in0=cs, in1=e2, op=ALU.subtract), reads=[Rs["cs"], Rs["e2"]], writes=[Rs["csx"]])
                k.op("act", lambda e: e.activation(out=Ep, in_=cs, func=AF.Exp, scale=-1.0), reads=[Rs["cs"]], writes=[Rs["Ep"]])
                k.op("act", lambda e: e.activation(out=Em, in_=cs, func=AF.Exp), reads=[Rs["cs"]], writes=[Rs["Em"]])
                k.op("act", lambda e: e.activation(out=Ex, in_=csx, func=AF.Exp, scale=-1.0), reads=[Rs["csx"]], writes=[Rs["Ex"]])
                k.op("dve", lambda e: e.tensor_scalar(out=kd, in0=aT, scalar1=vec[:, 40 + hp:41 + hp], scalar2=nvec[:, 16 + hp:17 + hp], op0=ALU.mult, op1=ALU.add), reads=[Rs["aT"], Rvec], writes=[Rs["kd"]])
                k.op("dve", lambda e: e.tensor_tensor(out=kd, in0=kd, in1=kT[:, ts_], op=ALU.mult), reads=[Rs["kd"], Rk], writes=[Rs["kd"]])
                k.op("dve", lambda e: e.tensor_tensor(out=cc, in0=kkT[:, ts_], in1=aT, op=ALU.mult), reads=[Rkk, Rs["aT"]], writes=[Rs["cc"]])
                k.op("dve", lambda e: e.tensor_tensor(out=QR[:, 0:128], in0=kkT[:, ts_], in1=Ex, op=ALU.mult), reads=[Rkk, Rs["Ex"]], writes=[Rs["QR"]])
                k.op("dve", lambda e: e.tensor_tensor(out=QR[:, 128:256], in0=rT[:, ts_], in1=Ep, op=ALU.mult), reads=[Rr, Rs["Ep"]], writes=[Rs["QR"]])
                k.op("dve", lambda e: e.tensor_tensor(out=KT_, in0=kd, in1=Em, op=ALU.mult), reads=[Rs["kd"], Rs["Em"]], writes=[Rs["KT"]])
                k.op("dve", lambda e: e.tensor_tensor(out=CT_, in0=cc, in1=Em, op=ALU.mult), reads=[Rs["cc"], Rs["Em"]], writes=[Rs["CT"]])
                k.op("dve", lambda e: e.tensor_scalar(out=KhT, in0=KT_, scalar1=Ep[:, endcol:endcol + 1], scalar2=None, op0=ALU.mult), reads=[Rs["KT"], Rs["Ep"]], writes=[Rs["KhT"]])
                k.op("dve", lambda e: e.tensor_scalar(out=ChT, in0=CT_, scalar1=Ep[:, endcol:endcol + 1], scalar2=-1.0, op0=ALU.mult, op1=ALU.mult), reads=[Rs["CT"], Rs["Ep"]], writes=[Rs["ChT"]])
                ps, rps = self.next_ps()
                k.op("pe", lambda e: e.transpose(ps[:, 0:128], KhT, self.identf[:]), reads=[Rs["KhT"], self.Rid], writes=[rps])
                k.op("pe", lambda e: e.transpose(ps[:, 128:256], ChT, self.identf[:]), reads=[Rs["ChT"], self.Rid], writes=[rps])
                k.op("act", lambda e: e.activation(out=Khat, in_=ps[:, 0:128], func=AF.Identity), reads=[rps], writes=[Rs["Khat"]])
                k.op("act", lambda e: e.activation(out=Chat, in_=ps[:, 128:256], func=AF.Identity), reads=[rps], writes=[Rs["Chat"]])
                pdl, rpdl = self.next_ps(pin=True)
                pY, rpY = self.next_ps(pin=True)
                for hl in range(2):
                    hs_ = slice(hl * 64, (hl + 1) * 64)
                    Vh = Vtok[:, i, hs_]
                    ps, rps = self.next_ps()
                    k.op("pe", lambda e: e.matmul(ps[:, 0:256], lhsT=KT_[hs_, :], rhs=QR[hs_, :], start=True, stop=True), reads=[Rs["KT"], Rs["QR"]], writes=[rps])
                    k.op("dve", lambda e: e.tensor_tensor(out=AkT[:, 0:128], in0=ps[:, 0:128], in1=m_str, op=ALU.mult), reads=[rps, self.Rmask], writes=[Rs["AkT"]])
                    k.op("dve", lambda e: e.tensor_tensor(out=AkT[:, 128:256], in0=ps[:, 128:256], in1=m_inc, op=ALU.mult), reads=[rps, self.Rmask], writes=[Rs["AkT"]])
                    ps, rps = self.next_ps()
                    k.op("pe", lambda e: e.matmul(ps[:, 0:256], lhsT=CT_[hs_, :], rhs=QR[hs_, :], start=True, stop=True), reads=[Rs["CT"], Rs["QR"]], writes=[rps])
                    k.op("dve", lambda e: e.tensor_tensor(out=AcT[:, 0:128], in0=ps[:, 0:128], in1=m_str, op=ALU.mult), reads=[rps, self.Rmask], writes=[Rs["AcT"]])
                    k.op("dve", lambda e: e.scalar_tensor_tensor(out=AcT[:, 128:256], in0=ps[:, 128:256], scalar=-1.0, in1=m_inc, op0=ALU.mult, op1=ALU.mult), reads=[rps, self.Rmask], writes=[Rs["AcT"]])
                    ps, rps = self.next_ps()
                    k.op("pe", lambda e: e.matmul(ps[:, 0:128], lhsT=QR[hs_, 0:128], rhs=CT_[hs_, :], start=True, stop=True), reads=[Rs["CT"], Rs["QR"]], writes=[rps])
                    k.op("dve", lambda e: e.tensor_tensor(out=MT, in0=ps[:, 0:128], in1=m_strT, op=ALU.mult), reads=[rps, self.Rmask], writes=[Rs["MT"]])
                    M_ = AcT[:, 0:128]
                    Dm, rD = slot(4), Rs["aT"]
                    DTm, rDT = slot(5), Rs["e2"]
                    Wm, rW = slot(6), Rs["cs"]
                    WTm, rWT = slot(7), Rs["csx"]
                    XT = [slot(11), slot(12)]; rXT = [Rs["kd"], Rs["cc"]]
                    rX = [Rs["X0"], Rs["X1"]]
                    bd = lambda q: self.masks[:, 5 + q, :]
                    k.op("dve", lambda e: e.tensor_tensor(out=Dm, in0=M_, in1=bd(0), op=ALU.mult), reads=[Rs["AcT"], self.Rmask], writes=[rD])
                    k.op("dve", lambda e: e.tensor_tensor(out=DTm, in0=MT, in1=bd(0), op=ALU.mult), reads=[Rs["MT"], self.Rmask], writes=[rDT])
                    k.op("dve", lambda e: e.scalar_tensor_tensor(out=X[0], in0=Dm, scalar=-1.0, in1=self.identf[:], op0=ALU.mult, op1=ALU.add), reads=[rD, self.Rid], writes=[rX[0]])
                    k.op("dve", lambda e: e.scalar_tensor_tensor(out=XT[0], in0=DTm, scalar=-1.0, in1=self.identf[:], op0=ALU.mult, op1=ALU.add), reads=[rDT, self.Rid], writes=[rXT[0]])
                    xi = 0
                    curP, curPT, rcur = Dm, DTm, [rD, rDT]
                    for lev in range(RWLEV):
                        pp, rpp = PP[lev % 2], Rs["PP%d" % (lev % 2)]
                        ps, rps = self.next_ps()
                        k.op("pe", lambda e: e.matmul(ps[:, 0:128], lhsT=curPT, rhs=curP, start=True, stop=True), reads=rcur, writes=[rps])
                        k.op("pe", lambda e: e.matmul(ps[:, 128:256], lhsT=curP, rhs=curPT, start=True, stop=True), reads=rcur, writes=[rps])
                        k.op("act", lambda e: e.activation(out=pp[:], in_=ps[:, 0:256], func=AF.Identity), reads=[rps], writes=[rpp])
                        curP, curPT, rcur = pp[:, 0:128], pp[:, 128:256], [rpp]
                        ps2, rps2 = self.next_ps()
                        k.op("pe", lambda e: e.matmul(ps2[:, 0:128], lhsT=curPT, rhs=X[xi], start=True, stop=True), reads=[rpp, rX[xi]], writes=[rps2])
                        k.op("pe", lambda e: e.matmul(ps2[:, 128:256], lhsT=curP, rhs=XT[xi], start=True, stop=True), reads=[rpp, rXT[xi]], writes=[rps2])
                        k.op("dve", lambda e: e.tensor_tensor(out=X[1 - xi], in0=ps2[:, 0:128], in1=X[xi], op=ALU.add), reads=[rps2, rX[xi]], writes=[rX[1 - xi]])
                        k.op("dve", lambda e: e.tensor_tensor(out=XT[1 - xi], in0=ps2[:, 128:256], in1=XT[xi], op=ALU.add), reads=[rps2, rXT[xi]], writes=[rXT[1 - xi]])
                        xi = 1 - xi
                    for q in RWQ:
                        last = (q == 3)
                        k.op("dve", lambda e: e.tensor_tensor(out=DTm, in0=MT, in1=bd(q), op=ALU.mult), reads=[Rs["MT"], self.Rmask], writes=[rDT])
                        ps, rps = self.next_ps()
                        k.op("pe", lambda e: e.matmul(ps[:, 0:128], lhsT=DTm, rhs=X[xi], start=True, stop=True), reads=[rDT, rX[xi]], writes=[rps])
                        if not last:
                            k.op("dve", lambda e: e.tensor_tensor(out=Dm, in0=M_, in1=bd(q), op=ALU.mult), reads=[Rs["AcT"], self.Rmask], writes=[rD])
                            k.op("pe", lambda e: e.matmul(ps[:, 128:256], lhsT=Dm, rhs=XT[xi], start=True, stop=True), reads=[rD, rXT[xi]], writes=[rps])
                        k.op("act", lambda e: e.activation(out=Wm, in_=ps[:, 0:128], func=AF.Identity), reads=[rps], writes=[rW])
                        if not last:
                            k.op("act", lambda e: e.activation(out=WTm, in_=ps[:, 128:256], func=AF.Identity), reads=[rps], writes=[rWT])
                        ps2, rps2 = self.next_ps()
                        k.op("pe", lambda e: e.matmul(ps2[:, 0:128], lhsT=XT[xi], rhs=Wm, start=True, stop=True), reads=[rXT[xi], rW], writes=[rps2])
                        if not last:
                            k.op("pe", lambda e: e.matmul(ps2[:, 128:256], lhsT=X[xi], rhs=WTm, start=True, stop=True), reads=[rX[xi], rWT], writes=[rps2])
                        k.op("dve", lambda e: e.tensor_tensor(out=X[1 - xi], in0=X[xi], in1=ps2[:, 0:128], op=ALU.subtract), reads=[rps2, rX[xi]], writes=[rX[1 - xi]])
                        if not last:
                            k.op("dve", lambda e: e.tensor_tensor(out=XT[1 - xi], in0=XT[xi], in1=ps2[:, 128:256], op=ALU.subtract), reads=[rps2, rXT[xi]], writes=[rXT[1 - xi]])
                        xi = 1 - xi
                    TT, rTT = X[xi], rX[xi]
                    ps, rps = self.next_ps()
                    k.op("pe", lambda e: e.matmul(ps[:, 0:64], lhsT=QR[hs_, 0:128], rhs=P[hs_, :], start=True, stop=False), reads=[Rs["QR"], RP], writes=[rps])
                    k.op("pe", lambda e: e.matmul(ps[:, 0:64], lhsT=AkT[:, 0:128], rhs=Vh, start=False, stop=True), reads=[Rs["AkT"], RVt[i]], writes=[rps])
                    k.op("act", lambda e: e.activation(out=RH[:], in_=ps[:, 0:64], func=AF.Identity), reads=[rps], writes=[RRH])
                    ps, rps = self.next_ps()
                    k.op("pe", lambda e: e.matmul(ps[:, 0:64], lhsT=TT, rhs=RH[:], start=True, stop=True), reads=[rTT, RRH], writes=[rps])
                    k.op("act", lambda e: e.activation(out=U[:], in_=ps[:, 0:64], func=AF.Identity), reads=[rps], writes=[RU])
                    k.op("pe", lambda e: e.matmul(pY[:, hs_], lhsT=QR[hs_, 128:256], rhs=P[hs_, :], start=True, stop=False), reads=[Rs["QR"], RP], writes=[rpY])
                    k.op("pe", lambda e: e.matmul(pY[:, hs_], lhsT=AkT[:, 128:256], rhs=Vh, start=False, stop=False), reads=[Rs["AkT"], RVt[i]], writes=[rpY])
                    k.op("pe", lambda e: e.matmul(pY[:, hs_], lhsT=AcT[:, 128:256], rhs=U[:], start=False, stop=True), reads=[Rs["AcT"], RU], writes=[rpY])
                    k.op("pe", lambda e: e.matmul(pdl[hs_, 0:64], lhsT=Khat[:, hs_], rhs=Vh, start=True, stop=False), reads=[Rs["Khat"], RVt[i]], writes=[rpdl])
                    k.op("pe", lambda e: e.matmul(pdl[hs_, 0:64], lhsT=Chat[:, hs_], rhs=U[:], start=False, stop=True), reads=[Rs["Chat"], RU], writes=[rpdl])
                k.op("dve", lambda e: e.scalar_tensor_tensor(out=P[:], in0=P[:], scalar=Ep[:, endcol:endcol + 1], in1=pdl[:, 0:64], op0=ALU.mult, op1=ALU.add),
                     reads=[RP, Rs["Ep"], rpdl], writes=[RP])
                if d == 0:
                    k.op("act", lambda e: e.activation(out=Yacc[:, i, :], in_=pY[:, 0:128], func=AF.Identity), reads=[rpY], writes=[RYa[i]])
                else:
                    k.op("dve", lambda e: e.tensor_tensor(out=Ys[:], in0=pY[:, 0:128], in1=Yacc[:, i, :], op=ALU.add), reads=[rpY, RYa[i]], writes=[RYs])
                self.unpin(rpdl, rpY)
                if i in seg_last[d]:
                    ps, rps = self.next_ps()
                    k.op("pe", lambda e: e.transpose(ps[0:64, 0:128], P[:], self.identf[:]), reads=[RP, self.Rid], writes=[rps])
                    k.op("act", lambda e: e.activation(out=Z[:], in_=ps[0:64, 0:128], func=AF.Identity), reads=[rps], writes=[RRH])
                    for hl in range(2):
                        k.dma("sp", "rso%d" % hl, self.dout["o_rwkv"][j, sgi, d, 2 * hp + hl], Z[:, hl * 64:(hl + 1) * 64], reads=[RRH])
                if d == 1:
                    Y3 = Ys[:].rearrange("p (h v) -> p h v", h=2)
                    k.op("dve", lambda e: e.tensor_reduce(out=sm[:, 2:4], in_=Y3, axis=AX.X, op=ALU.add), reads=[RYs], writes=[Rsm])
                    k.op("dve", lambda e: e.tensor_scalar(out=sm[:, 2:4], in0=sm[:, 2:4], scalar1=-1.0 / 64, scalar2=None, op0=ALU.mult), reads=[Rsm], writes=[Rsm])
                    k.op("dve", lambda e: e.tensor_tensor(out=Y3, in0=Y3, in1=sm[:, 2:4].unsqueeze(2).to_broadcast([128, 2, 64]), op=ALU.add), reads=[RYs, Rsm], writes=[RYs])
                    yn3 = yn[:].rearrange("p (h v) -> p h v", h=2)
                    k.op("dve", lambda e: e.tensor_tensor(out=yn[:], in0=Ys[:], in1=Ys[:], op=ALU.mult), reads=[RYs], writes=[Ryn])
                    k.op("dve", lambda e: e.tensor_reduce(out=sm[:, 4:6], in_=yn3, axis=AX.X, op=ALU.add), reads=[Ryn], writes=[Rsm])
                    k.op("act", lambda e: e.activation(out=sm[:, 4:6], in_=sm[:, 4:6], func=AF.Ln, scale=1.0 / 64, bias=cst[:, 2:3]), reads=[Rsm, Rvec], writes=[Rsm])
                    k.op("act", lambda e: e.activation(out=sm[:, 4:6], in_=sm[:, 4:6], func=AF.Exp, scale=-0.5), reads=[Rsm], writes=[Rsm])
                    k.op("dve", lambda e: e.tensor_tensor(out=yn3, in0=Y3, in1=sm[:, 4:6].unsqueeze(2).to_broadcast([128, 2, 64]), op=ALU.mult), reads=[RYs, Rsm], writes=[Ryn])
                    ps, rps = self.next_ps()
                    k.op("pe", lambda e: e.transpose(ps[:, 0:128], yn[:], self.identf[:]), reads=[Ryn, self.Rid], writes=[rps])
                    k.op("pe", lambda e: e.matmul(ps[:, 128:256], lhsT=lw3[:, 2, :], rhs=sgT[:, ts_], start=True, stop=True), reads=[Rlw, Rlo], writes=[rps])
                    k.op("act", lambda e: e.activation(out=Ys[:], in_=ps[:, 0:128], func=AF.Identity, scale=vec[:, 56 + hp:57 + hp], bias=vec[:, 64 + hp:65 + hp]), reads=[rps, Rvec], writes=[RYs])
                    k.op("dve", lambda e: e.tensor_tensor(out=Ys[:], in0=Ys[:], in1=bonT[:, ts_], op=ALU.add), reads=[RYs, Rbon], writes=[RYs])
                    k.op("dve", lambda e: e.tensor_tensor(out=yT[:, 8 + hp, ts_], in0=Ys[:], in1=ps[:, 128:256], op=ALU.mult), reads=[RYs, rps], writes=[RyT[i]])
        k.barrier()


Prog.rwkv = _rwkv


RWQ = (1, 2, 3)
RWLEV = 3
PARTS = ("gla", "ml", "ssd", "rw")


def kernel(**inputs):
    inputs = {k: np.asarray(v) for k, v in inputs.items()}
    p = Prog(nl=4, do_mix=True, parts=PARTS)
    nc = p.build()
    in_maps = []
    for core in range(8):
        m = _prep_core_inputs(inputs, core)
        in_maps.append({k: np.ascontiguousarray(v, dtype=np.float32) for k, v in m.items() if k in p.din})
    res = run_bass_kernel_spmd(nc, in_maps, core_ids=list(range(8)))
    R = res.results
    y_sample = np.stack([R[c]["y"] for c in range(4)], axis=0).astype(np.float32)
    y_prompt = np.concatenate([R[c]["y"].reshape(4, 256, D) for c in range(4, 8)], axis=0).astype(np.float32)

    def states(name, shape):
        if name not in R[4]:
            return np.zeros((16, 2, 2) + shape, np.float32)
        out = np.zeros((16, 2, 2) + shape, np.float32)
        for c in range(4, 8):
            o = R[c][name]
            for g in range(4):
                out[4 * (c - 4) + g] = o[:, g]
        return out
    new_ssd = states("o_ssd", (16, 128, 64))
    new_rwkv = states("o_rwkv", (16, 64, 64))
    new_gla = states("o_gla", (4, 128, 256))
    new_mc = states("o_mc", (4, 128, 256))
    new_mn = states("o_mn", (4, 128))
    new_mm = states("o_mm", (4,))
    return (y_prompt, y_sample, new_ssd, new_rwkv, new_gla, new_mc, new_mn, new_mm)
```

```python
import contextlib
import numpy as np
import concourse.bass as bass
import concourse.mybir as mybir
from concourse.bass_utils import run_bass_kernel_spmd

F32 = mybir.dt.float32
BF16 = mybir.dt.bfloat16
AF = mybir.ActivationFunctionType
ALU = mybir.AluOpType
AX = mybir.AxisListType

T = 1024
D = 1024
NT = 8
DFF = 4096
EPS = 1e-6


class Res:
    __slots__ = ("name", "w", "r")

    def __init__(self, name=""):
        self.name = name
        self.w = None
        self.r = {}


class KB:
    ENG = ("pe", "act", "dve", "pool", "sp")

    def __init__(self, nc, es):
        self.nc = nc
        self.es = es
        self.e = {"pe": nc.tensor, "act": nc.scalar, "dve": nc.vector, "pool": nc.gpsimd, "sp": nc.sync}
        self.sem = {k: es.enter_context(nc.semaphore("sem_" + k)) for k in self.ENG}
        self.cnt = {k: 0 for k in self.ENG}
        self.seen = {k: {} for k in self.ENG}
        self.chan = {}
        self.ninst = 0
        self.nwait = 0

    def sb(self, name, shape, dt=F32):
        nb = int(np.prod(shape[1:])) * (4 if dt == F32 else 2)
        self.sbtot = getattr(self, "sbtot", 0) + nb
        return self.es.enter_context(self.nc.sbuf_tensor(name, shape, dt))

    def ps(self, name, shape, dt=F32):
        return self.es.enter_context(self.nc.psum_tensor(name, shape, dt))

    def channel(self, name):
        if name not in self.chan:
            s = self.es.enter_context(self.nc.semaphore("ch_" + name))
            self.chan[name] = [s, 0]
        return name

    def _deps(self, reads, writes):
        deps = {}

        def add(tok):
            if tok is None:
                return
            k, v = tok
            if deps.get(k, 0) < v:
                deps[k] = v
        for r in reads:
            add(r.w)
        for w in writes:
            add(w.w)
            for k, v in w.r.items():
                add((k, v))
        return deps

    def _emit_waits(self, eng, deps):
        seen = self.seen[eng]
        E = self.e[eng]
        for k, v in deps.items():
            if seen.get(k, 0) >= v:
                continue
            seen[k] = v
            s = self.sem[k] if k in self.sem else self.chan[k][0]
            E.wait_ge(s, v)
            self.nwait += 1

    def _mark(self, tok, reads, writes):
        k, v = tok
        for r in reads:
            if r.r.get(k, 0) < v:
                r.r[k] = v
        for w in writes:
            w.w = tok
            w.r = {}

    def op(self, eng, fn, reads=(), writes=()):
        self._emit_waits(eng, self._deps(reads, writes))
        ins = fn(self.e[eng])
        self.cnt[eng] += 1
        ins.then_inc(self.sem[eng], 1)
        tok = (eng, self.cnt[eng])
        self._mark(tok, reads, writes)
        self.ninst += 1
        return tok

    def dma(self, q, ch, out, in_, reads=(), writes=(), **kw):
        self.channel(ch)
        self._emit_waits(q, self._deps(reads, writes))
        ins = self.e[q].dma_start(out=out, in_=in_, **kw)
        c = self.chan[ch]
        c[1] += 16
        ins.then_inc(c[0], 16)
        tok = (ch, c[1])
        self._mark(tok, reads, writes)
        self.ninst += 1
        return tok

    def barrier(self):
        for eng in self.ENG:
            deps = {k: v[1] for k, v in self.chan.items() if v[1] > 0}
            for k in self.ENG:
                if k != eng and self.cnt[k] > 0:
                    deps[k] = self.cnt[k]
            self._emit_waits(eng, deps)

    def finish(self, eng="sp"):
        deps = {k: v[1] for k, v in self.chan.items() if v[1] > 0}
        for k in self.ENG:
            if k != eng and self.cnt[k] > 0:
                deps[k] = self.cnt[k]
        self._emit_waits(eng, deps)


class Prog:
    def __init__(self, nl=4, do_mix=True, layers=None, parts=("gla", "ml", "ssd", "rw")):
        self.nl = nl
        self.layers = list(range(nl)) if layers is None else layers
        self.parts = parts
        self.do_mix = do_mix
        self.nc = bass.Bass("TRN2", target_bir_lowering=False)
        self.es = contextlib.ExitStack()
        self.din = {}
        self.dout = {}

    def inp(self, name, shape, dt=F32):
        ap = self.nc.dram_tensor(name, list(shape), dt, kind="ExternalInput").ap()
        self.din[name] = ap
        return ap

    def outp(self, name, shape):
        ap = self.nc.dram_tensor(name, list(shape), F32, kind="ExternalOutput").ap()
        self.dout[name] = ap
        return ap

    def build(self):
        nc = self.nc
        with self.es as es:
            k = self.k = KB(nc, es)
            self.declare_io()
            self.alloc()
            self.load_consts()
            for l in self.layers:
                self.layer(l)
            self.store_y()
            k.finish()
            print("ninst", k.ninst, "nwait", k.nwait, k.cnt, flush=True)
        return nc

    def declare_io(self):
        self.x_d = self.inp("x", [T, D])
        self.cond_d = self.inp("cond", [128, 8])
        self.ident_d = self.inp("ident", [128, 128])
        self.w_mod_d = self.inp("w_mod", [4, D, 6 * D])
        self.b_mod_d = self.inp("b_mod", [4, 6 * D])
        self.norm_g_d = self.inp("norm_g", [4, 4, D])
        self.w_up_d = self.inp("w_mlp_up", [4, D, DFF])
        self.w_dn_d = self.inp("w_mlp_down", [4, DFF, D])
        self.y_d = self.outp("y", [T, D])
        if self.do_mix:
            self.inp("masks", [128, 9, 128])
            self.inp("flags", [128, 4])
            self.inp("w_in_cd", [2, D, 6192])
            self.inp("w_out_cd", [2, 2048, D])
            self.inp("gla_gate_w", [2, 2, 16, 512])
            self.inp("gla_gate_b", [2, 2, 512])
            self.inp("gla_norm", [2, 1024])
            self.inp("st_gla", [2, 2, 4, 128, 256])
            self.outp("o_gla", [2, 4, 2, 4, 128, 256])
            self.inp("w_in_ab", [2, D, 6560])
            self.inp("w_out_ab", [2, 2048, D])
            self.inp("ssd_dt_bias", [2, 2, 16])
            self.inp("ssd_a_log", [2, 2, 16])
            self.inp("ssd_d", [2, 16])
            self.inp("ssd_nw", [2, 128, 8])
            self.inp("ssd_cw", [2, 16, 128, 9])
            self.inp("ssd_cb", [2, 16, 128, 1])
            self.inp("st_ssd", [2, 2, 16, 128, 64])
            self.outp("o_ssd", [2, 4, 2, 16, 128, 64])
            self.inp("rw_vec", [2, 128, 72])
            self.inp("rw_mu", [2, 128, 27])
            self.inp("tsm", [2, T])
            self.inp("rwkv_w2", [2, 2, 64, 1024])
            self.inp("rwkv_a2", [2, 2, 64, 1024])
            self.inp("rwkv_g2", [2, 128, 1024])
            self.inp("st_rw", [2, 2, 16, 64, 64])
            self.outp("o_rwkv", [2, 4, 2, 16, 64, 64])
            self.inp("convm", [2, T])
            self.inp("tapflag", [128, 9])
            self.inp("ml_cw", [2, 8, 128, 9])
            self.inp("ml_cb", [2, 8, 128, 1])
            self.inp("mlstm_i_b", [2, 2, 4])
            self.inp("mlstm_f_b", [2, 2, 4])
            self.inp("mlstm_norm", [2, 1024])
            self.inp("st_mc", [2, 2, 4, 128, 256])
            self.inp("st_mn", [2, 2, 4, 128])
            self.inp("st_mm", [2, 2, 4])
            self.outp("o_mc", [2, 4, 2, 4, 128, 256])
            self.outp("o_mn", [2, 4, 2, 4, 128])
            self.outp("o_mm", [2, 4, 2, 4])

    def alloc(self):
        k = self.k
        self.X = k.sb("X", [128, NT, D])
        self.RX = [Res("X%d" % i) for i in range(NT)]
        self.hT = k.sb("hT", [128, 8, T], BF16)
        self.RhT = [Res("hT%d" % i) for i in range(NT)]
        self.modb = k.sb("modb", [128, 6 * D])
        self.Rmod = [Res("mod%d" % i) for i in range(6)]
        self.NSLOT = 2
        self.WA = k.sb("WA", [128, self.NSLOT * 4096], BF16)
        self.RW = [Res("W%d" % i) for i in range(self.NSLOT)]
        self.wslot = 0
        self.PS = [k.ps("ps%d" % i, [128, 512]) for i in range(6)]
        self.RPS = [Res("ps%d" % i) for i in range(6)]
        self.psi = 0
        self.PB = [k.ps("pb%d" % i, [128, 1024], BF16) for i in range(2)]
        self.RPB = [Res("pb%d" % i) for i in range(2)]
        self.pbi = 0
        self.identf = k.sb("identf", [128, 128])
        self.identb = k.sb("identb", [128, 128], BF16)
        self.Rid = Res("ident")
        self.cs = k.sb("cs", [128, 8])
        self.Rcond = Res("cond")
        self.ss = k.sb("ss", [128, 16])
        self.Rss = [Res("ss%d" % i) for i in range(16)]
        self.epsb = k.sb("epsb", [128, 1])
        self.Rjunk = Res("junk")
        self.tmpf = [k.sb("tmpf%d" % i, [128, D]) for i in range(2)]
        self.Rtmpf = [Res() for _ in range(2)]
        self.hb = [k.sb("hb%d" % i, [128, D], BF16) for i in range(2)]
        self.Rhb = [Res() for _ in range(2)]
        self.MA = k.sb("MA", [128, 36864], BF16)
        self.RMA = Res("MA")
        self.junk = k.sb("junk", [128, D])
        fa = self.MA[:, 16384:16384 + 4 * 2048].bitcast(F32)
        self.Ff = [fa[:, i * D:(i + 1) * D] for i in range(4)]
        self.RFf = [Res() for _ in range(4)]
        ga = self.MA[:, 2048:2048 + 2 * 2048].bitcast(F32)
        self.gsc = [ga[:, i * D:(i + 1) * D] for i in range(2)]
        self.RWD = Res("wdown")
        self.Rgsc = [Res() for _ in range(2)]

    def scr(self, name, shape, dt=F32):
        if not hasattr(self, "_scr"):
            self._scr = {}
        if name not in self._scr:
            self._scr[name] = self.k.sb(name, shape, dt)
        return self._scr[name]

    def next_ps(self, pin=False):
        if not hasattr(self, "pinned"):
            self.pinned = set()
        while True:
            i = self.psi
            self.psi = (self.psi + 1) % len(self.PS)
            if i not in self.pinned:
                break
        if pin:
            self.pinned.add(i)
        return self.PS[i], self.RPS[i]

    def unpin(self, *rs):
        for r in rs:
            self.pinned.discard(self.RPS.index(r))

    def next_pb(self):
        i = self.pbi
        self.pbi = (self.pbi + 1) % len(self.PB)
        return self.PB[i], self.RPB[i]

    def wslots(self, n):
        if self.wslot + n > self.NSLOT:
            self.wslot = 0
        s = self.wslot
        self.wslot += n
        self.wch = "ws%d" % s
        return self.WA[:, s * 4096:(s + n) * 4096], self.RW[s:s + n]

    def load_consts(self):
        k = self.k
        k.op("dve", lambda e: e.memset(self.epsb[:], EPS), writes=[self.Rid])
        k.dma("sp", "c0", self.identf[:], self.ident_d[:, :], writes=[self.Rid])
        k.op("dve", lambda e: e.tensor_copy(out=self.identb[:], in_=self.identf[:]), reads=[self.Rid], writes=[self.Rid])
        k.dma("sp", "c1", self.cs[:], self.cond_d[:, :], writes=[self.Rcond])
        k.op("act", lambda e: e.activation(out=self.cs[:], in_=self.cs[:], func=AF.Silu), reads=[self.Rcond], writes=[self.Rcond])
        for i in range(NT):
            k.dma("sp", "x%d" % i, self.X[:, i, :], self.x_d[i * 128:(i + 1) * 128, :], writes=[self.RX[i]])

    def store_y(self):
        k = self.k
        for i in range(NT):
            k.dma("sp", "y%d" % i, self.y_d[i * 128:(i + 1) * 128, :], self.X[:, i, :], reads=[self.RX[i]])

    def adaln(self, l):
        k = self.k
        modb = self.modb
        self.condB = self.MA[:, 0:2048].bitcast(F32).rearrange("p (kc m) -> p kc m", kc=8)
        k.op("dve", lambda e: e.tensor_copy(out=self.condB, in_=self.cs[:].unsqueeze(2).to_broadcast([128, 8, 128])),
             reads=[self.Rcond], writes=[self.Rcond])
        k.dma("sp", "bmod", modb[:], self.b_mod_d[l:l + 1, :].partition_broadcast(128), writes=self.Rmod)
        wv = self.w_mod_d[l].rearrange("(kc p) n -> p kc n", p=128)
        NB = 256
        for j in range(6 * D // NB):
            wap, wres = self.wslots(1)
            wf = wap.bitcast(F32).rearrange("p (kc n) -> p kc n", kc=8)
            k.dma("sp", self.wch, wf, wv[:, :, j * NB:(j + 1) * NB], writes=wres)
            ps, rps = self.next_ps()
            for kc in range(8):
                k.op("pe", lambda e, kc=kc: e.matmul(ps[:, 0:NB], lhsT=self.condB[:, kc, :], rhs=wf[:, kc, :],
                                                     start=(kc == 0), stop=(kc == 7)),
                     reads=[self.Rcond] + wres, writes=[rps])
            r = self.Rmod[j * NB // D]
            sl = modb[:, j * NB:(j + 1) * NB]
            k.op("dve", lambda e: e.tensor_tensor(out=sl, in0=ps[:, 0:NB], in1=sl, op=ALU.add), reads=[rps, r], writes=[r])
        for gi, (mi, isscale) in enumerate([(1, True), (2, False), (4, True), (5, False)]):
            g, rg = self.gsc[gi % 2], self.Rgsc[gi % 2]
            k.dma("sp", "g%d" % (gi % 2), g, self.norm_g_d[l, gi:gi + 1, :].partition_broadcast(128), writes=[rg])
            sl = modb[:, mi * D:(mi + 1) * D]
            if isscale:
                k.op("dve", lambda e: e.scalar_tensor_tensor(out=sl, in0=sl, scalar=1.0, in1=g, op0=ALU.add, op1=ALU.mult),
                     reads=[rg, self.Rmod[mi]], writes=[self.Rmod[mi]])
            else:
                k.op("dve", lambda e: e.tensor_tensor(out=sl, in0=sl, in1=g, op=ALU.mult),
                     reads=[rg, self.Rmod[mi]], writes=[self.Rmod[mi]])

    def rstd_of(self, src_ap, rsrc, col, n, junk=None, rjunk=None):
        k = self.k
        ssl = self.ss[:, col:col + 1]
        rss = self.Rss[col]
        k.op("dve", lambda e: e.memset(ssl, 0.0), writes=[rss])
        jk = self.junk[:, 0:n] if junk is None else junk
        rjk = self.Rjunk if rjunk is None else rjunk
        k.op("act", lambda e: e.activation(out=jk, in_=src_ap, func=AF.Square, accum_out=ssl),
             reads=[rsrc], writes=[rss, rjk])
        k.op("act", lambda e: e.activation(out=ssl, in_=ssl, func=AF.Ln, scale=1.0 / n, bias=self.epsb[:, 0:1]),
             reads=[rss, self.Rid], writes=[rss])
        k.op("act", lambda e: e.activation(out=ssl, in_=ssl, func=AF.Exp, scale=-0.5), reads=[rss], writes=[rss])
        return ssl

    def norm_mod_T(self, ai, si):
        k = self.k
        A = self.modb[:, ai * D:(ai + 1) * D]
        S = self.modb[:, si * D:(si + 1) * D]
        for i in range(NT):
            rs = self.rstd_of(self.X[:, i, :], self.RX[i], i, D)
            tf, rtf = self.tmpf[i % 2], self.Rtmpf[i % 2]
            hb, rhb = self.hb[i % 2], self.Rhb[i % 2]
            k.op("dve", lambda e: e.scalar_tensor_tensor(out=tf[:], in0=self.X[:, i, :], scalar=rs, in1=A, op0=ALU.mult, op1=ALU.mult),
                 reads=[self.RX[i], self.Rss[i], self.Rmod[ai]], writes=[rtf])
            k.op("dve", lambda e: e.tensor_tensor(out=hb[:], in0=tf[:], in1=S, op=ALU.add), reads=[rtf, self.Rmod[si]], writes=[rhb])
            pb, rpb = self.next_pb()
            for kc in range(8):
                k.op("pe", lambda e, kc=kc: e.transpose(pb[:, kc * 128:(kc + 1) * 128], hb[:, kc * 128:(kc + 1) * 128], self.identb[:]),
                     reads=[rhb, self.Rid], writes=[rpb])
            k.op("act", lambda e: e.activation(out=self.hT[:, :, i * 128:(i + 1) * 128],
                                               in_=pb[:].rearrange("p (kc t) -> p kc t", kc=8), func=AF.Identity),
                 reads=[rpb], writes=[self.RhT[i]])

    def resid_add(self, i, F, rF, gi):
        k = self.k
        G = self.modb[:, gi * D:(gi + 1) * D]
        rs = self.rstd_of(F, rF, 8 + i, D)
        k.op("dve", lambda e: e.scalar_tensor_tensor(out=F, in0=F, scalar=rs, in1=G, op0=ALU.mult, op1=ALU.mult),
             reads=[rF, self.Rss[8 + i], self.Rmod[gi]], writes=[rF])
        k.op("dve", lambda e: e.tensor_tensor(out=self.X[:, i, :], in0=self.X[:, i, :], in1=F, op=ALU.add),
             reads=[rF, self.RX[i]], writes=[self.RX[i]])

    def mlp(self, l):
        k = self.k
        self.norm_mod_T(4, 3)
        wu = self.w_up_d[l].rearrange("(kc p) n -> p kc n", p=128)
        wd = self.w_dn_d[l].rearrange("(fc p) n -> p fc n", p=128)
        uT = self.MA[:, 0:32 * 512].rearrange("p (fc t) -> p fc t", fc=32)
        Ru = [Res("u%d" % i) for i in range(8)]
        for half in range(2):
            t0 = half * 512
            rh = self.RhT[half * 4:(half + 1) * 4]
            for fb in range(8):
                wap, wres = self.wslots(1)
                w = wap.rearrange("p (kc n) -> p kc n", kc=8)
                k.dma("pool", self.wch, w, wu[:, :, fb * 512:(fb + 1) * 512], writes=wres)
                for fc in range(4):
                    ps, rps = self.next_ps()
                    for kc in range(8):
                        k.op("pe", lambda e, kc=kc: e.matmul(ps[:, :], lhsT=w[:, kc, fc * 128:(fc + 1) * 128], rhs=self.hT[:, kc, t0:t0 + 512],
                                                             start=(kc == 0), stop=(kc == 7)), reads=wres + rh, writes=[rps])
                    tf, rtf = self.tmpf[fc % 2], self.Rtmpf[fc % 2]
                    k.op("act", lambda e: e.activation(out=tf[:, 0:512], in_=ps[:, :], func=AF.Relu), reads=[rps], writes=[rtf])
                    k.op("dve", lambda e: e.tensor_tensor(out=uT[:, fb * 4 + fc, :], in0=tf[:, 0:512], in1=tf[:, 0:512], op=ALU.mult),
                         reads=[rtf], writes=[Ru[fb]])
            Fs = {}
            for nh in range(4):
                wres = [self.RWD]
                w = self.MA[:, 24576:32768].rearrange("p (fc n) -> p fc n", fc=32)
                for q in range(4):
                    k.dma("pool", "wdn", w[:, q * 8:(q + 1) * 8, :], wd[:, q * 8:(q + 1) * 8, nh * 256:(nh + 1) * 256], writes=wres)
                for ti in range(4):
                    i = half * 4 + ti
                    ps, rps = self.next_ps()
                    for fc in range(32):
                        k.op("pe", lambda e, fc=fc: e.matmul(ps[:, 0:256], lhsT=uT[:, fc, ti * 128:(ti + 1) * 128], rhs=w[:, fc, :],
                                                             start=(fc == 0), stop=(fc == 31)), reads=wres + [Ru[fc // 4]], writes=[rps])
                    if nh == 0:
                        Fs[ti] = (self.Ff[ti], self.RFf[ti])
                    F, rF = Fs[ti]
                    k.op("act", lambda e: e.activation(out=F[:, nh * 256:(nh + 1) * 256], in_=ps[:, 0:256], func=AF.Identity),
                         reads=[rps], writes=[rF])
            for ti in range(4):
                F, rF = Fs[ti]
                self.resid_add(half * 4 + ti, F, rF, 5)

    def layer(self, l):
        self.adaln(l)
        self.k.barrier()
        if self.do_mix:
            self.norm_mod_T(1, 0)
            self.mixer(l)
            self.k.barrier()
        self.mlp(l)
        self.k.barrier()


def _prep_core_inputs(inputs, core):
    if core < 4:
        x = np.ascontiguousarray(inputs["x_sample"][core])
        cond = inputs["c"][core]
    else:
        j = core - 4
        x = np.ascontiguousarray(inputs["x_prompt"][4 * j:4 * j + 4].reshape(T, D))
        cond = inputs["c_ctx"]
    m = {"x": x, "cond": np.ascontiguousarray(cond.reshape(8, 128).T), "ident": np.eye(128, dtype=np.float32)}
    for n in ("w_mod", "b_mod", "norm_g", "w_mlp_up", "w_mlp_down", "w_in_cd", "w_out_cd", "gla_gate_w", "gla_gate_b", "gla_norm"):
        m[n] = inputs[n]
    r = np.arange(128)
    bdm = lambda n: (r[:, None] // n) == (r[None, :] // n)
    m["masks"] = np.ascontiguousarray(np.stack([r[:, None] <= r[None, :], r[:, None] >= r[None, :], r[:, None] > r[None, :], r[:, None] < r[None, :],
                                                (r[:, None] // 64) == (r[None, :] // 64),
                                                bdm(16), bdm(32) & ~bdm(16), bdm(64) & ~bdm(32), ~bdm(64)], axis=1).astype(np.float32))
    fl = np.zeros((128, 4), np.float32)
    fl[:, 0] = 1.0 if core < 4 else 0.0
    m["flags"] = fl
    for n in ("mlstm_i_b", "mlstm_f_b", "mlstm_norm"):
        m[n] = inputs[n]
    t = np.arange(T)
    per = 64 if core < 4 else 256
    m["convm"] = np.stack([(t % per) != 0, (t % per) != per - 1]).astype(np.float32)
    tf = np.ones((128, 9), np.float32)
    if core >= 4:
        tf[:, 0:3] = 0.0
        tf[:, 6:9] = 0.0
    m["tapflag"] = tf
    cw = inputs["mlstm_conv_w"]
    m["ml_cw"] = np.ascontiguousarray(cw.reshape(2, 9, 8, 128).transpose(0, 2, 3, 1))
    m["ml_cb"] = np.ascontiguousarray(inputs["mlstm_conv_b"].reshape(2, 8, 128, 1))
    for n in ("w_in_ab", "w_out_ab", "ssd_dt_bias", "ssd_a_log", "ssd_d"):
        m[n] = inputs[n]
    m["ssd_nw"] = np.ascontiguousarray(inputs["ssd_norm"].reshape(2, 8, 128).transpose(0, 2, 1))
    m["ssd_cw"] = np.ascontiguousarray(inputs["ssd_conv_w"].reshape(2, 9, 16, 128).transpose(0, 2, 3, 1))
    m["ssd_cb"] = np.ascontiguousarray(inputs["ssd_conv_b"].reshape(2, 16, 128, 1))
    for n in ("rwkv_w2", "rwkv_a2", "rwkv_g2"):
        m[n] = inputs[n]
    pc = lambda a: a.reshape(2, 8, 128).transpose(0, 2, 1)
    kinds = [inputs["rwkv_w0"][:, 0], inputs["rwkv_w0"][:, 1], inputs["rwkv_a0"][:, 0], inputs["rwkv_a0"][:, 1], inputs["rwkv_k_k"], inputs["rwkv_k_a"],
             inputs["rwkv_r_k"].reshape(2, 1024), inputs["rwkv_ln_w"], inputs["rwkv_ln_b"]]
    m["rw_vec"] = np.ascontiguousarray(np.stack([pc(a) for a in kinds], axis=2).reshape(2, 128, 72))
    m["rw_mu"] = np.ascontiguousarray(inputs["rwkv_mu"].reshape(2, 27, 128).transpose(0, 2, 1))
    sl = 1024 if core < 4 else 256
    m["tsm"] = np.stack([(t % sl) != 0, (t % sl) != sl - 1]).astype(np.float32)
    names = {"st_rw": "state_rwkv", "st_ssd": "state_ssd", "st_gla": "state_gla", "st_mc": "state_mlstm_c", "st_mn": "state_mlstm_n", "st_mm": "state_mlstm_m"}
    for kk, src in names.items():
        a = inputs[src]
        m[kk] = np.ascontiguousarray(a[core]) if core < 4 else np.zeros(a.shape[1:], np.float32)
    return m


def _load_w(self, Wv, c0, n):
    wap, wres = self.wslots(1)
    w = wap[:, 0:8 * n].rearrange("p (kc n) -> p kc n", kc=8)
    self.k.dma("pool", self.wch, w, Wv[:, :, c0:c0 + n], writes=wres)
    return w, wres


def _proj_tok(self, Wv, c0, n, evac, tiles=None):
    k = self.k
    w, wres = self.load_w(Wv, c0, n)
    for i in (range(NT) if tiles is None else tiles):
        ps, rps = self.next_ps()
        for kc in range(8):
            k.op("pe", lambda e, kc=kc: e.matmul(ps[:, 0:n], lhsT=self.hT[:, kc, i * 128:(i + 1) * 128], rhs=w[:, kc, :],
                                                 start=(kc == 0), stop=(kc == 7)), reads=wres + [self.RhT[i]], writes=[rps])
        evac(i, ps[:, 0:n], rps)


def _proj_feat(self, Wv, c0, n, evac):
    k = self.k
    w, wres = self.load_w(Wv, c0, n)
    for c in range((n + 127) // 128):
        m = min(128, n - c * 128)
        for half in range(2):
            ps, rps = self.next_ps()
            for kc in range(8):
                k.op("pe", lambda e, kc=kc: e.matmul(ps[0:m, :], lhsT=w[:, kc, c * 128:c * 128 + m], rhs=self.hT[:, kc, half * 512:(half + 1) * 512],
                                                     start=(kc == 0), stop=(kc == 7)), reads=wres + self.RhT[half * 4:half * 4 + 4], writes=[rps])
            evac(c, half, ps[0:m, :], rps, m)


def _transpose_to(self, src_bf, rsrc, dst3, rdst, ncol=8):
    k = self.k
    pb, rpb = self.next_pb()
    for c in range(ncol):
        k.op("pe", lambda e, c=c: e.transpose(pb[:, c * 128:(c + 1) * 128], src_bf[:, c * 128:(c + 1) * 128], self.identb[:]),
             reads=[rsrc, self.Rid], writes=[rpb])
    k.op("act", lambda e: e.activation(out=dst3, in_=pb[:, 0:ncol * 128].rearrange("p (c t) -> p c t", c=ncol), func=AF.Identity),
         reads=[rpb], writes=[rdst])


def _out_proj(self, Wd, yT, RyT, gi=2):
    k = self.k
    wv = Wd.rearrange("(kc p) n -> p kc n", p=128)
    Fall = self.MA[:, 16384:32768].bitcast(F32).rearrange("p (i n) -> p i n", i=NT)
    RF = [Res() for _ in range(NT)]
    for nh in range(2):
        wap, wres = self.wslots(2)
        w = wap.rearrange("p (kc n) -> p kc n", kc=16)
        for q in range(2):
            k.dma("pool", self.wch, w[:, q * 8:(q + 1) * 8, :], wv[:, q * 8:(q + 1) * 8, nh * 512:(nh + 1) * 512], writes=wres)
        for i in range(NT):
            ps, rps = self.next_ps()
            for kc in range(16):
                k.op("pe", lambda e, kc=kc: e.matmul(ps[:, :], lhsT=yT[:, kc, i * 128:(i + 1) * 128], rhs=w[:, kc, :],
                                                     start=(kc == 0), stop=(kc == 15)), reads=wres + [RyT[i]], writes=[rps])
            k.op("act", lambda e: e.activation(out=Fall[:, i, nh * 512:(nh + 1) * 512], in_=ps[:, :], func=AF.Identity),
                 reads=[rps], writes=[RF[i]])
    for i in range(NT):
        self.resid_add(i, Fall[:, i, :], RF[i], gi)


Prog.load_w = _load_w
Prog.proj_tok = _proj_tok
Prog.proj_feat = _proj_feat
Prog.transpose_to = _transpose_to
Prog.out_proj = _out_proj


def _mix_consts(self):
    k = self.k
    if hasattr(self, "masks"):
        return
    self.masks = k.sb("masks_sb", [128, 9, 128])
    self.Rmask = Res("masks")
    k.dma("sp", "c2", self.masks[:], self.din["masks"][:, :, :], writes=[self.Rmask])
    self.onesc = k.sb("onesc", [128, 1])
    self.flags = k.sb("flags_sb", [128, 4])
    k.op("dve", lambda e: e.memset(self.onesc[:], 1.0), writes=[self.Rmask])
    k.dma("sp", "c3", self.flags[:], self.din["flags"][:, :], writes=[self.Rmask])


def _mixer_cd(self, l):
    k = self.k
    j = l // 2
    self.mix_consts()
    Wv = self.din["w_in_cd"][j].rearrange("(kc p) n -> p kc n", p=128)
    yT = self.MA[:, 0:16384].rearrange("p (c t) -> p c t", c=16)
    RyT = [Res() for _ in range(NT)]
    if "gla" in self.parts:
        self.gla(l, j, Wv, yT, RyT)
    else:
        k.op("dve", lambda e: e.memset(yT[:, 0:8, :], 0.0), writes=RyT)
    k.barrier()
    if "ml" in self.parts:
        self.mlstm(l, j, Wv, yT, RyT)
    else:
        k.op("dve", lambda e: e.memset(yT[:, 8:16, :], 0.0), writes=RyT)
    k.barrier()
    self.out_proj(self.din["w_out_cd"][j], yT, RyT)


def _gla(self, l, j, Wv, yT, RyT):
    k = self.k
    MA = self.MA
    qT = MA[:, 16384:18432].rearrange("p (h t) -> p h t", h=2)
    kT = MA[:, 18432:20480].rearrange("p (h t) -> p h t", h=2)
    ktok = MA[:, 20480:22528].rearrange("p (i c) -> p i c", i=NT)
    vtok = MA[:, 22528:26624].rearrange("p (i c) -> p i c", i=NT)
    Oacc = MA[:, 26624:30720].rearrange("p (i c) -> p i c", i=NT)
    sc = 128 ** -0.5
    gdT = [MA[0:17, 30720 + d * 2048:30720 + (d + 1) * 2048].bitcast(F32) for d in range(2)]
    gw = [MA[0:17, 34816 + d * 1024:34816 + (d + 1) * 1024].bitcast(F32) for d in range(2)]
    Rgd = [Res(), Res()]
    for d in range(2):
        k.op("dve", lambda e: e.memset(gdT[d], 1.0), writes=[Rgd[d]])
        self.proj_feat(Wv, 3072 + 16 * d, 16, lambda c, half, ps, rps, m: k.op(
            "act", lambda e: e.activation(out=gdT[d][0:16, half * 512:(half + 1) * 512], in_=ps, func=AF.Identity), reads=[rps], writes=[Rgd[d]]))
        k.dma("sp", "gw", gw[d][0:16, :], self.din["gla_gate_w"][j, d], writes=[Rgd[d]])
        k.dma("sp", "gw", gw[d][16:17, :], self.din["gla_gate_b"][j, d:d + 1, :], writes=[Rgd[d]])
    gnb = self.scr("nrmb", [128, 1024])
    Rgn = Res()
    k.dma("sp", "gnb", gnb[:], self.din["gla_norm"][j:j + 1, :].partition_broadcast(128), writes=[Rgn])
    S = [self.scr("S%d" % h, [128, 256]) for h in range(2)]
    Sbf = [self.scr("Sb%d" % h, [128, 256], BF16) for h in range(2)]
    la = self.tmpf[0][:, 0:512]; Rla = self.Rtmpf[0]
    te = self.tmpf[0][:, 512:1024]
    ebuf = self.tmpf[1]; Reb = self.Rtmpf[1]
    khat = self.scr("khat", [128, 256], BF16)
    qTt = self.scr("qTt", [128, 2, 128], BF16); kTt = self.scr("kTt", [128, 2, 128], BF16)
    Gt = self.scr("Gt", [128, 4])
    scT = [self.scr("scT%d" % i, [128, 128], BF16) for i in range(2)]
    Ofin = self.junk; ROf = self.Rjunk
    yb = self.hb[0]; Ryb = self.Rhb[0]
    seg_first = {0: [0, 2, 4, 6], 1: [7, 5, 3, 1]}
    seg_last = {0: [1, 3, 5, 7], 1: [6, 4, 2, 0]}
    for hh in range(2):
        RqT, RkT = Res(), Res()
        Rkt = [Res() for _ in range(NT)]
        Rvt = [Res() for _ in range(NT)]
        ROa = [Res() for _ in range(NT)]
        RS = [Res() for _ in range(2)]
        Rkh = Res(); Rqk = [Res() for _ in range(2)]; RG = [Res() for _ in range(2)]; Rsc = [Res(), Res()]
        self.proj_feat(Wv, hh * 256, 256, lambda c, half, ps, rps, m: k.op(
            "act", lambda e: e.activation(out=qT[:, c, half * 512:(half + 1) * 512], in_=ps, func=AF.Identity, scale=sc), reads=[rps], writes=[RqT]))
        self.proj_feat(Wv, 512 + hh * 256, 256, lambda c, half, ps, rps, m: k.op(
            "act", lambda e: e.activation(out=kT[:, c, half * 512:(half + 1) * 512], in_=ps, func=AF.Identity), reads=[rps], writes=[RkT]))
        self.proj_tok(Wv, 512 + hh * 256, 256, lambda i, ps, rps: k.op(
            "act", lambda e: e.activation(out=ktok[:, i, :], in_=ps, func=AF.Identity), reads=[rps], writes=[Rkt[i]]))
        self.proj_tok(Wv, 1024 + hh * 512, 512, lambda i, ps, rps: k.op(
            "act", lambda e: e.activation(out=vtok[:, i, :], in_=ps, func=AF.Identity), reads=[rps], writes=[Rvt[i]]))
        for d in range(2):
            order = list(range(NT)) if d == 0 else list(range(NT - 1, -1, -1))
            mcum = self.masks[:, d, :]
            mend = self.masks[:, 2 + d, :]
            endcol = 127 if d == 0 else 0
            if d == 1:
                ggw = self.load_w(Wv, 2048 + hh * 512, 512)
            for ci, i in enumerate(order):
                seg = i // 2
                if i in seg_first[d]:
                    for hl in range(2):
                        if ci == 0:
                            k.dma("sp", "gst%d" % hl, S[hl][:], self.din["st_gla"][j, d, 2 * hh + hl], writes=[RS[hl]])
                        else:
                            k.op("dve", lambda e: e.tensor_scalar(out=S[hl][:], in0=S[hl][:], scalar1=self.flags[:, 0:1], scalar2=None, op0=ALU.mult),
                                 reads=[RS[hl], self.Rmask], writes=[RS[hl]])
                        k.op("act", lambda e: e.activation(out=Sbf[hl][:], in_=S[hl][:], func=AF.Identity), reads=[RS[hl]], writes=[RS[hl]])
                ps, rps = self.next_ps()
                k.op("pe", lambda e: e.matmul(ps[:, 0:256], lhsT=gdT[d][:, i * 128:(i + 1) * 128], rhs=gw[d][:, hh * 256:(hh + 1) * 256], start=True, stop=True),
                     reads=[Rgd[d]], writes=[rps])
                k.op("act", lambda e: e.activation(out=la[:, 0:256], in_=ps[:, 0:256], func=AF.Exp, scale=-1.0), reads=[rps], writes=[Rla])
                k.op("act", lambda e: e.activation(out=la[:, 0:256], in_=la[:, 0:256], func=AF.Ln, bias=self.onesc[:, 0:1]), reads=[Rla, self.Rmask], writes=[Rla])
                k.op("dve", lambda e: e.tensor_scalar(out=la[:, 0:256], in0=la[:, 0:256], scalar1=-1.0 / 16.0, scalar2=None, op0=ALU.mult), reads=[Rla], writes=[Rla])
                ps, rps = self.next_ps()
                k.op("pe", lambda e: e.matmul(ps[:, 0:256], lhsT=mend, rhs=la[:, 0:256], start=True, stop=True), reads=[Rla, self.Rmask], writes=[rps])
                k.op("act", lambda e: e.activation(out=te[:, 0:256], in_=ps[:, 0:256], func=AF.Exp), reads=[rps], writes=[Rla])
                k.op("dve", lambda e: e.tensor_tensor(out=khat[:], in0=ktok[:, i, :], in1=te[:, 0:256], op=ALU.mult), reads=[Rla, Rkt[i]], writes=[Rkh])
                ps, rps = self.next_ps()
                for hl in range(2):
                    k.op("pe", lambda e: e.matmul(ps[:, hl * 128:(hl + 1) * 128], lhsT=la[:, hl * 128:(hl + 1) * 128], rhs=mcum, start=True, stop=True),
                         reads=[Rla, self.Rmask], writes=[rps])
                k.op("act", lambda e: e.activation(out=ebuf[:, 0:256], in_=ps[:, 0:256], func=AF.Exp), reads=[rps], writes=[Reb])
                k.op("act", lambda e: e.activation(out=ebuf[:, 256:512], in_=ps[:, 0:256], func=AF.Exp, scale=-1.0), reads=[rps], writes=[Reb])
                for hl in range(2):
                    k.op("dve", lambda e: e.tensor_tensor(out=qTt[:, hl, :], in0=qT[:, hl, i * 128:(i + 1) * 128], in1=ebuf[:, hl * 128:(hl + 1) * 128], op=ALU.mult),
                         reads=[Reb, RqT], writes=[Rqk[hl]])
                    k.op("dve", lambda e: e.tensor_tensor(out=kTt[:, hl, :], in0=kT[:, hl, i * 128:(i + 1) * 128], in1=ebuf[:, 256 + hl * 128:256 + (hl + 1) * 128], op=ALU.mult),
                         reads=[Reb, RkT], writes=[Rqk[hl]])
                    k.op("act", lambda e: e.activation(out=Gt[:, hl:hl + 1], in_=ebuf[:, hl * 128 + endcol:hl * 128 + endcol + 1], func=AF.Identity),
                         reads=[Reb], writes=[RG[hl]])
                for hl in range(2):
                    vs = vtok[:, i, hl * 256:(hl + 1) * 256]
                    ps, rps = self.next_ps()
                    k.op("pe", lambda e: e.matmul(ps[:, 0:128], lhsT=kTt[:, hl, :], rhs=qTt[:, hl, :], start=True, stop=True), reads=[Rqk[hl]], writes=[rps])
                    s_, rs_ = scT[hl], Rsc[hl]
                    k.op("dve", lambda e: e.tensor_tensor(out=s_[:], in0=ps[:, 0:128], in1=mcum, op=ALU.mult), reads=[rps, self.Rmask], writes=[rs_])
                    po, rpo = self.next_ps()
                    k.op("pe", lambda e: e.matmul(po[:, 0:256], lhsT=s_[:], rhs=vs, start=True, stop=False), reads=[rs_, Rvt[i]], writes=[rpo])
                    k.op("pe", lambda e: e.matmul(po[:, 0:256], lhsT=qTt[:, hl, :], rhs=Sbf[hl][:], start=False, stop=True), reads=[Rqk[hl], RS[hl]], writes=[rpo])
                    if d == 0:
                        k.op("act", lambda e: e.activation(out=Oacc[:, i, hl * 256:(hl + 1) * 256], in_=po[:, 0:256], func=AF.Identity), reads=[rpo], writes=[ROa[i]])
                    else:
                        k.op("dve", lambda e: e.tensor_tensor(out=Ofin[:, hl * 256:(hl + 1) * 256], in0=po[:, 0:256], in1=Oacc[:, i, hl * 256:(hl + 1) * 256], op=ALU.add),
                             reads=[rpo, ROa[i]], writes=[ROf])
                    pd, rpd = self.next_ps()
                    k.op("pe", lambda e: e.matmul(pd[:, 0:256], lhsT=khat[:, hl * 128:(hl + 1) * 128], rhs=vs, start=True, stop=True), reads=[Rkh, Rvt[i]], writes=[rpd])
                    k.op("dve", lambda e: e.scalar_tensor_tensor(out=S[hl][:], in0=S[hl][:], scalar=Gt[:, hl:hl + 1], in1=pd[:, 0:256], op0=ALU.mult, op1=ALU.add),
                         reads=[RS[hl], RG[hl], rpd], writes=[RS[hl]])
                    k.op("act", lambda e: e.activation(out=Sbf[hl][:], in_=S[hl][:], func=AF.Identity), reads=[RS[hl]], writes=[RS[hl]])
                    if i in seg_last[d]:
                        k.dma("sp", "gso%d" % hl, self.dout["o_gla"][j, seg, d, 2 * hh + hl], S[hl][:], reads=[RS[hl]])
                if d == 1:
                    for hl in range(2):
                        gcol = (2 * hh + hl) * 256
                        rs = self.rstd_of(Ofin[:, hl * 256:(hl + 1) * 256], ROf, 8 + hl, 256, junk=te[:, 0:256], rjunk=Rla)
                        k.op("dve", lambda e: e.scalar_tensor_tensor(out=Ofin[:, hl * 256:(hl + 1) * 256], in0=Ofin[:, hl * 256:(hl + 1) * 256], scalar=rs,
                                                                     in1=gnb[:, gcol:gcol + 256], op0=ALU.mult, op1=ALU.mult),
                             reads=[ROf, self.Rss[8 + hl], Rgn], writes=[ROf])
                    w, wres = ggw
                    ps, rps = self.next_ps()
                    for kc in range(8):
                        k.op("pe", lambda e, kc=kc: e.matmul(ps[:, :], lhsT=self.hT[:, kc, i * 128:(i + 1) * 128], rhs=w[:, kc, :], start=(kc == 0), stop=(kc == 7)),
                             reads=wres + [self.RhT[i]], writes=[rps])
                    k.op("act", lambda e: e.activation(out=te, in_=ps[:, :], func=AF.Silu), reads=[rps], writes=[Rla])
                    k.op("dve", lambda e: e.tensor_tensor(out=yb[:, 0:512], in0=Ofin[:, 0:512], in1=te, op=ALU.mult), reads=[ROf, Rla], writes=[Ryb])
                    self.transpose_to(yb, Ryb, yT[:, hh * 4:(hh + 1) * 4, i * 128:(i + 1) * 128], RyT[i], ncol=4)
        k.barrier()


Prog.mix_consts = _mix_consts
Prog.mixer = lambda self, l: (self.mixer_cd(l) if l % 2 == 1 else self.mixer_ab(l))
Prog.mixer_cd = _mixer_cd
Prog.gla = _gla


def _conv_setup(self, base):
    k = self.k
    MA = self.MA
    cin = MA[:, base:base + 2308].bitcast(F32)
    mLR = MA[:, base + 2308:base + 2308 + 4096].bitcast(F32).rearrange("p (a t) -> p a t", a=2)
    R = {"cin": cin, "mLR": mLR, "Rcin": Res(), "Rm": Res()}
    k.op("dve", lambda e: e.memset(cin[:, 0:65], 0.0), writes=[R["Rcin"]])
    k.op("dve", lambda e: e.memset(cin[:, 1089:1154], 0.0), writes=[R["Rcin"]])
    for a in range(2):
        k.dma("sp", "cm%d" % a, mLR[:, a, :], self.din["convm"][a:a + 1, :].partition_broadcast(128), writes=[R["Rm"]])
    if not hasattr(self, "tapf"):
        self.tapf = self.scr("tapf", [128, 9])
        self.Rtapf = Res()
        k.dma("sp", "tapf", self.tapf[:], self.din["tapflag"][:, :], writes=[self.Rtapf])
    return R


def _conv_chunk(self, C, Wv, col0, cw_ap, cb_ap, out_ap, rout, oscale=1.0):
    k = self.k
    cin, mLR, Rcin, Rm = C["cin"], C["mLR"], C["Rcin"], C["Rm"]
    wc = self.scr("convw", [128, 10])
    Rwc = C.setdefault("Rwc", Res())
    k.dma("sp", "cw", wc[:, 0:9], cw_ap, writes=[Rwc])
    k.dma("sp", "cw", wc[:, 9:10], cb_ap, writes=[Rwc])
    k.op("dve", lambda e: e.tensor_tensor(out=wc[:, 0:9], in0=wc[:, 0:9], in1=self.tapf[:], op=ALU.mult), reads=[self.Rtapf, Rwc], writes=[Rwc])
    self.proj_feat(Wv, col0, 128, lambda c, half, ps, rps, m: k.op(
        "act", lambda e: e.activation(out=cin[:, 65 + half * 512:65 + (half + 1) * 512], in_=ps, func=AF.Identity), reads=[rps], writes=[Rcin]))
    accs = [(self.tmpf[0], self.Rtmpf[0]), (self.tmpf[1], self.Rtmpf[1]), (self.junk, self.Rjunk)]
    for dc, (acc, racc) in zip((-1, 0, 1), accs):
        eng = "dve"
        for n, dr in enumerate((0, -1, 1)):
            tap = (dr + 1) * 3 + (dc + 1)
            off = 65 + 64 * dr + dc
            if n == 0:
                k.op(eng, lambda e: e.tensor_scalar(out=acc[:], in0=cin[:, off:off + T], scalar1=wc[:, tap:tap + 1], scalar2=None, op0=ALU.mult),
                     reads=[Rcin, Rwc], writes=[racc])
            else:
                k.op(eng, lambda e: e.scalar_tensor_tensor(out=acc[:], in0=cin[:, off:off + T], scalar=wc[:, tap:tap + 1], in1=acc[:], op0=ALU.mult, op1=ALU.add),
                     reads=[Rcin, Rwc, racc], writes=[racc])
    (aL, rL), (aC, rC), (aR, rR) = accs
    k.op("pool", lambda e: e.tensor_tensor(out=aL[:], in0=aL[:], in1=mLR[:, 0, :], op=ALU.mult), reads=[rL, Rm], writes=[rL])
    k.op("dve", lambda e: e.tensor_tensor(out=aR[:], in0=aR[:], in1=mLR[:, 1, :], op=ALU.mult), reads=[rR, Rm], writes=[rR])
    k.op("dve", lambda e: e.tensor_tensor(out=aC[:], in0=aC[:], in1=aL[:], op=ALU.add), reads=[rC, rL], writes=[rC])
    k.op("dve", lambda e: e.tensor_tensor(out=aC[:], in0=aC[:], in1=aR[:], op=ALU.add), reads=[rC, rR], writes=[rC])
    k.op("act", lambda e: e.activation(out=aC[:], in_=aC[:], func=AF.Silu, bias=wc[:, 9:10]), reads=[rC, Rwc], writes=[rC])
    k.op("dve", lambda e: e.tensor_scalar(out=out_ap, in0=aC[:], scalar1=float(oscale), scalar2=None, op0=ALU.mult), reads=[rC], writes=[rout])


Prog.conv_setup = _conv_setup
Prog.conv_chunk = _conv_chunk


def _mlstm(self, l, j, Wv, yT, RyT):
    k = self.k
    MA = self.MA
    B0 = 3104
    qT = MA[:, 16384:18432].rearrange("p (h t) -> p h t", h=2)
    kT = MA[:, 18432:20480].rearrange("p (h t) -> p h t", h=2)
    ktok = MA[:, 20480:22528].rearrange("p (i c) -> p i c", i=NT)
    vtok = MA[:, 22528:26624].rearrange("p (i c) -> p i c", i=NT)
    Oacc = MA[:, 26624:30720].rearrange("p (i c) -> p i c", i=NT)
    sc = 128 ** -0.5
    gat = self.scr("mgat", [128, NT, 16])
    lf = self.scr("mlf", [128, NT, 8])
    gbias = self.scr("mgb", [128, 16])
    Rg = Res()
    k.dma("sp", "mgb", gbias[:, 0:8], self.din["mlstm_i_b"][j:j + 1].rearrange("o d h -> o (d h)").partition_broadcast(128), writes=[Rg])
    k.dma("sp", "mgb", gbias[:, 8:16], self.din["mlstm_f_b"][j:j + 1].rearrange("o d h -> o (d h)").partition_broadcast(128), writes=[Rg])
    self.proj_tok(Wv, B0 + 3072, 16, lambda i, ps, rps: k.op(
        "dve", lambda e: e.tensor_tensor(out=gat[:, i, :], in0=ps, in1=gbias[:], op=ALU.add), reads=[rps, Rg], writes=[Rg]))
    k.op("act", lambda e: e.activation(out=lf[:], in_=gat[:, :, 8:16], func=AF.Exp, scale=-1.0), reads=[Rg], writes=[Rg])
    k.op("act", lambda e: e.activation(out=lf[:], in_=lf[:], func=AF.Ln, bias=self.onesc[:, 0:1]), reads=[Rg, self.Rmask], writes=[Rg])
    k.op("dve", lambda e: e.tensor_scalar(out=lf[:], in0=lf[:], scalar1=-1.0, scalar2=None, op0=ALU.mult), reads=[Rg], writes=[Rg])
    gnb = self.scr("nrmb", [128, 1024])
    Rgn = Res()
    k.dma("sp", "gnb", gnb[:], self.din["mlstm_norm"][j:j + 1, :].partition_broadcast(128), writes=[Rgn])
    onesb = self.scr("onesb", [128, 1], BF16)
    ones128 = self.scr("ones128", [128, 128])
    k.op("dve", lambda e: e.memset(onesb[:], 1.0), writes=[Rgn])
    k.op("dve", lambda e: e.memset(ones128[:], 1.0), writes=[Rgn])
    S = [self.scr("mS%d" % h, [128, 260]) for h in range(2)]
    Sbf = [self.scr("mSb%d" % h, [128, 260], BF16) for h in range(2)]
    khat = self.scr("khat", [128, 256], BF16)
    scT = [self.scr("scT%d" % i, [128, 128], BF16) for i in range(2)]
    sm = self.scr("msm", [128, 32])
    mcur = self.scr("mcur", [4, 4])
    dg = self.scr("mdg", [4, 4])
    stage = self.tmpf[0]; Rstage = self.Rtmpf[0]
    Ofin = self.junk; ROf = self.Rjunk
    yb = self.hb[0]; Ryb = self.Rhb[0]
    seg_first = {0: [0, 2, 4, 6], 1: [7, 5, 3, 1]}
    seg_last = {0: [1, 3, 5, 7], 1: [6, 4, 2, 0]}
    for hh in range(2):
        RqT, RkT = Res(), Res()
        Rkt = [Res() for _ in range(NT)]
        Rvt = [Res() for _ in range(NT)]
        ROa = [Res() for _ in range(NT)]
        RS = [Res() for _ in range(2)]
        Rkh = Res(); Rsc = [Res(), Res()]; Rsm = Res(); Rm = Res()
        C = self.conv_setup(22528)
        for hl in range(2):
            h = 2 * hh + hl
            self.conv_chunk(C, Wv, B0 + h * 128, self.din["ml_cw"][j, h], self.din["ml_cb"][j, h], qT[:, hl, :], RqT, 1.0)
            self.conv_chunk(C, Wv, B0 + 512 + h * 128, self.din["ml_cw"][j, 4 + h], self.din["ml_cb"][j, 4 + h], kT[:, hl, :], RkT, sc)
        k.barrier()
        for i in range(NT):
            pb, rpb = self.next_pb()
            for hl in range(2):
                k.op("pe", lambda e: e.transpose(pb[:, hl * 128:(hl + 1) * 128], kT[:, hl, i * 128:(i + 1) * 128], self.identb[:]), reads=[RkT, self.Rid], writes=[rpb])
            k.op("act", lambda e: e.activation(out=ktok[:, i, :], in_=pb[:, 0:256], func=AF.Identity), reads=[rpb], writes=[Rkt[i]])
        self.proj_tok(Wv, B0 + 1024 + hh * 512, 512, lambda i, ps, rps: k.op(
            "act", lambda e: e.activation(out=vtok[:, i, :], in_=ps, func=AF.Identity), reads=[rps], writes=[Rvt[i]]))
        for d in range(2):
            order = list(range(NT)) if d == 0 else list(range(NT - 1, -1, -1))
            mcum = self.masks[:, d, :]
            if d == 1:
                ggw = self.load_w(Wv, B0 + 2048 + hh * 512, 512)
            for ci, i in enumerate(order):
                seg = i // 2
                if i in seg_first[d]:
                    if ci == 0:
                        k.dma("sp", "mm0", mcur[:, 0:1], self.din["st_mm"][j, d:d + 1, :].rearrange("o h -> h o"), writes=[Rm])
                        for hl in range(2):
                            h = 2 * hh + hl
                            k.dma("sp", "mst%d" % hl, S[hl][:, 0:256], self.din["st_mc"][j, d, h], writes=[RS[hl]])
                            k.dma("sp", "mst%d" % hl, S[hl][:, 256:257], self.din["st_mn"][j, d, h].rearrange("(p o) -> p o", o=1), writes=[RS[hl]])
                            k.dma("sp", "mem0", sm[:, 24:25], self.din["st_mm"][j, d:d + 1, h:h + 1].partition_broadcast(128), writes=[Rsm])
                            k.op("act", lambda e: e.activation(out=sm[:, 24:25], in_=sm[:, 24:25], func=AF.Exp), reads=[Rsm], writes=[Rsm])
                            k.op("dve", lambda e: e.tensor_scalar(out=S[hl][:, 0:257], in0=S[hl][:, 0:257], scalar1=sm[:, 24:25], scalar2=None, op0=ALU.mult),
                                 reads=[RS[hl], Rsm], writes=[RS[hl]])
                    else:
                        k.op("dve", lambda e: e.tensor_scalar(out=mcur[:, 0:1], in0=mcur[:, 0:1], scalar1=self.flags[0:4, 0:1], scalar2=None, op0=ALU.mult),
                             reads=[Rm, self.Rmask], writes=[Rm])
                        for hl in range(2):
                            k.op("dve", lambda e: e.tensor_scalar(out=S[hl][:, 0:257], in0=S[hl][:, 0:257], scalar1=self.flags[:, 0:1], scalar2=None, op0=ALU.mult),
                                 reads=[RS[hl], self.Rmask], writes=[RS[hl]])
                    for hl in range(2):
                        k.op("act", lambda e: e.activation(out=Sbf[hl][:, 0:257], in_=S[hl][:, 0:257], func=AF.Identity), reads=[RS[hl]], writes=[RS[hl]])
                lfd = lf[:, i, d * 4:(d + 1) * 4]
                igd = gat[:, i, d * 4:(d + 1) * 4]
                ps, rps = self.next_ps()
                k.op("pe", lambda e: e.matmul(ps[:, 0:4], lhsT=mcum, rhs=lfd, start=True, stop=True), reads=[Rg, self.Rmask], writes=[rps])
                k.op("pe", lambda e: e.matmul(ps[:, 4:8], lhsT=ones128[:], rhs=lfd, start=True, stop=True), reads=[Rg, Rgn], writes=[rps])
                k.op("pe", lambda e: e.matmul(ps[0:4, 8:9], lhsT=lfd, rhs=self.onesc[:, 0:1], start=True, stop=True), reads=[Rg, self.Rmask], writes=[rps])
                k.op("dve", lambda e: e.tensor_tensor(out=sm[:, 0:4], in0=igd, in1=ps[:, 0:4], op=ALU.subtract), reads=[Rg, rps], writes=[Rsm])
                k.op("act", lambda e: e.activation(out=sm[:, 4:12], in_=ps[:, 0:8], func=AF.Exp), reads=[rps], writes=[Rsm])
                k.op("act", lambda e: e.activation(out=mcur[:, 2:3], in_=ps[0:4, 8:9], func=AF.Identity), reads=[rps], writes=[Rm])
                pt, rpt = self.next_ps()
                k.op("pe", lambda e: e.transpose(pt[0:4, 0:128], sm[:, 0:4], self.identf[:]), reads=[Rsm, self.Rid], writes=[rpt])
                k.op("dve", lambda e: e.tensor_reduce(out=mcur[:, 1:2], in_=pt[0:4, 0:128], axis=AX.X, op=ALU.max), reads=[rpt], writes=[Rm])
                k.op("dve", lambda e: e.tensor_scalar(out=mcur[:, 0:1], in0=mcur[:, 0:1], scalar1=mcur[:, 1:2], scalar2=mcur[:, 2:3], op0=ALU.max, op1=ALU.add),
                     reads=[Rm], writes=[Rm])
                k.op("act", lambda e: e.activation(out=sm[:, 0:4], in_=sm[:, 0:4], func=AF.Exp), reads=[Rsm], writes=[Rsm])
                k.op("dve", lambda e: e.tensor_tensor(out=sm[:, 12:16], in0=sm[:, 0:4], in1=sm[:, 8:12], op=ALU.mult), reads=[Rsm], writes=[Rsm])
                for hl in range(2):
                    h = 2 * hh + hl
                    k.op("dve", lambda e: e.tensor_scalar(out=khat[:, hl * 128:(hl + 1) * 128], in0=ktok[:, i, hl * 128:(hl + 1) * 128],
                                                          scalar1=sm[:, 12 + h:13 + h], scalar2=None, op0=ALU.mult), reads=[Rkt[i], Rsm], writes=[Rkh])
                for hl in range(2):
                    h = 2 * hh + hl
                    vs = vtok[:, i, hl * 256:(hl + 1) * 256]
                    qt = qT[:, hl, i * 128:(i + 1) * 128]
                    ps, rps = self.next_ps()
                    k.op("pe", lambda e: e.matmul(ps[:, 0:128], lhsT=kT[:, hl, i * 128:(i + 1) * 128], rhs=qt, start=True, stop=True), reads=[RqT, RkT], writes=[rps])
                    s_, rs_ = scT[hl], Rsc[hl]
                    k.op("dve", lambda e: e.scalar_tensor_tensor(out=s_[:], in0=ps[:, 0:128], scalar=sm[:, h:h + 1], in1=mcum, op0=ALU.mult, op1=ALU.mult),
                         reads=[rps, Rsm, self.Rmask], writes=[rs_])
                    po, rpo = self.next_ps()
                    k.op("pe", lambda e: e.matmul(po[:, 0:256], lhsT=s_[:], rhs=vs, start=True, stop=False), reads=[rs_, Rvt[i]], writes=[rpo])
                    k.op("pe", lambda e: e.matmul(po[:, 0:256], lhsT=qt, rhs=Sbf[hl][:, 0:256], start=False, stop=True), reads=[RqT, RS[hl]], writes=[rpo])
                    k.op("pe", lambda e: e.matmul(po[:, 256:257], lhsT=s_[:], rhs=onesb[:], start=True, stop=False), reads=[rs_, Rgn], writes=[rpo])
                    k.op("pe", lambda e: e.matmul(po[:, 256:257], lhsT=qt, rhs=Sbf[hl][:, 256:257], start=False, stop=True), reads=[RqT, RS[hl]], writes=[rpo])
                    dn = sm[:, 16 + hl:17 + hl]
                    k.op("act", lambda e: e.activation(out=dn, in_=po[:, 256:257], func=AF.Abs, scale=sm[:, 4 + h:5 + h]), reads=[rpo, Rsm], writes=[Rsm])
                    k.op("dve", lambda e: e.tensor_scalar(out=dn, in0=dn, scalar1=1.0, scalar2=None, op0=ALU.max), reads=[Rsm], writes=[Rsm])
                    k.op("dve", lambda e: e.reciprocal(out=dn, in_=dn), reads=[Rsm], writes=[Rsm])
                    k.op("dve", lambda e: e.tensor_tensor(out=dn, in0=dn, in1=sm[:, 4 + h:5 + h], op=ALU.mult), reads=[Rsm], writes=[Rsm])
                    if d == 0:
                        k.op("act", lambda e: e.activation(out=Oacc[:, i, hl * 256:(hl + 1) * 256], in_=po[:, 0:256], func=AF.Identity, scale=dn), reads=[rpo, Rsm], writes=[ROa[i]])
                    else:
                        k.op("dve", lambda e: e.scalar_tensor_tensor(out=Ofin[:, hl * 256:(hl + 1) * 256], in0=po[:, 0:256], scalar=dn, in1=Oacc[:, i, hl * 256:(hl + 1) * 256],
                                                                     op0=ALU.mult, op1=ALU.add), reads=[rpo, Rsm, ROa[i]], writes=[ROf])
                    pd, rpd = self.next_ps()
                    k.op("pe", lambda e: e.matmul(pd[:, 0:256], lhsT=khat[:, hl * 128:(hl + 1) * 128], rhs=vs, start=True, stop=True), reads=[Rkh, Rvt[i]], writes=[rpd])
                    k.op("pe", lambda e: e.matmul(pd[:, 256:257], lhsT=khat[:, hl * 128:(hl + 1) * 128], rhs=onesb[:], start=True, stop=True), reads=[Rkh, Rgn], writes=[rpd])
                    k.op("dve", lambda e: e.scalar_tensor_tensor(out=S[hl][:, 0:257], in0=S[hl][:, 0:257], scalar=sm[:, 8 + h:9 + h], in1=pd[:, 0:257], op0=ALU.mult, op1=ALU.add),
                         reads=[RS[hl], Rsm, rpd], writes=[RS[hl]])
                    k.op("act", lambda e: e.activation(out=Sbf[hl][:, 0:257], in_=S[hl][:, 0:257], func=AF.Identity), reads=[RS[hl]], writes=[RS[hl]])
                if i in seg_last[d]:
                    k.op("dve", lambda e: e.tensor_scalar(out=dg[:], in0=self.identf[0:4, 0:4], scalar1=mcur[:, 0:1], scalar2=None, op0=ALU.mult), reads=[Rm, self.Rid], writes=[Rm])
                    pm, rpm = self.next_ps()
                    k.op("pe", lambda e: e.matmul(pm[:, 0:4], lhsT=ones128[0:4, :], rhs=dg[:], start=True, stop=True), reads=[Rm, Rgn], writes=[rpm])
                    k.op("act", lambda e: e.activation(out=sm[:, 20:24], in_=pm[:, 0:4], func=AF.Exp, scale=-1.0), reads=[rpm], writes=[Rsm])
                    if hh == 0:
                        k.dma("sp", "mmo", self.dout["o_mm"][j, seg, d:d + 1, :].rearrange("o h -> h o"), mcur[:, 0:1], reads=[Rm])
                    for hl in range(2):
                        h = 2 * hh + hl
                        k.op("dve", lambda e: e.tensor_scalar(out=stage[:, hl * 260:hl * 260 + 257], in0=S[hl][:, 0:257], scalar1=sm[:, 20 + h:21 + h], scalar2=None, op0=ALU.mult),
                             reads=[RS[hl], Rsm], writes=[Rstage])
                        k.dma("sp", "mco%d" % hl, self.dout["o_mc"][j, seg, d, h], stage[:, hl * 260:hl * 260 + 256], reads=[Rstage])
                        k.dma("sp", "mno%d" % hl, self.dout["o_mn"][j, seg, d, h].rearrange("(p o) -> p o", o=1), stage[:, hl * 260 + 256:hl * 260 + 257], reads=[Rstage])
                if d == 1:
                    for hl in range(2):
                        gcol = (2 * hh + hl) * 256
                        rs = self.rstd_of(Ofin[:, hl * 256:(hl + 1) * 256], ROf, 8 + hl, 256, junk=self.tmpf[1][:, 0:256], rjunk=self.Rtmpf[1])
                        k.op("dve", lambda e: e.scalar_tensor_tensor(out=Ofin[:, hl * 256:(hl + 1) * 256], in0=Ofin[:, hl * 256:(hl + 1) * 256], scalar=rs,
                                                                     in1=gnb[:, gcol:gcol + 256], op0=ALU.mult, op1=ALU.mult),
                             reads=[ROf, self.Rss[8 + hl], Rgn], writes=[ROf])
                    w, wres = ggw
                    ps, rps = self.next_ps()
                    for kc in range(8):
                        k.op("pe", lambda e, kc=kc: e.matmul(ps[:, :], lhsT=self.hT[:, kc, i * 128:(i + 1) * 128], rhs=w[:, kc, :], start=(kc == 0), stop=(kc == 7)),
                             reads=wres + [self.RhT[i]], writes=[rps])
                    te = self.tmpf[1][:, 512:1024]
                    k.op("act", lambda e: e.activation(out=te, in_=ps[:, :], func=AF.Sigmoid), reads=[rps], writes=[self.Rtmpf[1]])
                    k.op("dve", lambda e: e.tensor_tensor(out=yb[:, 0:512], in0=Ofin[:, 0:512], in1=te, op=ALU.mult), reads=[ROf, self.Rtmpf[1]], writes=[Ryb])
                    self.transpose_to(yb, Ryb, yT[:, 8 + hh * 4:8 + (hh + 1) * 4, i * 128:(i + 1) * 128], RyT[i], ncol=4)
        k.barrier()


Prog.mlstm = _mlstm


def _mixer_ab(self, l):
    k = self.k
    j = l // 2
    self.mix_consts()
    Wv = self.din["w_in_ab"][j].rearrange("(kc p) n -> p kc n", p=128)
    yT = self.MA[:, 0:16384].rearrange("p (c t) -> p c t", c=16)
    RyT = [Res() for _ in range(NT)]
    self.rs_ssd = self.scr("rs_ssd", [128, NT])
    self.Rrs = Res()
    if "ssd" in self.parts:
        self.ssd(l, j, Wv, yT, RyT)
    else:
        k.op("dve", lambda e: e.memset(yT[:, 0:8, :], 0.0), writes=RyT)
        k.op("dve", lambda e: e.memset(self.rs_ssd[:], 1.0), writes=[self.Rrs])
    k.barrier()
    if "rw" in self.parts:
        self.rwkv(l, j, Wv, yT, RyT)
    else:
        k.op("dve", lambda e: e.memset(yT[:, 8:16, :], 0.0), writes=RyT)
    k.barrier()
    self.out_proj(self.din["w_out_ab"][j], yT, RyT, row_scale=(self.rs_ssd, self.Rrs))


def _out_proj(self, Wd, yT, RyT, gi=2, row_scale=None):
    k = self.k
    wv = Wd.rearrange("(kc p) n -> p kc n", p=128)
    Fall = self.MA[:, 16384:32768].bitcast(F32).rearrange("p (i n) -> p i n", i=NT)
    RF = [Res() for _ in range(NT)]
    for nh in range(2):
        wap, wres = self.wslots(2)
        w = wap.rearrange("p (kc n) -> p kc n", kc=16)
        for q in range(2):
            k.dma("pool", self.wch, w[:, q * 8:(q + 1) * 8, :], wv[:, q * 8:(q + 1) * 8, nh * 512:(nh + 1) * 512], writes=wres)
        for i in range(NT):
            Fi = Fall[:, i, nh * 512:(nh + 1) * 512]
            if row_scale is None:
                ps, rps = self.next_ps()
                for kc in range(16):
                    k.op("pe", lambda e, kc=kc: e.matmul(ps[:, :], lhsT=yT[:, kc, i * 128:(i + 1) * 128], rhs=w[:, kc, :],
                                                         start=(kc == 0), stop=(kc == 15)), reads=wres + [RyT[i]], writes=[rps])
                k.op("act", lambda e: e.activation(out=Fi, in_=ps[:, :], func=AF.Identity), reads=[rps], writes=[RF[i]])
            else:
                rsc, rrs = row_scale
                ps2, rps2 = self.next_ps()
                for kc in range(8, 16):
                    k.op("pe", lambda e, kc=kc: e.matmul(ps2[:, :], lhsT=yT[:, kc, i * 128:(i + 1) * 128], rhs=w[:, kc, :],
                                                         start=(kc == 8), stop=(kc == 15)), reads=wres + [RyT[i]], writes=[rps2])
                k.op("act", lambda e: e.activation(out=Fi, in_=ps2[:, :], func=AF.Identity), reads=[rps2], writes=[RF[i]])
                ps1, rps1 = self.next_ps()
                for kc in range(8):
                    k.op("pe", lambda e, kc=kc: e.matmul(ps1[:, :], lhsT=yT[:, kc, i * 128:(i + 1) * 128], rhs=w[:, kc, :],
                                                         start=(kc == 0), stop=(kc == 7)), reads=wres + [RyT[i]], writes=[rps1])
                k.op("dve", lambda e: e.scalar_tensor_tensor(out=Fi, in0=ps1[:, :], scalar=rsc[:, i:i + 1], in1=Fi, op0=ALU.mult, op1=ALU.add),
                     reads=[rps1, rrs, RF[i]], writes=[RF[i]])
    for i in range(NT):
        self.resid_add(i, Fall[:, i, :], RF[i], gi)


def _ssd(self, l, j, Wv, yT, RyT):
    k = self.k
    MA = self.MA
    xT = MA[:, 16384:18432].rearrange("p (c t) -> p c t", c=2)
    BT = MA[:, 18432:19456]
    CT = MA[:, 19456:20480]
    xs = MA[:, 20480:22528].rearrange("p (i c) -> p i c", i=NT)
    Btok = MA[:, 22528:23552].rearrange("p (i c) -> p i c", i=NT)
    Oacc = MA[:, 23552:25600].rearrange("p (i c) -> p i c", i=NT)
    dt = self.scr("sdt", [128, NT, 32])
    la = self.scr("sla", [128, NT, 32])
    cst = self.scr("scst", [128, 96])
    Rdt = Res()
    k.dma("sp", "sc0", cst[:, 0:32], self.din["ssd_dt_bias"][j:j + 1].rearrange("o d h -> o (d h)").partition_broadcast(128), writes=[Rdt])
    k.dma("sp", "sc0", cst[:, 32:64], self.din["ssd_a_log"][j:j + 1].rearrange("o d h -> o (d h)").partition_broadcast(128), writes=[Rdt])
    k.dma("sp", "sc0", cst[:, 64:80], self.din["ssd_d"][j:j + 1, :].partition_broadcast(128), writes=[Rdt])
    k.op("act", lambda e: e.activation(out=cst[:, 32:64], in_=cst[:, 32:64], func=AF.Exp), reads=[Rdt], writes=[Rdt])
    self.proj_tok(Wv, 3072, 32, lambda i, ps, rps: k.op(
        "dve", lambda e: e.tensor_tensor(out=dt[:, i, :], in0=ps, in1=cst[:, 0:32], op=ALU.add), reads=[rps, Rdt], writes=[Rdt]))
    k.op("act", lambda e: e.activation(out=dt[:], in_=dt[:], func=AF.Exp), reads=[Rdt], writes=[Rdt])
    k.op("act", lambda e: e.activation(out=dt[:], in_=dt[:], func=AF.Ln, bias=self.onesc[:, 0:1]), reads=[Rdt, self.Rmask], writes=[Rdt])
    k.op("dve", lambda e: e.scalar_tensor_tensor(out=la[:], in0=dt[:], scalar=-1.0, in1=cst[:, 32:64].unsqueeze(1).to_broadcast([128, NT, 32]),
                                                 op0=ALU.mult, op1=ALU.mult), reads=[Rdt], writes=[Rdt])
    nw = self.scr("snw", [128, 8])
    k.dma("sp", "sc1", nw[:], self.din["ssd_nw"][j], writes=[Rdt])
    ones128 = self.scr("ones128", [128, 128])
    k.op("dve", lambda e: e.memset(ones128[:], 1.0), writes=[Rdt])
    ssq = self.scr("sssq", [128, NT, 4])
    k.op("dve", lambda e: e.memset(ssq[:], 0.0), writes=[self.Rrs])
    S = self.scr("S0", [128, 256]); Sbf = self.scr("Sb0", [128, 256], BF16)
    st = self.scr("sst", [128, 80])
    CBm = self.scr("sCBm", [128, 128])
    seg_ = [self.scr("sseg%d" % i, [128, 128]) for i in range(2)]
    scT = [self.scr("scT%d" % i, [128, 128], BF16) for i in range(2)]
    khat = self.scr("khat", [128, 256], BF16)
    Ofin = self.junk; ROf = self.Rjunk
    tin = self.tmpf[0]; Rtin = self.Rtmpf[0]
    zs = self.tmpf[1]; Rzs = self.Rtmpf[1]
    yb = self.hb[0]; Ryb = self.Rhb[0]
    seg_first = {0: [0, 2, 4, 6], 1: [7, 5, 3, 1]}
    seg_last = {0: [1, 3, 5, 7], 1: [6, 4, 2, 0]}
    for g in range(4):
        RxT, RBT, RCT = Res(), Res(), Res()
        Rxs = [Res() for _ in range(NT)]
        RBt = [Res() for _ in range(NT)]
        ROa = [Res() for _ in range(NT)]
        RS = Res(); Rst = Res(); RCB = Res(); Rseg = [Res(), Res()]; Rsc = [Res(), Res()]; Rkh = Res()
        C = self.conv_setup(25600)
        cw, cb = self.din["ssd_cw"], self.din["ssd_cb"]
        for c in range(2):
            ch = 2 * g + c
            self.conv_chunk(C, Wv, 1024 + ch * 128, cw[j, ch], cb[j, ch], xT[:, c, :], RxT)
        self.conv_chunk(C, Wv, 1024 + (8 + g) * 128, cw[j, 8 + g], cb[j, 8 + g], BT, RBT)
        self.conv_chunk(C, Wv, 1024 + (12 + g) * 128, cw[j, 12 + g], cb[j, 12 + g], CT, RCT)
        k.barrier()
        for i in range(NT):
            pb, rpb = self.next_pb()
            for c in range(2):
                k.op("pe", lambda e: e.transpose(pb[:, c * 128:(c + 1) * 128], xT[:, c, i * 128:(i + 1) * 128], self.identb[:]), reads=[RxT, self.Rid], writes=[rpb])
            k.op("pe", lambda e: e.transpose(pb[:, 256:384], BT[:, i * 128:(i + 1) * 128], self.identb[:]), reads=[RBT, self.Rid], writes=[rpb])
            k.op("act", lambda e: e.activation(out=xs[:, i, :], in_=pb[:, 0:256], func=AF.Identity), reads=[rpb], writes=[Rxs[i]])
            k.op("act", lambda e: e.activation(out=Btok[:, i, :], in_=pb[:, 256:384], func=AF.Identity), reads=[rpb], writes=[RBt[i]])
        for d in range(2):
            order = list(range(NT)) if d == 0 else list(range(NT - 1, -1, -1))
            mcum = self.masks[:, d, :]
            if d == 1:
                zw = self.load_w(Wv, g * 256, 256)
            for ci, i in enumerate(order):
                sgi = i // 2
                if i in seg_first[d]:
                    if ci == 0:
                        for hl in range(4):
                            k.dma("sp", "sst%d" % hl, S[:, hl * 64:(hl + 1) * 64], self.din["st_ssd"][j, d, 4 * g + hl], writes=[RS])
                    else:
                        k.op("dve", lambda e: e.tensor_scalar(out=S[:], in0=S[:], scalar1=self.flags[:, 0:1], scalar2=None, op0=ALU.mult), reads=[RS, self.Rmask], writes=[RS])
                    k.op("act", lambda e: e.activation(out=Sbf[:], in_=S[:], func=AF.Identity), reads=[RS], writes=[RS])
                lad = la[:, i, d * 16:(d + 1) * 16]
                dtd = dt[:, i, d * 16:(d + 1) * 16]
                ps, rps = self.next_ps()
                k.op("pe", lambda e: e.matmul(ps[:, 0:16], lhsT=mcum, rhs=lad, start=True, stop=True), reads=[Rdt, self.Rmask], writes=[rps])
                k.op("pe", lambda e: e.matmul(ps[:, 16:32], lhsT=ones128[:], rhs=lad, start=True, stop=True), reads=[Rdt], writes=[rps])
                k.op("act", lambda e: e.activation(out=st[:, 0:16], in_=ps[:, 0:16], func=AF.Identity), reads=[rps], writes=[Rst])
                k.op("dve", lambda e: e.tensor_tensor(out=st[:, 16:32], in0=ps[:, 16:32], in1=st[:, 0:16], op=ALU.subtract), reads=[rps, Rst], writes=[Rst])
                k.op("act", lambda e: e.activation(out=st[:, 16:32], in_=st[:, 16:32], func=AF.Exp), reads=[Rst], writes=[Rst])
                k.op("dve", lambda e: e.tensor_tensor(out=st[:, 16:32], in0=st[:, 16:32], in1=dtd, op=ALU.mult), reads=[Rst, Rdt], writes=[Rst])
                k.op("act", lambda e: e.activation(out=st[:, 32:48], in_=ps[:, 16:32], func=AF.Exp), reads=[rps], writes=[Rst])
                k.op("act", lambda e: e.activation(out=st[:, 48:64], in_=st[:, 0:16], func=AF.Exp), reads=[Rst], writes=[Rst])
                ps, rps = self.next_ps()
                k.op("pe", lambda e: e.matmul(ps[:, 0:128], lhsT=BT[:, i * 128:(i + 1) * 128], rhs=CT[:, i * 128:(i + 1) * 128], start=True, stop=True), reads=[RBT, RCT], writes=[rps])
                k.op("dve", lambda e: e.tensor_tensor(out=CBm[:], in0=ps[:, 0:128], in1=mcum, op=ALU.mult), reads=[rps, self.Rmask], writes=[RCB])
                po, rpo = self.next_ps(pin=True)
                pd, rpd = self.next_ps(pin=True)
                for hl in range(4):
                    h = 4 * g + hl
                    pbt, rpbt = self.next_ps()
                    k.op("pe", lambda e: e.matmul(pbt[:, 0:128], lhsT=la[:, i, d * 16 + h:d * 16 + h + 1].to_broadcast([128, 128]), rhs=mcum, start=True, stop=True),
                         reads=[Rdt, self.Rmask], writes=[rpbt])
                    sg, rsg = seg_[hl % 2], Rseg[hl % 2]
                    k.op("dve", lambda e: e.tensor_scalar(out=sg[:], in0=pbt[:, 0:128], scalar1=st[:, h:h + 1], scalar2=0.0, op0=ALU.subtract, op1=ALU.min),
                         reads=[rpbt, Rst], writes=[rsg])
                    k.op("act", lambda e: e.activation(out=sg[:], in_=sg[:], func=AF.Exp), reads=[rsg], writes=[rsg])
                    s_, rs_ = scT[hl % 2], Rsc[hl % 2]
                    k.op("dve", lambda e: e.scalar_tensor_tensor(out=s_[:], in0=sg[:], scalar=dt[:, i, d * 16 + h:d * 16 + h + 1], in1=CBm[:], op0=ALU.mult, op1=ALU.mult),
                         reads=[rsg, Rdt, RCB], writes=[rs_])
                    k.op("pe", lambda e: e.matmul(po[:, hl * 64:(hl + 1) * 64], lhsT=s_[:], rhs=xs[:, i, hl * 64:(hl + 1) * 64], start=True, stop=True), reads=[rs_, Rxs[i]], writes=[rpo])
                    k.op("dve", lambda e: e.tensor_scalar(out=khat[:, (hl % 2) * 128:(hl % 2 + 1) * 128], in0=Btok[:, i, :], scalar1=st[:, 16 + h:17 + h], scalar2=None, op0=ALU.mult),
                         reads=[RBt[i], Rst], writes=[Rkh])
                    k.op("pe", lambda e: e.matmul(pd[:, hl * 64:(hl + 1) * 64], lhsT=khat[:, (hl % 2) * 128:(hl % 2 + 1) * 128], rhs=xs[:, i, hl * 64:(hl + 1) * 64], start=True, stop=True),
                         reads=[Rkh, Rxs[i]], writes=[rpd])
                pi_, rpi = self.next_ps()
                k.op("pe", lambda e: e.matmul(pi_[:, 0:256], lhsT=CT[:, i * 128:(i + 1) * 128], rhs=Sbf[:], start=True, stop=True), reads=[RCT, RS], writes=[rpi])
                ebx = st[:, 48 + 4 * g:52 + 4 * g].unsqueeze(2).to_broadcast([128, 4, 64])
                k.op("dve", lambda e: e.tensor_tensor(out=tin[:, 0:256].rearrange("p (h v) -> p h v", h=4), in0=pi_[:, 0:256].rearrange("p (h v) -> p h v", h=4), in1=ebx, op=ALU.mult),
                     reads=[rpi, Rst], writes=[Rtin])
                if d == 0:
                    k.op("dve", lambda e: e.tensor_tensor(out=Oacc[:, i, :], in0=po[:, 0:256], in1=tin[:, 0:256], op=ALU.add), reads=[rpo, Rtin], writes=[ROa[i]])
                else:
                    k.op("dve", lambda e: e.tensor_tensor(out=Ofin[:, 0:256], in0=po[:, 0:256], in1=tin[:, 0:256], op=ALU.add), reads=[rpo, Rtin], writes=[ROf])
                    k.op("dve", lambda e: e.tensor_tensor(out=Ofin[:, 0:256], in0=Ofin[:, 0:256], in1=Oacc[:, i, :], op=ALU.add), reads=[ROf, ROa[i]], writes=[ROf])
                Gx = st[:, 32 + 4 * g:36 + 4 * g].unsqueeze(2).to_broadcast([128, 4, 64])
                k.op("dve", lambda e: e.tensor_tensor(out=S[:].rearrange("p (h v) -> p h v", h=4), in0=S[:].rearrange("p (h v) -> p h v", h=4), in1=Gx, op=ALU.mult), reads=[RS, Rst], writes=[RS])
                k.op("dve", lambda e: e.tensor_tensor(out=S[:], in0=S[:], in1=pd[:, 0:256], op=ALU.add), reads=[RS, rpd], writes=[RS])
                k.op("act", lambda e: e.activation(out=Sbf[:], in_=S[:], func=AF.Identity), reads=[RS], writes=[RS])
                self.unpin(rpo, rpd)
                if i in seg_last[d]:
                    for hl in range(4):
                        k.dma("sp", "sso%d" % hl, self.dout["o_ssd"][j, sgi, d, 4 * g + hl], S[:, hl * 64:(hl + 1) * 64], reads=[RS])
                if d == 1:
                    Dx = cst[:, 64 + 4 * g:68 + 4 * g].unsqueeze(2).to_broadcast([128, 4, 64])
                    k.op("dve", lambda e: e.tensor_tensor(out=tin[:, 256:512].rearrange("p (h v) -> p h v", h=4), in0=xs[:, i, :].rearrange("p (h v) -> p h v", h=4), in1=Dx, op=ALU.mult),
                         reads=[Rxs[i], Rdt], writes=[Rtin])
                    k.op("dve", lambda e: e.tensor_tensor(out=Ofin[:, 0:256], in0=Ofin[:, 0:256], in1=tin[:, 256:512], op=ALU.add), reads=[ROf, Rtin], writes=[ROf])
                    w, wres = zw
                    ps, rps = self.next_ps()
                    for kc in range(8):
                        k.op("pe", lambda e, kc=kc: e.matmul(ps[:, 0:256], lhsT=self.hT[:, kc, i * 128:(i + 1) * 128], rhs=w[:, kc, :], start=(kc == 0), stop=(kc == 7)),
                             reads=wres + [self.RhT[i]], writes=[rps])
                    k.op("act", lambda e: e.activation(out=zs[:, 0:256], in_=ps[:, 0:256], func=AF.Silu), reads=[rps], writes=[Rzs])
                    k.op("dve", lambda e: e.tensor_tensor(out=Ofin[:, 0:256], in0=Ofin[:, 0:256], in1=zs[:, 0:256], op=ALU.mult), reads=[ROf, Rzs], writes=[ROf])
                    k.op("act", lambda e: e.activation(out=zs[:, 256:512], in_=Ofin[:, 0:256], func=AF.Square, accum_out=ssq[:, i, g:g + 1]), reads=[ROf, self.Rrs], writes=[Rzs, self.Rrs])
                    k.op("act", lambda e: e.activation(out=yb[:, 0:256], in_=Ofin[:, 0:256], func=AF.Identity), reads=[ROf], writes=[Ryb])
                    pb, rpb = self.next_pb()
                    for c in range(2):
                        k.op("pe", lambda e: e.transpose(pb[:, c * 128:(c + 1) * 128], yb[:, c * 128:(c + 1) * 128], self.identb[:]), reads=[Ryb, self.Rid], writes=[rpb])
                    for c in range(2):
                        kc = 2 * g + c
                        k.op("act", lambda e: e.activation(out=yT[:, kc, i * 128:(i + 1) * 128], in_=pb[:, c * 128:(c + 1) * 128], func=AF.Identity, scale=nw[:, kc:kc + 1]),
                             reads=[rpb, Rdt], writes=[RyT[i]])
        k.barrier()
    rs = self.rs_ssd
    k.op("dve", lambda e: e.tensor_reduce(out=rs[:], in_=ssq[:], axis=AX.X, op=ALU.add), reads=[self.Rrs], writes=[self.Rrs])
    k.op("act", lambda e: e.activation(out=rs[:], in_=rs[:], func=AF.Ln, scale=1.0 / 1024, bias=self.epsb[:, 0:1]), reads=[self.Rrs, self.Rid], writes=[self.Rrs])
    k.op("act", lambda e: e.activation(out=rs[:], in_=rs[:], func=AF.Exp, scale=-0.5), reads=[self.Rrs], writes=[self.Rrs])


Prog.mixer_ab = _mixer_ab
Prog.out_proj = _out_proj
Prog.ssd = _ssd


def _rwkv(self, l, j, Wv, yT, RyT):
    k = self.k
    MA = self.MA
    B0 = 3104
    f32v = lambda a, b: MA[:, a:b].bitcast(F32)
    twT = MA[:, 16384:17408]; adT = MA[:, 17408:18432]; sgT = MA[:, 18432:19456]
    rT = f32v(19456, 21504); kT = f32v(21504, 23552); vT = f32v(23552, 25600); kkT = f32v(25600, 27648)
    Vtok = f32v(27648, 29696).rearrange("p (i c) -> p i c", i=NT)
    Yacc = MA[:, 29696:30720].rearrange("p (i c) -> p i c", i=NT)
    bonT = MA[:, 30720:31744]
    raw = f32v(31744, 33796)
    tsm = MA[:, 33796:35844].rearrange("p (a t) -> p a t", a=2)
    t1, Rt1 = self.tmpf[0], self.Rtmpf[0]
    t2, Rt2 = self.tmpf[1], self.Rtmpf[1]
    Rsh = Res(); Rraw = Res(); Rvec = Res()
    vec = self.scr("rwvec", [128, 72])
    mu = self.scr("rwmu", [128, 56])
    cst = self.scr("rwc", [128, 4])
    k.dma("sp", "rv0", vec[:], self.din["rw_vec"][j], writes=[Rvec])
    k.dma("sp", "rv0", mu[:, 28:55], self.din["rw_mu"][j], writes=[Rvec])
    k.op("dve", lambda e: e.tensor_scalar(out=mu[:, 0:27], in0=mu[:, 28:55], scalar1=-1.0, scalar2=1.0, op0=ALU.mult, op1=ALU.add), reads=[Rvec], writes=[Rvec])
    k.op("dve", lambda e: e.tensor_scalar(out=mu[:, 28:55], in0=mu[:, 28:55], scalar1=0.5, scalar2=None, op0=ALU.mult), reads=[Rvec], writes=[Rvec])
    k.op("dve", lambda e: e.memset(cst[:, 0:1], 1e-12), writes=[Rvec])
    k.op("dve", lambda e: e.memset(cst[:, 1:2], -0.5), writes=[Rvec])
    k.op("dve", lambda e: e.memset(cst[:, 2:3], 64e-5), writes=[Rvec])
    nvec = self.scr("rwnvec", [128, 24])
    k.op("dve", lambda e: e.tensor_scalar(out=nvec[:, 0:16], in0=vec[:, 0:16], scalar1=-1.0, scalar2=None, op0=ALU.mult), reads=[Rvec], writes=[Rvec])
    k.op("dve", lambda e: e.tensor_scalar(out=nvec[:, 16:24], in0=vec[:, 40:48], scalar1=-1.0, scalar2=1.0, op0=ALU.mult, op1=ALU.add), reads=[Rvec], writes=[Rvec])
    k.op("dve", lambda e: e.memset(raw[:, 0:1], 0.0), writes=[Rraw])
    k.op("dve", lambda e: e.memset(raw[:, 1025:1026], 0.0), writes=[Rraw])
    for a in range(2):
        k.dma("pool", "tsm%d" % a, tsm[:, a, :], self.din["tsm"][a:a + 1, :].partition_broadcast(128), writes=[Rsh])
    ones128 = self.scr("ones128", [128, 128])
    k.op("dve", lambda e: e.memset(ones128[:], 1.0), writes=[Rvec])
    BD = self.masks[:, 4, :]

    def rw_block(cb, post):
        self.proj_feat(Wv, B0 + cb * 128, 128, lambda c, half, ps, rps, m: k.op(
            "act", lambda e: e.activation(out=raw[:, 1 + half * 512:1 + (half + 1) * 512], in_=ps, func=AF.Identity), reads=[rps], writes=[Rraw]))
        k.op("dve", lambda e: e.tensor_tensor(out=t1[:], in0=raw[:, 0:1024], in1=tsm[:, 0, :], op=ALU.mult), reads=[Rraw, Rsh], writes=[Rt1])
        k.op("dve", lambda e: e.tensor_tensor(out=t2[:], in0=raw[:, 2:1026], in1=tsm[:, 1, :], op=ALU.mult), reads=[Rraw, Rsh], writes=[Rt2])
        k.op("dve", lambda e: e.tensor_tensor(out=t1[:], in0=t1[:], in1=t2[:], op=ALU.add), reads=[Rt1, Rt2], writes=[Rt1])
        k.op("dve", lambda e: e.tensor_scalar(out=t2[:], in0=raw[:, 1:1025], scalar1=mu[:, cb:cb + 1], scalar2=None, op0=ALU.mult), reads=[Rraw, Rvec], writes=[Rt2])
        k.op("dve", lambda e: e.scalar_tensor_tensor(out=t1[:], in0=t1[:], scalar=mu[:, 28 + cb:29 + cb], in1=t2[:], op0=ALU.mult, op1=ALU.add), reads=[Rt1, Rt2, Rvec], writes=[Rt1])
        post()

    Rlo = Res()
    rw_block(24, lambda: k.op("act", lambda e: e.activation(out=twT, in_=t1[:], func=AF.Tanh), reads=[Rt1], writes=[Rlo]))
    rw_block(25, lambda: k.op("act", lambda e: e.activation(out=adT, in_=t1[:], func=AF.Identity), reads=[Rt1], writes=[Rlo]))
    rw_block(26, lambda: k.op("act", lambda e: e.activation(out=sgT, in_=t1[:], func=AF.Sigmoid), reads=[Rt1], writes=[Rlo]))
    w2v = self.din["rwkv_w2"][j].rearrange("d r c -> (d r) c")
    a2v = self.din["rwkv_a2"][j].rearrange("d r c -> (d r) c")
    g2v = self.din["rwkv_g2"][j]
    lw3 = self.scr("rwlw3", [128, 3, 128], BF16)
    P = self.scr("rwP", [128, 64]); Z = self.scr("rwZ", [64, 128])
    U = self.scr("rwU", [128, 64]); RH = self.scr("rwRH", [128, 64])
    sm = self.scr("rwsm", [128, 16])
    slot = lambda n: (self.tmpf[0], self.tmpf[1], self.junk)[n // 8][:, (n % 8) * 128:(n % 8 + 1) * 128]
    QR = self.tmpf[0][:, 0:256]
    KT_ = slot(2); CT_ = slot(3); aT = slot(4); e2 = slot(5); cs = slot(6); csx = slot(7)
    Ep = slot(8); Em = slot(9); Ex = slot(10); kd = slot(11); cc = slot(12); Khat = slot(13); Chat = slot(14); KhT = slot(15)
    AkT = self.junk[:, 0:256]; AcT = self.junk[:, 256:512]; MT = slot(20); X = [slot(21), slot(22)]; ChT = slot(23)
    PP = [self.scr("rwPP%d" % i, [128, 256]) for i in range(2)]
    nb = self.scr("nrmb", [128, 1024])
    hbf = [self.hb[0][:].bitcast(F32), self.hb[1][:].bitcast(F32)]
    khf = self.scr("khat", [128, 256], BF16)[:].bitcast(F32)
    nsl = lambda n: nb[:, n * 128:(n + 1) * 128]
    U1 = self.scr("rwU1", [128, 64]); RH1 = self.scr("rwRH1", [128, 64])
    TS = [
        {"AkT": AkT, "AcT": AcT, "MT": MT, "X": X, "XT": [slot(11), slot(12)], "D": slot(4), "DT": slot(5), "W": slot(6), "WT": slot(7), "PP": PP, "RH": RH, "U": U},
        {"AkT": nb[:, 0:256], "AcT": nb[:, 256:512], "MT": nsl(4), "X": [nsl(5), nsl(6)], "XT": [nsl(7), hbf[0][:, 0:128]], "D": hbf[0][:, 128:256], "DT": hbf[0][:, 256:384],
         "W": hbf[0][:, 384:512], "WT": khf, "PP": [hbf[1][:, 0:256], hbf[1][:, 256:512]], "RH": RH1, "U": U1},
    ]
    Ys = self.scr("rwYs", [128, 128]); yn = self.scr("rwyn", [128, 128])
    seg_first = {0: [0, 2, 4, 6], 1: [7, 5, 3, 1]}
    seg_last = {0: [1, 3, 5, 7], 1: [6, 4, 2, 0]}
    k.barrier()
    for hp in range(8):
        cp = slice(hp * 128, (hp + 1) * 128)
        Rr, Rk, Rv, Rkk, Rbon = Res(), Res(), Res(), Res(), Res()
        RVt = [Res() for _ in range(NT)]; RYa = [Res() for _ in range(NT)]
        Rlw = Res(); RP = Res(); RU = Res(); RRH = Res(); Rsm = Res(); RYs = Res(); Ryn = Res()
        RT1 = {n: Res() for n in ("AkT", "AcT", "MT", "X0", "X1", "XT0", "XT1", "D", "DT", "W", "WT", "PP0", "PP1", "RH", "U")}
        Rs = {n: Res() for n in ("QR", "KT", "CT", "aT", "e2", "cs", "csx", "Ep", "Em", "Ex", "kd", "cc", "Khat", "Chat", "KhT", "ChT", "AkT", "AcT", "MT", "X0", "X1", "PP0", "PP1")}
        rw_block(hp, lambda: k.op("act", lambda e: e.activation(out=rT, in_=t1[:], func=AF.Identity), reads=[Rt1], writes=[Rr]))
        rw_block(8 + hp, lambda: k.op("act", lambda e: e.activation(out=kT, in_=t1[:], func=AF.Identity), reads=[Rt1], writes=[Rk]))
        rw_block(16 + hp, lambda: k.op("act", lambda e: e.activation(out=vT, in_=t1[:], func=AF.Identity), reads=[Rt1], writes=[Rv]))
        k.dma("pool", "lw3a", lw3[:, 0, :], w2v[:, cp], writes=[Rlw])
        k.dma("pool", "lw3b", lw3[:, 1, :], a2v[:, cp], writes=[Rlw])
        k.dma("pool", "lw3c", lw3[:, 2, :], g2v[:, cp], writes=[Rlw])
        k.op("dve", lambda e: e.tensor_scalar(out=kkT, in0=kT, scalar1=vec[:, 32 + hp:33 + hp], scalar2=None, op0=ALU.mult), reads=[Rk, Rvec], writes=[Rkk])
        k.op("dve", lambda e: e.tensor_tensor(out=t1[:], in0=kkT, in1=kkT, op=ALU.mult), reads=[Rkk], writes=[Rt1])
        for half in range(2):
            hs = slice(half * 512, (half + 1) * 512)
            ps, rps = self.next_ps()
            k.op("pe", lambda e: e.matmul(ps[:, :], lhsT=BD, rhs=t1[:, hs], start=True, stop=True), reads=[Rt1, self.Rmask], writes=[rps])
            k.op("act", lambda e: e.activation(out=t2[:, hs], in_=ps[:, :], func=AF.Ln, bias=cst[:, 0:1]), reads=[rps, Rvec], writes=[Rt2])
        k.op("act", lambda e: e.activation(out=t2[:], in_=t2[:], func=AF.Exp, scale=-0.5), reads=[Rt2], writes=[Rt2])
        k.op("dve", lambda e: e.tensor_tensor(out=kkT, in0=kkT, in1=t2[:], op=ALU.mult), reads=[Rkk, Rt2], writes=[Rkk])
        k.op("dve", lambda e: e.scalar_tensor_tensor(out=t1[:], in0=rT, scalar=vec[:, 48 + hp:49 + hp], in1=kT, op0=ALU.mult, op1=ALU.mult), reads=[Rr, Rk, Rvec], writes=[Rt1])
        for half in range(2):
            hs = slice(half * 512, (half + 1) * 512)
            ps, rps = self.next_ps()
            k.op("pe", lambda e: e.matmul(ps[:, :], lhsT=BD, rhs=t1[:, hs], start=True, stop=True), reads=[Rt1, self.Rmask], writes=[rps])
            k.op("dve", lambda e: e.tensor_tensor(out=bonT[:, hs], in0=ps[:, :], in1=vT[:, hs], op=ALU.mult), reads=[rps, Rv], writes=[Rbon])
        for i in range(NT):
            ps, rps = self.next_ps()
            k.op("pe", lambda e: e.transpose(ps[:, 0:128], vT[:, i * 128:(i + 1) * 128], self.identf[:]), reads=[Rv, self.Rid], writes=[rps])
            k.op("act", lambda e: e.activation(out=Vtok[:, i, :], in_=ps[:, 0:128], func=AF.Identity), reads=[rps], writes=[RVt[i]])
        k.barrier()
        for d in range(2):
            order = list(range(NT)) if d == 0 else list(range(NT - 1, -1, -1))
            ds_ = slice(d * 64, (d + 1) * 64)
            m_inc = self.masks[:, d, :]
            m_str = self.masks[:, 3 - d, :]
            m_strT = self.masks[:, 2 + d, :]
            endcol = 127 if d == 0 else 0
            for ci, i in enumerate(order):
                ts_ = slice(i * 128, (i + 1) * 128)
                sgi = i // 2
                if i in seg_first[d]:
                    if ci == 0:
                        for hl in range(2):
                            k.dma("sp", "rst%d" % hl, Z[:, hl * 64:(hl + 1) * 64], self.din["st_rw"][j, d, 2 * hp + hl], writes=[RP])
                        ps, rps = self.next_ps()
                        k.op("pe", lambda e: e.transpose(ps[:, 0:64], Z[:], self.identf[0:64, 0:64]), reads=[RP, self.Rid], writes=[rps])
                        k.op("act", lambda e: e.activation(out=P[:], in_=ps[:, 0:64], func=AF.Identity), reads=[rps], writes=[RP])
                    else:
                        k.op("dve", lambda e: e.tensor_scalar(out=P[:], in0=P[:], scalar1=self.flags[:, 0:1], scalar2=None, op0=ALU.mult), reads=[RP, self.Rmask], writes=[RP])
                ps, rps = self.next_ps()
                k.op("pe", lambda e: e.matmul(ps[:, 0:128], lhsT=lw3[ds_, 1, :], rhs=adT[ds_, ts_], start=True, stop=True), reads=[Rlw, Rlo], writes=[rps])
                k.op("pe", lambda e: e.matmul(ps[:, 128:256], lhsT=lw3[ds_, 0, :], rhs=twT[ds_, ts_], start=True, stop=True), reads=[Rlw, Rlo], writes=[rps])
                k.op("act", lambda e: e.activation(out=aT, in_=ps[:, 0:128], func=AF.Sigmoid, bias=vec[:, 16 + 8 * d + hp:17 + 8 * d + hp]), reads=[rps, Rvec], writes=[Rs["aT"]])
                k.op("act", lambda e: e.activation(out=e2, in_=ps[:, 128:256], func=AF.Exp, scale=-1.0, bias=nvec[:, 8 * d + hp:8 * d + hp + 1]), reads=[rps, Rvec], writes=[Rs["e2"]])
                k.op("act", lambda e: e.activation(out=e2, in_=e2, func=AF.Ln, bias=self.onesc[:, 0:1]), reads=[Rs["e2"], self.Rmask], writes=[Rs["e2"]])
                k.op("act", lambda e: e.activation(out=e2, in_=e2, func=AF.Exp, scale=-1.0, bias=cst[:, 1:2]), reads=[Rs["e2"], Rvec], writes=[Rs["e2"]])
                k.op("dve", lambda e: e.tensor_tensor_scan(out=cs, data0=ones128[:], data1=e2, initial=0.0, op0=ALU.mult, op1=ALU.add), reads=[Rs["e2"], Rvec], writes=[Rs["cs"]])
                if d == 1:
                    k.op("dve", lambda e: e.tensor_copy(out=sm[:, 0:1], in_=cs[:, 127:128]), reads=[Rs["cs"]], writes=[Rsm])
                    k.op("dve", lambda e: e.scalar_tensor_tensor(out=cs, in0=e2, scalar=sm[:, 0:1], in1=cs, op0=ALU.add, op1=ALU.subtract), reads=[Rs["e2"], Rs["cs"], Rsm], writes=[Rs["cs"]])
                k.op("dve", lambda e: e.tensor_tensor(out=csx, in0=cs, in1=e2, op=ALU.subtract), reads=[Rs["cs"], Rs["e2"]], writes=[Rs["csx"]])
                k.op("act", lambda e: e.activation(out=Ep, in_=cs, func=AF.Exp, scale=-1.0), reads=[Rs["cs"]], writes=[Rs["Ep"]])
                k.op("act", lambda e: e.activation(out=Em, in_=cs, func=AF.Exp), reads=[Rs["cs"]], writes=[Rs["Em"]])
                k.op("act", lambda e: e.activation(out=Ex, in_=csx, func=AF.Exp, scale=-1.0), reads=[Rs["csx"]], writes=[Rs["Ex"]])
                k.op("dve", lambda e: e.tensor_scalar(out=kd, in0=aT, scalar1=vec[:, 40 + hp:41 + hp], scalar2=nvec[:, 16 + hp:17 + hp], op0=ALU.mult, op1=ALU.add), reads=[Rs["aT"], Rvec], writes=[Rs["kd"]])
                k.op("dve", lambda e: e.tensor_tensor(out=kd, in0=kd, in1=kT[:, ts_], op=ALU.mult), reads=[Rs["kd"], Rk], writes=[Rs["kd"]])
                k.op("dve", lambda e: e.tensor_tensor(out=cc, in0=kkT[:, ts_], in1=aT, op=ALU.mult), reads=[Rkk, Rs["aT"]], writes=[Rs["cc"]])
                k.op("dve", lambda e: e.tensor_tensor(out=QR[:, 0:128], in0=kkT[:, ts_], in1=Ex, op=ALU.mult), reads=[Rkk, Rs["Ex"]], writes=[Rs["QR"]])
                k.op("dve", lambda e: e.tensor_tensor(out=QR[:, 128:256], in0=rT[:, ts_], in1=Ep, op=ALU.mult), reads=[Rr, Rs["Ep"]], writes=[Rs["QR"]])
                k.op("dve", lambda e: e.tensor_tensor(out=KT_, in0=kd, in1=Em, op=ALU.mult), reads=[Rs["kd"], Rs["Em"]], writes=[Rs["KT"]])
                k.op("dve", lambda e: e.tensor_tensor(out=CT_, in0=cc, in1=Em, op=ALU.mult), reads=[Rs["cc"], Rs["Em"]], writes=[Rs["CT"]])
                k.op("dve", lambda e: e.tensor_scalar(out=KhT, in0=KT_, scalar1=Ep[:, endcol:endcol + 1], scalar2=None, op0=ALU.mult), reads=[Rs["KT"], Rs["Ep"]], writes=[Rs["KhT"]])
                k.op("dve", lambda e: e.tensor_scalar(out=ChT, in0=CT_, scalar1=Ep[:, endcol:endcol + 1], scalar2=-1.0, op0=ALU.mult, op1=ALU.mult), reads=[Rs["CT"], Rs["Ep"]], writes=[Rs["ChT"]])
                ps, rps = self.next_ps()
                k.op("pe", lambda e: e.transpose(ps[:, 0:128], KhT, self.identf[:]), reads=[Rs["KhT"], self.Rid], writes=[rps])
                k.op("pe", lambda e: e.transpose(ps[:, 128:256], ChT, self.identf[:]), reads=[Rs["ChT"], self.Rid], writes=[rps])
                k.op("act", lambda e: e.activation(out=Khat, in_=ps[:, 0:128], func=AF.Identity), reads=[rps], writes=[Rs["Khat"]])
                k.op("act", lambda e: e.activation(out=Chat, in_=ps[:, 128:256], func=AF.Identity), reads=[rps], writes=[Rs["Chat"]])
                TS[0]["R"] = {"AkT": Rs["AkT"], "AcT": Rs["AcT"], "MT": Rs["MT"], "X0": Rs["X0"], "X1": Rs["X1"], "XT0": Rs["kd"], "XT1": Rs["cc"], "D": Rs["aT"], "DT": Rs["e2"],
                              "W": Rs["cs"], "WT": Rs["csx"], "PP0": Rs["PP0"], "PP1": Rs["PP1"], "RH": RRH, "U": RU}
                TS[1]["R"] = RT1
                pdl, rpdl = self.next_ps(pin=True)
                pY, rpY = self.next_ps(pin=True)
                def head_gen(hl):
                    hs_ = slice(hl * 64, (hl + 1) * 64)
                    Vh = Vtok[:, i, hs_]
                    tl = TS[hl]
                    mybanks = freeb[2 * hl:2 * hl + 2]
                    cnt_ = [0]

                    def hps():
                        bi = mybanks[cnt_[0] % 2]
                        cnt_[0] += 1
                        return self.PS[bi], self.RPS[bi]
                    AkT, AcT, MT, X, XT, Dm, DTm, Wm, WTm, PPh, RH, U = (tl[n] for n in ("AkT", "AcT", "MT", "X", "XT", "D", "DT", "W", "WT", "PP", "RH", "U"))
                    rr = tl["R"]
                    rX = [rr["X0"], rr["X1"]]; rXT = [rr["XT0"], rr["XT1"]]
                    rD, rDT, rW, rWT, RRH, RU = rr["D"], rr["DT"], rr["W"], rr["WT"], rr["RH"], rr["U"]
                    ps, rps = hps()
                    k.op("pe", lambda e: e.matmul(ps[:, 0:256], lhsT=KT_[hs_, :], rhs=QR[hs_, :], start=True, stop=True), reads=[Rs["KT"], Rs["QR"]], writes=[rps])
                    yield
                    k.op("dve", lambda e: e.tensor_tensor(out=AkT[:, 0:128], in0=ps[:, 0:128], in1=m_str, op=ALU.mult), reads=[rps, self.Rmask], writes=[rr["AkT"]])
                    yield
                    k.op("dve", lambda e: e.tensor_tensor(out=AkT[:, 128:256], in0=ps[:, 128:256], in1=m_inc, op=ALU.mult), reads=[rps, self.Rmask], writes=[rr["AkT"]])
                    yield
                    ps, rps = hps()
                    k.op("pe", lambda e: e.matmul(ps[:, 0:256], lhsT=CT_[hs_, :], rhs=QR[hs_, :], start=True, stop=True), reads=[Rs["CT"], Rs["QR"]], writes=[rps])
                    yield
                    k.op("dve", lambda e: e.tensor_tensor(out=AcT[:, 0:128], in0=ps[:, 0:128], in1=m_str, op=ALU.mult), reads=[rps, self.Rmask], writes=[rr["AcT"]])
                    yield
                    k.op("dve", lambda e: e.scalar_tensor_tensor(out=AcT[:, 128:256], in0=ps[:, 128:256], scalar=-1.0, in1=m_inc, op0=ALU.mult, op1=ALU.mult), reads=[rps, self.Rmask], writes=[rr["AcT"]])
                    yield
                    ps, rps = hps()
                    k.op("pe", lambda e: e.matmul(ps[:, 0:128], lhsT=QR[hs_, 0:128], rhs=CT_[hs_, :], start=True, stop=True), reads=[Rs["CT"], Rs["QR"]], writes=[rps])
                    yield
                    k.op("dve", lambda e: e.tensor_tensor(out=MT, in0=ps[:, 0:128], in1=m_strT, op=ALU.mult), reads=[rps, self.Rmask], writes=[rr["MT"]])
                    yield
                    M_ = AcT[:, 0:128]
                    bd = lambda q: self.masks[:, 5 + q, :]
                    k.op("dve", lambda e: e.tensor_tensor(out=Dm, in0=M_, in1=bd(0), op=ALU.mult), reads=[rr["AcT"], self.Rmask], writes=[rD])
                    yield
                    k.op("dve", lambda e: e.tensor_tensor(out=DTm, in0=MT, in1=bd(0), op=ALU.mult), reads=[rr["MT"], self.Rmask], writes=[rDT])
                    yield
                    k.op("dve", lambda e: e.scalar_tensor_tensor(out=X[0], in0=Dm, scalar=-1.0, in1=self.identf[:], op0=ALU.mult, op1=ALU.add), reads=[rD, self.Rid], writes=[rX[0]])
                    yield
                    k.op("dve", lambda e: e.scalar_tensor_tensor(out=XT[0], in0=DTm, scalar=-1.0, in1=self.identf[:], op0=ALU.mult, op1=ALU.add), reads=[rDT, self.Rid], writes=[rXT[0]])
                    yield
                    xi = 0
                    curP, curPT, rcur = Dm, DTm, [rD, rDT]
                    for lev in range(RWLEV):
                        pp, rpp = PPh[lev % 2], rr["PP%d" % (lev % 2)]
                        ps, rps = hps()
                        k.op("pe", lambda e: e.matmul(ps[:, 0:128], lhsT=curPT, rhs=curP, start=True, stop=True), reads=rcur, writes=[rps])
                        yield
                        k.op("pe", lambda e: e.matmul(ps[:, 128:256], lhsT=curP, rhs=curPT, start=True, stop=True), reads=rcur, writes=[rps])
                        yield
                        k.op("act", lambda e: e.activation(out=pp[:], in_=ps[:, 0:256], func=AF.Identity), reads=[rps], writes=[rpp])
                        yield
                        curP, curPT, rcur = pp[:, 0:128], pp[:, 128:256], [rpp]
                        ps2, rps2 = hps()
                        k.op("pe", lambda e: e.matmul(ps2[:, 0:128], lhsT=curPT, rhs=X[xi], start=True, stop=True), reads=[rpp, rX[xi]], writes=[rps2])
                        yield
                        k.op("pe", lambda e: e.matmul(ps2[:, 128:256], lhsT=curP, rhs=XT[xi], start=True, stop=True), reads=[rpp, rXT[xi]], writes=[rps2])
                        yield
                        k.op("dve", lambda e: e.tensor_tensor(out=X[1 - xi], in0=ps2[:, 0:128], in1=X[xi], op=ALU.add), reads=[rps2, rX[xi]], writes=[rX[1 - xi]])
                        yield
                        k.op("dve", lambda e: e.tensor_tensor(out=XT[1 - xi], in0=ps2[:, 128:256], in1=XT[xi], op=ALU.add), reads=[rps2, rXT[xi]], writes=[rXT[1 - xi]])
                        yield
                        xi = 1 - xi
                    for q in RWQ:
                        last = (q == 3)
                        k.op("dve", lambda e: e.tensor_tensor(out=DTm, in0=MT, in1=bd(q), op=ALU.mult), reads=[rr["MT"], self.Rmask], writes=[rDT])
                        yield
                        ps, rps = hps()
                        k.op("pe", lambda e: e.matmul(ps[:, 0:128], lhsT=DTm, rhs=X[xi], start=True, stop=True), reads=[rDT, rX[xi]], writes=[rps])
                        yield
                        if not last:
                            k.op("dve", lambda e: e.tensor_tensor(out=Dm, in0=M_, in1=bd(q), op=ALU.mult), reads=[rr["AcT"], self.Rmask], writes=[rD])
                            yield
                            k.op("pe", lambda e: e.matmul(ps[:, 128:256], lhsT=Dm, rhs=XT[xi], start=True, stop=True), reads=[rD, rXT[xi]], writes=[rps])
                            yield
                        k.op("act", lambda e: e.activation(out=Wm, in_=ps[:, 0:128], func=AF.Identity), reads=[rps], writes=[rW])
                        yield
                        if not last:
                            k.op("act", lambda e: e.activation(out=WTm, in_=ps[:, 128:256], func=AF.Identity), reads=[rps], writes=[rWT])
                            yield
                        ps2, rps2 = hps()
                        k.op("pe", lambda e: e.matmul(ps2[:, 0:128], lhsT=XT[xi], rhs=Wm, start=True, stop=True), reads=[rXT[xi], rW], writes=[rps2])
                        yield
                        if not last:
                            k.op("pe", lambda e: e.matmul(ps2[:, 128:256], lhsT=X[xi], rhs=WTm, start=True, stop=True), reads=[rX[xi], rWT], writes=[rps2])
                            yield
                        k.op("dve", lambda e: e.tensor_tensor(out=X[1 - xi], in0=X[xi], in1=ps2[:, 0:128], op=ALU.subtract), reads=[rps2, rX[xi]], writes=[rX[1 - xi]])
                        yield
                        if not last:
                            k.op("dve", lambda e: e.tensor_tensor(out=XT[1 - xi], in0=XT[xi], in1=ps2[:, 128:256], op=ALU.subtract), reads=[rps2, rXT[xi]], writes=[rXT[1 - xi]])
                            yield
                        xi = 1 - xi
                    TT, rTT = X[xi], rX[xi]
                    ps, rps = hps()
                    k.op("pe", lambda e: e.matmul(ps[:, 0:64], lhsT=QR[hs_, 0:128], rhs=P[hs_, :], start=True, stop=False), reads=[Rs["QR"], RP], writes=[rps])
                    yield
                    k.op("pe", lambda e: e.matmul(ps[:, 0:64], lhsT=AkT[:, 0:128], rhs=Vh, start=False, stop=True), reads=[rr["AkT"], RVt[i]], writes=[rps])
                    yield
                    k.op("act", lambda e: e.activation(out=RH[:], in_=ps[:, 0:64], func=AF.Identity), reads=[rps], writes=[RRH])
                    yield
                    ps, rps = hps()
                    k.op("pe", lambda e: e.matmul(ps[:, 0:64], lhsT=TT, rhs=RH[:], start=True, stop=True), reads=[rTT, RRH], writes=[rps])
                    yield
                    k.op("act", lambda e: e.activation(out=U[:], in_=ps[:, 0:64], func=AF.Identity), reads=[rps], writes=[RU])
                    yield
                    k.op("pe", lambda e: e.matmul(pY[:, hs_], lhsT=QR[hs_, 128:256], rhs=P[hs_, :], start=True, stop=False), reads=[Rs["QR"], RP], writes=[rpY])
                    k.op("pe", lambda e: e.matmul(pY[:, hs_], lhsT=AkT[:, 128:256], rhs=Vh, start=False, stop=False), reads=[rr["AkT"], RVt[i]], writes=[rpY])
                    k.op("pe", lambda e: e.matmul(pY[:, hs_], lhsT=AcT[:, 128:256], rhs=U[:], start=False, stop=True), reads=[rr["AcT"], RU], writes=[rpY])
                    yield
                    k.op("pe", lambda e: e.matmul(pdl[hs_, 0:64], lhsT=Khat[:, hs_], rhs=Vh, start=True, stop=False), reads=[Rs["Khat"], RVt[i]], writes=[rpdl])
                    k.op("pe", lambda e: e.matmul(pdl[hs_, 0:64], lhsT=Chat[:, hs_], rhs=U[:], start=False, stop=True), reads=[Rs["Chat"], RU], writes=[rpdl])
                    yield

                freeb = [bi for bi in range(len(self.PS)) if bi not in self.pinned]
                gens = [head_gen(0), head_gen(1)]
                while gens:
                    for g_ in list(gens):
                        try:
                            next(g_)
                        except StopIteration:
                            gens.remove(g_)
                k.op("dve", lambda e: e.scalar_tensor_tensor(out=P[:], in0=P[:], scalar=Ep[:, endcol:endcol + 1], in1=pdl[:, 0:64], op0=ALU.mult, op1=ALU.add),
                     reads=[RP, Rs["Ep"], rpdl], writes=[RP])
                if d == 0:
                    k.op("act", lambda e: e.activation(out=Yacc[:, i, :], in_=pY[:, 0:128], func=AF.Identity), reads=[rpY], writes=[RYa[i]])
                else:
                    k.op("dve", lambda e: e.tensor_tensor(out=Ys[:], in0=pY[:, 0:128], in1=Yacc[:, i, :], op=ALU.add), reads=[rpY, RYa[i]], writes=[RYs])
                self.unpin(rpdl, rpY)
                if i in seg_last[d]:
                    ps, rps = self.next_ps()
                    k.op("pe", lambda e: e.transpose(ps[0:64, 0:128], P[:], self.identf[:]), reads=[RP, self.Rid], writes=[rps])
                    k.op("act", lambda e: e.activation(out=Z[:], in_=ps[0:64, 0:128], func=AF.Identity), reads=[rps], writes=[RRH])
                    for hl in range(2):
                        k.dma("sp", "rso%d" % hl, self.dout["o_rwkv"][j, sgi, d, 2 * hp + hl], Z[:, hl * 64:(hl + 1) * 64], reads=[RRH])
                if d == 1:
                    Y3 = Ys[:].rearrange("p (h v) -> p h v", h=2)
                    k.op("dve", lambda e: e.tensor_reduce(out=sm[:, 2:4], in_=Y3, axis=AX.X, op=ALU.add), reads=[RYs], writes=[Rsm])
                    k.op("dve", lambda e: e.tensor_scalar(out=sm[:, 2:4], in0=sm[:, 2:4], scalar1=-1.0 / 64, scalar2=None, op0=ALU.mult), reads=[Rsm], writes=[Rsm])
                    k.op("dve", lambda e: e.tensor_tensor(out=Y3, in0=Y3, in1=sm[:, 2:4].unsqueeze(2).to_broadcast([128, 2, 64]), op=ALU.add), reads=[RYs, Rsm], writes=[RYs])
                    yn3 = yn[:].rearrange("p (h v) -> p h v", h=2)
                    k.op("dve", lambda e: e.tensor_tensor(out=yn[:], in0=Ys[:], in1=Ys[:], op=ALU.mult), reads=[RYs], writes=[Ryn])
                    k.op("dve", lambda e: e.tensor_reduce(out=sm[:, 4:6], in_=yn3, axis=AX.X, op=ALU.add), reads=[Ryn], writes=[Rsm])
                    k.op("act", lambda e: e.activation(out=sm[:, 4:6], in_=sm[:, 4:6], func=AF.Ln, scale=1.0 / 64, bias=cst[:, 2:3]), reads=[Rsm, Rvec], writes=[Rsm])
                    k.op("act", lambda e: e.activation(out=sm[:, 4:6], in_=sm[:, 4:6], func=AF.Exp, scale=-0.5), reads=[Rsm], writes=[Rsm])
                    k.op("dve", lambda e: e.tensor_tensor(out=yn3, in0=Y3, in1=sm[:, 4:6].unsqueeze(2).to_broadcast([128, 2, 64]), op=ALU.mult), reads=[RYs, Rsm], writes=[Ryn])
                    ps, rps = self.next_ps()
                    k.op("pe", lambda e: e.transpose(ps[:, 0:128], yn[:], self.identf[:]), reads=[Ryn, self.Rid], writes=[rps])
                    k.op("pe", lambda e: e.matmul(ps[:, 128:256], lhsT=lw3[:, 2, :], rhs=sgT[:, ts_], start=True, stop=True), reads=[Rlw, Rlo], writes=[rps])
                    k.op("act", lambda e: e.activation(out=Ys[:], in_=ps[:, 0:128], func=AF.Identity, scale=vec[:, 56 + hp:57 + hp], bias=vec[:, 64 + hp:65 + hp]), reads=[rps, Rvec], writes=[RYs])
                    k.op("dve", lambda e: e.tensor_tensor(out=Ys[:], in0=Ys[:], in1=bonT[:, ts_], op=ALU.add), reads=[RYs, Rbon], writes=[RYs])
                    k.op("dve", lambda e: e.tensor_tensor(out=yT[:, 8 + hp, ts_], in0=Ys[:], in1=ps[:, 128:256], op=ALU.mult), reads=[RYs, rps], writes=[RyT[i]])
        k.barrier()


Prog.rwkv = _rwkv


RWQ = (1, 2, 3)
RWLEV = 3
PARTS = ("gla", "ml", "ssd", "rw")


def kernel(**inputs):
    inputs = {k: np.asarray(v) for k, v in inputs.items()}
    p = Prog(nl=4, do_mix=True, parts=PARTS)
    nc = p.build()
    in_maps = []
    for core in range(8):
        m = _prep_core_inputs(inputs, core)
        in_maps.append({k: np.ascontiguousarray(v, dtype=np.float32) for k, v in m.items() if k in p.din})
    res = run_bass_kernel_spmd(nc, in_maps, core_ids=list(range(8)))
    R = res.results
    y_sample = np.stack([R[c]["y"] for c in range(4)], axis=0).astype(np.float32)
    y_prompt = np.concatenate([R[c]["y"].reshape(4, 256, D) for c in range(4, 8)], axis=0).astype(np.float32)

    def states(name, shape):
        if name not in R[4]:
            return np.zeros((16, 2, 2) + shape, np.float32)
        out = np.zeros((16, 2, 2) + shape, np.float32)
        for c in range(4, 8):
            o = R[c][name]
            for g in range(4):
                out[4 * (c - 4) + g] = o[:, g]
        return out
    new_ssd = states("o_ssd", (16, 128, 64))
    new_rwkv = states("o_rwkv", (16, 64, 64))
    new_gla = states("o_gla", (4, 128, 256))
    new_mc = states("o_mc", (4, 128, 256))
    new_mn = states("o_mn", (4, 128))
    new_mm = states("o_mm", (4,))
    return (y_prompt, y_sample, new_ssd, new_rwkv, new_gla, new_mc, new_mn, new_mm)
```

```python
import contextlib
import numpy as np
import concourse.bass as bass
import concourse.mybir as mybir
from concourse.bass_utils import run_bass_kernel_spmd

F32 = mybir.dt.float32
BF16 = mybir.dt.bfloat16
AF = mybir.ActivationFunctionType
ALU = mybir.AluOpType
AX = mybir.AxisListType

T = 1024
D = 1024
NT = 8
DFF = 4096
EPS = 1e-6
EMBED_WAIT = True


class Res:
    __slots__ = ("name", "w", "r")

    def __init__(self, name=""):
        self.name = name
        self.w = None
        self.r = {}


class KB:
    ENG = ("pe", "act", "dve", "pool", "sp")

    def __init__(self, nc, es):
        self.nc = nc
        self.es = es
        self.e = {"pe": nc.tensor, "act": nc.scalar, "dve": nc.vector, "pool": nc.gpsimd, "sp": nc.sync}
        self.sem = {k: es.enter_context(nc.semaphore("sem_" + k)) for k in self.ENG}
        self.cnt = {k: 0 for k in self.ENG}
        self.seen = {k: {} for k in self.ENG}
        self.chan = {}
        self.ninst = 0
        self.nwait = 0

    def sb(self, name, shape, dt=F32):
        nb = int(np.prod(shape[1:])) * (4 if dt == F32 else 2)
        self.sbtot = getattr(self, "sbtot", 0) + nb
        return self.es.enter_context(self.nc.sbuf_tensor(name, shape, dt))

    def ps(self, name, shape, dt=F32):
        return self.es.enter_context(self.nc.psum_tensor(name, shape, dt))

    def channel(self, name):
        if name not in self.chan:
            s = self.es.enter_context(self.nc.semaphore("ch_" + name))
            self.chan[name] = [s, 0]
        return name

    def _deps(self, reads, writes):
        deps = {}

        def add(tok):
            if tok is None:
                return
            k, v = tok
            if deps.get(k, 0) < v:
                deps[k] = v
        for r in reads:
            add(r.w)
        for w in writes:
            add(w.w)
            for k, v in w.r.items():
                add((k, v))
        return deps

    def _emit_waits(self, eng, deps, embed=False):
        seen = self.seen[eng]
        E = self.e[eng]
        todo = []
        for k, v in deps.items():
            if seen.get(k, 0) >= v:
                continue
            seen[k] = v
            todo.append((self.sem[k] if k in self.sem else self.chan[k][0], v))
        held = todo.pop() if (embed and todo) else None
        for s_, v in todo:
            E.wait_ge(s_, v)
            self.nwait += 1
        return held

    def _mark(self, tok, reads, writes):
        k, v = tok
        for r in reads:
            if r.r.get(k, 0) < v:
                r.r[k] = v
        for w in writes:
            w.w = tok
            w.r = {}

    def op(self, eng, fn, reads=(), writes=()):
        held = self._emit_waits(eng, self._deps(reads, writes), embed=EMBED_WAIT)
        ins = fn(self.e[eng])
        if held is not None:
            ins._wait_ge(held[0], held[1])
        self.cnt[eng] += 1
        ins.then_inc(self.sem[eng], 1)
        tok = (eng, self.cnt[eng])
        self._mark(tok, reads, writes)
        self.ninst += 1
        return tok

    def dma(self, q, ch, out, in_, reads=(), writes=(), **kw):
        self.channel(ch)
        self._emit_waits(q, self._deps(reads, writes))
        ins = self.e[q].dma_start(out=out, in_=in_, **kw)
        c = self.chan[ch]
        c[1] += 16
        ins.then_inc(c[0], 16)
        tok = (ch, c[1])
        self._mark(tok, reads, writes)
        self.ninst += 1
        return tok

    def barrier(self):
        for eng in self.ENG:
            deps = {k: v[1] for k, v in self.chan.items() if v[1] > 0}
            for k in self.ENG:
                if k != eng and self.cnt[k] > 0:
                    deps[k] = self.cnt[k]
            self._emit_waits(eng, deps)

    def finish(self, eng="sp"):
        deps = {k: v[1] for k, v in self.chan.items() if v[1] > 0}
        for k in self.ENG:
            if k != eng and self.cnt[k] > 0:
                deps[k] = self.cnt[k]
        self._emit_waits(eng, deps)


class Prog:
    def __init__(self, nl=4, do_mix=True, layers=None, parts=("gla", "ml", "ssd", "rw")):
        self.nl = nl
        self.layers = list(range(nl)) if layers is None else layers
        self.parts = parts
        self.do_mix = do_mix
        self.nc = bass.Bass("TRN2", target_bir_lowering=False)
        self.es = contextlib.ExitStack()
        self.din = {}
        self.dout = {}

    def inp(self, name, shape, dt=F32):
        ap = self.nc.dram_tensor(name, list(shape), dt, kind="ExternalInput").ap()
        self.din[name] = ap
        return ap

    def outp(self, name, shape):
        ap = self.nc.dram_tensor(name, list(shape), F32, kind="ExternalOutput").ap()
        self.dout[name] = ap
        return ap

    def build(self):
        nc = self.nc
        with self.es as es:
            k = self.k = KB(nc, es)
            self.declare_io()
            self.alloc()
            self.load_consts()
            for l in self.layers:
                self.layer(l)
            self.store_y()
            k.finish()
            print("ninst", k.ninst, "nwait", k.nwait, k.cnt, flush=True)
        return nc

    def declare_io(self):
        self.x_d = self.inp("x", [T, D])
        self.cond_d = self.inp("cond", [128, 8])
        self.ident_d = self.inp("ident", [128, 128])
        self.w_mod_d = self.inp("w_mod", [4, D, 6 * D])
        self.b_mod_d = self.inp("b_mod", [4, 6 * D])
        self.norm_g_d = self.inp("norm_g", [4, 4, D])
        self.w_up_d = self.inp("w_mlp_up", [4, D, DFF])
        self.w_dn_d = self.inp("w_mlp_down", [4, DFF, D])
        self.y_d = self.outp("y", [T, D])
        if self.do_mix:
            self.inp("masks", [128, 9, 128])
            self.inp("flags", [128, 4])
            self.inp("w_in_cd", [2, D, 6192])
            self.inp("w_out_cd", [2, 2048, D])
            self.inp("gla_gate_w", [2, 2, 16, 512])
            self.inp("gla_gate_b", [2, 2, 512])
            self.inp("gla_norm", [2, 1024])
            self.inp("st_gla", [2, 2, 4, 128, 256])
            self.outp("o_gla", [2, 4, 2, 4, 128, 256])
            self.inp("w_in_ab", [2, D, 6560])
            self.inp("w_out_ab", [2, 2048, D])
            self.inp("ssd_dt_bias", [2, 2, 16])
            self.inp("ssd_a_log", [2, 2, 16])
            self.inp("ssd_d", [2, 16])
            self.inp("ssd_nw", [2, 128, 8])
            self.inp("ssd_cw", [2, 16, 128, 9])
            self.inp("ssd_cb", [2, 16, 128, 1])
            self.inp("st_ssd", [2, 2, 16, 128, 64])
            self.outp("o_ssd", [2, 4, 2, 16, 128, 64])
            self.inp("rw_vec", [2, 128, 72])
            self.inp("rw_mu", [2, 128, 27])
            self.inp("tsm", [2, T])
            self.inp("rwkv_w2", [2, 2, 64, 1024])
            self.inp("rwkv_a2", [2, 2, 64, 1024])
            self.inp("rwkv_g2", [2, 128, 1024])
            self.inp("st_rw", [2, 2, 16, 64, 64])
            self.outp("o_rwkv", [2, 4, 2, 16, 64, 64])
            self.inp("convm", [2, T])
            self.inp("tapflag", [128, 9])
            self.inp("ml_cw", [2, 8, 128, 9])
            self.inp("ml_cb", [2, 8, 128, 1])
            self.inp("mlstm_i_b", [2, 2, 4])
            self.inp("mlstm_f_b", [2, 2, 4])
            self.inp("mlstm_norm", [2, 1024])
            self.inp("st_mc", [2, 2, 4, 128, 256])
            self.inp("st_mn", [2, 2, 4, 128])
            self.inp("st_mm", [2, 2, 4])
            self.outp("o_mc", [2, 4, 2, 4, 128, 256])
            self.outp("o_mn", [2, 4, 2, 4, 128])
            self.outp("o_mm", [2, 4, 2, 4])

    def alloc(self):
        k = self.k
        self.X = k.sb("X", [128, NT, D])
        self.RX = [Res("X%d" % i) for i in range(NT)]
        self.hT = k.sb("hT", [128, 8, T], BF16)
        self.RhT = [Res("hT%d" % i) for i in range(NT)]
        self.modb = k.sb("modb", [128, 6 * D])
        self.Rmod = [Res("mod%d" % i) for i in range(6)]
        self.NSLOT = 2
        self.WA = k.sb("WA", [128, self.NSLOT * 4096], BF16)
        self.RW = [Res("W%d" % i) for i in range(self.NSLOT)]
        self.wslot = 0
        self.PS = [k.ps("ps%d" % i, [128, 512]) for i in range(6)]
        self.RPS = [Res("ps%d" % i) for i in range(6)]
        self.psi = 0
        self.PB = [k.ps("pb%d" % i, [128, 1024], BF16) for i in range(2)]
        self.RPB = [Res("pb%d" % i) for i in range(2)]
        self.pbi = 0
        self.identf = k.sb("identf", [128, 128])
        self.identb = k.sb("identb", [128, 128], BF16)
        self.Rid = Res("ident")
        self.cs = k.sb("cs", [128, 8])
        self.Rcond = Res("cond")
        self.ss = k.sb("ss", [128, 16])
        self.Rss = [Res("ss%d" % i) for i in range(16)]
        self.epsb = k.sb("epsb", [128, 1])
        self.Rjunk = Res("junk")
        self.tmpf = [k.sb("tmpf%d" % i, [128, D]) for i in range(2)]
        self.Rtmpf = [Res() for _ in range(2)]
        self.hb = [k.sb("hb%d" % i, [128, D], BF16) for i in range(2)]
        self.Rhb = [Res() for _ in range(2)]
        self.MA = k.sb("MA", [128, 36864], BF16)
        self.RMA = Res("MA")
        self.junk = k.sb("junk", [128, D])
        fa = self.MA[:, 16384:16384 + 4 * 2048].bitcast(F32)
        self.Ff = [fa[:, i * D:(i + 1) * D] for i in range(4)]
        self.RFf = [Res() for _ in range(4)]
        ga = self.MA[:, 2048:2048 + 2 * 2048].bitcast(F32)
        self.gsc = [ga[:, i * D:(i + 1) * D] for i in range(2)]
        self.RWD = Res("wdown")
        self.Rgsc = [Res() for _ in range(2)]

    def scr(self, name, shape, dt=F32):
        if not hasattr(self, "_scr"):
            self._scr = {}
        if name not in self._scr:
            self._scr[name] = self.k.sb(name, shape, dt)
        return self._scr[name]

    def next_ps(self, pin=False):
        if not hasattr(self, "pinned"):
            self.pinned = set()
        while True:
            i = self.psi
            self.psi = (self.psi + 1) % len(self.PS)
            if i not in self.pinned:
                break
        if pin:
            self.pinned.add(i)
        return self.PS[i], self.RPS[i]

    def unpin(self, *rs):
        for r in rs:
            self.pinned.discard(self.RPS.index(r))

    def next_pb(self):
        i = self.pbi
        self.pbi = (self.pbi + 1) % len(self.PB)
        return self.PB[i], self.RPB[i]

    def wslots(self, n):
        if self.wslot + n > self.NSLOT:
            self.wslot = 0
        s = self.wslot
        self.wslot += n
        self.wch = "ws%d" % s
        return self.WA[:, s * 4096:(s + n) * 4096], self.RW[s:s + n]

    def load_consts(self):
        k = self.k
        k.op("dve", lambda e: e.memset(self.epsb[:], EPS), writes=[self.Rid])
        k.dma("sp", "c0", self.identf[:], self.ident_d[:, :], writes=[self.Rid])
        k.op("dve", lambda e: e.tensor_copy(out=self.identb[:], in_=self.identf[:]), reads=[self.Rid], writes=[self.Rid])
        k.dma("sp", "c1", self.cs[:], self.cond_d[:, :], writes=[self.Rcond])
        k.op("act", lambda e: e.activation(out=self.cs[:], in_=self.cs[:], func=AF.Silu), reads=[self.Rcond], writes=[self.Rcond])
        for i in range(NT):
            k.dma("sp", "x%d" % i, self.X[:, i, :], self.x_d[i * 128:(i + 1) * 128, :], writes=[self.RX[i]])

    def store_y(self):
        k = self.k
        for i in range(NT):
            k.dma("sp", "y%d" % i, self.y_d[i * 128:(i + 1) * 128, :], self.X[:, i, :], reads=[self.RX[i]])

    def adaln(self, l):
        k = self.k
        modb = self.modb
        self.condB = self.MA[:, 0:2048].bitcast(F32).rearrange("p (kc m) -> p kc m", kc=8)
        k.op("dve", lambda e: e.tensor_copy(out=self.condB, in_=self.cs[:].unsqueeze(2).to_broadcast([128, 8, 128])),
             reads=[self.Rcond], writes=[self.Rcond])
        k.dma("sp", "bmod", modb[:], self.b_mod_d[l:l + 1, :].partition_broadcast(128), writes=self.Rmod)
        wv = self.w_mod_d[l].rearrange("(kc p) n -> p kc n", p=128)
        NB = 256
        for j in range(6 * D // NB):
            wap, wres = self.wslots(1)
            wf = wap.bitcast(F32).rearrange("p (kc n) -> p kc n", kc=8)
            k.dma("sp", self.wch, wf, wv[:, :, j * NB:(j + 1) * NB], writes=wres)
            ps, rps = self.next_ps()
            for kc in range(8):
                k.op("pe", lambda e, kc=kc: e.matmul(ps[:, 0:NB], lhsT=self.condB[:, kc, :], rhs=wf[:, kc, :],
                                                     start=(kc == 0), stop=(kc == 7)),
                     reads=[self.Rcond] + wres, writes=[rps])
            r = self.Rmod[j * NB // D]
            sl = modb[:, j * NB:(j + 1) * NB]
            k.op("dve", lambda e: e.tensor_tensor(out=sl, in0=ps[:, 0:NB], in1=sl, op=ALU.add), reads=[rps, r], writes=[r])
        for gi, (mi, isscale) in enumerate([(1, True), (2, False), (4, True), (5, False)]):
            g, rg = self.gsc[gi % 2], self.Rgsc[gi % 2]
            k.dma("sp", "g%d" % (gi % 2), g, self.norm_g_d[l, gi:gi + 1, :].partition_broadcast(128), writes=[rg])
            sl = modb[:, mi * D:(mi + 1) * D]
            if isscale:
                k.op("dve", lambda e: e.scalar_tensor_tensor(out=sl, in0=sl, scalar=1.0, in1=g, op0=ALU.add, op1=ALU.mult),
                     reads=[rg, self.Rmod[mi]], writes=[self.Rmod[mi]])
            else:
                k.op("dve", lambda e: e.tensor_tensor(out=sl, in0=sl, in1=g, op=ALU.mult),
                     reads=[rg, self.Rmod[mi]], writes=[self.Rmod[mi]])

    def rstd_of(self, src_ap, rsrc, col, n, junk=None, rjunk=None):
        k = self.k
        ssl = self.ss[:, col:col + 1]
        rss = self.Rss[col]
        k.op("dve", lambda e: e.memset(ssl, 0.0), writes=[rss])
        jk = self.junk[:, 0:n] if junk is None else junk
        rjk = self.Rjunk if rjunk is None else rjunk
        k.op("act", lambda e: e.activation(out=jk, in_=src_ap, func=AF.Square, accum_out=ssl),
             reads=[rsrc], writes=[rss, rjk])
        k.op("act", lambda e: e.activation(out=ssl, in_=ssl, func=AF.Ln, scale=1.0 / n, bias=self.epsb[:, 0:1]),
             reads=[rss, self.Rid], writes=[rss])
        k.op("act", lambda e: e.activation(out=ssl, in_=ssl, func=AF.Exp, scale=-0.5), reads=[rss], writes=[rss])
        return ssl

    def norm_mod_T(self, ai, si):
        k = self.k
        A = self.modb[:, ai * D:(ai + 1) * D]
        S = self.modb[:, si * D:(si + 1) * D]
        for i in range(NT):
            rs = self.rstd_of(self.X[:, i, :], self.RX[i], i, D)
            tf, rtf = self.tmpf[i % 2], self.Rtmpf[i % 2]
            hb, rhb = self.hb[i % 2], self.Rhb[i % 2]
            k.op("dve", lambda e: e.scalar_tensor_tensor(out=tf[:], in0=self.X[:, i, :], scalar=rs, in1=A, op0=ALU.mult, op1=ALU.mult),
                 reads=[self.RX[i], self.Rss[i], self.Rmod[ai]], writes=[rtf])
            k.op("dve", lambda e: e.tensor_tensor(out=hb[:], in0=tf[:], in1=S, op=ALU.add), reads=[rtf, self.Rmod[si]], writes=[rhb])
            pb, rpb = self.next_pb()
            for kc in range(8):
                k.op("pe", lambda e, kc=kc: e.transpose(pb[:, kc * 128:(kc + 1) * 128], hb[:, kc * 128:(kc + 1) * 128], self.identb[:]),
                     reads=[rhb, self.Rid], writes=[rpb])
            k.op("act", lambda e: e.activation(out=self.hT[:, :, i * 128:(i + 1) * 128],
                                               in_=pb[:].rearrange("p (kc t) -> p kc t", kc=8), func=AF.Identity),
                 reads=[rpb], writes=[self.RhT[i]])

    def resid_add(self, i, F, rF, gi):
        k = self.k
        G = self.modb[:, gi * D:(gi + 1) * D]
        rs = self.rstd_of(F, rF, 8 + i, D)
        k.op("dve", lambda e: e.scalar_tensor_tensor(out=F, in0=F, scalar=rs, in1=G, op0=ALU.mult, op1=ALU.mult),
             reads=[rF, self.Rss[8 + i], self.Rmod[gi]], writes=[rF])
        k.op("dve", lambda e: e.tensor_tensor(out=self.X[:, i, :], in0=self.X[:, i, :], in1=F, op=ALU.add),
             reads=[rF, self.RX[i]], writes=[self.RX[i]])

    def mlp(self, l):
        k = self.k
        self.norm_mod_T(4, 3)
        wu = self.w_up_d[l].rearrange("(kc p) n -> p kc n", p=128)
        wd = self.w_dn_d[l].rearrange("(fc p) n -> p fc n", p=128)
        uT = self.MA[:, 0:32 * 512].rearrange("p (fc t) -> p fc t", fc=32)
        Ru = [Res("u%d" % i) for i in range(8)]
        for half in range(2):
            t0 = half * 512
            rh = self.RhT[half * 4:(half + 1) * 4]
            for fb in range(8):
                wap, wres = self.wslots(1)
                w = wap.rearrange("p (kc n) -> p kc n", kc=8)
                k.dma("pool", self.wch, w, wu[:, :, fb * 512:(fb + 1) * 512], writes=wres)
                for fc in range(4):
                    ps, rps = self.next_ps()
                    for kc in range(8):
                        k.op("pe", lambda e, kc=kc: e.matmul(ps[:, :], lhsT=w[:, kc, fc * 128:(fc + 1) * 128], rhs=self.hT[:, kc, t0:t0 + 512],
                                                             start=(kc == 0), stop=(kc == 7)), reads=wres + rh, writes=[rps])
                    tf, rtf = self.tmpf[fc % 2], self.Rtmpf[fc % 2]
                    k.op("act", lambda e: e.activation(out=tf[:, 0:512], in_=ps[:, :], func=AF.Relu), reads=[rps], writes=[rtf])
                    k.op("dve", lambda e: e.tensor_tensor(out=uT[:, fb * 4 + fc, :], in0=tf[:, 0:512], in1=tf[:, 0:512], op=ALU.mult),
                         reads=[rtf], writes=[Ru[fb]])
            Fs = {}
            for nh in range(4):
                wres = [self.RWD]
                w = self.MA[:, 24576:32768].rearrange("p (fc n) -> p fc n", fc=32)
                for q in range(4):
                    k.dma("pool", "wdn", w[:, q * 8:(q + 1) * 8, :], wd[:, q * 8:(q + 1) * 8, nh * 256:(nh + 1) * 256], writes=wres)
                for ti in range(4):
                    i = half * 4 + ti
                    ps, rps = self.next_ps()
                    for fc in range(32):
                        k.op("pe", lambda e, fc=fc: e.matmul(ps[:, 0:256], lhsT=uT[:, fc, ti * 128:(ti + 1) * 128], rhs=w[:, fc, :],
                                                             start=(fc == 0), stop=(fc == 31)), reads=wres + [Ru[fc // 4]], writes=[rps])
                    if nh == 0:
                        Fs[ti] = (self.Ff[ti], self.RFf[ti])
                    F, rF = Fs[ti]
                    k.op("act", lambda e: e.activation(out=F[:, nh * 256:(nh + 1) * 256], in_=ps[:, 0:256], func=AF.Identity),
                         reads=[rps], writes=[rF])
            for ti in range(4):
                F, rF = Fs[ti]
                self.resid_add(half * 4 + ti, F, rF, 5)

    def layer(self, l):
        self.adaln(l)
        self.k.barrier()
        if self.do_mix:
            self.norm_mod_T(1, 0)
            self.mixer(l)
            self.k.barrier()
        self.mlp(l)
        self.k.barrier()


def _prep_core_inputs(inputs, core):
    if core < 4:
        x = np.ascontiguousarray(inputs["x_sample"][core])
        cond = inputs["c"][core]
    else:
        j = core - 4
        x = np.ascontiguousarray(inputs["x_prompt"][4 * j:4 * j + 4].reshape(T, D))
        cond = inputs["c_ctx"]
    m = {"x": x, "cond": np.ascontiguousarray(cond.reshape(8, 128).T), "ident": np.eye(128, dtype=np.float32)}
    for n in ("w_mod", "b_mod", "norm_g", "w_mlp_up", "w_mlp_down", "w_in_cd", "w_out_cd", "gla_gate_w", "gla_gate_b", "gla_norm"):
        m[n] = inputs[n]
    r = np.arange(128)
    bdm = lambda n: (r[:, None] // n) == (r[None, :] // n)
    m["masks"] = np.ascontiguousarray(np.stack([r[:, None] <= r[None, :], r[:, None] >= r[None, :], r[:, None] > r[None, :], r[:, None] < r[None, :],
                                                (r[:, None] // 64) == (r[None, :] // 64),
                                                bdm(16), bdm(32) & ~bdm(16), bdm(64) & ~bdm(32), ~bdm(64)], axis=1).astype(np.float32))
    fl = np.zeros((128, 4), np.float32)
    fl[:, 0] = 1.0 if core < 4 else 0.0
    m["flags"] = fl
    for n in ("mlstm_i_b", "mlstm_f_b", "mlstm_norm"):
        m[n] = inputs[n]
    t = np.arange(T)
    per = 64 if core < 4 else 256
    m["convm"] = np.stack([(t % per) != 0, (t % per) != per - 1]).astype(np.float32)
    tf = np.ones((128, 9), np.float32)
    if core >= 4:
        tf[:, 0:3] = 0.0
        tf[:, 6:9] = 0.0
    m["tapflag"] = tf
    cw = inputs["mlstm_conv_w"]
    m["ml_cw"] = np.ascontiguousarray(cw.reshape(2, 9, 8, 128).transpose(0, 2, 3, 1))
    m["ml_cb"] = np.ascontiguousarray(inputs["mlstm_conv_b"].reshape(2, 8, 128, 1))
    for n in ("w_in_ab", "w_out_ab", "ssd_dt_bias", "ssd_a_log", "ssd_d"):
        m[n] = inputs[n]
    m["ssd_nw"] = np.ascontiguousarray(inputs["ssd_norm"].reshape(2, 8, 128).transpose(0, 2, 1))
    m["ssd_cw"] = np.ascontiguousarray(inputs["ssd_conv_w"].reshape(2, 9, 16, 128).transpose(0, 2, 3, 1))
    m["ssd_cb"] = np.ascontiguousarray(inputs["ssd_conv_b"].reshape(2, 16, 128, 1))
    for n in ("rwkv_w2", "rwkv_a2", "rwkv_g2"):
        m[n] = inputs[n]
    pc = lambda a: a.reshape(2, 8, 128).transpose(0, 2, 1)
    kinds = [inputs["rwkv_w0"][:, 0], inputs["rwkv_w0"][:, 1], inputs["rwkv_a0"][:, 0], inputs["rwkv_a0"][:, 1], inputs["rwkv_k_k"], inputs["rwkv_k_a"],
             inputs["rwkv_r_k"].reshape(2, 1024), inputs["rwkv_ln_w"], inputs["rwkv_ln_b"]]
    m["rw_vec"] = np.ascontiguousarray(np.stack([pc(a) for a in kinds], axis=2).reshape(2, 128, 72))
    m["rw_mu"] = np.ascontiguousarray(inputs["rwkv_mu"].reshape(2, 27, 128).transpose(0, 2, 1))
    sl = 1024 if core < 4 else 256
    m["tsm"] = np.stack([(t % sl) != 0, (t % sl) != sl - 1]).astype(np.float32)
    names = {"st_rw": "state_rwkv", "st_ssd": "state_ssd", "st_gla": "state_gla", "st_mc": "state_mlstm_c", "st_mn": "state_mlstm_n", "st_mm": "state_mlstm_m"}
    for kk, src in names.items():
        a = inputs[src]
        m[kk] = np.ascontiguousarray(a[core]) if core < 4 else np.zeros(a.shape[1:], np.float32)
    return m


def _load_w(self, Wv, c0, n):
    wap, wres = self.wslots(1)
    w = wap[:, 0:8 * n].rearrange("p (kc n) -> p kc n", kc=8)
    self.k.dma("pool", self.wch, w, Wv[:, :, c0:c0 + n], writes=wres)
    return w, wres


def _proj_tok(self, Wv, c0, n, evac, tiles=None):
    k = self.k
    w, wres = self.load_w(Wv, c0, n)
    for i in (range(NT) if tiles is None else tiles):
        ps, rps = self.next_ps()
        for kc in range(8):
            k.op("pe", lambda e, kc=kc: e.matmul(ps[:, 0:n], lhsT=self.hT[:, kc, i * 128:(i + 1) * 128], rhs=w[:, kc, :],
                                                 start=(kc == 0), stop=(kc == 7)), reads=wres + [self.RhT[i]], writes=[rps])
        evac(i, ps[:, 0:n], rps)


def _proj_feat(self, Wv, c0, n, evac):
    k = self.k
    w, wres = self.load_w(Wv, c0, n)
    for c in range((n + 127) // 128):
        m = min(128, n - c * 128)
        for half in range(2):
            ps, rps = self.next_ps()
            for kc in range(8):
                k.op("pe", lambda e, kc=kc: e.matmul(ps[0:m, :], lhsT=w[:, kc, c * 128:c * 128 + m], rhs=self.hT[:, kc, half * 512:(half + 1) * 512],
                                                     start=(kc == 0), stop=(kc == 7)), reads=wres + self.RhT[half * 4:half * 4 + 4], writes=[rps])
            evac(c, half, ps[0:m, :], rps, m)


def _transpose_to(self, src_bf, rsrc, dst3, rdst, ncol=8):
    k = self.k
    pb, rpb = self.next_pb()
    for c in range(ncol):
        k.op("pe", lambda e, c=c: e.transpose(pb[:, c * 128:(c + 1) * 128], src_bf[:, c * 128:(c + 1) * 128], self.identb[:]),
             reads=[rsrc, self.Rid], writes=[rpb])
    k.op("act", lambda e: e.activation(out=dst3, in_=pb[:, 0:ncol * 128].rearrange("p (c t) -> p c t", c=ncol), func=AF.Identity),
         reads=[rpb], writes=[rdst])


def _out_proj(self, Wd, yT, RyT, gi=2):
    k = self.k
    wv = Wd.rearrange("(kc p) n -> p kc n", p=128)
    Fall = self.MA[:, 16384:32768].bitcast(F32).rearrange("p (i n) -> p i n", i=NT)
    RF = [Res() for _ in range(NT)]
    for nh in range(2):
        wap, wres = self.wslots(2)
        w = wap.rearrange("p (kc n) -> p kc n", kc=16)
        for q in range(2):
            k.dma("pool", self.wch, w[:, q * 8:(q + 1) * 8, :], wv[:, q * 8:(q + 1) * 8, nh * 512:(nh + 1) * 512], writes=wres)
        for i in range(NT):
            ps, rps = self.next_ps()
            for kc in range(16):
                k.op("pe", lambda e, kc=kc: e.matmul(ps[:, :], lhsT=yT[:, kc, i * 128:(i + 1) * 128], rhs=w[:, kc, :],
                                                     start=(kc == 0), stop=(kc == 15)), reads=wres + [RyT[i]], writes=[rps])
            k.op("act", lambda e: e.activation(out=Fall[:, i, nh * 512:(nh + 1) * 512], in_=ps[:, :], func=AF.Identity),
                 reads=[rps], writes=[RF[i]])
    for i in range(NT):
        self.resid_add(i, Fall[:, i, :], RF[i], gi)


Prog.load_w = _load_w
Prog.proj_tok = _proj_tok
Prog.proj_feat = _proj_feat
Prog.transpose_to = _transpose_to
Prog.out_proj = _out_proj


def _mix_consts(self):
    k = self.k
    if hasattr(self, "masks"):
        return
    self.masks = k.sb("masks_sb", [128, 9, 128])
    self.Rmask = Res("masks")
    k.dma("sp", "c2", self.masks[:], self.din["masks"][:, :, :], writes=[self.Rmask])
    self.onesc = k.sb("onesc", [128, 1])
    self.flags = k.sb("flags_sb", [128, 4])
    k.op("dve", lambda e: e.memset(self.onesc[:], 1.0), writes=[self.Rmask])
    k.dma("sp", "c3", self.flags[:], self.din["flags"][:, :], writes=[self.Rmask])


def _mixer_cd(self, l):
    k = self.k
    j = l // 2
    self.mix_consts()
    Wv = self.din["w_in_cd"][j].rearrange("(kc p) n -> p kc n", p=128)
    yT = self.MA[:, 0:16384].rearrange("p (c t) -> p c t", c=16)
    RyT = [Res() for _ in range(NT)]
    if "gla" in self.parts:
        self.gla(l, j, Wv, yT, RyT)
    else:
        k.op("dve", lambda e: e.memset(yT[:, 0:8, :], 0.0), writes=RyT)
    k.barrier()
    if "ml" in self.parts:
        self.mlstm(l, j, Wv, yT, RyT)
    else:
        k.op("dve", lambda e: e.memset(yT[:, 8:16, :], 0.0), writes=RyT)
    k.barrier()
    self.out_proj(self.din["w_out_cd"][j], yT, RyT)


def _gla(self, l, j, Wv, yT, RyT):
    k = self.k
    MA = self.MA
    qT = MA[:, 16384:18432].rearrange("p (h t) -> p h t", h=2)
    kT = MA[:, 18432:20480].rearrange("p (h t) -> p h t", h=2)
    ktok = MA[:, 20480:22528].rearrange("p (i c) -> p i c", i=NT)
    vtok = MA[:, 22528:26624].rearrange("p (i c) -> p i c", i=NT)
    Oacc = MA[:, 26624:30720].rearrange("p (i c) -> p i c", i=NT)
    sc = 128 ** -0.5
    gdT = [MA[0:17, 30720 + d * 2048:30720 + (d + 1) * 2048].bitcast(F32) for d in range(2)]
    gw = [MA[0:17, 34816 + d * 1024:34816 + (d + 1) * 1024].bitcast(F32) for d in range(2)]
    Rgd = [Res(), Res()]
    for d in range(2):
        k.op("dve", lambda e: e.memset(gdT[d], 1.0), writes=[Rgd[d]])
        self.proj_feat(Wv, 3072 + 16 * d, 16, lambda c, half, ps, rps, m: k.op(
            "act", lambda e: e.activation(out=gdT[d][0:16, half * 512:(half + 1) * 512], in_=ps, func=AF.Identity), reads=[rps], writes=[Rgd[d]]))
        k.dma("sp", "gw", gw[d][0:16, :], self.din["gla_gate_w"][j, d], writes=[Rgd[d]])
        k.dma("sp", "gw", gw[d][16:17, :], self.din["gla_gate_b"][j, d:d + 1, :], writes=[Rgd[d]])
    gnb = self.scr("nrmb", [128, 1024])
    Rgn = Res()
    k.dma("sp", "gnb", gnb[:], self.din["gla_norm"][j:j + 1, :].partition_broadcast(128), writes=[Rgn])
    S = [self.scr("S%d" % h, [128, 256]) for h in range(2)]
    Sbf = [self.scr("Sb%d" % h, [128, 256], BF16) for h in range(2)]
    la = self.tmpf[0][:, 0:512]; Rla = self.Rtmpf[0]
    te = self.tmpf[0][:, 512:1024]
    ebuf = self.tmpf[1]; Reb = self.Rtmpf[1]
    khat = self.scr("khat", [128, 256], BF16)
    qTt = self.scr("qTt", [128, 2, 128], BF16); kTt = self.scr("kTt", [128, 2, 128], BF16)
    Gt = self.scr("Gt", [128, 4])
    scT = [self.scr("scT%d" % i, [128, 128], BF16) for i in range(2)]
    Ofin = self.junk; ROf = self.Rjunk
    yb = self.hb[0]; Ryb = self.Rhb[0]
    seg_first = {0: [0, 2, 4, 6], 1: [7, 5, 3, 1]}
    seg_last = {0: [1, 3, 5, 7], 1: [6, 4, 2, 0]}
    for hh in range(2):
        RqT, RkT = Res(), Res()
        Rkt = [Res() for _ in range(NT)]
        Rvt = [Res() for _ in range(NT)]
        ROa = [Res() for _ in range(NT)]
        RS = [Res() for _ in range(2)]
        Rkh = Res(); Rqk = [Res() for _ in range(2)]; RG = [Res() for _ in range(2)]; Rsc = [Res(), Res()]
        self.proj_feat(Wv, hh * 256, 256, lambda c, half, ps, rps, m: k.op(
            "act", lambda e: e.activation(out=qT[:, c, half * 512:(half + 1) * 512], in_=ps, func=AF.Identity, scale=sc), reads=[rps], writes=[RqT]))
        self.proj_feat(Wv, 512 + hh * 256, 256, lambda c, half, ps, rps, m: k.op(
            "act", lambda e: e.activation(out=kT[:, c, half * 512:(half + 1) * 512], in_=ps, func=AF.Identity), reads=[rps], writes=[RkT]))
        self.proj_tok(Wv, 512 + hh * 256, 256, lambda i, ps, rps: k.op(
            "act", lambda e: e.activation(out=ktok[:, i, :], in_=ps, func=AF.Identity), reads=[rps], writes=[Rkt[i]]))
        self.proj_tok(Wv, 1024 + hh * 512, 512, lambda i, ps, rps: k.op(
            "act", lambda e: e.activation(out=vtok[:, i, :], in_=ps, func=AF.Identity), reads=[rps], writes=[Rvt[i]]))
        for d in range(2):
            order = list(range(NT)) if d == 0 else list(range(NT - 1, -1, -1))
            mcum = self.masks[:, d, :]
            mend = self.masks[:, 2 + d, :]
            endcol = 127 if d == 0 else 0
            if d == 1:
                ggw = self.load_w(Wv, 2048 + hh * 512, 512)
            for ci, i in enumerate(order):
                seg = i // 2
                if i in seg_first[d]:
                    for hl in range(2):
                        if ci == 0:
                            k.dma("sp", "gst%d" % hl, S[hl][:], self.din["st_gla"][j, d, 2 * hh + hl], writes=[RS[hl]])
                        else:
                            k.op("dve", lambda e: e.tensor_scalar(out=S[hl][:], in0=S[hl][:], scalar1=self.flags[:, 0:1], scalar2=None, op0=ALU.mult),
                                 reads=[RS[hl], self.Rmask], writes=[RS[hl]])
                        k.op("act", lambda e: e.activation(out=Sbf[hl][:], in_=S[hl][:], func=AF.Identity), reads=[RS[hl]], writes=[RS[hl]])
                ps, rps = self.next_ps()
                k.op("pe", lambda e: e.matmul(ps[:, 0:256], lhsT=gdT[d][:, i * 128:(i + 1) * 128], rhs=gw[d][:, hh * 256:(hh + 1) * 256], start=True, stop=True),
                     reads=[Rgd[d]], writes=[rps])
                k.op("act", lambda e: e.activation(out=la[:, 0:256], in_=ps[:, 0:256], func=AF.Exp, scale=-1.0), reads=[rps], writes=[Rla])
                k.op("act", lambda e: e.activation(out=la[:, 0:256], in_=la[:, 0:256], func=AF.Ln, bias=self.onesc[:, 0:1]), reads=[Rla, self.Rmask], writes=[Rla])
                k.op("dve", lambda e: e.tensor_scalar(out=la[:, 0:256], in0=la[:, 0:256], scalar1=-1.0 / 16.0, scalar2=None, op0=ALU.mult), reads=[Rla], writes=[Rla])
                ps, rps = self.next_ps()
                k.op("pe", lambda e: e.matmul(ps[:, 0:256], lhsT=mend, rhs=la[:, 0:256], start=True, stop=True), reads=[Rla, self.Rmask], writes=[rps])
                k.op("act", lambda e: e.activation(out=te[:, 0:256], in_=ps[:, 0:256], func=AF.Exp), reads=[rps], writes=[Rla])
                k.op("dve", lambda e: e.tensor_tensor(out=khat[:], in0=ktok[:, i, :], in1=te[:, 0:256], op=ALU.mult), reads=[Rla, Rkt[i]], writes=[Rkh])
                ps, rps = self.next_ps()
                for hl in range(2):
                    k.op("pe", lambda e: e.matmul(ps[:, hl * 128:(hl + 1) * 128], lhsT=la[:, hl * 128:(hl + 1) * 128], rhs=mcum, start=True, stop=True),
                         reads=[Rla, self.Rmask], writes=[rps])
                k.op("act", lambda e: e.activation(out=ebuf[:, 0:256], in_=ps[:, 0:256], func=AF.Exp), reads=[rps], writes=[Reb])
                k.op("act", lambda e: e.activation(out=ebuf[:, 256:512], in_=ps[:, 0:256], func=AF.Exp, scale=-1.0), reads=[rps], writes=[Reb])
                for hl in range(2):
                    k.op("dve", lambda e: e.tensor_tensor(out=qTt[:, hl, :], in0=qT[:, hl, i * 128:(i + 1) * 128], in1=ebuf[:, hl * 128:(hl + 1) * 128], op=ALU.mult),
                         reads=[Reb, RqT], writes=[Rqk[hl]])
                    k.op("dve", lambda e: e.tensor_tensor(out=kTt[:, hl, :], in0=kT[:, hl, i * 128:(i + 1) * 128], in1=ebuf[:, 256 + hl * 128:256 + (hl + 1) * 128], op=ALU.mult),
                         reads=[Reb, RkT], writes=[Rqk[hl]])
                    k.op("act", lambda e: e.activation(out=Gt[:, hl:hl + 1], in_=ebuf[:, hl * 128 + endcol:hl * 128 + endcol + 1], func=AF.Identity),
                         reads=[Reb], writes=[RG[hl]])
                def _hg(hl):
                    vs = vtok[:, i, hl * 256:(hl + 1) * 256]
                    ps, rps = self.next_ps()
                    k.op("pe", lambda e: e.matmul(ps[:, 0:128], lhsT=kTt[:, hl, :], rhs=qTt[:, hl, :], start=True, stop=True), reads=[Rqk[hl]], writes=[rps])
                    yield
                    s_, rs_ = scT[hl], Rsc[hl]
                    k.op("dve", lambda e: e.tensor_tensor(out=s_[:], in0=ps[:, 0:128], in1=mcum, op=ALU.mult), reads=[rps, self.Rmask], writes=[rs_])
                    yield
                    po, rpo = self.next_ps()
                    k.op("pe", lambda e: e.matmul(po[:, 0:256], lhsT=s_[:], rhs=vs, start=True, stop=False), reads=[rs_, Rvt[i]], writes=[rpo])
                    k.op("pe", lambda e: e.matmul(po[:, 0:256], lhsT=qTt[:, hl, :], rhs=Sbf[hl][:], start=False, stop=True), reads=[Rqk[hl], RS[hl]], writes=[rpo])
                    yield
                    if d == 0:
                        k.op("act", lambda e: e.activation(out=Oacc[:, i, hl * 256:(hl + 1) * 256], in_=po[:, 0:256], func=AF.Identity), reads=[rpo], writes=[ROa[i]])
                        yield
                    else:
                        k.op("dve", lambda e: e.tensor_tensor(out=Ofin[:, hl * 256:(hl + 1) * 256], in0=po[:, 0:256], in1=Oacc[:, i, hl * 256:(hl + 1) * 256], op=ALU.add),
                             reads=[rpo, ROa[i]], writes=[ROf])
                        yield
                    pd, rpd = self.next_ps()
                    k.op("pe", lambda e: e.matmul(pd[:, 0:256], lhsT=khat[:, hl * 128:(hl + 1) * 128], rhs=vs, start=True, stop=True), reads=[Rkh, Rvt[i]], writes=[rpd])
                    yield
                    k.op("dve", lambda e: e.scalar_tensor_tensor(out=S[hl][:], in0=S[hl][:], scalar=Gt[:, hl:hl + 1], in1=pd[:, 0:256], op0=ALU.mult, op1=ALU.add),
                         reads=[RS[hl], RG[hl], rpd], writes=[RS[hl]])
                    yield
                    k.op("act", lambda e: e.activation(out=Sbf[hl][:], in_=S[hl][:], func=AF.Identity), reads=[RS[hl]], writes=[RS[hl]])
                    yield
                    if i in seg_last[d]:
                        k.dma("sp", "gso%d" % hl, self.dout["o_gla"][j, seg, d, 2 * hh + hl], S[hl][:], reads=[RS[hl]])
                        yield
                    yield
                def _chain(hls):
                    for hl_ in hls:
                        yield from _hg(hl_)
                gens_ = [_chain(c_) for c_ in ([0], [1])]
                while gens_:
                    for g_ in list(gens_):
                        try:
                            next(g_)
                        except StopIteration:
                            gens_.remove(g_)
                if d == 1:
                    for hl in range(2):
                        gcol = (2 * hh + hl) * 256
                        rs = self.rstd_of(Ofin[:, hl * 256:(hl + 1) * 256], ROf, 8 + hl, 256, junk=te[:, 0:256], rjunk=Rla)
                        k.op("dve", lambda e: e.scalar_tensor_tensor(out=Ofin[:, hl * 256:(hl + 1) * 256], in0=Ofin[:, hl * 256:(hl + 1) * 256], scalar=rs,
                                                                     in1=gnb[:, gcol:gcol + 256], op0=ALU.mult, op1=ALU.mult),
                             reads=[ROf, self.Rss[8 + hl], Rgn], writes=[ROf])
                    w, wres = ggw
                    ps, rps = self.next_ps()
                    for kc in range(8):
                        k.op("pe", lambda e, kc=kc: e.matmul(ps[:, :], lhsT=self.hT[:, kc, i * 128:(i + 1) * 128], rhs=w[:, kc, :], start=(kc == 0), stop=(kc == 7)),
                             reads=wres + [self.RhT[i]], writes=[rps])
                    k.op("act", lambda e: e.activation(out=te, in_=ps[:, :], func=AF.Silu), reads=[rps], writes=[Rla])
                    k.op("dve", lambda e: e.tensor_tensor(out=yb[:, 0:512], in0=Ofin[:, 0:512], in1=te, op=ALU.mult), reads=[ROf, Rla], writes=[Ryb])
                    self.transpose_to(yb, Ryb, yT[:, hh * 4:(hh + 1) * 4, i * 128:(i + 1) * 128], RyT[i], ncol=4)
        k.barrier()


Prog.mix_consts = _mix_consts
Prog.mixer = lambda self, l: (self.mixer_cd(l) if l % 2 == 1 else self.mixer_ab(l))
Prog.mixer_cd = _mixer_cd
Prog.gla = _gla


def _conv_setup(self, base):
    k = self.k
    MA = self.MA
    cin = MA[:, base:base + 2308].bitcast(F32)
    mLR = MA[:, base + 2308:base + 2308 + 4096].bitcast(F32).rearrange("p (a t) -> p a t", a=2)
    R = {"cin": cin, "mLR": mLR, "Rcin": Res(), "Rm": Res()}
    k.op("dve", lambda e: e.memset(cin[:, 0:65], 0.0), writes=[R["Rcin"]])
    k.op("dve", lambda e: e.memset(cin[:, 1089:1154], 0.0), writes=[R["Rcin"]])
    for a in range(2):
        k.dma("sp", "cm%d" % a, mLR[:, a, :], self.din["convm"][a:a + 1, :].partition_broadcast(128), writes=[R["Rm"]])
    if not hasattr(self, "tapf"):
        self.tapf = self.scr("tapf", [128, 9])
        self.Rtapf = Res()
        k.dma("sp", "tapf", self.tapf[:], self.din["tapflag"][:, :], writes=[self.Rtapf])
    return R


def _conv_chunk(self, C, Wv, col0, cw_ap, cb_ap, out_ap, rout, oscale=1.0):
    k = self.k
    cin, mLR, Rcin, Rm = C["cin"], C["mLR"], C["Rcin"], C["Rm"]
    wc = self.scr("convw", [128, 10])
    Rwc = C.setdefault("Rwc", Res())
    k.dma("sp", "cw", wc[:, 0:9], cw_ap, writes=[Rwc])
    k.dma("sp", "cw", wc[:, 9:10], cb_ap, writes=[Rwc])
    k.op("dve", lambda e: e.tensor_tensor(out=wc[:, 0:9], in0=wc[:, 0:9], in1=self.tapf[:], op=ALU.mult), reads=[self.Rtapf, Rwc], writes=[Rwc])
    self.proj_feat(Wv, col0, 128, lambda c, half, ps, rps, m: k.op(
        "act", lambda e: e.activation(out=cin[:, 65 + half * 512:65 + (half + 1) * 512], in_=ps, func=AF.Identity), reads=[rps], writes=[Rcin]))
    accs = [(self.tmpf[0], self.Rtmpf[0]), (self.tmpf[1], self.Rtmpf[1]), (self.junk, self.Rjunk)]
    for dc, (acc, racc) in zip((-1, 0, 1), accs):
        eng = "dve"
        for n, dr in enumerate((0, -1, 1)):
            tap = (dr + 1) * 3 + (dc + 1)
            off = 65 + 64 * dr + dc
            if n == 0:
                k.op(eng, lambda e: e.tensor_scalar(out=acc[:], in0=cin[:, off:off + T], scalar1=wc[:, tap:tap + 1], scalar2=None, op0=ALU.mult),
                     reads=[Rcin, Rwc], writes=[racc])
            else:
                k.op(eng, lambda e: e.scalar_tensor_tensor(out=acc[:], in0=cin[:, off:off + T], scalar=wc[:, tap:tap + 1], in1=acc[:], op0=ALU.mult, op1=ALU.add),
                     reads=[Rcin, Rwc, racc], writes=[racc])
    (aL, rL), (aC, rC), (aR, rR) = accs
    k.op("pool", lambda e: e.tensor_tensor(out=aL[:], in0=aL[:], in1=mLR[:, 0, :], op=ALU.mult), reads=[rL, Rm], writes=[rL])
    k.op("dve", lambda e: e.tensor_tensor(out=aR[:], in0=aR[:], in1=mLR[:, 1, :], op=ALU.mult), reads=[rR, Rm], writes=[rR])
    k.op("dve", lambda e: e.tensor_tensor(out=aC[:], in0=aC[:], in1=aL[:], op=ALU.add), reads=[rC, rL], writes=[rC])
    k.op("dve", lambda e: e.tensor_tensor(out=aC[:], in0=aC[:], in1=aR[:], op=ALU.add), reads=[rC, rR], writes=[rC])
    k.op("act", lambda e: e.activation(out=aC[:], in_=aC[:], func=AF.Silu, bias=wc[:, 9:10]), reads=[rC, Rwc], writes=[rC])
    k.op("dve", lambda e: e.tensor_scalar(out=out_ap, in0=aC[:], scalar1=float(oscale), scalar2=None, op0=ALU.mult), reads=[rC], writes=[rout])


Prog.conv_setup = _conv_setup
Prog.conv_chunk = _conv_chunk


def _mlstm(self, l, j, Wv, yT, RyT):
    k = self.k
    MA = self.MA
    B0 = 3104
    qT = MA[:, 16384:18432].rearrange("p (h t) -> p h t", h=2)
    kT = MA[:, 18432:20480].rearrange("p (h t) -> p h t", h=2)
    ktok = MA[:, 20480:22528].rearrange("p (i c) -> p i c", i=NT)
    vtok = MA[:, 22528:26624].rearrange("p (i c) -> p i c", i=NT)
    Oacc = MA[:, 26624:30720].rearrange("p (i c) -> p i c", i=NT)
    sc = 128 ** -0.5
    gat = self.scr("mgat", [128, NT, 16])
    lf = self.scr("mlf", [128, NT, 8])
    gbias = self.scr("mgb", [128, 16])
    Rg = Res()
    k.dma("sp", "mgb", gbias[:, 0:8], self.din["mlstm_i_b"][j:j + 1].rearrange("o d h -> o (d h)").partition_broadcast(128), writes=[Rg])
    k.dma("sp", "mgb", gbias[:, 8:16], self.din["mlstm_f_b"][j:j + 1].rearrange("o d h -> o (d h)").partition_broadcast(128), writes=[Rg])
    self.proj_tok(Wv, B0 + 3072, 16, lambda i, ps, rps: k.op(
        "dve", lambda e: e.tensor_tensor(out=gat[:, i, :], in0=ps, in1=gbias[:], op=ALU.add), reads=[rps, Rg], writes=[Rg]))
    k.op("act", lambda e: e.activation(out=lf[:], in_=gat[:, :, 8:16], func=AF.Exp, scale=-1.0), reads=[Rg], writes=[Rg])
    k.op("act", lambda e: e.activation(out=lf[:], in_=lf[:], func=AF.Ln, bias=self.onesc[:, 0:1]), reads=[Rg, self.Rmask], writes=[Rg])
    k.op("dve", lambda e: e.tensor_scalar(out=lf[:], in0=lf[:], scalar1=-1.0, scalar2=None, op0=ALU.mult), reads=[Rg], writes=[Rg])
    gnb = self.scr("nrmb", [128, 1024])
    Rgn = Res()
    k.dma("sp", "gnb", gnb[:], self.din["mlstm_norm"][j:j + 1, :].partition_broadcast(128), writes=[Rgn])
    onesb = self.scr("onesb", [128, 1], BF16)
    ones128 = self.scr("ones128", [128, 128])
    k.op("dve", lambda e: e.memset(onesb[:], 1.0), writes=[Rgn])
    k.op("dve", lambda e: e.memset(ones128[:], 1.0), writes=[Rgn])
    S = [self.scr("mS%d" % h, [128, 260]) for h in range(2)]
    Sbf = [self.scr("mSb%d" % h, [128, 260], BF16) for h in range(2)]
    khat = self.scr("khat", [128, 256], BF16)
    scT = [self.scr("scT%d" % i, [128, 128], BF16) for i in range(2)]
    sm = self.scr("msm", [128, 32])
    mcur = self.scr("mcur", [4, 4])
    dg = self.scr("mdg", [4, 4])
    stage = self.tmpf[0]; Rstage = self.Rtmpf[0]
    Ofin = self.junk; ROf = self.Rjunk
    yb = self.hb[0]; Ryb = self.Rhb[0]
    seg_first = {0: [0, 2, 4, 6], 1: [7, 5, 3, 1]}
    seg_last = {0: [1, 3, 5, 7], 1: [6, 4, 2, 0]}
    for hh in range(2):
        RqT, RkT = Res(), Res()
        Rkt = [Res() for _ in range(NT)]
        Rvt = [Res() for _ in range(NT)]
        ROa = [Res() for _ in range(NT)]
        RS = [Res() for _ in range(2)]
        Rkh = Res(); Rsc = [Res(), Res()]; Rsm = Res(); Rm = Res()
        C = self.conv_setup(22528)
        for hl in range(2):
            h = 2 * hh + hl
            self.conv_chunk(C, Wv, B0 + h * 128, self.din["ml_cw"][j, h], self.din["ml_cb"][j, h], qT[:, hl, :], RqT, 1.0)
            self.conv_chunk(C, Wv, B0 + 512 + h * 128, self.din["ml_cw"][j, 4 + h], self.din["ml_cb"][j, 4 + h], kT[:, hl, :], RkT, sc)
        k.barrier()
        for i in range(NT):
            pb, rpb = self.next_pb()
            for hl in range(2):
                k.op("pe", lambda e: e.transpose(pb[:, hl * 128:(hl + 1) * 128], kT[:, hl, i * 128:(i + 1) * 128], self.identb[:]), reads=[RkT, self.Rid], writes=[rpb])
            k.op("act", lambda e: e.activation(out=ktok[:, i, :], in_=pb[:, 0:256], func=AF.Identity), reads=[rpb], writes=[Rkt[i]])
        self.proj_tok(Wv, B0 + 1024 + hh * 512, 512, lambda i, ps, rps: k.op(
            "act", lambda e: e.activation(out=vtok[:, i, :], in_=ps, func=AF.Identity), reads=[rps], writes=[Rvt[i]]))
        for d in range(2):
            order = list(range(NT)) if d == 0 else list(range(NT - 1, -1, -1))
            mcum = self.masks[:, d, :]
            if d == 1:
                ggw = self.load_w(Wv, B0 + 2048 + hh * 512, 512)
            for ci, i in enumerate(order):
                seg = i // 2
                if i in seg_first[d]:
                    if ci == 0:
                        k.dma("sp", "mm0", mcur[:, 0:1], self.din["st_mm"][j, d:d + 1, :].rearrange("o h -> h o"), writes=[Rm])
                        for hl in range(2):
                            h = 2 * hh + hl
                            k.dma("sp", "mst%d" % hl, S[hl][:, 0:256], self.din["st_mc"][j, d, h], writes=[RS[hl]])
                            k.dma("sp", "mst%d" % hl, S[hl][:, 256:257], self.din["st_mn"][j, d, h].rearrange("(p o) -> p o", o=1), writes=[RS[hl]])
                            k.dma("sp", "mem0", sm[:, 24:25], self.din["st_mm"][j, d:d + 1, h:h + 1].partition_broadcast(128), writes=[Rsm])
                            k.op("act", lambda e: e.activation(out=sm[:, 24:25], in_=sm[:, 24:25], func=AF.Exp), reads=[Rsm], writes=[Rsm])
                            k.op("dve", lambda e: e.tensor_scalar(out=S[hl][:, 0:257], in0=S[hl][:, 0:257], scalar1=sm[:, 24:25], scalar2=None, op0=ALU.mult),
                                 reads=[RS[hl], Rsm], writes=[RS[hl]])
                    else:
                        k.op("dve", lambda e: e.tensor_scalar(out=mcur[:, 0:1], in0=mcur[:, 0:1], scalar1=self.flags[0:4, 0:1], scalar2=None, op0=ALU.mult),
                             reads=[Rm, self.Rmask], writes=[Rm])
                        for hl in range(2):
                            k.op("dve", lambda e: e.tensor_scalar(out=S[hl][:, 0:257], in0=S[hl][:, 0:257], scalar1=self.flags[:, 0:1], scalar2=None, op0=ALU.mult),
                                 reads=[RS[hl], self.Rmask], writes=[RS[hl]])
                    for hl in range(2):
                        k.op("act", lambda e: e.activation(out=Sbf[hl][:, 0:257], in_=S[hl][:, 0:257], func=AF.Identity), reads=[RS[hl]], writes=[RS[hl]])
                lfd = lf[:, i, d * 4:(d + 1) * 4]
                igd = gat[:, i, d * 4:(d + 1) * 4]
                ps, rps = self.next_ps()
                k.op("pe", lambda e: e.matmul(ps[:, 0:4], lhsT=mcum, rhs=lfd, start=True, stop=True), reads=[Rg, self.Rmask], writes=[rps])
                k.op("pe", lambda e: e.matmul(ps[:, 4:8], lhsT=ones128[:], rhs=lfd, start=True, stop=True), reads=[Rg, Rgn], writes=[rps])
                k.op("pe", lambda e: e.matmul(ps[0:4, 8:9], lhsT=lfd, rhs=self.onesc[:, 0:1], start=True, stop=True), reads=[Rg, self.Rmask], writes=[rps])
                k.op("dve", lambda e: e.tensor_tensor(out=sm[:, 0:4], in0=igd, in1=ps[:, 0:4], op=ALU.subtract), reads=[Rg, rps], writes=[Rsm])
                k.op("act", lambda e: e.activation(out=sm[:, 4:12], in_=ps[:, 0:8], func=AF.Exp), reads=[rps], writes=[Rsm])
                k.op("act", lambda e: e.activation(out=mcur[:, 2:3], in_=ps[0:4, 8:9], func=AF.Identity), reads=[rps], writes=[Rm])
                pt, rpt = self.next_ps()
                k.op("pe", lambda e: e.transpose(pt[0:4, 0:128], sm[:, 0:4], self.identf[:]), reads=[Rsm, self.Rid], writes=[rpt])
                k.op("dve", lambda e: e.tensor_reduce(out=mcur[:, 1:2], in_=pt[0:4, 0:128], axis=AX.X, op=ALU.max), reads=[rpt], writes=[Rm])
                k.op("dve", lambda e: e.tensor_scalar(out=mcur[:, 0:1], in0=mcur[:, 0:1], scalar1=mcur[:, 1:2], scalar2=mcur[:, 2:3], op0=ALU.max, op1=ALU.add),
                     reads=[Rm], writes=[Rm])
                k.op("act", lambda e: e.activation(out=sm[:, 0:4], in_=sm[:, 0:4], func=AF.Exp), reads=[Rsm], writes=[Rsm])
                k.op("dve", lambda e: e.tensor_tensor(out=sm[:, 12:16], in0=sm[:, 0:4], in1=sm[:, 8:12], op=ALU.mult), reads=[Rsm], writes=[Rsm])
                for hl in range(2):
                    h = 2 * hh + hl
                    k.op("dve", lambda e: e.tensor_scalar(out=khat[:, hl * 128:(hl + 1) * 128], in0=ktok[:, i, hl * 128:(hl + 1) * 128],
                                                          scalar1=sm[:, 12 + h:13 + h], scalar2=None, op0=ALU.mult), reads=[Rkt[i], Rsm], writes=[Rkh])
                def _hg(hl):
                    h = 2 * hh + hl
                    vs = vtok[:, i, hl * 256:(hl + 1) * 256]
                    qt = qT[:, hl, i * 128:(i + 1) * 128]
                    ps, rps = self.next_ps()
                    k.op("pe", lambda e: e.matmul(ps[:, 0:128], lhsT=kT[:, hl, i * 128:(i + 1) * 128], rhs=qt, start=True, stop=True), reads=[RqT, RkT], writes=[rps])
                    yield
                    s_, rs_ = scT[hl], Rsc[hl]
                    k.op("dve", lambda e: e.scalar_tensor_tensor(out=s_[:], in0=ps[:, 0:128], scalar=sm[:, h:h + 1], in1=mcum, op0=ALU.mult, op1=ALU.mult),
                         reads=[rps, Rsm, self.Rmask], writes=[rs_])
                    yield
                    po, rpo = self.next_ps()
                    k.op("pe", lambda e: e.matmul(po[:, 0:256], lhsT=s_[:], rhs=vs, start=True, stop=False), reads=[rs_, Rvt[i]], writes=[rpo])
                    k.op("pe", lambda e: e.matmul(po[:, 0:256], lhsT=qt, rhs=Sbf[hl][:, 0:256], start=False, stop=True), reads=[RqT, RS[hl]], writes=[rpo])
                    k.op("pe", lambda e: e.matmul(po[:, 256:257], lhsT=s_[:], rhs=onesb[:], start=True, stop=False), reads=[rs_, Rgn], writes=[rpo])
                    k.op("pe", lambda e: e.matmul(po[:, 256:257], lhsT=qt, rhs=Sbf[hl][:, 256:257], start=False, stop=True), reads=[RqT, RS[hl]], writes=[rpo])
                    yield
                    dn = sm[:, 16 + hl:17 + hl]
                    k.op("act", lambda e: e.activation(out=dn, in_=po[:, 256:257], func=AF.Abs, scale=sm[:, 4 + h:5 + h]), reads=[rpo, Rsm], writes=[Rsm])
                    yield
                    k.op("dve", lambda e: e.tensor_scalar(out=dn, in0=dn, scalar1=1.0, scalar2=None, op0=ALU.max), reads=[Rsm], writes=[Rsm])
                    yield
                    k.op("dve", lambda e: e.reciprocal(out=dn, in_=dn), reads=[Rsm], writes=[Rsm])
                    yield
                    k.op("dve", lambda e: e.tensor_tensor(out=dn, in0=dn, in1=sm[:, 4 + h:5 + h], op=ALU.mult), reads=[Rsm], writes=[Rsm])
                    yield
                    if d == 0:
                        k.op("act", lambda e: e.activation(out=Oacc[:, i, hl * 256:(hl + 1) * 256], in_=po[:, 0:256], func=AF.Identity, scale=dn), reads=[rpo, Rsm], writes=[ROa[i]])
                        yield
                    else:
                        k.op("dve", lambda e: e.scalar_tensor_tensor(out=Ofin[:, hl * 256:(hl + 1) * 256], in0=po[:, 0:256], scalar=dn, in1=Oacc[:, i, hl * 256:(hl + 1) * 256],
                                                                     op0=ALU.mult, op1=ALU.add), reads=[rpo, Rsm, ROa[i]], writes=[ROf])
                        yield
                    pd, rpd = self.next_ps()
                    k.op("pe", lambda e: e.matmul(pd[:, 0:256], lhsT=khat[:, hl * 128:(hl + 1) * 128], rhs=vs, start=True, stop=True), reads=[Rkh, Rvt[i]], writes=[rpd])
                    k.op("pe", lambda e: e.matmul(pd[:, 256:257], lhsT=khat[:, hl * 128:(hl + 1) * 128], rhs=onesb[:], start=True, stop=True), reads=[Rkh, Rgn], writes=[rpd])
                    yield
                    k.op("dve", lambda e: e.scalar_tensor_tensor(out=S[hl][:, 0:257], in0=S[hl][:, 0:257], scalar=sm[:, 8 + h:9 + h], in1=pd[:, 0:257], op0=ALU.mult, op1=ALU.add),
                         reads=[RS[hl], Rsm, rpd], writes=[RS[hl]])
                    yield
                    k.op("act", lambda e: e.activation(out=Sbf[hl][:, 0:257], in_=S[hl][:, 0:257], func=AF.Identity), reads=[RS[hl]], writes=[RS[hl]])
                    yield
                    yield
                def _chain(hls):
                    for hl_ in hls:
                        yield from _hg(hl_)
                gens_ = [_chain(c_) for c_ in ([0], [1])]
                while gens_:
                    for g_ in list(gens_):
                        try:
                            next(g_)
                        except StopIteration:
                            gens_.remove(g_)
                if i in seg_last[d]:
                    k.op("dve", lambda e: e.tensor_scalar(out=dg[:], in0=self.identf[0:4, 0:4], scalar1=mcur[:, 0:1], scalar2=None, op0=ALU.mult), reads=[Rm, self.Rid], writes=[Rm])
                    pm, rpm = self.next_ps()
                    k.op("pe", lambda e: e.matmul(pm[:, 0:4], lhsT=ones128[0:4, :], rhs=dg[:], start=True, stop=True), reads=[Rm, Rgn], writes=[rpm])
                    k.op("act", lambda e: e.activation(out=sm[:, 20:24], in_=pm[:, 0:4], func=AF.Exp, scale=-1.0), reads=[rpm], writes=[Rsm])
                    if hh == 0:
                        k.dma("sp", "mmo", self.dout["o_mm"][j, seg, d:d + 1, :].rearrange("o h -> h o"), mcur[:, 0:1], reads=[Rm])
                    for hl in range(2):
                        h = 2 * hh + hl
                        k.op("dve", lambda e: e.tensor_scalar(out=stage[:, hl * 260:hl * 260 + 257], in0=S[hl][:, 0:257], scalar1=sm[:, 20 + h:21 + h], scalar2=None, op0=ALU.mult),
                             reads=[RS[hl], Rsm], writes=[Rstage])
                        k.dma("sp", "mco%d" % hl, self.dout["o_mc"][j, seg, d, h], stage[:, hl * 260:hl * 260 + 256], reads=[Rstage])
                        k.dma("sp", "mno%d" % hl, self.dout["o_mn"][j, seg, d, h].rearrange("(p o) -> p o", o=1), stage[:, hl * 260 + 256:hl * 260 + 257], reads=[Rstage])
                if d == 1:
                    for hl in range(2):
                        gcol = (2 * hh + hl) * 256
                        rs = self.rstd_of(Ofin[:, hl * 256:(hl + 1) * 256], ROf, 8 + hl, 256, junk=self.tmpf[1][:, 0:256], rjunk=self.Rtmpf[1])
                        k.op("dve", lambda e: e.scalar_tensor_tensor(out=Ofin[:, hl * 256:(hl + 1) * 256], in0=Ofin[:, hl * 256:(hl + 1) * 256], scalar=rs,
                                                                     in1=gnb[:, gcol:gcol + 256], op0=ALU.mult, op1=ALU.mult),
                             reads=[ROf, self.Rss[8 + hl], Rgn], writes=[ROf])
                    w, wres = ggw
                    ps, rps = self.next_ps()
                    for kc in range(8):
                        k.op("pe", lambda e, kc=kc: e.matmul(ps[:, :], lhsT=self.hT[:, kc, i * 128:(i + 1) * 128], rhs=w[:, kc, :], start=(kc == 0), stop=(kc == 7)),
                             reads=wres + [self.RhT[i]], writes=[rps])
                    te = self.tmpf[1][:, 512:1024]
                    k.op("act", lambda e: e.activation(out=te, in_=ps[:, :], func=AF.Sigmoid), reads=[rps], writes=[self.Rtmpf[1]])
                    k.op("dve", lambda e: e.tensor_tensor(out=yb[:, 0:512], in0=Ofin[:, 0:512], in1=te, op=ALU.mult), reads=[ROf, self.Rtmpf[1]], writes=[Ryb])
                    self.transpose_to(yb, Ryb, yT[:, 8 + hh * 4:8 + (hh + 1) * 4, i * 128:(i + 1) * 128], RyT[i], ncol=4)
        k.barrier()


Prog.mlstm = _mlstm


def _mixer_ab(self, l):
    k = self.k
    j = l // 2
    self.mix_consts()
    Wv = self.din["w_in_ab"][j].rearrange("(kc p) n -> p kc n", p=128)
    yT = self.MA[:, 0:16384].rearrange("p (c t) -> p c t", c=16)
    RyT = [Res() for _ in range(NT)]
    self.rs_ssd = self.scr("rs_ssd", [128, NT])
    self.Rrs = Res()
    if "ssd" in self.parts:
        self.ssd(l, j, Wv, yT, RyT)
    else:
        k.op("dve", lambda e: e.memset(yT[:, 0:8, :], 0.0), writes=RyT)
        k.op("dve", lambda e: e.memset(self.rs_ssd[:], 1.0), writes=[self.Rrs])
    k.barrier()
    if "rw" in self.parts:
        self.rwkv(l, j, Wv, yT, RyT)
    else:
        k.op("dve", lambda e: e.memset(yT[:, 8:16, :], 0.0), writes=RyT)
    k.barrier()
    self.out_proj(self.din["w_out_ab"][j], yT, RyT, row_scale=(self.rs_ssd, self.Rrs))


def _out_proj(self, Wd, yT, RyT, gi=2, row_scale=None):
    k = self.k
    wv = Wd.rearrange("(kc p) n -> p kc n", p=128)
    Fall = self.MA[:, 16384:32768].bitcast(F32).rearrange("p (i n) -> p i n", i=NT)
    RF = [Res() for _ in range(NT)]
    for nh in range(2):
        wap, wres = self.wslots(2)
        w = wap.rearrange("p (kc n) -> p kc n", kc=16)
        for q in range(2):
            k.dma("pool", self.wch, w[:, q * 8:(q + 1) * 8, :], wv[:, q * 8:(q + 1) * 8, nh * 512:(nh + 1) * 512], writes=wres)
        for i in range(NT):
            Fi = Fall[:, i, nh * 512:(nh + 1) * 512]
            if row_scale is None:
                ps, rps = self.next_ps()
                for kc in range(16):
                    k.op("pe", lambda e, kc=kc: e.matmul(ps[:, :], lhsT=yT[:, kc, i * 128:(i + 1) * 128], rhs=w[:, kc, :],
                                                         start=(kc == 0), stop=(kc == 15)), reads=wres + [RyT[i]], writes=[rps])
                k.op("act", lambda e: e.activation(out=Fi, in_=ps[:, :], func=AF.Identity), reads=[rps], writes=[RF[i]])
            else:
                rsc, rrs = row_scale
                ps2, rps2 = self.next_ps()
                for kc in range(8, 16):
                    k.op("pe", lambda e, kc=kc: e.matmul(ps2[:, :], lhsT=yT[:, kc, i * 128:(i + 1) * 128], rhs=w[:, kc, :],
                                                         start=(kc == 8), stop=(kc == 15)), reads=wres + [RyT[i]], writes=[rps2])
                k.op("act", lambda e: e.activation(out=Fi, in_=ps2[:, :], func=AF.Identity), reads=[rps2], writes=[RF[i]])
                ps1, rps1 = self.next_ps()
                for kc in range(8):
                    k.op("pe", lambda e, kc=kc: e.matmul(ps1[:, :], lhsT=yT[:, kc, i * 128:(i + 1) * 128], rhs=w[:, kc, :],
                                                         start=(kc == 0), stop=(kc == 7)), reads=wres + [RyT[i]], writes=[rps1])
                k.op("dve", lambda e: e.scalar_tensor_tensor(out=Fi, in0=ps1[:, :], scalar=rsc[:, i:i + 1], in1=Fi, op0=ALU.mult, op1=ALU.add),
                     reads=[rps1, rrs, RF[i]], writes=[RF[i]])
    for i in range(NT):
        self.resid_add(i, Fall[:, i, :], RF[i], gi)


def _ssd(self, l, j, Wv, yT, RyT):
    k = self.k
    MA = self.MA
    xT = MA[:, 16384:18432].rearrange("p (c t) -> p c t", c=2)
    BT = MA[:, 18432:19456]
    CT = MA[:, 19456:20480]
    xs = MA[:, 20480:22528].rearrange("p (i c) -> p i c", i=NT)
    Btok = MA[:, 22528:23552].rearrange("p (i c) -> p i c", i=NT)
    Oacc = MA[:, 23552:25600].rearrange("p (i c) -> p i c", i=NT)
    dt = self.scr("sdt", [128, NT, 32])
    la = self.scr("sla", [128, NT, 32])
    cst = self.scr("scst", [128, 96])
    Rdt = Res()
    k.dma("sp", "sc0", cst[:, 0:32], self.din["ssd_dt_bias"][j:j + 1].rearrange("o d h -> o (d h)").partition_broadcast(128), writes=[Rdt])
    k.dma("sp", "sc0", cst[:, 32:64], self.din["ssd_a_log"][j:j + 1].rearrange("o d h -> o (d h)").partition_broadcast(128), writes=[Rdt])
    k.dma("sp", "sc0", cst[:, 64:80], self.din["ssd_d"][j:j + 1, :].partition_broadcast(128), writes=[Rdt])
    k.op("act", lambda e: e.activation(out=cst[:, 32:64], in_=cst[:, 32:64], func=AF.Exp), reads=[Rdt], writes=[Rdt])
    self.proj_tok(Wv, 3072, 32, lambda i, ps, rps: k.op(
        "dve", lambda e: e.tensor_tensor(out=dt[:, i, :], in0=ps, in1=cst[:, 0:32], op=ALU.add), reads=[rps, Rdt], writes=[Rdt]))
    k.op("act", lambda e: e.activation(out=dt[:], in_=dt[:], func=AF.Exp), reads=[Rdt], writes=[Rdt])
    k.op("act", lambda e: e.activation(out=dt[:], in_=dt[:], func=AF.Ln, bias=self.onesc[:, 0:1]), reads=[Rdt, self.Rmask], writes=[Rdt])
    k.op("dve", lambda e: e.scalar_tensor_tensor(out=la[:], in0=dt[:], scalar=-1.0, in1=cst[:, 32:64].unsqueeze(1).to_broadcast([128, NT, 32]),
                                                 op0=ALU.mult, op1=ALU.mult), reads=[Rdt], writes=[Rdt])
    nw = self.scr("snw", [128, 8])
    k.dma("sp", "sc1", nw[:], self.din["ssd_nw"][j], writes=[Rdt])
    ones128 = self.scr("ones128", [128, 128])
    k.op("dve", lambda e: e.memset(ones128[:], 1.0), writes=[Rdt])
    ssq = self.scr("sssq", [128, NT, 4])
    k.op("dve", lambda e: e.memset(ssq[:], 0.0), writes=[self.Rrs])
    S = self.scr("S0", [128, 256]); Sbf = self.scr("Sb0", [128, 256], BF16)
    st = self.scr("sst", [128, 80])
    CBm = self.scr("sCBm", [128, 128])
    seg_ = [self.scr("sseg%d" % i, [128, 128]) for i in range(2)]
    scT = [self.scr("scT%d" % i, [128, 128], BF16) for i in range(2)]
    khat = self.scr("khat", [128, 256], BF16)
    Ofin = self.junk; ROf = self.Rjunk
    tin = self.tmpf[0]; Rtin = self.Rtmpf[0]
    zs = self.tmpf[1]; Rzs = self.Rtmpf[1]
    yb = self.hb[0]; Ryb = self.Rhb[0]
    seg_first = {0: [0, 2, 4, 6], 1: [7, 5, 3, 1]}
    seg_last = {0: [1, 3, 5, 7], 1: [6, 4, 2, 0]}
    for g in range(4):
        RxT, RBT, RCT = Res(), Res(), Res()
        Rxs = [Res() for _ in range(NT)]
        RBt = [Res() for _ in range(NT)]
        ROa = [Res() for _ in range(NT)]
        RS = Res(); Rst = Res(); RCB = Res(); Rseg = [Res(), Res()]; Rsc = [Res(), Res()]; Rkh = Res()
        C = self.conv_setup(25600)
        cw, cb = self.din["ssd_cw"], self.din["ssd_cb"]
        for c in range(2):
            ch = 2 * g + c
            self.conv_chunk(C, Wv, 1024 + ch * 128, cw[j, ch], cb[j, ch], xT[:, c, :], RxT)
        self.conv_chunk(C, Wv, 1024 + (8 + g) * 128, cw[j, 8 + g], cb[j, 8 + g], BT, RBT)
        self.conv_chunk(C, Wv, 1024 + (12 + g) * 128, cw[j, 12 + g], cb[j, 12 + g], CT, RCT)
        k.barrier()
        for i in range(NT):
            pb, rpb = self.next_pb()
            for c in range(2):
                k.op("pe", lambda e: e.transpose(pb[:, c * 128:(c + 1) * 128], xT[:, c, i * 128:(i + 1) * 128], self.identb[:]), reads=[RxT, self.Rid], writes=[rpb])
            k.op("pe", lambda e: e.transpose(pb[:, 256:384], BT[:, i * 128:(i + 1) * 128], self.identb[:]), reads=[RBT, self.Rid], writes=[rpb])
            k.op("act", lambda e: e.activation(out=xs[:, i, :], in_=pb[:, 0:256], func=AF.Identity), reads=[rpb], writes=[Rxs[i]])
            k.op("act", lambda e: e.activation(out=Btok[:, i, :], in_=pb[:, 256:384], func=AF.Identity), reads=[rpb], writes=[RBt[i]])
        for d in range(2):
            order = list(range(NT)) if d == 0 else list(range(NT - 1, -1, -1))
            mcum = self.masks[:, d, :]
            if d == 1:
                zw = self.load_w(Wv, g * 256, 256)
            for ci, i in enumerate(order):
                sgi = i // 2
                if i in seg_first[d]:
                    if ci == 0:
                        for hl in range(4):
                            k.dma("sp", "sst%d" % hl, S[:, hl * 64:(hl + 1) * 64], self.din["st_ssd"][j, d, 4 * g + hl], writes=[RS])
                    else:
                        k.op("dve", lambda e: e.tensor_scalar(out=S[:], in0=S[:], scalar1=self.flags[:, 0:1], scalar2=None, op0=ALU.mult), reads=[RS, self.Rmask], writes=[RS])
                    k.op("act", lambda e: e.activation(out=Sbf[:], in_=S[:], func=AF.Identity), reads=[RS], writes=[RS])
                lad = la[:, i, d * 16:(d + 1) * 16]
                dtd = dt[:, i, d * 16:(d + 1) * 16]
                ps, rps = self.next_ps()
                k.op("pe", lambda e: e.matmul(ps[:, 0:16], lhsT=mcum, rhs=lad, start=True, stop=True), reads=[Rdt, self.Rmask], writes=[rps])
                k.op("pe", lambda e: e.matmul(ps[:, 16:32], lhsT=ones128[:], rhs=lad, start=True, stop=True), reads=[Rdt], writes=[rps])
                k.op("act", lambda e: e.activation(out=st[:, 0:16], in_=ps[:, 0:16], func=AF.Identity), reads=[rps], writes=[Rst])
                k.op("dve", lambda e: e.tensor_tensor(out=st[:, 16:32], in0=ps[:, 16:32], in1=st[:, 0:16], op=ALU.subtract), reads=[rps, Rst], writes=[Rst])
                k.op("act", lambda e: e.activation(out=st[:, 16:32], in_=st[:, 16:32], func=AF.Exp), reads=[Rst], writes=[Rst])
                k.op("dve", lambda e: e.tensor_tensor(out=st[:, 16:32], in0=st[:, 16:32], in1=dtd, op=ALU.mult), reads=[Rst, Rdt], writes=[Rst])
                k.op("act", lambda e: e.activation(out=st[:, 32:48], in_=ps[:, 16:32], func=AF.Exp), reads=[rps], writes=[Rst])
                k.op("act", lambda e: e.activation(out=st[:, 48:64], in_=st[:, 0:16], func=AF.Exp), reads=[Rst], writes=[Rst])
                ps, rps = self.next_ps()
                k.op("pe", lambda e: e.matmul(ps[:, 0:128], lhsT=BT[:, i * 128:(i + 1) * 128], rhs=CT[:, i * 128:(i + 1) * 128], start=True, stop=True), reads=[RBT, RCT], writes=[rps])
                k.op("dve", lambda e: e.tensor_tensor(out=CBm[:], in0=ps[:, 0:128], in1=mcum, op=ALU.mult), reads=[rps, self.Rmask], writes=[RCB])
                po, rpo = self.next_ps(pin=True)
                pd, rpd = self.next_ps(pin=True)
                def _hg(hl):
                    h = 4 * g + hl
                    pbt, rpbt = self.next_ps()
                    k.op("pe", lambda e: e.matmul(pbt[:, 0:128], lhsT=la[:, i, d * 16 + h:d * 16 + h + 1].to_broadcast([128, 128]), rhs=mcum, start=True, stop=True),
                         reads=[Rdt, self.Rmask], writes=[rpbt])
                    yield
                    sg, rsg = seg_[hl % 2], Rseg[hl % 2]
                    k.op("dve", lambda e: e.tensor_scalar(out=sg[:], in0=pbt[:, 0:128], scalar1=st[:, h:h + 1], scalar2=0.0, op0=ALU.subtract, op1=ALU.min),
                         reads=[rpbt, Rst], writes=[rsg])
                    yield
                    k.op("act", lambda e: e.activation(out=sg[:], in_=sg[:], func=AF.Exp), reads=[rsg], writes=[rsg])
                    yield
                    s_, rs_ = scT[hl % 2], Rsc[hl % 2]
                    k.op("dve", lambda e: e.scalar_tensor_tensor(out=s_[:], in0=sg[:], scalar=dt[:, i, d * 16 + h:d * 16 + h + 1], in1=CBm[:], op0=ALU.mult, op1=ALU.mult),
                         reads=[rsg, Rdt, RCB], writes=[rs_])
                    yield
                    k.op("pe", lambda e: e.matmul(po[:, hl * 64:(hl + 1) * 64], lhsT=s_[:], rhs=xs[:, i, hl * 64:(hl + 1) * 64], start=True, stop=True), reads=[rs_, Rxs[i]], writes=[rpo])
                    yield
                    k.op("dve", lambda e: e.tensor_scalar(out=khat[:, (hl % 2) * 128:(hl % 2 + 1) * 128], in0=Btok[:, i, :], scalar1=st[:, 16 + h:17 + h], scalar2=None, op0=ALU.mult),
                         reads=[RBt[i], Rst], writes=[Rkh])
                    yield
                    k.op("pe", lambda e: e.matmul(pd[:, hl * 64:(hl + 1) * 64], lhsT=khat[:, (hl % 2) * 128:(hl % 2 + 1) * 128], rhs=xs[:, i, hl * 64:(hl + 1) * 64], start=True, stop=True),
                         reads=[Rkh, Rxs[i]], writes=[rpd])
                    yield
                    yield
                def _chain(hls):
                    for hl_ in hls:
                        yield from _hg(hl_)
                gens_ = [_chain(c_) for c_ in ([0, 2], [1, 3])]
                while gens_:
                    for g_ in list(gens_):
                        try:
                            next(g_)
                        except StopIteration:
                            gens_.remove(g_)
                pi_, rpi = self.next_ps()
                k.op("pe", lambda e: e.matmul(pi_[:, 0:256], lhsT=CT[:, i * 128:(i + 1) * 128], rhs=Sbf[:], start=True, stop=True), reads=[RCT, RS], writes=[rpi])
                ebx = st[:, 48 + 4 * g:52 + 4 * g].unsqueeze(2).to_broadcast([128, 4, 64])
                k.op("dve", lambda e: e.tensor_tensor(out=tin[:, 0:256].rearrange("p (h v) -> p h v", h=4), in0=pi_[:, 0:256].rearrange("p (h v) -> p h v", h=4), in1=ebx, op=ALU.mult),
                     reads=[rpi, Rst], writes=[Rtin])
                if d == 0:
                    k.op("dve", lambda e: e.tensor_tensor(out=Oacc[:, i, :], in0=po[:, 0:256], in1=tin[:, 0:256], op=ALU.add), reads=[rpo, Rtin], writes=[ROa[i]])
                else:
                    k.op("dve", lambda e: e.tensor_tensor(out=Ofin[:, 0:256], in0=po[:, 0:256], in1=tin[:, 0:256], op=ALU.add), reads=[rpo, Rtin], writes=[ROf])
                    k.op("dve", lambda e: e.tensor_tensor(out=Ofin[:, 0:256], in0=Ofin[:, 0:256], in1=Oacc[:, i, :], op=ALU.add), reads=[ROf, ROa[i]], writes=[ROf])
                Gx = st[:, 32 + 4 * g:36 + 4 * g].unsqueeze(2).to_broadcast([128, 4, 64])
                k.op("dve", lambda e: e.tensor_tensor(out=S[:].rearrange("p (h v) -> p h v", h=4), in0=S[:].rearrange("p (h v) -> p h v", h=4), in1=Gx, op=ALU.mult), reads=[RS, Rst], writes=[RS])
                k.op("dve", lambda e: e.tensor_tensor(out=S[:], in0=S[:], in1=pd[:, 0:256], op=ALU.add), reads=[RS, rpd], writes=[RS])
                k.op("act", lambda e: e.activation(out=Sbf[:], in_=S[:], func=AF.Identity), reads=[RS], writes=[RS])
                self.unpin(rpo, rpd)
                if i in seg_last[d]:
                    for hl in range(4):
                        k.dma("sp", "sso%d" % hl, self.dout["o_ssd"][j, sgi, d, 4 * g + hl], S[:, hl * 64:(hl + 1) * 64], reads=[RS])
                if d == 1:
                    Dx = cst[:, 64 + 4 * g:68 + 4 * g].unsqueeze(2).to_broadcast([128, 4, 64])
                    k.op("dve", lambda e: e.tensor_tensor(out=tin[:, 256:512].rearrange("p (h v) -> p h v", h=4), in0=xs[:, i, :].rearrange("p (h v) -> p h v", h=4), in1=Dx, op=ALU.mult),
                         reads=[Rxs[i], Rdt], writes=[Rtin])
                    k.op("dve", lambda e: e.tensor_tensor(out=Ofin[:, 0:256], in0=Ofin[:, 0:256], in1=tin[:, 256:512], op=ALU.add), reads=[ROf, Rtin], writes=[ROf])
                    w, wres = zw
                    ps, rps = self.next_ps()
                    for kc in range(8):
                        k.op("pe", lambda e, kc=kc: e.matmul(ps[:, 0:256], lhsT=self.hT[:, kc, i * 128:(i + 1) * 128], rhs=w[:, kc, :], start=(kc == 0), stop=(kc == 7)),
                             reads=wres + [self.RhT[i]], writes=[rps])
                    k.op("act", lambda e: e.activation(out=zs[:, 0:256], in_=ps[:, 0:256], func=AF.Silu), reads=[rps], writes=[Rzs])
                    k.op("dve", lambda e: e.tensor_tensor(out=Ofin[:, 0:256], in0=Ofin[:, 0:256], in1=zs[:, 0:256], op=ALU.mult), reads=[ROf, Rzs], writes=[ROf])
                    k.op("act", lambda e: e.activation(out=zs[:, 256:512], in_=Ofin[:, 0:256], func=AF.Square, accum_out=ssq[:, i, g:g + 1]), reads=[ROf, self.Rrs], writes=[Rzs, self.Rrs])
                    k.op("act", lambda e: e.activation(out=yb[:, 0:256], in_=Ofin[:, 0:256], func=AF.Identity), reads=[ROf], writes=[Ryb])
                    pb, rpb = self.next_pb()
                    for c in range(2):
                        k.op("pe", lambda e: e.transpose(pb[:, c * 128:(c + 1) * 128], yb[:, c * 128:(c + 1) * 128], self.identb[:]), reads=[Ryb, self.Rid], writes=[rpb])
                    for c in range(2):
                        kc = 2 * g + c
                        k.op("act", lambda e: e.activation(out=yT[:, kc, i * 128:(i + 1) * 128], in_=pb[:, c * 128:(c + 1) * 128], func=AF.Identity, scale=nw[:, kc:kc + 1]),
                             reads=[rpb, Rdt], writes=[RyT[i]])
        k.barrier()
    rs = self.rs_ssd
    k.op("dve", lambda e: e.tensor_reduce(out=rs[:], in_=ssq[:], axis=AX.X, op=ALU.add), reads=[self.Rrs], writes=[self.Rrs])
    k.op("act", lambda e: e.activation(out=rs[:], in_=rs[:], func=AF.Ln, scale=1.0 / 1024, bias=self.epsb[:, 0:1]), reads=[self.Rrs, self.Rid], writes=[self.Rrs])
    k.op("act", lambda e: e.activation(out=rs[:], in_=rs[:], func=AF.Exp, scale=-0.5), reads=[self.Rrs], writes=[self.Rrs])


Prog.mixer_ab = _mixer_ab
Prog.out_proj = _out_proj
Prog.ssd = _ssd


def _rwkv(self, l, j, Wv, yT, RyT):
    k = self.k
    MA = self.MA
    B0 = 3104
    f32v = lambda a, b: MA[:, a:b].bitcast(F32)
    twT = MA[:, 16384:17408]; adT = MA[:, 17408:18432]; sgT = MA[:, 18432:19456]
    rT = f32v(19456, 21504); kT = f32v(21504, 23552); vT = f32v(23552, 25600); kkT = f32v(25600, 27648)
    Vtok = f32v(27648, 29696).rearrange("p (i c) -> p i c", i=NT)
    Yacc = MA[:, 29696:30720].rearrange("p (i c) -> p i c", i=NT)
    bonT = MA[:, 30720:31744]
    raw = f32v(31744, 33796)
    tsm = MA[:, 33796:35844].rearrange("p (a t) -> p a t", a=2)
    t1, Rt1 = self.tmpf[0], self.Rtmpf[0]
    t2, Rt2 = self.tmpf[1], self.Rtmpf[1]
    Rsh = Res(); Rraw = Res(); Rvec = Res()
    vec = self.scr("rwvec", [128, 72])
    mu = self.scr("rwmu", [128, 56])
    cst = self.scr("rwc", [128, 4])
    k.dma("sp", "rv0", vec[:], self.din["rw_vec"][j], writes=[Rvec])
    k.dma("sp", "rv0", mu[:, 28:55], self.din["rw_mu"][j], writes=[Rvec])
    k.op("dve", lambda e: e.tensor_scalar(out=mu[:, 0:27], in0=mu[:, 28:55], scalar1=-1.0, scalar2=1.0, op0=ALU.mult, op1=ALU.add), reads=[Rvec], writes=[Rvec])
    k.op("dve", lambda e: e.tensor_scalar(out=mu[:, 28:55], in0=mu[:, 28:55], scalar1=0.5, scalar2=None, op0=ALU.mult), reads=[Rvec], writes=[Rvec])
    k.op("dve", lambda e: e.memset(cst[:, 0:1], 1e-12), writes=[Rvec])
    k.op("dve", lambda e: e.memset(cst[:, 1:2], -0.5), writes=[Rvec])
    k.op("dve", lambda e: e.memset(cst[:, 2:3], 64e-5), writes=[Rvec])
    nvec = self.scr("rwnvec", [128, 24])
    k.op("dve", lambda e: e.tensor_scalar(out=nvec[:, 0:16], in0=vec[:, 0:16], scalar1=-1.0, scalar2=None, op0=ALU.mult), reads=[Rvec], writes=[Rvec])
    k.op("dve", lambda e: e.tensor_scalar(out=nvec[:, 16:24], in0=vec[:, 40:48], scalar1=-1.0, scalar2=1.0, op0=ALU.mult, op1=ALU.add), reads=[Rvec], writes=[Rvec])
    k.op("dve", lambda e: e.memset(raw[:, 0:1], 0.0), writes=[Rraw])
    k.op("dve", lambda e: e.memset(raw[:, 1025:1026], 0.0), writes=[Rraw])
    for a in range(2):
        k.dma("pool", "tsm%d" % a, tsm[:, a, :], self.din["tsm"][a:a + 1, :].partition_broadcast(128), writes=[Rsh])
    ones128 = self.scr("ones128", [128, 128])
    k.op("dve", lambda e: e.memset(ones128[:], 1.0), writes=[Rvec])
    BD = self.masks[:, 4, :]

    def rw_block(cb, post):
        self.proj_feat(Wv, B0 + cb * 128, 128, lambda c, half, ps, rps, m: k.op(
            "act", lambda e: e.activation(out=raw[:, 1 + half * 512:1 + (half + 1) * 512], in_=ps, func=AF.Identity), reads=[rps], writes=[Rraw]))
        k.op("dve", lambda e: e.tensor_tensor(out=t1[:], in0=raw[:, 0:1024], in1=tsm[:, 0, :], op=ALU.mult), reads=[Rraw, Rsh], writes=[Rt1])
        k.op("dve", lambda e: e.tensor_tensor(out=t2[:], in0=raw[:, 2:1026], in1=tsm[:, 1, :], op=ALU.mult), reads=[Rraw, Rsh], writes=[Rt2])
        k.op("dve", lambda e: e.tensor_tensor(out=t1[:], in0=t1[:], in1=t2[:], op=ALU.add), reads=[Rt1, Rt2], writes=[Rt1])
        k.op("dve", lambda e: e.tensor_scalar(out=t2[:], in0=raw[:, 1:1025], scalar1=mu[:, cb:cb + 1], scalar2=None, op0=ALU.mult), reads=[Rraw, Rvec], writes=[Rt2])
        k.op("dve", lambda e: e.scalar_tensor_tensor(out=t1[:], in0=t1[:], scalar=mu[:, 28 + cb:29 + cb], in1=t2[:], op0=ALU.mult, op1=ALU.add), reads=[Rt1, Rt2, Rvec], writes=[Rt1])
        post()

    Rlo = Res()
    rw_block(24, lambda: k.op("act", lambda e: e.activation(out=twT, in_=t1[:], func=AF.Tanh), reads=[Rt1], writes=[Rlo]))
    rw_block(25, lambda: k.op("act", lambda e: e.activation(out=adT, in_=t1[:], func=AF.Identity), reads=[Rt1], writes=[Rlo]))
    rw_block(26, lambda: k.op("act", lambda e: e.activation(out=sgT, in_=t1[:], func=AF.Sigmoid), reads=[Rt1], writes=[Rlo]))
    w2v = self.din["rwkv_w2"][j].rearrange("d r c -> (d r) c")
    a2v = self.din["rwkv_a2"][j].rearrange("d r c -> (d r) c")
    g2v = self.din["rwkv_g2"][j]
    lw3 = self.scr("rwlw3", [128, 3, 128], BF16)
    P = self.scr("rwP", [128, 64]); Z = self.scr("rwZ", [64, 128])
    U = self.scr("rwU", [128, 64]); RH = self.scr("rwRH", [128, 64])
    sm = self.scr("rwsm", [128, 16])
    slot = lambda n: (self.tmpf[0], self.tmpf[1], self.junk)[n // 8][:, (n % 8) * 128:(n % 8 + 1) * 128]
    QR = self.tmpf[0][:, 0:256]
    KT_ = slot(2); CT_ = slot(3); aT = slot(4); e2 = slot(5); cs = slot(6); csx = slot(7)
    Ep = slot(8); Em = slot(9); Ex = slot(10); kd = slot(11); cc = slot(12); Khat = slot(13); Chat = slot(14); KhT = slot(15)
    AkT = self.junk[:, 0:256]; AcT = self.junk[:, 256:512]; MT = slot(20); X = [slot(21), slot(22)]; ChT = slot(23)
    PP = [self.scr("rwPP%d" % i, [128, 256]) for i in range(2)]
    nb = self.scr("nrmb", [128, 1024])
    hbf = [self.hb[0][:].bitcast(F32), self.hb[1][:].bitcast(F32)]
    khf = self.scr("khat", [128, 256], BF16)[:].bitcast(F32)
    nsl = lambda n: nb[:, n * 128:(n + 1) * 128]
    U1 = self.scr("rwU1", [128, 64]); RH1 = self.scr("rwRH1", [128, 64])
    TS = [
        {"AkT": AkT, "AcT": AcT, "MT": MT, "X": X, "XT": [slot(11), slot(12)], "D": slot(4), "DT": slot(5), "W": slot(6), "WT": slot(7), "PP": PP, "RH": RH, "U": U},
        {"AkT": nb[:, 0:256], "AcT": nb[:, 256:512], "MT": nsl(4), "X": [nsl(5), nsl(6)], "XT": [nsl(7), hbf[0][:, 0:128]], "D": hbf[0][:, 128:256], "DT": hbf[0][:, 256:384],
         "W": hbf[0][:, 384:512], "WT": khf, "PP": [hbf[1][:, 0:256], hbf[1][:, 256:512]], "RH": RH1, "U": U1},
    ]
    Ys = self.scr("rwYs", [128, 128]); yn = self.scr("rwyn", [128, 128])
    seg_first = {0: [0, 2, 4, 6], 1: [7, 5, 3, 1]}
    seg_last = {0: [1, 3, 5, 7], 1: [6, 4, 2, 0]}
    k.barrier()
    for hp in range(8):
        cp = slice(hp * 128, (hp + 1) * 128)
        Rr, Rk, Rv, Rkk, Rbon = Res(), Res(), Res(), Res(), Res()
        RVt = [Res() for _ in range(NT)]; RYa = [Res() for _ in range(NT)]
        Rlw = Res(); RP = Res(); RU = Res(); RRH = Res(); Rsm = Res(); RYs = Res(); Ryn = Res()
        RT1 = {n: Res() for n in ("AkT", "AcT", "MT", "X0", "X1", "XT0", "XT1", "D", "DT", "W", "WT", "PP0", "PP1", "RH", "U")}
        Rs = {n: Res() for n in ("QR", "KT", "CT", "aT", "e2", "cs", "csx", "Ep", "Em", "Ex", "kd", "cc", "Khat", "Chat", "KhT", "ChT", "AkT", "AcT", "MT", "X0", "X1", "PP0", "PP1")}
        rw_block(hp, lambda: k.op("act", lambda e: e.activation(out=rT, in_=t1[:], func=AF.Identity), reads=[Rt1], writes=[Rr]))
        rw_block(8 + hp, lambda: k.op("act", lambda e: e.activation(out=kT, in_=t1[:], func=AF.Identity), reads=[Rt1], writes=[Rk]))
        rw_block(16 + hp, lambda: k.op("act", lambda e: e.activation(out=vT, in_=t1[:], func=AF.Identity), reads=[Rt1], writes=[Rv]))
        k.dma("pool", "lw3a", lw3[:, 0, :], w2v[:, cp], writes=[Rlw])
        k.dma("pool", "lw3b", lw3[:, 1, :], a2v[:, cp], writes=[Rlw])
        k.dma("pool", "lw3c", lw3[:, 2, :], g2v[:, cp], writes=[Rlw])
        k.op("dve", lambda e: e.tensor_scalar(out=kkT, in0=kT, scalar1=vec[:, 32 + hp:33 + hp], scalar2=None, op0=ALU.mult), reads=[Rk, Rvec], writes=[Rkk])
        k.op("dve", lambda e: e.tensor_tensor(out=t1[:], in0=kkT, in1=kkT, op=ALU.mult), reads=[Rkk], writes=[Rt1])
        for half in range(2):
            hs = slice(half * 512, (half + 1) * 512)
            ps, rps = self.next_ps()
            k.op("pe", lambda e: e.matmul(ps[:, :], lhsT=BD, rhs=t1[:, hs], start=True, stop=True), reads=[Rt1, self.Rmask], writes=[rps])
            k.op("act", lambda e: e.activation(out=t2[:, hs], in_=ps[:, :], func=AF.Ln, bias=cst[:, 0:1]), reads=[rps, Rvec], writes=[Rt2])
        k.op("act", lambda e: e.activation(out=t2[:], in_=t2[:], func=AF.Exp, scale=-0.5), reads=[Rt2], writes=[Rt2])
        k.op("dve", lambda e: e.tensor_tensor(out=kkT, in0=kkT, in1=t2[:], op=ALU.mult), reads=[Rkk, Rt2], writes=[Rkk])
        k.op("dve", lambda e: e.scalar_tensor_tensor(out=t1[:], in0=rT, scalar=vec[:, 48 + hp:49 + hp], in1=kT, op0=ALU.mult, op1=ALU.mult), reads=[Rr, Rk, Rvec], writes=[Rt1])
        for half in range(2):
            hs = slice(half * 512, (half + 1) * 512)
            ps, rps = self.next_ps()
            k.op("pe", lambda e: e.matmul(ps[:, :], lhsT=BD, rhs=t1[:, hs], start=True, stop=True), reads=[Rt1, self.Rmask], writes=[rps])
            k.op("dve", lambda e: e.tensor_tensor(out=bonT[:, hs], in0=ps[:, :], in1=vT[:, hs], op=ALU.mult), reads=[rps, Rv], writes=[Rbon])
        for i in range(NT):
            ps, rps = self.next_ps()
            k.op("pe", lambda e: e.transpose(ps[:, 0:128], vT[:, i * 128:(i + 1) * 128], self.identf[:]), reads=[Rv, self.Rid], writes=[rps])
            k.op("act", lambda e: e.activation(out=Vtok[:, i, :], in_=ps[:, 0:128], func=AF.Identity), reads=[rps], writes=[RVt[i]])
        k.barrier()
        for d in range(2):
            order = list(range(NT)) if d == 0 else list(range(NT - 1, -1, -1))
            ds_ = slice(d * 64, (d + 1) * 64)
            m_inc = self.masks[:, d, :]
            m_str = self.masks[:, 3 - d, :]
            m_strT = self.masks[:, 2 + d, :]
            endcol = 127 if d == 0 else 0
            for ci, i in enumerate(order):
                ts_ = slice(i * 128, (i + 1) * 128)
                sgi = i // 2
                if i in seg_first[d]:
                    if ci == 0:
                        for hl in range(2):
                            k.dma("sp", "rst%d" % hl, Z[:, hl * 64:(hl + 1) * 64], self.din["st_rw"][j, d, 2 * hp + hl], writes=[RP])
                        ps, rps = self.next_ps()
                        k.op("pe", lambda e: e.transpose(ps[:, 0:64], Z[:], self.identf[0:64, 0:64]), reads=[RP, self.Rid], writes=[rps])
                        k.op("act", lambda e: e.activation(out=P[:], in_=ps[:, 0:64], func=AF.Identity), reads=[rps], writes=[RP])
                    else:
                        k.op("dve", lambda e: e.tensor_scalar(out=P[:], in0=P[:], scalar1=self.flags[:, 0:1], scalar2=None, op0=ALU.mult), reads=[RP, self.Rmask], writes=[RP])
                ps, rps = self.next_ps()
                k.op("pe", lambda e: e.matmul(ps[:, 0:128], lhsT=lw3[ds_, 1, :], rhs=adT[ds_, ts_], start=True, stop=True), reads=[Rlw, Rlo], writes=[rps])
                k.op("pe", lambda e: e.matmul(ps[:, 128:256], lhsT=lw3[ds_, 0, :], rhs=twT[ds_, ts_], start=True, stop=True), reads=[Rlw, Rlo], writes=[rps])
                k.op("act", lambda e: e.activation(out=aT, in_=ps[:, 0:128], func=AF.Sigmoid, bias=vec[:, 16 + 8 * d + hp:17 + 8 * d + hp]), reads=[rps, Rvec], writes=[Rs["aT"]])
                k.op("act", lambda e: e.activation(out=e2, in_=ps[:, 128:256], func=AF.Exp, scale=-1.0, bias=nvec[:, 8 * d + hp:8 * d + hp + 1]), reads=[rps, Rvec], writes=[Rs["e2"]])
                k.op("act", lambda e: e.activation(out=e2, in_=e2, func=AF.Ln, bias=self.onesc[:, 0:1]), reads=[Rs["e2"], self.Rmask], writes=[Rs["e2"]])
                k.op("act", lambda e: e.activation(out=e2, in_=e2, func=AF.Exp, scale=-1.0, bias=cst[:, 1:2]), reads=[Rs["e2"], Rvec], writes=[Rs["e2"]])
                k.op("dve", lambda e: e.tensor_tensor_scan(out=cs, data0=ones128[:], data1=e2, initial=0.0, op0=ALU.mult, op1=ALU.add), reads=[Rs["e2"], Rvec], writes=[Rs["cs"]])
                if d == 1:
                    k.op("dve", lambda e: e.tensor_copy(out=sm[:, 0:1], in_=cs[:, 127:128]), reads=[Rs["cs"]], writes=[Rsm])
                    k.op("dve", lambda e: e.scalar_tensor_tensor(out=cs, in0=e2, scalar=sm[:, 0:1], in1=cs, op0=ALU.add, op1=ALU.subtract), reads=[Rs["e2"], Rs["cs"], Rsm], writes=[Rs["cs"]])
                k.op("dve", lambda e: e.tensor_tensor(out=csx, in0=cs, in1=e2, op=ALU.subtract), reads=[Rs["cs"], Rs["e2"]], writes=[Rs["csx"]])
                k.op("act", lambda e: e.activation(out=Ep, in_=cs, func=AF.Exp, scale=-1.0), reads=[Rs["cs"]], writes=[Rs["Ep"]])
                k.op("act", lambda e: e.activation(out=Em, in_=cs, func=AF.Exp), reads=[Rs["cs"]], writes=[Rs["Em"]])
                k.op("act", lambda e: e.activation(out=Ex, in_=csx, func=AF.Exp, scale=-1.0), reads=[Rs["csx"]], writes=[Rs["Ex"]])
                k.op("dve", lambda e: e.tensor_scalar(out=kd, in0=aT, scalar1=vec[:, 40 + hp:41 + hp], scalar2=nvec[:, 16 + hp:17 + hp], op0=ALU.mult, op1=ALU.add), reads=[Rs["aT"], Rvec], writes=[Rs["kd"]])
                k.op("dve", lambda e: e.tensor_tensor(out=kd, in0=kd, in1=kT[:, ts_], op=ALU.mult), reads=[Rs["kd"], Rk], writes=[Rs["kd"]])
                k.op("dve", lambda e: e.tensor_tensor(out=cc, in0=kkT[:, ts_], in1=aT, op=ALU.mult), reads=[Rkk, Rs["aT"]], writes=[Rs["cc"]])
                k.op("dve", lambda e: e.tensor_tensor(out=QR[:, 0:128], in0=kkT[:, ts_], in1=Ex, op=ALU.mult), reads=[Rkk, Rs["Ex"]], writes=[Rs["QR"]])
                k.op("dve", lambda e: e.tensor_tensor(out=QR[:, 128:256], in0=rT[:, ts_], in1=Ep, op=ALU.mult), reads=[Rr, Rs["Ep"]], writes=[Rs["QR"]])
                k.op("dve", lambda e: e.tensor_tensor(out=KT_, in0=kd, in1=Em, op=ALU.mult), reads=[Rs["kd"], Rs["Em"]], writes=[Rs["KT"]])
                k.op("dve", lambda e: e.tensor_tensor(out=CT_, in0=cc, in1=Em, op=ALU.mult), reads=[Rs["cc"], Rs["Em"]], writes=[Rs["CT"]])
                k.op("dve", lambda e: e.tensor_scalar(out=KhT, in0=KT_, scalar1=Ep[:, endcol:endcol + 1], scalar2=None, op0=ALU.mult), reads=[Rs["KT"], Rs["Ep"]], writes=[Rs["KhT"]])
                k.op("dve", lambda e: e.tensor_scalar(out=ChT, in0=CT_, scalar1=Ep[:, endcol:endcol + 1], scalar2=-1.0, op0=ALU.mult, op1=ALU.mult), reads=[Rs["CT"], Rs["Ep"]], writes=[Rs["ChT"]])
                ps, rps = self.next_ps()
                k.op("pe", lambda e: e.transpose(ps[:, 0:128], KhT, self.identf[:]), reads=[Rs["KhT"], self.Rid], writes=[rps])
                k.op("pe", lambda e: e.transpose(ps[:, 128:256], ChT, self.identf[:]), reads=[Rs["ChT"], self.Rid], writes=[rps])
                k.op("act", lambda e: e.activation(out=Khat, in_=ps[:, 0:128], func=AF.Identity), reads=[rps], writes=[Rs["Khat"]])
                k.op("act", lambda e: e.activation(out=Chat, in_=ps[:, 128:256], func=AF.Identity), reads=[rps], writes=[Rs["Chat"]])
                TS[0]["R"] = {"AkT": Rs["AkT"], "AcT": Rs["AcT"], "MT": Rs["MT"], "X0": Rs["X0"], "X1": Rs["X1"], "XT0": Rs["kd"], "XT1": Rs["cc"], "D": Rs["aT"], "DT": Rs["e2"],
                              "W": Rs["cs"], "WT": Rs["csx"], "PP0": Rs["PP0"], "PP1": Rs["PP1"], "RH": RRH, "U": RU}
                TS[1]["R"] = RT1
                pdl, rpdl = self.next_ps(pin=True)
                pY, rpY = self.next_ps(pin=True)
                def head_gen(hl):
                    hs_ = slice(hl * 64, (hl + 1) * 64)
                    Vh = Vtok[:, i, hs_]
                    tl = TS[hl]
                    mybanks = freeb[2 * hl:2 * hl + 2]
                    cnt_ = [0]

                    def hps():
                        bi = mybanks[cnt_[0] % 2]
                        cnt_[0] += 1
                        return self.PS[bi], self.RPS[bi]
                    AkT, AcT, MT, X, XT, Dm, DTm, Wm, WTm, PPh, RH, U = (tl[n] for n in ("AkT", "AcT", "MT", "X", "XT", "D", "DT", "W", "WT", "PP", "RH", "U"))
                    rr = tl["R"]
                    rX = [rr["X0"], rr["X1"]]; rXT = [rr["XT0"], rr["XT1"]]
                    rD, rDT, rW, rWT, RRH, RU = rr["D"], rr["DT"], rr["W"], rr["WT"], rr["RH"], rr["U"]
                    ps, rps = hps()
                    k.op("pe", lambda e: e.matmul(ps[:, 0:256], lhsT=KT_[hs_, :], rhs=QR[hs_, :], start=True, stop=True), reads=[Rs["KT"], Rs["QR"]], writes=[rps])
                    yield
                    k.op("dve", lambda e: e.tensor_tensor(out=AkT[:, 0:128], in0=ps[:, 0:128], in1=m_str, op=ALU.mult), reads=[rps, self.Rmask], writes=[rr["AkT"]])
                    yield
                    k.op("dve", lambda e: e.tensor_tensor(out=AkT[:, 128:256], in0=ps[:, 128:256], in1=m_inc, op=ALU.mult), reads=[rps, self.Rmask], writes=[rr["AkT"]])
                    yield
                    ps, rps = hps()
                    k.op("pe", lambda e: e.matmul(ps[:, 0:256], lhsT=CT_[hs_, :], rhs=QR[hs_, :], start=True, stop=True), reads=[Rs["CT"], Rs["QR"]], writes=[rps])
                    yield
                    k.op("dve", lambda e: e.tensor_tensor(out=AcT[:, 0:128], in0=ps[:, 0:128], in1=m_str, op=ALU.mult), reads=[rps, self.Rmask], writes=[rr["AcT"]])
                    yield
                    k.op("dve", lambda e: e.scalar_tensor_tensor(out=AcT[:, 128:256], in0=ps[:, 128:256], scalar=-1.0, in1=m_inc, op0=ALU.mult, op1=ALU.mult), reads=[rps, self.Rmask], writes=[rr["AcT"]])
                    yield
                    ps, rps = hps()
                    k.op("pe", lambda e: e.matmul(ps[:, 0:128], lhsT=QR[hs_, 0:128], rhs=CT_[hs_, :], start=True, stop=True), reads=[Rs["CT"], Rs["QR"]], writes=[rps])
                    yield
                    k.op("dve", lambda e: e.tensor_tensor(out=MT, in0=ps[:, 0:128], in1=m_strT, op=ALU.mult), reads=[rps, self.Rmask], writes=[rr["MT"]])
                    yield
                    M_ = AcT[:, 0:128]
                    bd = lambda q: self.masks[:, 5 + q, :]
                    k.op("dve", lambda e: e.tensor_tensor(out=Dm, in0=M_, in1=bd(0), op=ALU.mult), reads=[rr["AcT"], self.Rmask], writes=[rD])
                    yield
                    k.op("dve", lambda e: e.tensor_tensor(out=DTm, in0=MT, in1=bd(0), op=ALU.mult), reads=[rr["MT"], self.Rmask], writes=[rDT])
                    yield
                    k.op("dve", lambda e: e.scalar_tensor_tensor(out=X[0], in0=Dm, scalar=-1.0, in1=self.identf[:], op0=ALU.mult, op1=ALU.add), reads=[rD, self.Rid], writes=[rX[0]])
                    yield
                    k.op("dve", lambda e: e.scalar_tensor_tensor(out=XT[0], in0=DTm, scalar=-1.0, in1=self.identf[:], op0=ALU.mult, op1=ALU.add), reads=[rDT, self.Rid], writes=[rXT[0]])
                    yield
                    xi = 0
                    curP, curPT, rcur = Dm, DTm, [rD, rDT]
                    for lev in range(RWLEV):
                        pp, rpp = PPh[lev % 2], rr["PP%d" % (lev % 2)]
                        ps, rps = hps()
                        k.op("pe", lambda e: e.matmul(ps[:, 0:128], lhsT=curPT, rhs=curP, start=True, stop=True), reads=rcur, writes=[rps])
                        yield
                        k.op("pe", lambda e: e.matmul(ps[:, 128:256], lhsT=curP, rhs=curPT, start=True, stop=True), reads=rcur, writes=[rps])
                        yield
                        k.op("act", lambda e: e.activation(out=pp[:], in_=ps[:, 0:256], func=AF.Identity), reads=[rps], writes=[rpp])
                        yield
                        curP, curPT, rcur = pp[:, 0:128], pp[:, 128:256], [rpp]
                        ps2, rps2 = hps()
                        k.op("pe", lambda e: e.matmul(ps2[:, 0:128], lhsT=curPT, rhs=X[xi], start=True, stop=True), reads=[rpp, rX[xi]], writes=[rps2])
                        yield
                        k.op("pe", lambda e: e.matmul(ps2[:, 128:256], lhsT=curP, rhs=XT[xi], start=True, stop=True), reads=[rpp, rXT[xi]], writes=[rps2])
                        yield
                        k.op("dve", lambda e: e.tensor_tensor(out=X[1 - xi], in0=ps2[:, 0:128], in1=X[xi], op=ALU.add), reads=[rps2, rX[xi]], writes=[rX[1 - xi]])
                        yield
                        k.op("dve", lambda e: e.tensor_tensor(out=XT[1 - xi], in0=ps2[:, 128:256], in1=XT[xi], op=ALU.add), reads=[rps2, rXT[xi]], writes=[rXT[1 - xi]])
                        yield
                        xi = 1 - xi
                    for q in RWQ:
                        last = (q == 3)
                        k.op("dve", lambda e: e.tensor_tensor(out=DTm, in0=MT, in1=bd(q), op=ALU.mult), reads=[rr["MT"], self.Rmask], writes=[rDT])
                        yield
                        ps, rps = hps()
                        k.op("pe", lambda e: e.matmul(ps[:, 0:128], lhsT=DTm, rhs=X[xi], start=True, stop=True), reads=[rDT, rX[xi]], writes=[rps])
                        yield
                        if not last:
                            k.op("dve", lambda e: e.tensor_tensor(out=Dm, in0=M_, in1=bd(q), op=ALU.mult), reads=[rr["AcT"], self.Rmask], writes=[rD])
                            yield
                            k.op("pe", lambda e: e.matmul(ps[:, 128:256], lhsT=Dm, rhs=XT[xi], start=True, stop=True), reads=[rD, rXT[xi]], writes=[rps])
                            yield
                        k.op("act", lambda e: e.activation(out=Wm, in_=ps[:, 0:128], func=AF.Identity), reads=[rps], writes=[rW])
                        yield
                        if not last:
                            k.op("act", lambda e: e.activation(out=WTm, in_=ps[:, 128:256], func=AF.Identity), reads=[rps], writes=[rWT])
                            yield
                        ps2, rps2 = hps()
                        k.op("pe", lambda e: e.matmul(ps2[:, 0:128], lhsT=XT[xi], rhs=Wm, start=True, stop=True), reads=[rXT[xi], rW], writes=[rps2])
                        yield
                        if not last:
                            k.op("pe", lambda e: e.matmul(ps2[:, 128:256], lhsT=X[xi], rhs=WTm, start=True, stop=True), reads=[rX[xi], rWT], writes=[rps2])
                            yield
                        k.op("dve", lambda e: e.tensor_tensor(out=X[1 - xi], in0=X[xi], in1=ps2[:, 0:128], op=ALU.subtract), reads=[rps2, rX[xi]], writes=[rX[1 - xi]])
                        yield
                        if not last:
                            k.op("dve", lambda e: e.tensor_tensor(out=XT[1 - xi], in0=XT[xi], in1=ps2[:, 128:256], op=ALU.subtract), reads=[rps2, rXT[xi]], writes=[rXT[1 - xi]])
                            yield
                        xi = 1 - xi
                    TT, rTT = X[xi], rX[xi]
                    ps, rps = hps()
                    k.op("pe", lambda e: e.matmul(ps[:, 0:64], lhsT=QR[hs_, 0:128], rhs=P[hs_, :], start=True, stop=False), reads=[Rs["QR"], RP], writes=[rps])
                    yield
                    k.op("pe", lambda e: e.matmul(ps[:, 0:64], lhsT=AkT[:, 0:128], rhs=Vh, start=False, stop=True), reads=[rr["AkT"], RVt[i]], writes=[rps])
                    yield
                    k.op("act", lambda e: e.activation(out=RH[:], in_=ps[:, 0:64], func=AF.Identity), reads=[rps], writes=[RRH])
                    yield
                    ps, rps = hps()
                    k.op("pe", lambda e: e.matmul(ps[:, 0:64], lhsT=TT, rhs=RH[:], start=True, stop=True), reads=[rTT, RRH], writes=[rps])
                    yield
                    k.op("act", lambda e: e.activation(out=U[:], in_=ps[:, 0:64], func=AF.Identity), reads=[rps], writes=[RU])
                    yield
                    k.op("pe", lambda e: e.matmul(pY[:, hs_], lhsT=QR[hs_, 128:256], rhs=P[hs_, :], start=True, stop=False), reads=[Rs["QR"], RP], writes=[rpY])
                    k.op("pe", lambda e: e.matmul(pY[:, hs_], lhsT=AkT[:, 128:256], rhs=Vh, start=False, stop=False), reads=[rr["AkT"], RVt[i]], writes=[rpY])
                    k.op("pe", lambda e: e.matmul(pY[:, hs_], lhsT=AcT[:, 128:256], rhs=U[:], start=False, stop=True), reads=[rr["AcT"], RU], writes=[rpY])
                    yield
                    k.op("pe", lambda e: e.matmul(pdl[hs_, 0:64], lhsT=Khat[:, hs_], rhs=Vh, start=True, stop=False), reads=[Rs["Khat"], RVt[i]], writes=[rpdl])
                    k.op("pe", lambda e: e.matmul(pdl[hs_, 0:64], lhsT=Chat[:, hs_], rhs=U[:], start=False, stop=True), reads=[Rs["Chat"], RU], writes=[rpdl])
                    yield

                freeb = [bi for bi in range(len(self.PS)) if bi not in self.pinned]
                gens = [head_gen(0), head_gen(1)]
                while gens:
                    for g_ in list(gens):
                        try:
                            next(g_)
                        except StopIteration:
                            gens.remove(g_)
                k.op("dve", lambda e: e.scalar_tensor_tensor(out=P[:], in0=P[:], scalar=Ep[:, endcol:endcol + 1], in1=pdl[:, 0:64], op0=ALU.mult, op1=ALU.add),
                     reads=[RP, Rs["Ep"], rpdl], writes=[RP])
                if d == 0:
                    k.op("act", lambda e: e.activation(out=Yacc[:, i, :], in_=pY[:, 0:128], func=AF.Identity), reads=[rpY], writes=[RYa[i]])
                else:
                    k.op("dve", lambda e: e.tensor_tensor(out=Ys[:], in0=pY[:, 0:128], in1=Yacc[:, i, :], op=ALU.add), reads=[rpY, RYa[i]], writes=[RYs])
                self.unpin(rpdl, rpY)
                if i in seg_last[d]:
                    ps, rps = self.next_ps()
                    k.op("pe", lambda e: e.transpose(ps[0:64, 0:128], P[:], self.identf[:]), reads=[RP, self.Rid], writes=[rps])
                    k.op("act", lambda e: e.activation(out=Z[:], in_=ps[0:64, 0:128], func=AF.Identity), reads=[rps], writes=[RRH])
                    for hl in range(2):
                        k.dma("sp", "rso%d" % hl, self.dout["o_rwkv"][j, sgi, d, 2 * hp + hl], Z[:, hl * 64:(hl + 1) * 64], reads=[RRH])
                if d == 1:
                    Y3 = Ys[:].rearrange("p (h v) -> p h v", h=2)
                    k.op("dve", lambda e: e.tensor_reduce(out=sm[:, 2:4], in_=Y3, axis=AX.X, op=ALU.add), reads=[RYs], writes=[Rsm])
                    k.op("dve", lambda e: e.tensor_scalar(out=sm[:, 2:4], in0=sm[:, 2:4], scalar1=-1.0 / 64, scalar2=None, op0=ALU.mult), reads=[Rsm], writes=[Rsm])
                    k.op("dve", lambda e: e.tensor_tensor(out=Y3, in0=Y3, in1=sm[:, 2:4].unsqueeze(2).to_broadcast([128, 2, 64]), op=ALU.add), reads=[RYs, Rsm], writes=[RYs])
                    yn3 = yn[:].rearrange("p (h v) -> p h v", h=2)
                    k.op("dve", lambda e: e.tensor_tensor(out=yn[:], in0=Ys[:], in1=Ys[:], op=ALU.mult), reads=[RYs], writes=[Ryn])
                    k.op("dve", lambda e: e.tensor_reduce(out=sm[:, 4:6], in_=yn3, axis=AX.X, op=ALU.add), reads=[Ryn], writes=[Rsm])
                    k.op("act", lambda e: e.activation(out=sm[:, 4:6], in_=sm[:, 4:6], func=AF.Ln, scale=1.0 / 64, bias=cst[:, 2:3]), reads=[Rsm, Rvec], writes=[Rsm])
                    k.op("act", lambda e: e.activation(out=sm[:, 4:6], in_=sm[:, 4:6], func=AF.Exp, scale=-0.5), reads=[Rsm], writes=[Rsm])
                    k.op("dve", lambda e: e.tensor_tensor(out=yn3, in0=Y3, in1=sm[:, 4:6].unsqueeze(2).to_broadcast([128, 2, 64]), op=ALU.mult), reads=[RYs, Rsm], writes=[Ryn])
                    ps, rps = self.next_ps()
                    k.op("pe", lambda e: e.transpose(ps[:, 0:128], yn[:], self.identf[:]), reads=[Ryn, self.Rid], writes=[rps])
                    k.op("pe", lambda e: e.matmul(ps[:, 128:256], lhsT=lw3[:, 2, :], rhs=sgT[:, ts_], start=True, stop=True), reads=[Rlw, Rlo], writes=[rps])
                    k.op("act", lambda e: e.activation(out=Ys[:], in_=ps[:, 0:128], func=AF.Identity, scale=vec[:, 56 + hp:57 + hp], bias=vec[:, 64 + hp:65 + hp]), reads=[rps, Rvec], writes=[RYs])
                    k.op("dve", lambda e: e.tensor_tensor(out=Ys[:], in0=Ys[:], in1=bonT[:, ts_], op=ALU.add), reads=[RYs, Rbon], writes=[RYs])
                    k.op("dve", lambda e: e.tensor_tensor(out=yT[:, 8 + hp, ts_], in0=Ys[:], in1=ps[:, 128:256], op=ALU.mult), reads=[RYs, rps], writes=[RyT[i]])
        k.barrier()


Prog.rwkv = _rwkv


RWQ = (1, 2, 3)
RWLEV = 3
PARTS = ("gla", "ml", "ssd", "rw")


def kernel(**inputs):
    inputs = {k: np.asarray(v) for k, v in inputs.items()}
    p = Prog(nl=4, do_mix=True, parts=PARTS)
    nc = p.build()
    in_maps = []
    for core in range(8):
        m = _prep_core_inputs(inputs, core)
        in_maps.append({k: np.ascontiguousarray(v, dtype=np.float32) for k, v in m.items() if k in p.din})
    res = run_bass_kernel_spmd(nc, in_maps, core_ids=list(range(8)))
    R = res.results
    y_sample = np.stack([R[c]["y"] for c in range(4)], axis=0).astype(np.float32)
    y_prompt = np.concatenate([R[c]["y"].reshape(4, 256, D) for c in range(4, 8)], axis=0).astype(np.float32)

    def states(name, shape):
        if name not in R[4]:
            return np.zeros((16, 2, 2) + shape, np.float32)
        out = np.zeros((16, 2, 2) + shape, np.float32)
        for c in range(4, 8):
            o = R[c][name]
            for g in range(4):
                out[4 * (c - 4) + g] = o[:, g]
        return out
    new_ssd = states("o_ssd", (16, 128, 64))
    new_rwkv = states("o_rwkv", (16, 64, 64))
    new_gla = states("o_gla", (4, 128, 256))
    new_mc = states("o_mc", (4, 128, 256))
    new_mn = states("o_mn", (4, 128))
    new_mm = states("o_mm", (4,))
    return (y_prompt, y_sample, new_ssd, new_rwkv, new_gla, new_mc, new_mn, new_mm)
```

```python
import contextlib
import numpy as np
import concourse.bass as bass
import concourse.mybir as mybir
from concourse.bass_utils import run_bass_kernel_spmd

F32 = mybir.dt.float32
BF16 = mybir.dt.bfloat16
AF = mybir.ActivationFunctionType
ALU = mybir.AluOpType
AX = mybir.AxisListType

T = 1024
D = 1024
NT = 8
DFF = 4096
EPS = 1e-6
EMBED_WAIT = True


class Res:
    __slots__ = ("name", "w", "r")

    def __init__(self, name=""):
        self.name = name
        self.w = None
        self.r = {}


class KB:
    ENG = ("pe", "act", "dve", "pool", "sp")

    def __init__(self, nc, es):
        self.nc = nc
        self.es = es
        self.e = {"pe": nc.tensor, "act": nc.scalar, "dve": nc.vector, "pool": nc.gpsimd, "sp": nc.sync}
        self.sem = {k: es.enter_context(nc.semaphore("sem_" + k)) for k in self.ENG}
        self.cnt = {k: 0 for k in self.ENG}
        self.seen = {k: {} for k in self.ENG}
        self.chan = {}
        self.ninst = 0
        self.nwait = 0

    def sb(self, name, shape, dt=F32):
        nb = int(np.prod(shape[1:])) * (4 if dt == F32 else 2)
        self.sbtot = getattr(self, "sbtot", 0) + nb
        return self.es.enter_context(self.nc.sbuf_tensor(name, shape, dt))

    def ps(self, name, shape, dt=F32):
        return self.es.enter_context(self.nc.psum_tensor(name, shape, dt))

    def channel(self, name):
        if name not in self.chan:
            s = self.es.enter_context(self.nc.semaphore("ch_" + name))
            self.chan[name] = [s, 0]
        return name

    def _deps(self, reads, writes):
        deps = {}

        def add(tok):
            if tok is None:
                return
            k, v = tok
            if deps.get(k, 0) < v:
                deps[k] = v
        for r in reads:
            add(r.w)
        for w in writes:
            add(w.w)
            for k, v in w.r.items():
                add((k, v))
        return deps

    def _emit_waits(self, eng, deps, embed=False):
        seen = self.seen[eng]
        E = self.e[eng]
        todo = []
        for k, v in deps.items():
            if seen.get(k, 0) >= v:
                continue
            seen[k] = v
            if eng == "pe" and k == "pe":
                continue
            todo.append((self.sem[k] if k in self.sem else self.chan[k][0], v))
        held = todo.pop() if (embed and todo) else None
        for s_, v in todo:
            E.wait_ge(s_, v)
            self.nwait += 1
        return held

    def _mark(self, tok, reads, writes):
        k, v = tok
        for r in reads:
            if r.r.get(k, 0) < v:
                r.r[k] = v
        for w in writes:
            w.w = tok
            w.r = {}

    def op(self, eng, fn, reads=(), writes=()):
        held = self._emit_waits(eng, self._deps(reads, writes), embed=EMBED_WAIT)
        ins = fn(self.e[eng])
        if held is not None:
            ins._wait_ge(held[0], held[1])
        self.cnt[eng] += 1
        ins.then_inc(self.sem[eng], 1)
        tok = (eng, self.cnt[eng])
        self._mark(tok, reads, writes)
        self.ninst += 1
        return tok

    def dma(self, q, ch, out, in_, reads=(), writes=(), **kw):
        self.channel(ch)
        self._emit_waits(q, self._deps(reads, writes))
        ins = self.e[q].dma_start(out=out, in_=in_, **kw)
        c = self.chan[ch]
        c[1] += 16
        ins.then_inc(c[0], 16)
        tok = (ch, c[1])
        self._mark(tok, reads, writes)
        self.ninst += 1
        return tok

    def barrier(self):
        for eng in self.ENG:
            deps = {k: v[1] for k, v in self.chan.items() if v[1] > 0}
            for k in self.ENG:
                if k != eng and self.cnt[k] > 0:
                    deps[k] = self.cnt[k]
            self._emit_waits(eng, deps)

    def finish(self, eng="sp"):
        deps = {k: v[1] for k, v in self.chan.items() if v[1] > 0}
        for k in self.ENG:
            if k != eng and self.cnt[k] > 0:
                deps[k] = self.cnt[k]
        self._emit_waits(eng, deps)


class Prog:
    def __init__(self, nl=4, do_mix=True, layers=None, parts=("gla", "ml", "ssd", "rw")):
        self.nl = nl
        self.layers = list(range(nl)) if layers is None else layers
        self.parts = parts
        self.do_mix = do_mix
        self.nc = bass.Bass("TRN2", target_bir_lowering=False)
        self.es = contextlib.ExitStack()
        self.din = {}
        self.dout = {}

    def inp(self, name, shape, dt=F32):
        ap = self.nc.dram_tensor(name, list(shape), dt, kind="ExternalInput").ap()
        self.din[name] = ap
        return ap

    def outp(self, name, shape):
        ap = self.nc.dram_tensor(name, list(shape), F32, kind="ExternalOutput").ap()
        self.dout[name] = ap
        return ap

    def build(self):
        nc = self.nc
        with self.es as es:
            k = self.k = KB(nc, es)
            self.declare_io()
            self.alloc()
            self.load_consts()
            for l in self.layers:
                self.layer(l)
            self.store_y()
            k.finish()
            print("ninst", k.ninst, "nwait", k.nwait, k.cnt, flush=True)
        return nc

    def declare_io(self):
        self.x_d = self.inp("x", [T, D])
        self.cond_d = self.inp("cond", [128, 8])
        self.ident_d = self.inp("ident", [128, 128])
        self.w_mod_d = self.inp("w_mod", [4, D, 6 * D])
        self.b_mod_d = self.inp("b_mod", [4, 6 * D])
        self.norm_g_d = self.inp("norm_g", [4, 4, D])
        self.w_up_d = self.inp("w_mlp_up", [4, D, DFF])
        self.w_dn_d = self.inp("w_mlp_down", [4, DFF, D])
        self.y_d = self.outp("y", [T, D])
        if self.do_mix:
            self.inp("masks", [128, 9, 128])
            self.inp("flags", [128, 4])
            self.inp("w_in_cd", [2, D, 6192])
            self.inp("w_out_cd", [2, 2048, D])
            self.inp("gla_gate_w", [2, 2, 16, 512])
            self.inp("gla_gate_b", [2, 2, 512])
            self.inp("gla_norm", [2, 1024])
            self.inp("st_gla", [2, 2, 4, 128, 256])
            self.outp("o_gla", [2, 4, 2, 4, 128, 256])
            self.inp("w_in_ab", [2, D, 6560])
            self.inp("w_out_ab", [2, 2048, D])
            self.inp("ssd_dt_bias", [2, 2, 16])
            self.inp("ssd_a_log", [2, 2, 16])
            self.inp("ssd_d", [2, 16])
            self.inp("ssd_nw", [2, 128, 8])
            self.inp("ssd_cw", [2, 16, 128, 9])
            self.inp("ssd_cb", [2, 16, 128, 1])
            self.inp("st_ssd", [2, 2, 16, 128, 64])
            self.outp("o_ssd", [2, 4, 2, 16, 128, 64])
            self.inp("rw_vec", [2, 128, 72])
            self.inp("rw_mu", [2, 128, 27])
            self.inp("tsm", [2, T])
            self.inp("rwkv_w2", [2, 2, 64, 1024])
            self.inp("rwkv_a2", [2, 2, 64, 1024])
            self.inp("rwkv_g2", [2, 128, 1024])
            self.inp("st_rw", [2, 2, 16, 64, 64])
            self.outp("o_rwkv", [2, 4, 2, 16, 64, 64])
            self.inp("convm", [2, T])
            self.inp("tapflag", [128, 9])
            self.inp("ml_cw", [2, 8, 128, 9])
            self.inp("ml_cb", [2, 8, 128, 1])
            self.inp("mlstm_i_b", [2, 2, 4])
            self.inp("mlstm_f_b", [2, 2, 4])
            self.inp("mlstm_norm", [2, 1024])
            self.inp("st_mc", [2, 2, 4, 128, 256])
            self.inp("st_mn", [2, 2, 4, 128])
            self.inp("st_mm", [2, 2, 4])
            self.outp("o_mc", [2, 4, 2, 4, 128, 256])
            self.outp("o_mn", [2, 4, 2, 4, 128])
            self.outp("o_mm", [2, 4, 2, 4])

    def alloc(self):
        k = self.k
        self.X = k.sb("X", [128, NT, D])
        self.RX = [Res("X%d" % i) for i in range(NT)]
        self.hT = k.sb("hT", [128, 8, T], BF16)
        self.RhT = [Res("hT%d" % i) for i in range(NT)]
        self.modb = k.sb("modb", [128, 6 * D])
        self.Rmod = [Res("mod%d" % i) for i in range(6)]
        self.NSLOT = 2
        self.WA = k.sb("WA", [128, self.NSLOT * 4096], BF16)
        self.RW = [Res("W%d" % i) for i in range(self.NSLOT)]
        self.wslot = 0
        self.PS = [k.ps("ps%d" % i, [128, 512]) for i in range(6)]
        self.RPS = [Res("ps%d" % i) for i in range(6)]
        self.psi = 0
        self.PB = [k.ps("pb%d" % i, [128, 1024], BF16) for i in range(2)]
        self.RPB = [Res("pb%d" % i) for i in range(2)]
        self.pbi = 0
        self.identf = k.sb("identf", [128, 128])
        self.identb = k.sb("identb", [128, 128], BF16)
        self.Rid = Res("ident")
        self.cs = k.sb("cs", [128, 8])
        self.Rcond = Res("cond")
        self.ss = k.sb("ss", [128, 16])
        self.Rss = [Res("ss%d" % i) for i in range(16)]
        self.epsb = k.sb("epsb", [128, 1])
        self.Rjunk = Res("junk")
        self.tmpf = [k.sb("tmpf%d" % i, [128, D]) for i in range(2)]
        self.Rtmpf = [Res() for _ in range(2)]
        self.hb = [k.sb("hb%d" % i, [128, D], BF16) for i in range(2)]
        self.Rhb = [Res() for _ in range(2)]
        self.MA = k.sb("MA", [128, 36864], BF16)
        self.RMA = Res("MA")
        self.junk = k.sb("junk", [128, D])
        fa = self.MA[:, 16384:16384 + 4 * 2048].bitcast(F32)
        self.Ff = [fa[:, i * D:(i + 1) * D] for i in range(4)]
        self.RFf = [Res() for _ in range(4)]
        ga = self.MA[:, 2048:2048 + 2 * 2048].bitcast(F32)
        self.gsc = [ga[:, i * D:(i + 1) * D] for i in range(2)]
        self.RWD = Res("wdown")
        self.Rgsc = [Res() for _ in range(2)]

    def scr(self, name, shape, dt=F32):
        if not hasattr(self, "_scr"):
            self._scr = {}
        if name not in self._scr:
            self._scr[name] = self.k.sb(name, shape, dt)
        return self._scr[name]

    def next_ps(self, pin=False):
        if not hasattr(self, "pinned"):
            self.pinned = set()
        while True:
            i = self.psi
            self.psi = (self.psi + 1) % len(self.PS)
            if i not in self.pinned:
                break
        if pin:
            self.pinned.add(i)
        return self.PS[i], self.RPS[i]

    def unpin(self, *rs):
        for r in rs:
            self.pinned.discard(self.RPS.index(r))

    def next_pb(self):
        i = self.pbi
        self.pbi = (self.pbi + 1) % len(self.PB)
        return self.PB[i], self.RPB[i]

    def wslots(self, n):
        if self.wslot + n > self.NSLOT:
            self.wslot = 0
        s = self.wslot
        self.wslot += n
        self.wch = "ws%d" % s
        return self.WA[:, s * 4096:(s + n) * 4096], self.RW[s:s + n]

    def load_consts(self):
        k = self.k
        k.op("dve", lambda e: e.memset(self.epsb[:], EPS), writes=[self.Rid])
        k.dma("sp", "c0", self.identf[:], self.ident_d[:, :], writes=[self.Rid])
        k.op("dve", lambda e: e.tensor_copy(out=self.identb[:], in_=self.identf[:]), reads=[self.Rid], writes=[self.Rid])
        k.dma("sp", "c1", self.cs[:], self.cond_d[:, :], writes=[self.Rcond])
        k.op("act", lambda e: e.activation(out=self.cs[:], in_=self.cs[:], func=AF.Silu), reads=[self.Rcond], writes=[self.Rcond])
        for i in range(NT):
            k.dma("sp", "x%d" % i, self.X[:, i, :], self.x_d[i * 128:(i + 1) * 128, :], writes=[self.RX[i]])

    def store_y(self):
        k = self.k
        for i in range(NT):
            k.dma("sp", "y%d" % i, self.y_d[i * 128:(i + 1) * 128, :], self.X[:, i, :], reads=[self.RX[i]])

    def adaln(self, l):
        k = self.k
        modb = self.modb
        self.condB = self.MA[:, 0:2048].bitcast(F32).rearrange("p (kc m) -> p kc m", kc=8)
        k.op("dve", lambda e: e.tensor_copy(out=self.condB, in_=self.cs[:].unsqueeze(2).to_broadcast([128, 8, 128])),
             reads=[self.Rcond], writes=[self.Rcond])
        k.dma("sp", "bmod", modb[:], self.b_mod_d[l:l + 1, :].partition_broadcast(128), writes=self.Rmod)
        wv = self.w_mod_d[l].rearrange("(kc p) n -> p kc n", p=128)
        NB = 256
        for j in range(6 * D // NB):
            wap, wres = self.wslots(1)
            wf = wap.bitcast(F32).rearrange("p (kc n) -> p kc n", kc=8)
            k.dma("sp", self.wch, wf, wv[:, :, j * NB:(j + 1) * NB], writes=wres)
            ps, rps = self.next_ps()
            for kc in range(8):
                k.op("pe", lambda e, kc=kc: e.matmul(ps[:, 0:NB], lhsT=self.condB[:, kc, :], rhs=wf[:, kc, :],
                                                     start=(kc == 0), stop=(kc == 7)),
                     reads=[self.Rcond] + wres, writes=[rps])
            r = self.Rmod[j * NB // D]
            sl = modb[:, j * NB:(j + 1) * NB]
            k.op("dve", lambda e: e.tensor_tensor(out=sl, in0=ps[:, 0:NB], in1=sl, op=ALU.add), reads=[rps, r], writes=[r])
        for gi, (mi, isscale) in enumerate([(1, True), (2, False), (4, True), (5, False)]):
            g, rg = self.gsc[gi % 2], self.Rgsc[gi % 2]
            k.dma("sp", "g%d" % (gi % 2), g, self.norm_g_d[l, gi:gi + 1, :].partition_broadcast(128), writes=[rg])
            sl = modb[:, mi * D:(mi + 1) * D]
            if isscale:
                k.op("dve", lambda e: e.scalar_tensor_tensor(out=sl, in0=sl, scalar=1.0, in1=g, op0=ALU.add, op1=ALU.mult),
                     reads=[rg, self.Rmod[mi]], writes=[self.Rmod[mi]])
            else:
                k.op("dve", lambda e: e.tensor_tensor(out=sl, in0=sl, in1=g, op=ALU.mult),
                     reads=[rg, self.Rmod[mi]], writes=[self.Rmod[mi]])

    def rstd_of(self, src_ap, rsrc, col, n, junk=None, rjunk=None):
        k = self.k
        ssl = self.ss[:, col:col + 1]
        rss = self.Rss[col]
        k.op("dve", lambda e: e.memset(ssl, 0.0), writes=[rss])
        jk = self.junk[:, 0:n] if junk is None else junk
        rjk = self.Rjunk if rjunk is None else rjunk
        k.op("act", lambda e: e.activation(out=jk, in_=src_ap, func=AF.Square, accum_out=ssl),
             reads=[rsrc], writes=[rss, rjk])
        k.op("act", lambda e: e.activation(out=ssl, in_=ssl, func=AF.Ln, scale=1.0 / n, bias=self.epsb[:, 0:1]),
             reads=[rss, self.Rid], writes=[rss])
        k.op("act", lambda e: e.activation(out=ssl, in_=ssl, func=AF.Exp, scale=-0.5), reads=[rss], writes=[rss])
        return ssl

    def norm_mod_T(self, ai, si):
        k = self.k
        A = self.modb[:, ai * D:(ai + 1) * D]
        S = self.modb[:, si * D:(si + 1) * D]
        for i in range(NT):
            rs = self.rstd_of(self.X[:, i, :], self.RX[i], i, D)
            tf, rtf = self.tmpf[i % 2], self.Rtmpf[i % 2]
            hb, rhb = self.hb[i % 2], self.Rhb[i % 2]
            k.op("dve", lambda e: e.scalar_tensor_tensor(out=tf[:], in0=self.X[:, i, :], scalar=rs, in1=A, op0=ALU.mult, op1=ALU.mult),
                 reads=[self.RX[i], self.Rss[i], self.Rmod[ai]], writes=[rtf])
            k.op("dve", lambda e: e.tensor_tensor(out=hb[:], in0=tf[:], in1=S, op=ALU.add), reads=[rtf, self.Rmod[si]], writes=[rhb])
            pb, rpb = self.next_pb()
            for kc in range(8):
                k.op("pe", lambda e, kc=kc: e.transpose(pb[:, kc * 128:(kc + 1) * 128], hb[:, kc * 128:(kc + 1) * 128], self.identb[:]),
                     reads=[rhb, self.Rid], writes=[rpb])
            k.op("act", lambda e: e.activation(out=self.hT[:, :, i * 128:(i + 1) * 128],
                                               in_=pb[:].rearrange("p (kc t) -> p kc t", kc=8), func=AF.Identity),
                 reads=[rpb], writes=[self.RhT[i]])

    def resid_add(self, i, F, rF, gi):
        k = self.k
        G = self.modb[:, gi * D:(gi + 1) * D]
        rs = self.rstd_of(F, rF, 8 + i, D)
        k.op("dve", lambda e: e.scalar_tensor_tensor(out=F, in0=F, scalar=rs, in1=G, op0=ALU.mult, op1=ALU.mult),
             reads=[rF, self.Rss[8 + i], self.Rmod[gi]], writes=[rF])
        k.op("dve", lambda e: e.tensor_tensor(out=self.X[:, i, :], in0=self.X[:, i, :], in1=F, op=ALU.add),
             reads=[rF, self.RX[i]], writes=[self.RX[i]])

    def mlp(self, l):
        k = self.k
        self.norm_mod_T(4, 3)
        wu = self.w_up_d[l].rearrange("(kc p) n -> p kc n", p=128)
        wd = self.w_dn_d[l].rearrange("(fc p) n -> p fc n", p=128)
        uT = self.MA[:, 0:32 * 512].rearrange("p (fc t) -> p fc t", fc=32)
        Ru = [Res("u%d" % i) for i in range(8)]
        for half in range(2):
            t0 = half * 512
            rh = self.RhT[half * 4:(half + 1) * 4]
            for fb in range(8):
                wap, wres = self.wslots(1)
                w = wap.rearrange("p (kc n) -> p kc n", kc=8)
                k.dma("pool", self.wch, w, wu[:, :, fb * 512:(fb + 1) * 512], writes=wres)
                for fc in range(4):
                    ps, rps = self.next_ps()
                    for kc in range(8):
                        k.op("pe", lambda e, kc=kc: e.matmul(ps[:, :], lhsT=w[:, kc, fc * 128:(fc + 1) * 128], rhs=self.hT[:, kc, t0:t0 + 512],
                                                             start=(kc == 0), stop=(kc == 7)), reads=wres + rh, writes=[rps])
                    tf, rtf = self.tmpf[fc % 2], self.Rtmpf[fc % 2]
                    k.op("act", lambda e: e.activation(out=tf[:, 0:512], in_=ps[:, :], func=AF.Relu), reads=[rps], writes=[rtf])
                    k.op("dve", lambda e: e.tensor_tensor(out=uT[:, fb * 4 + fc, :], in0=tf[:, 0:512], in1=tf[:, 0:512], op=ALU.mult),
                         reads=[rtf], writes=[Ru[fb]])
            Fs = {}
            for nh in range(4):
                wres = [self.RWD]
                w = self.MA[:, 24576:32768].rearrange("p (fc n) -> p fc n", fc=32)
                for q in range(4):
                    k.dma("pool", "wdn", w[:, q * 8:(q + 1) * 8, :], wd[:, q * 8:(q + 1) * 8, nh * 256:(nh + 1) * 256], writes=wres)
                for ti in range(4):
                    i = half * 4 + ti
                    ps, rps = self.next_ps()
                    for fc in range(32):
                        k.op("pe", lambda e, fc=fc: e.matmul(ps[:, 0:256], lhsT=uT[:, fc, ti * 128:(ti + 1) * 128], rhs=w[:, fc, :],
                                                             start=(fc == 0), stop=(fc == 31)), reads=wres + [Ru[fc // 4]], writes=[rps])
                    if nh == 0:
                        Fs[ti] = (self.Ff[ti], self.RFf[ti])
                    F, rF = Fs[ti]
                    k.op("act", lambda e: e.activation(out=F[:, nh * 256:(nh + 1) * 256], in_=ps[:, 0:256], func=AF.Identity),
                         reads=[rps], writes=[rF])
            for ti in range(4):
                F, rF = Fs[ti]
                self.resid_add(half * 4 + ti, F, rF, 5)

    def layer(self, l):
        self.adaln(l)
        self.k.barrier()
        if self.do_mix:
            self.norm_mod_T(1, 0)
            self.mixer(l)
            self.k.barrier()
        self.mlp(l)
        self.k.barrier()


def _prep_core_inputs(inputs, core):
    if core < 4:
        x = np.ascontiguousarray(inputs["x_sample"][core])
        cond = inputs["c"][core]
    else:
        j = core - 4
        x = np.ascontiguousarray(inputs["x_prompt"][4 * j:4 * j + 4].reshape(T, D))
        cond = inputs["c_ctx"]
    m = {"x": x, "cond": np.ascontiguousarray(cond.reshape(8, 128).T), "ident": np.eye(128, dtype=np.float32)}
    for n in ("w_mod", "b_mod", "norm_g", "w_mlp_up", "w_mlp_down", "w_in_cd", "w_out_cd", "gla_gate_w", "gla_gate_b", "gla_norm"):
        m[n] = inputs[n]
    r = np.arange(128)
    bdm = lambda n: (r[:, None] // n) == (r[None, :] // n)
    m["masks"] = np.ascontiguousarray(np.stack([r[:, None] <= r[None, :], r[:, None] >= r[None, :], r[:, None] > r[None, :], r[:, None] < r[None, :],
                                                (r[:, None] // 64) == (r[None, :] // 64),
                                                bdm(16), bdm(32) & ~bdm(16), bdm(64) & ~bdm(32), ~bdm(64)], axis=1).astype(np.float32))
    fl = np.zeros((128, 4), np.float32)
    fl[:, 0] = 1.0 if core < 4 else 0.0
    m["flags"] = fl
    for n in ("mlstm_i_b", "mlstm_f_b", "mlstm_norm"):
        m[n] = inputs[n]
    t = np.arange(T)
    per = 64 if core < 4 else 256
    m["convm"] = np.stack([(t % per) != 0, (t % per) != per - 1]).astype(np.float32)
    tf = np.ones((128, 9), np.float32)
    if core >= 4:
        tf[:, 0:3] = 0.0
        tf[:, 6:9] = 0.0
    m["tapflag"] = tf
    cw = inputs["mlstm_conv_w"]
    m["ml_cw"] = np.ascontiguousarray(cw.reshape(2, 9, 8, 128).transpose(0, 2, 3, 1))
    m["ml_cb"] = np.ascontiguousarray(inputs["mlstm_conv_b"].reshape(2, 8, 128, 1))
    for n in ("w_in_ab", "w_out_ab", "ssd_dt_bias", "ssd_a_log", "ssd_d"):
        m[n] = inputs[n]
    m["ssd_nw"] = np.ascontiguousarray(inputs["ssd_norm"].reshape(2, 8, 128).transpose(0, 2, 1))
    m["ssd_cw"] = np.ascontiguousarray(inputs["ssd_conv_w"].reshape(2, 9, 16, 128).transpose(0, 2, 3, 1))
    m["ssd_cb"] = np.ascontiguousarray(inputs["ssd_conv_b"].reshape(2, 16, 128, 1))
    for n in ("rwkv_w2", "rwkv_a2", "rwkv_g2"):
        m[n] = inputs[n]
    pc = lambda a: a.reshape(2, 8, 128).transpose(0, 2, 1)
    kinds = [inputs["rwkv_w0"][:, 0], inputs["rwkv_w0"][:, 1], inputs["rwkv_a0"][:, 0], inputs["rwkv_a0"][:, 1], inputs["rwkv_k_k"], inputs["rwkv_k_a"],
             inputs["rwkv_r_k"].reshape(2, 1024), inputs["rwkv_ln_w"], inputs["rwkv_ln_b"]]
    m["rw_vec"] = np.ascontiguousarray(np.stack([pc(a) for a in kinds], axis=2).reshape(2, 128, 72))
    m["rw_mu"] = np.ascontiguousarray(inputs["rwkv_mu"].reshape(2, 27, 128).transpose(0, 2, 1))
    sl = 1024 if core < 4 else 256
    m["tsm"] = np.stack([(t % sl) != 0, (t % sl) != sl - 1]).astype(np.float32)
    names = {"st_rw": "state_rwkv", "st_ssd": "state_ssd", "st_gla": "state_gla", "st_mc": "state_mlstm_c", "st_mn": "state_mlstm_n", "st_mm": "state_mlstm_m"}
    for kk, src in names.items():
        a = inputs[src]
        m[kk] = np.ascontiguousarray(a[core]) if core < 4 else np.zeros(a.shape[1:], np.float32)
    return m


def _load_w(self, Wv, c0, n):
    wap, wres = self.wslots(1)
    w = wap[:, 0:8 * n].rearrange("p (kc n) -> p kc n", kc=8)
    self.k.dma("pool", self.wch, w, Wv[:, :, c0:c0 + n], writes=wres)
    return w, wres


def _proj_tok(self, Wv, c0, n, evac, tiles=None):
    k = self.k
    w, wres = self.load_w(Wv, c0, n)
    for i in (range(NT) if tiles is None else tiles):
        ps, rps = self.next_ps()
        for kc in range(8):
            k.op("pe", lambda e, kc=kc: e.matmul(ps[:, 0:n], lhsT=self.hT[:, kc, i * 128:(i + 1) * 128], rhs=w[:, kc, :],
                                                 start=(kc == 0), stop=(kc == 7)), reads=wres + [self.RhT[i]], writes=[rps])
        evac(i, ps[:, 0:n], rps)


def _proj_feat(self, Wv, c0, n, evac):
    k = self.k
    w, wres = self.load_w(Wv, c0, n)
    for c in range((n + 127) // 128):
        m = min(128, n - c * 128)
        for half in range(2):
            ps, rps = self.next_ps()
            for kc in range(8):
                k.op("pe", lambda e, kc=kc: e.matmul(ps[0:m, :], lhsT=w[:, kc, c * 128:c * 128 + m], rhs=self.hT[:, kc, half * 512:(half + 1) * 512],
                                                     start=(kc == 0), stop=(kc == 7)), reads=wres + self.RhT[half * 4:half * 4 + 4], writes=[rps])
            evac(c, half, ps[0:m, :], rps, m)


def _transpose_to(self, src_bf, rsrc, dst3, rdst, ncol=8):
    k = self.k
    pb, rpb = self.next_pb()
    for c in range(ncol):
        k.op("pe", lambda e, c=c: e.transpose(pb[:, c * 128:(c + 1) * 128], src_bf[:, c * 128:(c + 1) * 128], self.identb[:]),
             reads=[rsrc, self.Rid], writes=[rpb])
    k.op("act", lambda e: e.activation(out=dst3, in_=pb[:, 0:ncol * 128].rearrange("p (c t) -> p c t", c=ncol), func=AF.Identity),
         reads=[rpb], writes=[rdst])


def _out_proj(self, Wd, yT, RyT, gi=2):
    k = self.k
    wv = Wd.rearrange("(kc p) n -> p kc n", p=128)
    Fall = self.MA[:, 16384:32768].bitcast(F32).rearrange("p (i n) -> p i n", i=NT)
    RF = [Res() for _ in range(NT)]
    for nh in range(2):
        wap, wres = self.wslots(2)
        w = wap.rearrange("p (kc n) -> p kc n", kc=16)
        for q in range(2):
            k.dma("pool", self.wch, w[:, q * 8:(q + 1) * 8, :], wv[:, q * 8:(q + 1) * 8, nh * 512:(nh + 1) * 512], writes=wres)
        for i in range(NT):
            ps, rps = self.next_ps()
            for kc in range(16):
                k.op("pe", lambda e, kc=kc: e.matmul(ps[:, :], lhsT=yT[:, kc, i * 128:(i + 1) * 128], rhs=w[:, kc, :],
                                                     start=(kc == 0), stop=(kc == 15)), reads=wres + [RyT[i]], writes=[rps])
            k.op("act", lambda e: e.activation(out=Fall[:, i, nh * 512:(nh + 1) * 512], in_=ps[:, :], func=AF.Identity),
                 reads=[rps], writes=[RF[i]])
    for i in range(NT):
        self.resid_add(i, Fall[:, i, :], RF[i], gi)


Prog.load_w = _load_w
Prog.proj_tok = _proj_tok
Prog.proj_feat = _proj_feat
Prog.transpose_to = _transpose_to
Prog.out_proj = _out_proj


def _mix_consts(self):
    k = self.k
    if hasattr(self, "masks"):
        return
    self.masks = k.sb("masks_sb", [128, 9, 128])
    self.Rmask = Res("masks")
    k.dma("sp", "c2", self.masks[:], self.din["masks"][:, :, :], writes=[self.Rmask])
    self.onesc = k.sb("onesc", [128, 1])
    self.flags = k.sb("flags_sb", [128, 4])
    k.op("dve", lambda e: e.memset(self.onesc[:], 1.0), writes=[self.Rmask])
    k.dma("sp", "c3", self.flags[:], self.din["flags"][:, :], writes=[self.Rmask])


def _mixer_cd(self, l):
    k = self.k
    j = l // 2
    self.mix_consts()
    Wv = self.din["w_in_cd"][j].rearrange("(kc p) n -> p kc n", p=128)
    yT = self.MA[:, 0:16384].rearrange("p (c t) -> p c t", c=16)
    RyT = [Res() for _ in range(NT)]
    if "gla" in self.parts:
        self.gla(l, j, Wv, yT, RyT)
    else:
        k.op("dve", lambda e: e.memset(yT[:, 0:8, :], 0.0), writes=RyT)
    k.barrier()
    if "ml" in self.parts:
        self.mlstm(l, j, Wv, yT, RyT)
    else:
        k.op("dve", lambda e: e.memset(yT[:, 8:16, :], 0.0), writes=RyT)
    k.barrier()
    self.out_proj(self.din["w_out_cd"][j], yT, RyT)


def _gla(self, l, j, Wv, yT, RyT):
    k = self.k
    MA = self.MA
    qT = MA[:, 16384:18432].rearrange("p (h t) -> p h t", h=2)
    kT = MA[:, 18432:20480].rearrange("p (h t) -> p h t", h=2)
    ktok = MA[:, 20480:22528].rearrange("p (i c) -> p i c", i=NT)
    vtok = MA[:, 22528:26624].rearrange("p (i c) -> p i c", i=NT)
    Oacc = MA[:, 26624:30720].rearrange("p (i c) -> p i c", i=NT)
    sc = 128 ** -0.5
    gdT = [MA[0:17, 30720 + d * 2048:30720 + (d + 1) * 2048].bitcast(F32) for d in range(2)]
    gw = [MA[0:17, 34816 + d * 1024:34816 + (d + 1) * 1024].bitcast(F32) for d in range(2)]
    Rgd = [Res(), Res()]
    for d in range(2):
        k.op("dve", lambda e: e.memset(gdT[d], 1.0), writes=[Rgd[d]])
        self.proj_feat(Wv, 3072 + 16 * d, 16, lambda c, half, ps, rps, m: k.op(
            "act", lambda e: e.activation(out=gdT[d][0:16, half * 512:(half + 1) * 512], in_=ps, func=AF.Identity), reads=[rps], writes=[Rgd[d]]))
        k.dma("sp", "gw", gw[d][0:16, :], self.din["gla_gate_w"][j, d], writes=[Rgd[d]])
        k.dma("sp", "gw", gw[d][16:17, :], self.din["gla_gate_b"][j, d:d + 1, :], writes=[Rgd[d]])
    gnb = self.scr("nrmb", [128, 1024])
    Rgn = Res()
    k.dma("sp", "gnb", gnb[:], self.din["gla_norm"][j:j + 1, :].partition_broadcast(128), writes=[Rgn])
    S = [self.scr("S%d" % h, [128, 256]) for h in range(2)]
    Sbf = [self.scr("Sb%d" % h, [128, 256], BF16) for h in range(2)]
    la = self.tmpf[0][:, 0:512]; Rla = self.Rtmpf[0]
    te = self.tmpf[0][:, 512:1024]
    ebuf = self.tmpf[1]; Reb = self.Rtmpf[1]
    khat = self.scr("khat", [128, 256], BF16)
    qTt = self.scr("qTt", [128, 2, 128], BF16); kTt = self.scr("kTt", [128, 2, 128], BF16)
    Gt = self.scr("Gt", [128, 4])
    scT = [self.scr("scT%d" % i, [128, 128], BF16) for i in range(2)]
    Ofin = self.junk; ROf = self.Rjunk
    yb = self.hb[0]; Ryb = self.Rhb[0]
    seg_first = {0: [0, 2, 4, 6], 1: [7, 5, 3, 1]}
    seg_last = {0: [1, 3, 5, 7], 1: [6, 4, 2, 0]}
    for hh in range(2):
        RqT, RkT = Res(), Res()
        Rkt = [Res() for _ in range(NT)]
        Rvt = [Res() for _ in range(NT)]
        ROa = [Res() for _ in range(NT)]
        RS = [Res() for _ in range(2)]
        Rkh = Res(); Rqk = [Res() for _ in range(2)]; RG = [Res() for _ in range(2)]; Rsc = [Res(), Res()]
        self.proj_feat(Wv, hh * 256, 256, lambda c, half, ps, rps, m: k.op(
            "act", lambda e: e.activation(out=qT[:, c, half * 512:(half + 1) * 512], in_=ps, func=AF.Identity, scale=sc), reads=[rps], writes=[RqT]))
        self.proj_feat(Wv, 512 + hh * 256, 256, lambda c, half, ps, rps, m: k.op(
            "act", lambda e: e.activation(out=kT[:, c, half * 512:(half + 1) * 512], in_=ps, func=AF.Identity), reads=[rps], writes=[RkT]))
        self.proj_tok(Wv, 512 + hh * 256, 256, lambda i, ps, rps: k.op(
            "act", lambda e: e.activation(out=ktok[:, i, :], in_=ps, func=AF.Identity), reads=[rps], writes=[Rkt[i]]))
        self.proj_tok(Wv, 1024 + hh * 512, 512, lambda i, ps, rps: k.op(
            "act", lambda e: e.activation(out=vtok[:, i, :], in_=ps, func=AF.Identity), reads=[rps], writes=[Rvt[i]]))
        for d in range(2):
            order = list(range(NT)) if d == 0 else list(range(NT - 1, -1, -1))
            mcum = self.masks[:, d, :]
            mend = self.masks[:, 2 + d, :]
            endcol = 127 if d == 0 else 0
            if d == 1:
                ggw = self.load_w(Wv, 2048 + hh * 512, 512)
            for ci, i in enumerate(order):
                seg = i // 2
                if i in seg_first[d]:
                    for hl in range(2):
                        if ci == 0:
                            k.dma("sp", "gst%d" % hl, S[hl][:], self.din["st_gla"][j, d, 2 * hh + hl], writes=[RS[hl]])
                        else:
                            k.op("dve", lambda e: e.tensor_scalar(out=S[hl][:], in0=S[hl][:], scalar1=self.flags[:, 0:1], scalar2=None, op0=ALU.mult),
                                 reads=[RS[hl], self.Rmask], writes=[RS[hl]])
                        k.op("act", lambda e: e.activation(out=Sbf[hl][:], in_=S[hl][:], func=AF.Identity), reads=[RS[hl]], writes=[RS[hl]])
                ps, rps = self.next_ps()
                k.op("pe", lambda e: e.matmul(ps[:, 0:256], lhsT=gdT[d][:, i * 128:(i + 1) * 128], rhs=gw[d][:, hh * 256:(hh + 1) * 256], start=True, stop=True),
                     reads=[Rgd[d]], writes=[rps])
                k.op("act", lambda e: e.activation(out=la[:, 0:256], in_=ps[:, 0:256], func=AF.Exp, scale=-1.0), reads=[rps], writes=[Rla])
                k.op("act", lambda e: e.activation(out=la[:, 0:256], in_=la[:, 0:256], func=AF.Ln, bias=self.onesc[:, 0:1]), reads=[Rla, self.Rmask], writes=[Rla])
                k.op("dve", lambda e: e.tensor_scalar(out=la[:, 0:256], in0=la[:, 0:256], scalar1=-1.0 / 16.0, scalar2=None, op0=ALU.mult), reads=[Rla], writes=[Rla])
                ps, rps = self.next_ps()
                k.op("pe", lambda e: e.matmul(ps[:, 0:256], lhsT=mend, rhs=la[:, 0:256], start=True, stop=True), reads=[Rla, self.Rmask], writes=[rps])
                k.op("act", lambda e: e.activation(out=te[:, 0:256], in_=ps[:, 0:256], func=AF.Exp), reads=[rps], writes=[Rla])
                k.op("dve", lambda e: e.tensor_tensor(out=khat[:], in0=ktok[:, i, :], in1=te[:, 0:256], op=ALU.mult), reads=[Rla, Rkt[i]], writes=[Rkh])
                ps, rps = self.next_ps()
                for hl in range(2):
                    k.op("pe", lambda e: e.matmul(ps[:, hl * 128:(hl + 1) * 128], lhsT=la[:, hl * 128:(hl + 1) * 128], rhs=mcum, start=True, stop=True),
                         reads=[Rla, self.Rmask], writes=[rps])
                k.op("act", lambda e: e.activation(out=ebuf[:, 0:256], in_=ps[:, 0:256], func=AF.Exp), reads=[rps], writes=[Reb])
                k.op("act", lambda e: e.activation(out=ebuf[:, 256:512], in_=ps[:, 0:256], func=AF.Exp, scale=-1.0), reads=[rps], writes=[Reb])
                for hl in range(2):
                    k.op("dve", lambda e: e.tensor_tensor(out=qTt[:, hl, :], in0=qT[:, hl, i * 128:(i + 1) * 128], in1=ebuf[:, hl * 128:(hl + 1) * 128], op=ALU.mult),
                         reads=[Reb, RqT], writes=[Rqk[hl]])
                    k.op("dve", lambda e: e.tensor_tensor(out=kTt[:, hl, :], in0=kT[:, hl, i * 128:(i + 1) * 128], in1=ebuf[:, 256 + hl * 128:256 + (hl + 1) * 128], op=ALU.mult),
                         reads=[Reb, RkT], writes=[Rqk[hl]])
                    k.op("act", lambda e: e.activation(out=Gt[:, hl:hl + 1], in_=ebuf[:, hl * 128 + endcol:hl * 128 + endcol + 1], func=AF.Identity),
                         reads=[Reb], writes=[RG[hl]])
                def _hg(hl):
                    vs = vtok[:, i, hl * 256:(hl + 1) * 256]
                    ps, rps = self.next_ps()
                    k.op("pe", lambda e: e.matmul(ps[:, 0:128], lhsT=kTt[:, hl, :], rhs=qTt[:, hl, :], start=True, stop=True), reads=[Rqk[hl]], writes=[rps])
                    yield
                    s_, rs_ = scT[hl], Rsc[hl]
                    k.op("dve", lambda e: e.tensor_tensor(out=s_[:], in0=ps[:, 0:128], in1=mcum, op=ALU.mult), reads=[rps, self.Rmask], writes=[rs_])
                    yield
                    po, rpo = self.next_ps()
                    k.op("pe", lambda e: e.matmul(po[:, 0:256], lhsT=s_[:], rhs=vs, start=True, stop=False), reads=[rs_, Rvt[i]], writes=[rpo])
                    k.op("pe", lambda e: e.matmul(po[:, 0:256], lhsT=qTt[:, hl, :], rhs=Sbf[hl][:], start=False, stop=True), reads=[Rqk[hl], RS[hl]], writes=[rpo])
                    yield
                    if d == 0:
                        k.op("act", lambda e: e.activation(out=Oacc[:, i, hl * 256:(hl + 1) * 256], in_=po[:, 0:256], func=AF.Identity), reads=[rpo], writes=[ROa[i]])
                        yield
                    else:
                        k.op("dve", lambda e: e.tensor_tensor(out=Ofin[:, hl * 256:(hl + 1) * 256], in0=po[:, 0:256], in1=Oacc[:, i, hl * 256:(hl + 1) * 256], op=ALU.add),
                             reads=[rpo, ROa[i]], writes=[ROf])
                        yield
                    pd, rpd = self.next_ps()
                    k.op("pe", lambda e: e.matmul(pd[:, 0:256], lhsT=khat[:, hl * 128:(hl + 1) * 128], rhs=vs, start=True, stop=True), reads=[Rkh, Rvt[i]], writes=[rpd])
                    yield
                    k.op("dve", lambda e: e.scalar_tensor_tensor(out=S[hl][:], in0=S[hl][:], scalar=Gt[:, hl:hl + 1], in1=pd[:, 0:256], op0=ALU.mult, op1=ALU.add),
                         reads=[RS[hl], RG[hl], rpd], writes=[RS[hl]])
                    yield
                    k.op("act", lambda e: e.activation(out=Sbf[hl][:], in_=S[hl][:], func=AF.Identity), reads=[RS[hl]], writes=[RS[hl]])
                    yield
                    if i in seg_last[d]:
                        k.dma("sp", "gso%d" % hl, self.dout["o_gla"][j, seg, d, 2 * hh + hl], S[hl][:], reads=[RS[hl]])
                        yield
                    yield
                def _chain(hls):
                    for hl_ in hls:
                        yield from _hg(hl_)
                gens_ = [_chain(c_) for c_ in ([0], [1])]
                while gens_:
                    for g_ in list(gens_):
                        try:
                            next(g_)
                        except StopIteration:
                            gens_.remove(g_)
                if d == 1:
                    for hl in range(2):
                        gcol = (2 * hh + hl) * 256
                        rs = self.rstd_of(Ofin[:, hl * 256:(hl + 1) * 256], ROf, 8 + hl, 256, junk=te[:, 0:256], rjunk=Rla)
                        k.op("dve", lambda e: e.scalar_tensor_tensor(out=Ofin[:, hl * 256:(hl + 1) * 256], in0=Ofin[:, hl * 256:(hl + 1) * 256], scalar=rs,
                                                                     in1=gnb[:, gcol:gcol + 256], op0=ALU.mult, op1=ALU.mult),
                             reads=[ROf, self.Rss[8 + hl], Rgn], writes=[ROf])
                    w, wres = ggw
                    ps, rps = self.next_ps()
                    for kc in range(8):
                        k.op("pe", lambda e, kc=kc: e.matmul(ps[:, :], lhsT=self.hT[:, kc, i * 128:(i + 1) * 128], rhs=w[:, kc, :], start=(kc == 0), stop=(kc == 7)),
                             reads=wres + [self.RhT[i]], writes=[rps])
                    k.op("act", lambda e: e.activation(out=te, in_=ps[:, :], func=AF.Silu), reads=[rps], writes=[Rla])
                    k.op("dve", lambda e: e.tensor_tensor(out=yb[:, 0:512], in0=Ofin[:, 0:512], in1=te, op=ALU.mult), reads=[ROf, Rla], writes=[Ryb])
                    self.transpose_to(yb, Ryb, yT[:, hh * 4:(hh + 1) * 4, i * 128:(i + 1) * 128], RyT[i], ncol=4)
        k.barrier()


Prog.mix_consts = _mix_consts
Prog.mixer = lambda self, l: (self.mixer_cd(l) if l % 2 == 1 else self.mixer_ab(l))
Prog.mixer_cd = _mixer_cd
Prog.gla = _gla


def _conv_setup(self, base):
    k = self.k
    MA = self.MA
    cin = MA[:, base:base + 2308].bitcast(F32)
    mLR = MA[:, base + 2308:base + 2308 + 4096].bitcast(F32).rearrange("p (a t) -> p a t", a=2)
    R = {"cin": cin, "mLR": mLR, "Rcin": Res(), "Rm": Res()}
    k.op("dve", lambda e: e.memset(cin[:, 0:65], 0.0), writes=[R["Rcin"]])
    k.op("dve", lambda e: e.memset(cin[:, 1089:1154], 0.0), writes=[R["Rcin"]])
    for a in range(2):
        k.dma("sp", "cm%d" % a, mLR[:, a, :], self.din["convm"][a:a + 1, :].partition_broadcast(128), writes=[R["Rm"]])
    if not hasattr(self, "tapf"):
        self.tapf = self.scr("tapf", [128, 9])
        self.Rtapf = Res()
        k.dma("sp", "tapf", self.tapf[:], self.din["tapflag"][:, :], writes=[self.Rtapf])
    return R


def _conv_chunk(self, C, Wv, col0, cw_ap, cb_ap, out_ap, rout, oscale=1.0):
    k = self.k
    cin, mLR, Rcin, Rm = C["cin"], C["mLR"], C["Rcin"], C["Rm"]
    wc = self.scr("convw", [128, 10])
    Rwc = C.setdefault("Rwc", Res())
    k.dma("sp", "cw", wc[:, 0:9], cw_ap, writes=[Rwc])
    k.dma("sp", "cw", wc[:, 9:10], cb_ap, writes=[Rwc])
    k.op("dve", lambda e: e.tensor_tensor(out=wc[:, 0:9], in0=wc[:, 0:9], in1=self.tapf[:], op=ALU.mult), reads=[self.Rtapf, Rwc], writes=[Rwc])
    self.proj_feat(Wv, col0, 128, lambda c, half, ps, rps, m: k.op(
        "act", lambda e: e.activation(out=cin[:, 65 + half * 512:65 + (half + 1) * 512], in_=ps, func=AF.Identity), reads=[rps], writes=[Rcin]))
    accs = [(self.tmpf[0], self.Rtmpf[0]), (self.tmpf[1], self.Rtmpf[1]), (self.junk, self.Rjunk)]
    for dc, (acc, racc) in zip((-1, 0, 1), accs):
        eng = "dve"
        for n, dr in enumerate((0, -1, 1)):
            tap = (dr + 1) * 3 + (dc + 1)
            off = 65 + 64 * dr + dc
            if n == 0:
                k.op(eng, lambda e: e.tensor_scalar(out=acc[:], in0=cin[:, off:off + T], scalar1=wc[:, tap:tap + 1], scalar2=None, op0=ALU.mult),
                     reads=[Rcin, Rwc], writes=[racc])
            else:
                k.op(eng, lambda e: e.scalar_tensor_tensor(out=acc[:], in0=cin[:, off:off + T], scalar=wc[:, tap:tap + 1], in1=acc[:], op0=ALU.mult, op1=ALU.add),
                     reads=[Rcin, Rwc, racc], writes=[racc])
    (aL, rL), (aC, rC), (aR, rR) = accs
    k.op("pool", lambda e: e.tensor_tensor(out=aL[:], in0=aL[:], in1=mLR[:, 0, :], op=ALU.mult), reads=[rL, Rm], writes=[rL])
    k.op("dve", lambda e: e.tensor_tensor(out=aR[:], in0=aR[:], in1=mLR[:, 1, :], op=ALU.mult), reads=[rR, Rm], writes=[rR])
    k.op("dve", lambda e: e.tensor_tensor(out=aC[:], in0=aC[:], in1=aL[:], op=ALU.add), reads=[rC, rL], writes=[rC])
    k.op("dve", lambda e: e.tensor_tensor(out=aC[:], in0=aC[:], in1=aR[:], op=ALU.add), reads=[rC, rR], writes=[rC])
    k.op("act", lambda e: e.activation(out=aC[:], in_=aC[:], func=AF.Silu, bias=wc[:, 9:10]), reads=[rC, Rwc], writes=[rC])
    k.op("dve", lambda e: e.tensor_scalar(out=out_ap, in0=aC[:], scalar1=float(oscale), scalar2=None, op0=ALU.mult), reads=[rC], writes=[rout])


Prog.conv_setup = _conv_setup
Prog.conv_chunk = _conv_chunk


def _mlstm(self, l, j, Wv, yT, RyT):
    k = self.k
    MA = self.MA
    B0 = 3104
    qT = MA[:, 16384:18432].rearrange("p (h t) -> p h t", h=2)
    kT = MA[:, 18432:20480].rearrange("p (h t) -> p h t", h=2)
    ktok = MA[:, 20480:22528].rearrange("p (i c) -> p i c", i=NT)
    vtok = MA[:, 22528:26624].rearrange("p (i c) -> p i c", i=NT)
    Oacc = MA[:, 26624:30720].rearrange("p (i c) -> p i c", i=NT)
    sc = 128 ** -0.5
    gat = self.scr("mgat", [128, NT, 16])
    lf = self.scr("mlf", [128, NT, 8])
    gbias = self.scr("mgb", [128, 16])
    Rg = Res()
    k.dma("sp", "mgb", gbias[:, 0:8], self.din["mlstm_i_b"][j:j + 1].rearrange("o d h -> o (d h)").partition_broadcast(128), writes=[Rg])
    k.dma("sp", "mgb", gbias[:, 8:16], self.din["mlstm_f_b"][j:j + 1].rearrange("o d h -> o (d h)").partition_broadcast(128), writes=[Rg])
    self.proj_tok(Wv, B0 + 3072, 16, lambda i, ps, rps: k.op(
        "dve", lambda e: e.tensor_tensor(out=gat[:, i, :], in0=ps, in1=gbias[:], op=ALU.add), reads=[rps, Rg], writes=[Rg]))
    k.op("act", lambda e: e.activation(out=lf[:], in_=gat[:, :, 8:16], func=AF.Exp, scale=-1.0), reads=[Rg], writes=[Rg])
    k.op("act", lambda e: e.activation(out=lf[:], in_=lf[:], func=AF.Ln, bias=self.onesc[:, 0:1]), reads=[Rg, self.Rmask], writes=[Rg])
    k.op("dve", lambda e: e.tensor_scalar(out=lf[:], in0=lf[:], scalar1=-1.0, scalar2=None, op0=ALU.mult), reads=[Rg], writes=[Rg])
    gnb = self.scr("nrmb", [128, 1024])
    Rgn = Res()
    k.dma("sp", "gnb", gnb[:], self.din["mlstm_norm"][j:j + 1, :].partition_broadcast(128), writes=[Rgn])
    onesb = self.scr("onesb", [128, 1], BF16)
    ones128 = self.scr("ones128", [128, 128])
    k.op("dve", lambda e: e.memset(onesb[:], 1.0), writes=[Rgn])
    k.op("dve", lambda e: e.memset(ones128[:], 1.0), writes=[Rgn])
    S = [self.scr("mS%d" % h, [128, 260]) for h in range(2)]
    Sbf = [self.scr("mSb%d" % h, [128, 260], BF16) for h in range(2)]
    khat = self.scr("khat", [128, 256], BF16)
    scT = [self.scr("scT%d" % i, [128, 128], BF16) for i in range(2)]
    sm = self.scr("msm", [128, 32])
    mcur = self.scr("mcur", [4, 4])
    dg = self.scr("mdg", [4, 4])
    stage = self.tmpf[0]; Rstage = self.Rtmpf[0]
    Ofin = self.junk; ROf = self.Rjunk
    yb = self.hb[0]; Ryb = self.Rhb[0]
    seg_first = {0: [0, 2, 4, 6], 1: [7, 5, 3, 1]}
    seg_last = {0: [1, 3, 5, 7], 1: [6, 4, 2, 0]}
    for hh in range(2):
        RqT, RkT = Res(), Res()
        Rkt = [Res() for _ in range(NT)]
        Rvt = [Res() for _ in range(NT)]
        ROa = [Res() for _ in range(NT)]
        RS = [Res() for _ in range(2)]
        Rkh = Res(); Rsc = [Res(), Res()]; Rsm = Res(); Rm = Res()
        C = self.conv_setup(22528)
        for hl in range(2):
            h = 2 * hh + hl
            self.conv_chunk(C, Wv, B0 + h * 128, self.din["ml_cw"][j, h], self.din["ml_cb"][j, h], qT[:, hl, :], RqT, 1.0)
            self.conv_chunk(C, Wv, B0 + 512 + h * 128, self.din["ml_cw"][j, 4 + h], self.din["ml_cb"][j, 4 + h], kT[:, hl, :], RkT, sc)
        k.barrier()
        for i in range(NT):
            pb, rpb = self.next_pb()
            for hl in range(2):
                k.op("pe", lambda e: e.transpose(pb[:, hl * 128:(hl + 1) * 128], kT[:, hl, i * 128:(i + 1) * 128], self.identb[:]), reads=[RkT, self.Rid], writes=[rpb])
            k.op("act", lambda e: e.activation(out=ktok[:, i, :], in_=pb[:, 0:256], func=AF.Identity), reads=[rpb], writes=[Rkt[i]])
        self.proj_tok(Wv, B0 + 1024 + hh * 512, 512, lambda i, ps, rps: k.op(
            "act", lambda e: e.activation(out=vtok[:, i, :], in_=ps, func=AF.Identity), reads=[rps], writes=[Rvt[i]]))
        for d in range(2):
            order = list(range(NT)) if d == 0 else list(range(NT - 1, -1, -1))
            mcum = self.masks[:, d, :]
            if d == 1:
                ggw = self.load_w(Wv, B0 + 2048 + hh * 512, 512)
            for ci, i in enumerate(order):
                seg = i // 2
                if i in seg_first[d]:
                    if ci == 0:
                        k.dma("sp", "mm0", mcur[:, 0:1], self.din["st_mm"][j, d:d + 1, :].rearrange("o h -> h o"), writes=[Rm])
                        for hl in range(2):
                            h = 2 * hh + hl
                            k.dma("sp", "mst%d" % hl, S[hl][:, 0:256], self.din["st_mc"][j, d, h], writes=[RS[hl]])
                            k.dma("sp", "mst%d" % hl, S[hl][:, 256:257], self.din["st_mn"][j, d, h].rearrange("(p o) -> p o", o=1), writes=[RS[hl]])
                            k.dma("sp", "mem0", sm[:, 24:25], self.din["st_mm"][j, d:d + 1, h:h + 1].partition_broadcast(128), writes=[Rsm])
                            k.op("act", lambda e: e.activation(out=sm[:, 24:25], in_=sm[:, 24:25], func=AF.Exp), reads=[Rsm], writes=[Rsm])
                            k.op("dve", lambda e: e.tensor_scalar(out=S[hl][:, 0:257], in0=S[hl][:, 0:257], scalar1=sm[:, 24:25], scalar2=None, op0=ALU.mult),
                                 reads=[RS[hl], Rsm], writes=[RS[hl]])
                    else:
                        k.op("dve", lambda e: e.tensor_scalar(out=mcur[:, 0:1], in0=mcur[:, 0:1], scalar1=self.flags[0:4, 0:1], scalar2=None, op0=ALU.mult),
                             reads=[Rm, self.Rmask], writes=[Rm])
                        for hl in range(2):
                            k.op("dve", lambda e: e.tensor_scalar(out=S[hl][:, 0:257], in0=S[hl][:, 0:257], scalar1=self.flags[:, 0:1], scalar2=None, op0=ALU.mult),
                                 reads=[RS[hl], self.Rmask], writes=[RS[hl]])
                    for hl in range(2):
                        k.op("act", lambda e: e.activation(out=Sbf[hl][:, 0:257], in_=S[hl][:, 0:257], func=AF.Identity), reads=[RS[hl]], writes=[RS[hl]])
                lfd = lf[:, i, d * 4:(d + 1) * 4]
                igd = gat[:, i, d * 4:(d + 1) * 4]
                ps, rps = self.next_ps()
                k.op("pe", lambda e: e.matmul(ps[:, 0:4], lhsT=mcum, rhs=lfd, start=True, stop=True), reads=[Rg, self.Rmask], writes=[rps])
                k.op("pe", lambda e: e.matmul(ps[:, 4:8], lhsT=ones128[:], rhs=lfd, start=True, stop=True), reads=[Rg, Rgn], writes=[rps])
                k.op("pe", lambda e: e.matmul(ps[0:4, 8:9], lhsT=lfd, rhs=self.onesc[:, 0:1], start=True, stop=True), reads=[Rg, self.Rmask], writes=[rps])
                k.op("dve", lambda e: e.tensor_tensor(out=sm[:, 0:4], in0=igd, in1=ps[:, 0:4], op=ALU.subtract), reads=[Rg, rps], writes=[Rsm])
                k.op("act", lambda e: e.activation(out=sm[:, 4:12], in_=ps[:, 0:8], func=AF.Exp), reads=[rps], writes=[Rsm])
                k.op("act", lambda e: e.activation(out=mcur[:, 2:3], in_=ps[0:4, 8:9], func=AF.Identity), reads=[rps], writes=[Rm])
                pt, rpt = self.next_ps()
                k.op("pe", lambda e: e.transpose(pt[0:4, 0:128], sm[:, 0:4], self.identf[:]), reads=[Rsm, self.Rid], writes=[rpt])
                k.op("dve", lambda e: e.tensor_reduce(out=mcur[:, 1:2], in_=pt[0:4, 0:128], axis=AX.X, op=ALU.max), reads=[rpt], writes=[Rm])
                k.op("dve", lambda e: e.tensor_scalar(out=mcur[:, 0:1], in0=mcur[:, 0:1], scalar1=mcur[:, 1:2], scalar2=mcur[:, 2:3], op0=ALU.max, op1=ALU.add),
                     reads=[Rm], writes=[Rm])
                k.op("act", lambda e: e.activation(out=sm[:, 0:4], in_=sm[:, 0:4], func=AF.Exp), reads=[Rsm], writes=[Rsm])
                k.op("dve", lambda e: e.tensor_tensor(out=sm[:, 12:16], in0=sm[:, 0:4], in1=sm[:, 8:12], op=ALU.mult), reads=[Rsm], writes=[Rsm])
                for hl in range(2):
                    h = 2 * hh + hl
                    k.op("dve", lambda e: e.tensor_scalar(out=khat[:, hl * 128:(hl + 1) * 128], in0=ktok[:, i, hl * 128:(hl + 1) * 128],
                                                          scalar1=sm[:, 12 + h:13 + h], scalar2=None, op0=ALU.mult), reads=[Rkt[i], Rsm], writes=[Rkh])
                def _hg(hl):
                    h = 2 * hh + hl
                    vs = vtok[:, i, hl * 256:(hl + 1) * 256]
                    qt = qT[:, hl, i * 128:(i + 1) * 128]
                    ps, rps = self.next_ps()
                    k.op("pe", lambda e: e.matmul(ps[:, 0:128], lhsT=kT[:, hl, i * 128:(i + 1) * 128], rhs=qt, start=True, stop=True), reads=[RqT, RkT], writes=[rps])
                    yield
                    s_, rs_ = scT[hl], Rsc[hl]
                    k.op("dve", lambda e: e.scalar_tensor_tensor(out=s_[:], in0=ps[:, 0:128], scalar=sm[:, h:h + 1], in1=mcum, op0=ALU.mult, op1=ALU.mult),
                         reads=[rps, Rsm, self.Rmask], writes=[rs_])
                    yield
                    po, rpo = self.next_ps()
                    k.op("pe", lambda e: e.matmul(po[:, 0:256], lhsT=s_[:], rhs=vs, start=True, stop=False), reads=[rs_, Rvt[i]], writes=[rpo])
                    k.op("pe", lambda e: e.matmul(po[:, 0:256], lhsT=qt, rhs=Sbf[hl][:, 0:256], start=False, stop=True), reads=[RqT, RS[hl]], writes=[rpo])
                    k.op("pe", lambda e: e.matmul(po[:, 256:257], lhsT=s_[:], rhs=onesb[:], start=True, stop=False), reads=[rs_, Rgn], writes=[rpo])
                    k.op("pe", lambda e: e.matmul(po[:, 256:257], lhsT=qt, rhs=Sbf[hl][:, 256:257], start=False, stop=True), reads=[RqT, RS[hl]], writes=[rpo])
                    yield
                    dn = sm[:, 16 + hl:17 + hl]
                    k.op("act", lambda e: e.activation(out=dn, in_=po[:, 256:257], func=AF.Abs, scale=sm[:, 4 + h:5 + h]), reads=[rpo, Rsm], writes=[Rsm])
                    yield
                    k.op("dve", lambda e: e.tensor_scalar(out=dn, in0=dn, scalar1=1.0, scalar2=None, op0=ALU.max), reads=[Rsm], writes=[Rsm])
                    yield
                    k.op("dve", lambda e: e.reciprocal(out=dn, in_=dn), reads=[Rsm], writes=[Rsm])
                    yield
                    k.op("dve", lambda e: e.tensor_tensor(out=dn, in0=dn, in1=sm[:, 4 + h:5 + h], op=ALU.mult), reads=[Rsm], writes=[Rsm])
                    yield
                    if d == 0:
                        k.op("act", lambda e: e.activation(out=Oacc[:, i, hl * 256:(hl + 1) * 256], in_=po[:, 0:256], func=AF.Identity, scale=dn), reads=[rpo, Rsm], writes=[ROa[i]])
                        yield
                    else:
                        k.op("dve", lambda e: e.scalar_tensor_tensor(out=Ofin[:, hl * 256:(hl + 1) * 256], in0=po[:, 0:256], scalar=dn, in1=Oacc[:, i, hl * 256:(hl + 1) * 256],
                                                                     op0=ALU.mult, op1=ALU.add), reads=[rpo, Rsm, ROa[i]], writes=[ROf])
                        yield
                    pd, rpd = self.next_ps()
                    k.op("pe", lambda e: e.matmul(pd[:, 0:256], lhsT=khat[:, hl * 128:(hl + 1) * 128], rhs=vs, start=True, stop=True), reads=[Rkh, Rvt[i]], writes=[rpd])
                    k.op("pe", lambda e: e.matmul(pd[:, 256:257], lhsT=khat[:, hl * 128:(hl + 1) * 128], rhs=onesb[:], start=True, stop=True), reads=[Rkh, Rgn], writes=[rpd])
                    yield
                    k.op("dve", lambda e: e.scalar_tensor_tensor(out=S[hl][:, 0:257], in0=S[hl][:, 0:257], scalar=sm[:, 8 + h:9 + h], in1=pd[:, 0:257], op0=ALU.mult, op1=ALU.add),
                         reads=[RS[hl], Rsm, rpd], writes=[RS[hl]])
                    yield
                    k.op("act", lambda e: e.activation(out=Sbf[hl][:, 0:257], in_=S[hl][:, 0:257], func=AF.Identity), reads=[RS[hl]], writes=[RS[hl]])
                    yield
                    yield
                def _chain(hls):
                    for hl_ in hls:
                        yield from _hg(hl_)
                gens_ = [_chain(c_) for c_ in ([0], [1])]
                while gens_:
                    for g_ in list(gens_):
                        try:
                            next(g_)
                        except StopIteration:
                            gens_.remove(g_)
                if i in seg_last[d]:
                    k.op("dve", lambda e: e.tensor_scalar(out=dg[:], in0=self.identf[0:4, 0:4], scalar1=mcur[:, 0:1], scalar2=None, op0=ALU.mult), reads=[Rm, self.Rid], writes=[Rm])
                    pm, rpm = self.next_ps()
                    k.op("pe", lambda e: e.matmul(pm[:, 0:4], lhsT=ones128[0:4, :], rhs=dg[:], start=True, stop=True), reads=[Rm, Rgn], writes=[rpm])
                    k.op("act", lambda e: e.activation(out=sm[:, 20:24], in_=pm[:, 0:4], func=AF.Exp, scale=-1.0), reads=[rpm], writes=[Rsm])
                    if hh == 0:
                        k.dma("sp", "mmo", self.dout["o_mm"][j, seg, d:d + 1, :].rearrange("o h -> h o"), mcur[:, 0:1], reads=[Rm])
                    for hl in range(2):
                        h = 2 * hh + hl
                        k.op("dve", lambda e: e.tensor_scalar(out=stage[:, hl * 260:hl * 260 + 257], in0=S[hl][:, 0:257], scalar1=sm[:, 20 + h:21 + h], scalar2=None, op0=ALU.mult),
                             reads=[RS[hl], Rsm], writes=[Rstage])
                        k.dma("sp", "mco%d" % hl, self.dout["o_mc"][j, seg, d, h], stage[:, hl * 260:hl * 260 + 256], reads=[Rstage])
                        k.dma("sp", "mno%d" % hl, self.dout["o_mn"][j, seg, d, h].rearrange("(p o) -> p o", o=1), stage[:, hl * 260 + 256:hl * 260 + 257], reads=[Rstage])
                if d == 1:
                    for hl in range(2):
                        gcol = (2 * hh + hl) * 256
                        rs = self.rstd_of(Ofin[:, hl * 256:(hl + 1) * 256], ROf, 8 + hl, 256, junk=self.tmpf[1][:, 0:256], rjunk=self.Rtmpf[1])
                        k.op("dve", lambda e: e.scalar_tensor_tensor(out=Ofin[:, hl * 256:(hl + 1) * 256], in0=Ofin[:, hl * 256:(hl + 1) * 256], scalar=rs,
                                                                     in1=gnb[:, gcol:gcol + 256], op0=ALU.mult, op1=ALU.mult),
                             reads=[ROf, self.Rss[8 + hl], Rgn], writes=[ROf])
                    w, wres = ggw
                    ps, rps = self.next_ps()
                    for kc in range(8):
                        k.op("pe", lambda e, kc=kc: e.matmul(ps[:, :], lhsT=self.hT[:, kc, i * 128:(i + 1) * 128], rhs=w[:, kc, :], start=(kc == 0), stop=(kc == 7)),
                             reads=wres + [self.RhT[i]], writes=[rps])
                    te = self.tmpf[1][:, 512:1024]
                    k.op("act", lambda e: e.activation(out=te, in_=ps[:, :], func=AF.Sigmoid), reads=[rps], writes=[self.Rtmpf[1]])
                    k.op("dve", lambda e: e.tensor_tensor(out=yb[:, 0:512], in0=Ofin[:, 0:512], in1=te, op=ALU.mult), reads=[ROf, self.Rtmpf[1]], writes=[Ryb])
                    self.transpose_to(yb, Ryb, yT[:, 8 + hh * 4:8 + (hh + 1) * 4, i * 128:(i + 1) * 128], RyT[i], ncol=4)
        k.barrier()


Prog.mlstm = _mlstm


def _mixer_ab(self, l):
    k = self.k
    j = l // 2
    self.mix_consts()
    Wv = self.din["w_in_ab"][j].rearrange("(kc p) n -> p kc n", p=128)
    yT = self.MA[:, 0:16384].rearrange("p (c t) -> p c t", c=16)
    RyT = [Res() for _ in range(NT)]
    self.rs_ssd = self.scr("rs_ssd", [128, NT])
    self.Rrs = Res()
    if "ssd" in self.parts:
        self.ssd(l, j, Wv, yT, RyT)
    else:
        k.op("dve", lambda e: e.memset(yT[:, 0:8, :], 0.0), writes=RyT)
        k.op("dve", lambda e: e.memset(self.rs_ssd[:], 1.0), writes=[self.Rrs])
    k.barrier()
    if "rw" in self.parts:
        self.rwkv(l, j, Wv, yT, RyT)
    else:
        k.op("dve", lambda e: e.memset(yT[:, 8:16, :], 0.0), writes=RyT)
    k.barrier()
    self.out_proj(self.din["w_out_ab"][j], yT, RyT, row_scale=(self.rs_ssd, self.Rrs))


def _out_proj(self, Wd, yT, RyT, gi=2, row_scale=None):
    k = self.k
    wv = Wd.rearrange("(kc p) n -> p kc n", p=128)
    Fall = self.MA[:, 16384:32768].bitcast(F32).rearrange("p (i n) -> p i n", i=NT)
    RF = [Res() for _ in range(NT)]
    for nh in range(2):
        wap, wres = self.wslots(2)
        w = wap.rearrange("p (kc n) -> p kc n", kc=16)
        for q in range(2):
            k.dma("pool", self.wch, w[:, q * 8:(q + 1) * 8, :], wv[:, q * 8:(q + 1) * 8, nh * 512:(nh + 1) * 512], writes=wres)
        for i in range(NT):
            Fi = Fall[:, i, nh * 512:(nh + 1) * 512]
            if row_scale is None:
                ps, rps = self.next_ps()
                for kc in range(16):
                    k.op("pe", lambda e, kc=kc: e.matmul(ps[:, :], lhsT=yT[:, kc, i * 128:(i + 1) * 128], rhs=w[:, kc, :],
                                                         start=(kc == 0), stop=(kc == 15)), reads=wres + [RyT[i]], writes=[rps])
                k.op("act", lambda e: e.activation(out=Fi, in_=ps[:, :], func=AF.Identity), reads=[rps], writes=[RF[i]])
            else:
                rsc, rrs = row_scale
                ps2, rps2 = self.next_ps()
                for kc in range(8, 16):
                    k.op("pe", lambda e, kc=kc: e.matmul(ps2[:, :], lhsT=yT[:, kc, i * 128:(i + 1) * 128], rhs=w[:, kc, :],
                                                         start=(kc == 8), stop=(kc == 15)), reads=wres + [RyT[i]], writes=[rps2])
                k.op("act", lambda e: e.activation(out=Fi, in_=ps2[:, :], func=AF.Identity), reads=[rps2], writes=[RF[i]])
                ps1, rps1 = self.next_ps()
                for kc in range(8):
                    k.op("pe", lambda e, kc=kc: e.matmul(ps1[:, :], lhsT=yT[:, kc, i * 128:(i + 1) * 128], rhs=w[:, kc, :],
                                                         start=(kc == 0), stop=(kc == 7)), reads=wres + [RyT[i]], writes=[rps1])
                k.op("dve", lambda e: e.scalar_tensor_tensor(out=Fi, in0=ps1[:, :], scalar=rsc[:, i:i + 1], in1=Fi, op0=ALU.mult, op1=ALU.add),
                     reads=[rps1, rrs, RF[i]], writes=[RF[i]])
    for i in range(NT):
        self.resid_add(i, Fall[:, i, :], RF[i], gi)


def _ssd(self, l, j, Wv, yT, RyT):
    k = self.k
    MA = self.MA
    xT = MA[:, 16384:18432].rearrange("p (c t) -> p c t", c=2)
    BT = MA[:, 18432:19456]
    CT = MA[:, 19456:20480]
    xs = MA[:, 20480:22528].rearrange("p (i c) -> p i c", i=NT)
    Btok = MA[:, 22528:23552].rearrange("p (i c) -> p i c", i=NT)
    Oacc = MA[:, 23552:25600].rearrange("p (i c) -> p i c", i=NT)
    dt = self.scr("sdt", [128, NT, 32])
    la = self.scr("sla", [128, NT, 32])
    cst = self.scr("scst", [128, 96])
    Rdt = Res()
    k.dma("sp", "sc0", cst[:, 0:32], self.din["ssd_dt_bias"][j:j + 1].rearrange("o d h -> o (d h)").partition_broadcast(128), writes=[Rdt])
    k.dma("sp", "sc0", cst[:, 32:64], self.din["ssd_a_log"][j:j + 1].rearrange("o d h -> o (d h)").partition_broadcast(128), writes=[Rdt])
    k.dma("sp", "sc0", cst[:, 64:80], self.din["ssd_d"][j:j + 1, :].partition_broadcast(128), writes=[Rdt])
    k.op("act", lambda e: e.activation(out=cst[:, 32:64], in_=cst[:, 32:64], func=AF.Exp), reads=[Rdt], writes=[Rdt])
    self.proj_tok(Wv, 3072, 32, lambda i, ps, rps: k.op(
        "dve", lambda e: e.tensor_tensor(out=dt[:, i, :], in0=ps, in1=cst[:, 0:32], op=ALU.add), reads=[rps, Rdt], writes=[Rdt]))
    k.op("act", lambda e: e.activation(out=dt[:], in_=dt[:], func=AF.Exp), reads=[Rdt], writes=[Rdt])
    k.op("act", lambda e: e.activation(out=dt[:], in_=dt[:], func=AF.Ln, bias=self.onesc[:, 0:1]), reads=[Rdt, self.Rmask], writes=[Rdt])
    k.op("dve", lambda e: e.scalar_tensor_tensor(out=la[:], in0=dt[:], scalar=-1.0, in1=cst[:, 32:64].unsqueeze(1).to_broadcast([128, NT, 32]),
                                                 op0=ALU.mult, op1=ALU.mult), reads=[Rdt], writes=[Rdt])
    nw = self.scr("snw", [128, 8])
    k.dma("sp", "sc1", nw[:], self.din["ssd_nw"][j], writes=[Rdt])
    ones128 = self.scr("ones128", [128, 128])
    k.op("dve", lambda e: e.memset(ones128[:], 1.0), writes=[Rdt])
    ssq = self.scr("sssq", [128, NT, 4])
    k.op("dve", lambda e: e.memset(ssq[:], 0.0), writes=[self.Rrs])
    S = self.scr("S0", [128, 256]); Sbf = self.scr("Sb0", [128, 256], BF16)
    st = self.scr("sst", [128, 80])
    CBm = self.scr("sCBm", [128, 128])
    seg_ = [self.scr("sseg%d" % i, [128, 128]) for i in range(2)]
    scT = [self.scr("scT%d" % i, [128, 128], BF16) for i in range(2)]
    khat = self.scr("khat", [128, 256], BF16)
    Ofin = self.junk; ROf = self.Rjunk
    tin = self.tmpf[0]; Rtin = self.Rtmpf[0]
    zs = self.tmpf[1]; Rzs = self.Rtmpf[1]
    yb = self.hb[0]; Ryb = self.Rhb[0]
    seg_first = {0: [0, 2, 4, 6], 1: [7, 5, 3, 1]}
    seg_last = {0: [1, 3, 5, 7], 1: [6, 4, 2, 0]}
    for g in range(4):
        RxT, RBT, RCT = Res(), Res(), Res()
        Rxs = [Res() for _ in range(NT)]
        RBt = [Res() for _ in range(NT)]
        ROa = [Res() for _ in range(NT)]
        RS = Res(); Rst = Res(); RCB = Res(); Rseg = [Res(), Res()]; Rsc = [Res(), Res()]; Rkh = Res()
        C = self.conv_setup(25600)
        cw, cb = self.din["ssd_cw"], self.din["ssd_cb"]
        for c in range(2):
            ch = 2 * g + c
            self.conv_chunk(C, Wv, 1024 + ch * 128, cw[j, ch], cb[j, ch], xT[:, c, :], RxT)
        self.conv_chunk(C, Wv, 1024 + (8 + g) * 128, cw[j, 8 + g], cb[j, 8 + g], BT, RBT)
        self.conv_chunk(C, Wv, 1024 + (12 + g) * 128, cw[j, 12 + g], cb[j, 12 + g], CT, RCT)
        k.barrier()
        for i in range(NT):
            pb, rpb = self.next_pb()
            for c in range(2):
                k.op("pe", lambda e: e.transpose(pb[:, c * 128:(c + 1) * 128], xT[:, c, i * 128:(i + 1) * 128], self.identb[:]), reads=[RxT, self.Rid], writes=[rpb])
            k.op("pe", lambda e: e.transpose(pb[:, 256:384], BT[:, i * 128:(i + 1) * 128], self.identb[:]), reads=[RBT, self.Rid], writes=[rpb])
            k.op("act", lambda e: e.activation(out=xs[:, i, :], in_=pb[:, 0:256], func=AF.Identity), reads=[rpb], writes=[Rxs[i]])
            k.op("act", lambda e: e.activation(out=Btok[:, i, :], in_=pb[:, 256:384], func=AF.Identity), reads=[rpb], writes=[RBt[i]])
        for d in range(2):
            order = list(range(NT)) if d == 0 else list(range(NT - 1, -1, -1))
            mcum = self.masks[:, d, :]
            if d == 1:
                zw = self.load_w(Wv, g * 256, 256)
            for ci, i in enumerate(order):
                sgi = i // 2
                if i in seg_first[d]:
                    if ci == 0:
                        for hl in range(4):
                            k.dma("sp", "sst%d" % hl, S[:, hl * 64:(hl + 1) * 64], self.din["st_ssd"][j, d, 4 * g + hl], writes=[RS])
                    else:
                        k.op("dve", lambda e: e.tensor_scalar(out=S[:], in0=S[:], scalar1=self.flags[:, 0:1], scalar2=None, op0=ALU.mult), reads=[RS, self.Rmask], writes=[RS])
                    k.op("act", lambda e: e.activation(out=Sbf[:], in_=S[:], func=AF.Identity), reads=[RS], writes=[RS])
                lad = la[:, i, d * 16:(d + 1) * 16]
                dtd = dt[:, i, d * 16:(d + 1) * 16]
                ps, rps = self.next_ps()
                k.op("pe", lambda e: e.matmul(ps[:, 0:16], lhsT=mcum, rhs=lad, start=True, stop=True), reads=[Rdt, self.Rmask], writes=[rps])
                k.op("pe", lambda e: e.matmul(ps[:, 16:32], lhsT=ones128[:], rhs=lad, start=True, stop=True), reads=[Rdt], writes=[rps])
                k.op("act", lambda e: e.activation(out=st[:, 0:16], in_=ps[:, 0:16], func=AF.Identity), reads=[rps], writes=[Rst])
                k.op("dve", lambda e: e.tensor_tensor(out=st[:, 16:32], in0=ps[:, 16:32], in1=st[:, 0:16], op=ALU.subtract), reads=[rps, Rst], writes=[Rst])
                k.op("act", lambda e: e.activation(out=st[:, 16:32], in_=st[:, 16:32], func=AF.Exp), reads=[Rst], writes=[Rst])
                k.op("dve", lambda e: e.tensor_tensor(out=st[:, 16:32], in0=st[:, 16:32], in1=dtd, op=ALU.mult), reads=[Rst, Rdt], writes=[Rst])
                k.op("act", lambda e: e.activation(out=st[:, 32:48], in_=ps[:, 16:32], func=AF.Exp), reads=[rps], writes=[Rst])
                k.op("act", lambda e: e.activation(out=st[:, 48:64], in_=st[:, 0:16], func=AF.Exp), reads=[Rst], writes=[Rst])
                ps, rps = self.next_ps()
                k.op("pe", lambda e: e.matmul(ps[:, 0:128], lhsT=BT[:, i * 128:(i + 1) * 128], rhs=CT[:, i * 128:(i + 1) * 128], start=True, stop=True), reads=[RBT, RCT], writes=[rps])
                k.op("dve", lambda e: e.tensor_tensor(out=CBm[:], in0=ps[:, 0:128], in1=mcum, op=ALU.mult), reads=[rps, self.Rmask], writes=[RCB])
                po, rpo = self.next_ps(pin=True)
                pd, rpd = self.next_ps(pin=True)
                def _hg(hl):
                    h = 4 * g + hl
                    pbt, rpbt = self.next_ps()
                    k.op("pe", lambda e: e.matmul(pbt[:, 0:128], lhsT=la[:, i, d * 16 + h:d * 16 + h + 1].to_broadcast([128, 128]), rhs=mcum, start=True, stop=True),
                         reads=[Rdt, self.Rmask], writes=[rpbt])
                    yield
                    sg, rsg = seg_[hl % 2], Rseg[hl % 2]
                    k.op("dve", lambda e: e.tensor_scalar(out=sg[:], in0=pbt[:, 0:128], scalar1=st[:, h:h + 1], scalar2=0.0, op0=ALU.subtract, op1=ALU.min),
                         reads=[rpbt, Rst], writes=[rsg])
                    yield
                    k.op("act", lambda e: e.activation(out=sg[:], in_=sg[:], func=AF.Exp), reads=[rsg], writes=[rsg])
                    yield
                    s_, rs_ = scT[hl % 2], Rsc[hl % 2]
                    k.op("dve", lambda e: e.scalar_tensor_tensor(out=s_[:], in0=sg[:], scalar=dt[:, i, d * 16 + h:d * 16 + h + 1], in1=CBm[:], op0=ALU.mult, op1=ALU.mult),
                         reads=[rsg, Rdt, RCB], writes=[rs_])
                    yield
                    k.op("pe", lambda e: e.matmul(po[:, hl * 64:(hl + 1) * 64], lhsT=s_[:], rhs=xs[:, i, hl * 64:(hl + 1) * 64], start=True, stop=True), reads=[rs_, Rxs[i]], writes=[rpo])
                    yield
                    k.op("dve", lambda e: e.tensor_scalar(out=khat[:, (hl % 2) * 128:(hl % 2 + 1) * 128], in0=Btok[:, i, :], scalar1=st[:, 16 + h:17 + h], scalar2=None, op0=ALU.mult),
                         reads=[RBt[i], Rst], writes=[Rkh])
                    yield
                    k.op("pe", lambda e: e.matmul(pd[:, hl * 64:(hl + 1) * 64], lhsT=khat[:, (hl % 2) * 128:(hl % 2 + 1) * 128], rhs=xs[:, i, hl * 64:(hl + 1) * 64], start=True, stop=True),
                         reads=[Rkh, Rxs[i]], writes=[rpd])
                    yield
                    yield
                def _chain(hls):
                    for hl_ in hls:
                        yield from _hg(hl_)
                gens_ = [_chain(c_) for c_ in ([0, 2], [1, 3])]
                while gens_:
                    for g_ in list(gens_):
                        try:
                            next(g_)
                        except StopIteration:
                            gens_.remove(g_)
                pi_, rpi = self.next_ps()
                k.op("pe", lambda e: e.matmul(pi_[:, 0:256], lhsT=CT[:, i * 128:(i + 1) * 128], rhs=Sbf[:], start=True, stop=True), reads=[RCT, RS], writes=[rpi])
                ebx = st[:, 48 + 4 * g:52 + 4 * g].unsqueeze(2).to_broadcast([128, 4, 64])
                k.op("dve", lambda e: e.tensor_tensor(out=tin[:, 0:256].rearrange("p (h v) -> p h v", h=4), in0=pi_[:, 0:256].rearrange("p (h v) -> p h v", h=4), in1=ebx, op=ALU.mult),
                     reads=[rpi, Rst], writes=[Rtin])
                if d == 0:
                    k.op("dve", lambda e: e.tensor_tensor(out=Oacc[:, i, :], in0=po[:, 0:256], in1=tin[:, 0:256], op=ALU.add), reads=[rpo, Rtin], writes=[ROa[i]])
                else:
                    k.op("dve", lambda e: e.tensor_tensor(out=Ofin[:, 0:256], in0=po[:, 0:256], in1=tin[:, 0:256], op=ALU.add), reads=[rpo, Rtin], writes=[ROf])
                    k.op("dve", lambda e: e.tensor_tensor(out=Ofin[:, 0:256], in0=Ofin[:, 0:256], in1=Oacc[:, i, :], op=ALU.add), reads=[ROf, ROa[i]], writes=[ROf])
                Gx = st[:, 32 + 4 * g:36 + 4 * g].unsqueeze(2).to_broadcast([128, 4, 64])
                k.op("dve", lambda e: e.tensor_tensor(out=S[:].rearrange("p (h v) -> p h v", h=4), in0=S[:].rearrange("p (h v) -> p h v", h=4), in1=Gx, op=ALU.mult), reads=[RS, Rst], writes=[RS])
                k.op("dve", lambda e: e.tensor_tensor(out=S[:], in0=S[:], in1=pd[:, 0:256], op=ALU.add), reads=[RS, rpd], writes=[RS])
                k.op("act", lambda e: e.activation(out=Sbf[:], in_=S[:], func=AF.Identity), reads=[RS], writes=[RS])
                self.unpin(rpo, rpd)
                if i in seg_last[d]:
                    for hl in range(4):
                        k.dma("sp", "sso%d" % hl, self.dout["o_ssd"][j, sgi, d, 4 * g + hl], S[:, hl * 64:(hl + 1) * 64], reads=[RS])
                if d == 1:
                    Dx = cst[:, 64 + 4 * g:68 + 4 * g].unsqueeze(2).to_broadcast([128, 4, 64])
                    k.op("dve", lambda e: e.tensor_tensor(out=tin[:, 256:512].rearrange("p (h v) -> p h v", h=4), in0=xs[:, i, :].rearrange("p (h v) -> p h v", h=4), in1=Dx, op=ALU.mult),
                         reads=[Rxs[i], Rdt], writes=[Rtin])
                    k.op("dve", lambda e: e.tensor_tensor(out=Ofin[:, 0:256], in0=Ofin[:, 0:256], in1=tin[:, 256:512], op=ALU.add), reads=[ROf, Rtin], writes=[ROf])
                    w, wres = zw
                    ps, rps = self.next_ps()
                    for kc in range(8):
                        k.op("pe", lambda e, kc=kc: e.matmul(ps[:, 0:256], lhsT=self.hT[:, kc, i * 128:(i + 1) * 128], rhs=w[:, kc, :], start=(kc == 0), stop=(kc == 7)),
                             reads=wres + [self.RhT[i]], writes=[rps])
                    k.op("act", lambda e: e.activation(out=zs[:, 0:256], in_=ps[:, 0:256], func=AF.Silu), reads=[rps], writes=[Rzs])
                    k.op("dve", lambda e: e.tensor_tensor(out=Ofin[:, 0:256], in0=Ofin[:, 0:256], in1=zs[:, 0:256], op=ALU.mult), reads=[ROf, Rzs], writes=[ROf])
                    k.op("act", lambda e: e.activation(out=zs[:, 256:512], in_=Ofin[:, 0:256], func=AF.Square, accum_out=ssq[:, i, g:g + 1]), reads=[ROf, self.Rrs], writes=[Rzs, self.Rrs])
                    k.op("act", lambda e: e.activation(out=yb[:, 0:256], in_=Ofin[:, 0:256], func=AF.Identity), reads=[ROf], writes=[Ryb])
                    pb, rpb = self.next_pb()
                    for c in range(2):
                        k.op("pe", lambda e: e.transpose(pb[:, c * 128:(c + 1) * 128], yb[:, c * 128:(c + 1) * 128], self.identb[:]), reads=[Ryb, self.Rid], writes=[rpb])
                    for c in range(2):
                        kc = 2 * g + c
                        k.op("act", lambda e: e.activation(out=yT[:, kc, i * 128:(i + 1) * 128], in_=pb[:, c * 128:(c + 1) * 128], func=AF.Identity, scale=nw[:, kc:kc + 1]),
                             reads=[rpb, Rdt], writes=[RyT[i]])
        k.barrier()
    rs = self.rs_ssd
    k.op("dve", lambda e: e.tensor_reduce(out=rs[:], in_=ssq[:], axis=AX.X, op=ALU.add), reads=[self.Rrs], writes=[self.Rrs])
    k.op("act", lambda e: e.activation(out=rs[:], in_=rs[:], func=AF.Ln, scale=1.0 / 1024, bias=self.epsb[:, 0:1]), reads=[self.Rrs, self.Rid], writes=[self.Rrs])
    k.op("act", lambda e: e.activation(out=rs[:], in_=rs[:], func=AF.Exp, scale=-0.5), reads=[self.Rrs], writes=[self.Rrs])


Prog.mixer_ab = _mixer_ab
Prog.out_proj = _out_proj
Prog.ssd = _ssd


def _rwkv(self, l, j, Wv, yT, RyT):
    k = self.k
    MA = self.MA
    B0 = 3104
    f32v = lambda a, b: MA[:, a:b].bitcast(F32)
    twT = MA[:, 16384:17408]; adT = MA[:, 17408:18432]; sgT = MA[:, 18432:19456]
    rT = f32v(19456, 21504); kT = f32v(21504, 23552); vT = f32v(23552, 25600); kkT = f32v(25600, 27648)
    Vtok = f32v(27648, 29696).rearrange("p (i c) -> p i c", i=NT)
    Yacc = MA[:, 29696:30720].rearrange("p (i c) -> p i c", i=NT)
    bonT = MA[:, 30720:31744]
    raw = f32v(31744, 33796)
    tsm = MA[:, 33796:35844].rearrange("p (a t) -> p a t", a=2)
    t1, Rt1 = self.tmpf[0], self.Rtmpf[0]
    t2, Rt2 = self.tmpf[1], self.Rtmpf[1]
    Rsh = Res(); Rraw = Res(); Rvec = Res()
    vec = self.scr("rwvec", [128, 72])
    mu = self.scr("rwmu", [128, 56])
    cst = self.scr("rwc", [128, 4])
    k.dma("sp", "rv0", vec[:], self.din["rw_vec"][j], writes=[Rvec])
    k.dma("sp", "rv0", mu[:, 28:55], self.din["rw_mu"][j], writes=[Rvec])
    k.op("dve", lambda e: e.tensor_scalar(out=mu[:, 0:27], in0=mu[:, 28:55], scalar1=-1.0, scalar2=1.0, op0=ALU.mult, op1=ALU.add), reads=[Rvec], writes=[Rvec])
    k.op("dve", lambda e: e.tensor_scalar(out=mu[:, 28:55], in0=mu[:, 28:55], scalar1=0.5, scalar2=None, op0=ALU.mult), reads=[Rvec], writes=[Rvec])
    k.op("dve", lambda e: e.memset(cst[:, 0:1], 1e-12), writes=[Rvec])
    k.op("dve", lambda e: e.memset(cst[:, 1:2], -0.5), writes=[Rvec])
    k.op("dve", lambda e: e.memset(cst[:, 2:3], 64e-5), writes=[Rvec])
    nvec = self.scr("rwnvec", [128, 24])
    k.op("dve", lambda e: e.tensor_scalar(out=nvec[:, 0:16], in0=vec[:, 0:16], scalar1=-1.0, scalar2=None, op0=ALU.mult), reads=[Rvec], writes=[Rvec])
    k.op("dve", lambda e: e.tensor_scalar(out=nvec[:, 16:24], in0=vec[:, 40:48], scalar1=-1.0, scalar2=1.0, op0=ALU.mult, op1=ALU.add), reads=[Rvec], writes=[Rvec])
    k.op("dve", lambda e: e.memset(raw[:, 0:1], 0.0), writes=[Rraw])
    k.op("dve", lambda e: e.memset(raw[:, 1025:1026], 0.0), writes=[Rraw])
    for a in range(2):
        k.dma("pool", "tsm%d" % a, tsm[:, a, :], self.din["tsm"][a:a + 1, :].partition_broadcast(128), writes=[Rsh])
    ones128 = self.scr("ones128", [128, 128])
    k.op("dve", lambda e: e.memset(ones128[:], 1.0), writes=[Rvec])
    BD = self.masks[:, 4, :]

    def rw_block(cb, post):
        self.proj_feat(Wv, B0 + cb * 128, 128, lambda c, half, ps, rps, m: k.op(
            "act", lambda e: e.activation(out=raw[:, 1 + half * 512:1 + (half + 1) * 512], in_=ps, func=AF.Identity), reads=[rps], writes=[Rraw]))
        k.op("dve", lambda e: e.tensor_tensor(out=t1[:], in0=raw[:, 0:1024], in1=tsm[:, 0, :], op=ALU.mult), reads=[Rraw, Rsh], writes=[Rt1])
        k.op("dve", lambda e: e.tensor_tensor(out=t2[:], in0=raw[:, 2:1026], in1=tsm[:, 1, :], op=ALU.mult), reads=[Rraw, Rsh], writes=[Rt2])
        k.op("dve", lambda e: e.tensor_tensor(out=t1[:], in0=t1[:], in1=t2[:], op=ALU.add), reads=[Rt1, Rt2], writes=[Rt1])
        k.op("dve", lambda e: e.tensor_scalar(out=t2[:], in0=raw[:, 1:1025], scalar1=mu[:, cb:cb + 1], scalar2=None, op0=ALU.mult), reads=[Rraw, Rvec], writes=[Rt2])
        k.op("dve", lambda e: e.scalar_tensor_tensor(out=t1[:], in0=t1[:], scalar=mu[:, 28 + cb:29 + cb], in1=t2[:], op0=ALU.mult, op1=ALU.add), reads=[Rt1, Rt2, Rvec], writes=[Rt1])
        post()

    Rlo = Res()
    rw_block(24, lambda: k.op("act", lambda e: e.activation(out=twT, in_=t1[:], func=AF.Tanh), reads=[Rt1], writes=[Rlo]))
    rw_block(25, lambda: k.op("act", lambda e: e.activation(out=adT, in_=t1[:], func=AF.Identity), reads=[Rt1], writes=[Rlo]))
    rw_block(26, lambda: k.op("act", lambda e: e.activation(out=sgT, in_=t1[:], func=AF.Sigmoid), reads=[Rt1], writes=[Rlo]))
    w2v = self.din["rwkv_w2"][j].rearrange("d r c -> (d r) c")
    a2v = self.din["rwkv_a2"][j].rearrange("d r c -> (d r) c")
    g2v = self.din["rwkv_g2"][j]
    lw3 = self.scr("rwlw3", [128, 3, 128], BF16)
    P = self.scr("rwP", [128, 64]); Z = self.scr("rwZ", [64, 128])
    U = self.scr("rwU", [128, 64]); RH = self.scr("rwRH", [128, 64])
    sm = self.scr("rwsm", [128, 16])
    slot = lambda n: (self.tmpf[0], self.tmpf[1], self.junk)[n // 8][:, (n % 8) * 128:(n % 8 + 1) * 128]
    QR = self.tmpf[0][:, 0:256]
    KT_ = slot(2); CT_ = slot(3); aT = slot(4); e2 = slot(5); cs = slot(6); csx = slot(7)
    Ep = slot(8); Em = slot(9); Ex = slot(10); kd = slot(11); cc = slot(12); Khat = slot(13); Chat = slot(14); KhT = slot(15)
    AkT = self.junk[:, 0:256]; AcT = self.junk[:, 256:512]; MT = slot(20); X = [slot(21), slot(22)]; ChT = slot(23)
    PP = [self.scr("rwPP%d" % i, [128, 256]) for i in range(2)]
    nb = self.scr("nrmb", [128, 1024])
    hbf = [self.hb[0][:].bitcast(F32), self.hb[1][:].bitcast(F32)]
    khf = self.scr("khat", [128, 256], BF16)[:].bitcast(F32)
    nsl = lambda n: nb[:, n * 128:(n + 1) * 128]
    U1 = self.scr("rwU1", [128, 64]); RH1 = self.scr("rwRH1", [128, 64])
    TS = [
        {"AkT": AkT, "AcT": AcT, "MT": MT, "X": X, "XT": [slot(11), slot(12)], "D": slot(4), "DT": slot(5), "W": slot(6), "WT": slot(7), "PP": PP, "RH": RH, "U": U},
        {"AkT": nb[:, 0:256], "AcT": nb[:, 256:512], "MT": nsl(4), "X": [nsl(5), nsl(6)], "XT": [nsl(7), hbf[0][:, 0:128]], "D": hbf[0][:, 128:256], "DT": hbf[0][:, 256:384],
         "W": hbf[0][:, 384:512], "WT": khf, "PP": [hbf[1][:, 0:256], hbf[1][:, 256:512]], "RH": RH1, "U": U1},
    ]
    Ys = self.scr("rwYs", [128, 128]); yn = self.scr("rwyn", [128, 128])
    seg_first = {0: [0, 2, 4, 6], 1: [7, 5, 3, 1]}
    seg_last = {0: [1, 3, 5, 7], 1: [6, 4, 2, 0]}
    k.barrier()
    for hp in range(8):
        cp = slice(hp * 128, (hp + 1) * 128)
        Rr, Rk, Rv, Rkk, Rbon = Res(), Res(), Res(), Res(), Res()
        RVt = [Res() for _ in range(NT)]; RYa = [Res() for _ in range(NT)]
        Rlw = Res(); RP = Res(); RU = Res(); RRH = Res(); Rsm = Res(); RYs = Res(); Ryn = Res()
        RT1 = {n: Res() for n in ("AkT", "AcT", "MT", "X0", "X1", "XT0", "XT1", "D", "DT", "W", "WT", "PP0", "PP1", "RH", "U")}
        Rs = {n: Res() for n in ("QR", "KT", "CT", "aT", "e2", "cs", "csx", "Ep", "Em", "Ex", "kd", "cc", "Khat", "Chat", "KhT", "ChT", "AkT", "AcT", "MT", "X0", "X1", "PP0", "PP1")}
        rw_block(hp, lambda: k.op("act", lambda e: e.activation(out=rT, in_=t1[:], func=AF.Identity), reads=[Rt1], writes=[Rr]))
        rw_block(8 + hp, lambda: k.op("act", lambda e: e.activation(out=kT, in_=t1[:], func=AF.Identity), reads=[Rt1], writes=[Rk]))
        rw_block(16 + hp, lambda: k.op("act", lambda e: e.activation(out=vT, in_=t1[:], func=AF.Identity), reads=[Rt1], writes=[Rv]))
        k.dma("pool", "lw3a", lw3[:, 0, :], w2v[:, cp], writes=[Rlw])
        k.dma("pool", "lw3b", lw3[:, 1, :], a2v[:, cp], writes=[Rlw])
        k.dma("pool", "lw3c", lw3[:, 2, :], g2v[:, cp], writes=[Rlw])
        k.op("dve", lambda e: e.tensor_scalar(out=kkT, in0=kT, scalar1=vec[:, 32 + hp:33 + hp], scalar2=None, op0=ALU.mult), reads=[Rk, Rvec], writes=[Rkk])
        k.op("dve", lambda e: e.tensor_tensor(out=t1[:], in0=kkT, in1=kkT, op=ALU.mult), reads=[Rkk], writes=[Rt1])
        for half in range(2):
            hs = slice(half * 512, (half + 1) * 512)
            ps, rps = self.next_ps()
            k.op("pe", lambda e: e.matmul(ps[:, :], lhsT=BD, rhs=t1[:, hs], start=True, stop=True), reads=[Rt1, self.Rmask], writes=[rps])
            k.op("act", lambda e: e.activation(out=t2[:, hs], in_=ps[:, :], func=AF.Ln, bias=cst[:, 0:1]), reads=[rps, Rvec], writes=[Rt2])
        k.op("act", lambda e: e.activation(out=t2[:], in_=t2[:], func=AF.Exp, scale=-0.5), reads=[Rt2], writes=[Rt2])
        k.op("dve", lambda e: e.tensor_tensor(out=kkT, in0=kkT, in1=t2[:], op=ALU.mult), reads=[Rkk, Rt2], writes=[Rkk])
        k.op("dve", lambda e: e.scalar_tensor_tensor(out=t1[:], in0=rT, scalar=vec[:, 48 + hp:49 + hp], in1=kT, op0=ALU.mult, op1=ALU.mult), reads=[Rr, Rk, Rvec], writes=[Rt1])
        for half in range(2):
            hs = slice(half * 512, (half + 1) * 512)
            ps, rps = self.next_ps()
            k.op("pe", lambda e: e.matmul(ps[:, :], lhsT=BD, rhs=t1[:, hs], start=True, stop=True), reads=[Rt1, self.Rmask], writes=[rps])
            k.op("dve", lambda e: e.tensor_tensor(out=bonT[:, hs], in0=ps[:, :], in1=vT[:, hs], op=ALU.mult), reads=[rps, Rv], writes=[Rbon])
        for i in range(NT):
            ps, rps = self.next_ps()
            k.op("pe", lambda e: e.transpose(ps[:, 0:128], vT[:, i * 128:(i + 1) * 128], self.identf[:]), reads=[Rv, self.Rid], writes=[rps])
            k.op("act", lambda e: e.activation(out=Vtok[:, i, :], in_=ps[:, 0:128], func=AF.Identity), reads=[rps], writes=[RVt[i]])
        k.barrier()
        for d in range(2):
            order = list(range(NT)) if d == 0 else list(range(NT - 1, -1, -1))
            ds_ = slice(d * 64, (d + 1) * 64)
            m_inc = self.masks[:, d, :]
            m_str = self.masks[:, 3 - d, :]
            m_strT = self.masks[:, 2 + d, :]
            endcol = 127 if d == 0 else 0
            for ci, i in enumerate(order):
                ts_ = slice(i * 128, (i + 1) * 128)
                sgi = i // 2
                if i in seg_first[d]:
                    if ci == 0:
                        for hl in range(2):
                            k.dma("sp", "rst%d" % hl, Z[:, hl * 64:(hl + 1) * 64], self.din["st_rw"][j, d, 2 * hp + hl], writes=[RP])
                        ps, rps = self.next_ps()
                        k.op("pe", lambda e: e.transpose(ps[:, 0:64], Z[:], self.identf[0:64, 0:64]), reads=[RP, self.Rid], writes=[rps])
                        k.op("act", lambda e: e.activation(out=P[:], in_=ps[:, 0:64], func=AF.Identity), reads=[rps], writes=[RP])
                    else:
                        k.op("dve", lambda e: e.tensor_scalar(out=P[:], in0=P[:], scalar1=self.flags[:, 0:1], scalar2=None, op0=ALU.mult), reads=[RP, self.Rmask], writes=[RP])
                ps, rps = self.next_ps()
                k.op("pe", lambda e: e.matmul(ps[:, 0:128], lhsT=lw3[ds_, 1, :], rhs=adT[ds_, ts_], start=True, stop=True), reads=[Rlw, Rlo], writes=[rps])
                k.op("pe", lambda e: e.matmul(ps[:, 128:256], lhsT=lw3[ds_, 0, :], rhs=twT[ds_, ts_], start=True, stop=True), reads=[Rlw, Rlo], writes=[rps])
                k.op("act", lambda e: e.activation(out=aT, in_=ps[:, 0:128], func=AF.Sigmoid, bias=vec[:, 16 + 8 * d + hp:17 + 8 * d + hp]), reads=[rps, Rvec], writes=[Rs["aT"]])
                k.op("act", lambda e: e.activation(out=e2, in_=ps[:, 128:256], func=AF.Exp, scale=-1.0, bias=nvec[:, 8 * d + hp:8 * d + hp + 1]), reads=[rps, Rvec], writes=[Rs["e2"]])
                k.op("act", lambda e: e.activation(out=e2, in_=e2, func=AF.Ln, bias=self.onesc[:, 0:1]), reads=[Rs["e2"], self.Rmask], writes=[Rs["e2"]])
                k.op("act", lambda e: e.activation(out=e2, in_=e2, func=AF.Exp, scale=-1.0, bias=cst[:, 1:2]), reads=[Rs["e2"], Rvec], writes=[Rs["e2"]])
                k.op("dve", lambda e: e.tensor_tensor_scan(out=cs, data0=ones128[:], data1=e2, initial=0.0, op0=ALU.mult, op1=ALU.add), reads=[Rs["e2"], Rvec], writes=[Rs["cs"]])
                if d == 1:
                    k.op("dve", lambda e: e.tensor_copy(out=sm[:, 0:1], in_=cs[:, 127:128]), reads=[Rs["cs"]], writes=[Rsm])
                    k.op("dve", lambda e: e.scalar_tensor_tensor(out=cs, in0=e2, scalar=sm[:, 0:1], in1=cs, op0=ALU.add, op1=ALU.subtract), reads=[Rs["e2"], Rs["cs"], Rsm], writes=[Rs["cs"]])
                k.op("dve", lambda e: e.tensor_tensor(out=csx, in0=cs, in1=e2, op=ALU.subtract), reads=[Rs["cs"], Rs["e2"]], writes=[Rs["csx"]])
                k.op("act", lambda e: e.activation(out=Ep, in_=cs, func=AF.Exp, scale=-1.0), reads=[Rs["cs"]], writes=[Rs["Ep"]])
                k.op("act", lambda e: e.activation(out=Em, in_=cs, func=AF.Exp), reads=[Rs["cs"]], writes=[Rs["Em"]])
                k.op("act", lambda e: e.activation(out=Ex, in_=csx, func=AF.Exp, scale=-1.0), reads=[Rs["csx"]], writes=[Rs["Ex"]])
                k.op("dve", lambda e: e.tensor_scalar(out=kd, in0=aT, scalar1=vec[:, 40 + hp:41 + hp], scalar2=nvec[:, 16 + hp:17 + hp], op0=ALU.mult, op1=ALU.add), reads=[Rs["aT"], Rvec], writes=[Rs["kd"]])
                k.op("dve", lambda e: e.tensor_tensor(out=kd, in0=kd, in1=kT[:, ts_], op=ALU.mult), reads=[Rs["kd"], Rk], writes=[Rs["kd"]])
                k.op("dve", lambda e: e.tensor_tensor(out=cc, in0=kkT[:, ts_], in1=aT, op=ALU.mult), reads=[Rkk, Rs["aT"]], writes=[Rs["cc"]])
                k.op("dve", lambda e: e.tensor_tensor(out=QR[:, 0:128], in0=kkT[:, ts_], in1=Ex, op=ALU.mult), reads=[Rkk, Rs["Ex"]], writes=[Rs["QR"]])
                k.op("dve", lambda e: e.tensor_tensor(out=QR[:, 128:256], in0=rT[:, ts_], in1=Ep, op=ALU.mult), reads=[Rr, Rs["Ep"]], writes=[Rs["QR"]])
                k.op("dve", lambda e: e.tensor_tensor(out=KT_, in0=kd, in1=Em, op=ALU.mult), reads=[Rs["kd"], Rs["Em"]], writes=[Rs["KT"]])
                k.op("dve", lambda e: e.tensor_tensor(out=CT_, in0=cc, in1=Em, op=ALU.mult), reads=[Rs["cc"], Rs["Em"]], writes=[Rs["CT"]])
                k.op("dve", lambda e: e.tensor_scalar(out=KhT, in0=KT_, scalar1=Ep[:, endcol:endcol + 1], scalar2=None, op0=ALU.mult), reads=[Rs["KT"], Rs["Ep"]], writes=[Rs["KhT"]])
                k.op("dve", lambda e: e.tensor_scalar(out=ChT, in0=CT_, scalar1=Ep[:, endcol:endcol + 1], scalar2=-1.0, op0=ALU.mult, op1=ALU.mult), reads=[Rs["CT"], Rs["Ep"]], writes=[Rs["ChT"]])
                ps, rps = self.next_ps()
                k.op("pe", lambda e: e.transpose(ps[:, 0:128], KhT, self.identf[:]), reads=[Rs["KhT"], self.Rid], writes=[rps])
                k.op("pe", lambda e: e.transpose(ps[:, 128:256], ChT, self.identf[:]), reads=[Rs["ChT"], self.Rid], writes=[rps])
                k.op("act", lambda e: e.activation(out=Khat, in_=ps[:, 0:128], func=AF.Identity), reads=[rps], writes=[Rs["Khat"]])
                k.op("act", lambda e: e.activation(out=Chat, in_=ps[:, 128:256], func=AF.Identity), reads=[rps], writes=[Rs["Chat"]])
                TS[0]["R"] = {"AkT": Rs["AkT"], "AcT": Rs["AcT"], "MT": Rs["MT"], "X0": Rs["X0"], "X1": Rs["X1"], "XT0": Rs["kd"], "XT1": Rs["cc"], "D": Rs["aT"], "DT": Rs["e2"],
                              "W": Rs["cs"], "WT": Rs["csx"], "PP0": Rs["PP0"], "PP1": Rs["PP1"], "RH": RRH, "U": RU}
                TS[1]["R"] = RT1
                pdl, rpdl = self.next_ps(pin=True)
                pY, rpY = self.next_ps(pin=True)
                def head_gen(hl):
                    hs_ = slice(hl * 64, (hl + 1) * 64)
                    Vh = Vtok[:, i, hs_]
                    tl = TS[hl]
                    mybanks = freeb[2 * hl:2 * hl + 2]
                    cnt_ = [0]

                    def hps():
                        bi = mybanks[cnt_[0] % 2]
                        cnt_[0] += 1
                        return self.PS[bi], self.RPS[bi]
                    AkT, AcT, MT, X, XT, Dm, DTm, Wm, WTm, PPh, RH, U = (tl[n] for n in ("AkT", "AcT", "MT", "X", "XT", "D", "DT", "W", "WT", "PP", "RH", "U"))
                    rr = tl["R"]
                    rX = [rr["X0"], rr["X1"]]; rXT = [rr["XT0"], rr["XT1"]]
                    rD, rDT, rW, rWT, RRH, RU = rr["D"], rr["DT"], rr["W"], rr["WT"], rr["RH"], rr["U"]
                    ps, rps = hps()
                    k.op("pe", lambda e: e.matmul(ps[:, 0:256], lhsT=KT_[hs_, :], rhs=QR[hs_, :], start=True, stop=True), reads=[Rs["KT"], Rs["QR"]], writes=[rps])
                    yield
                    k.op("dve", lambda e: e.tensor_tensor(out=AkT[:, 0:128], in0=ps[:, 0:128], in1=m_str, op=ALU.mult), reads=[rps, self.Rmask], writes=[rr["AkT"]])
                    yield
                    k.op("dve", lambda e: e.tensor_tensor(out=AkT[:, 128:256], in0=ps[:, 128:256], in1=m_inc, op=ALU.mult), reads=[rps, self.Rmask], writes=[rr["AkT"]])
                    yield
                    ps, rps = hps()
                    k.op("pe", lambda e: e.matmul(ps[:, 0:256], lhsT=CT_[hs_, :], rhs=QR[hs_, :], start=True, stop=True), reads=[Rs["CT"], Rs["QR"]], writes=[rps])
                    yield
                    k.op("dve", lambda e: e.tensor_tensor(out=AcT[:, 0:128], in0=ps[:, 0:128], in1=m_str, op=ALU.mult), reads=[rps, self.Rmask], writes=[rr["AcT"]])
                    yield
                    k.op("dve", lambda e: e.scalar_tensor_tensor(out=AcT[:, 128:256], in0=ps[:, 128:256], scalar=-1.0, in1=m_inc, op0=ALU.mult, op1=ALU.mult), reads=[rps, self.Rmask], writes=[rr["AcT"]])
                    yield
                    ps, rps = hps()
                    k.op("pe", lambda e: e.matmul(ps[:, 0:128], lhsT=QR[hs_, 0:128], rhs=CT_[hs_, :], start=True, stop=True), reads=[Rs["CT"], Rs["QR"]], writes=[rps])
                    yield
                    k.op("dve", lambda e: e.tensor_tensor(out=MT, in0=ps[:, 0:128], in1=m_strT, op=ALU.mult), reads=[rps, self.Rmask], writes=[rr["MT"]])
                    yield
                    M_ = AcT[:, 0:128]
                    bd = lambda q: self.masks[:, 5 + q, :]
                    k.op("dve", lambda e: e.tensor_tensor(out=Dm, in0=M_, in1=bd(0), op=ALU.mult), reads=[rr["AcT"], self.Rmask], writes=[rD])
                    yield
                    k.op("dve", lambda e: e.tensor_tensor(out=DTm, in0=MT, in1=bd(0), op=ALU.mult), reads=[rr["MT"], self.Rmask], writes=[rDT])
                    yield
                    k.op("dve", lambda e: e.scalar_tensor_tensor(out=X[0], in0=Dm, scalar=-1.0, in1=self.identf[:], op0=ALU.mult, op1=ALU.add), reads=[rD, self.Rid], writes=[rX[0]])
                    yield
                    k.op("dve", lambda e: e.scalar_tensor_tensor(out=XT[0], in0=DTm, scalar=-1.0, in1=self.identf[:], op0=ALU.mult, op1=ALU.add), reads=[rDT, self.Rid], writes=[rXT[0]])
                    yield
                    xi = 0
                    curP, curPT, rcur = Dm, DTm, [rD, rDT]
                    for lev in range(RWLEV):
                        pp, rpp = PPh[lev % 2], rr["PP%d" % (lev % 2)]
                        ps, rps = hps()
                        k.op("pe", lambda e: e.matmul(ps[:, 0:128], lhsT=curPT, rhs=curP, start=True, stop=True), reads=rcur, writes=[rps])
                        yield
                        k.op("pe", lambda e: e.matmul(ps[:, 128:256], lhsT=curP, rhs=curPT, start=True, stop=True), reads=rcur, writes=[rps])
                        yield
                        k.op("act", lambda e: e.activation(out=pp[:], in_=ps[:, 0:256], func=AF.Identity), reads=[rps], writes=[rpp])
                        yield
                        curP, curPT, rcur = pp[:, 0:128], pp[:, 128:256], [rpp]
                        ps2, rps2 = hps()
                        k.op("pe", lambda e: e.matmul(ps2[:, 0:128], lhsT=curPT, rhs=X[xi], start=True, stop=True), reads=[rpp, rX[xi]], writes=[rps2])
                        yield
                        k.op("pe", lambda e: e.matmul(ps2[:, 128:256], lhsT=curP, rhs=XT[xi], start=True, stop=True), reads=[rpp, rXT[xi]], writes=[rps2])
                        yield
                        k.op("dve", lambda e: e.tensor_tensor(out=X[1 - xi], in0=ps2[:, 0:128], in1=X[xi], op=ALU.add), reads=[rps2, rX[xi]], writes=[rX[1 - xi]])
                        yield
                        k.op("dve", lambda e: e.tensor_tensor(out=XT[1 - xi], in0=ps2[:, 128:256], in1=XT[xi], op=ALU.add), reads=[rps2, rXT[xi]], writes=[rXT[1 - xi]])
                        yield
                        xi = 1 - xi
                    for q in RWQ:
                        last = (q == 3)
                        k.op("dve", lambda e: e.tensor_tensor(out=DTm, in0=MT, in1=bd(q), op=ALU.mult), reads=[rr["MT"], self.Rmask], writes=[rDT])
                        yield
                        ps, rps = hps()
                        k.op("pe", lambda e: e.matmul(ps[:, 0:128], lhsT=DTm, rhs=X[xi], start=True, stop=True), reads=[rDT, rX[xi]], writes=[rps])
                        yield
                        if not last:
                            k.op("dve", lambda e: e.tensor_tensor(out=Dm, in0=M_, in1=bd(q), op=ALU.mult), reads=[rr["AcT"], self.Rmask], writes=[rD])
                            yield
                            k.op("pe", lambda e: e.matmul(ps[:, 128:256], lhsT=Dm, rhs=XT[xi], start=True, stop=True), reads=[rD, rXT[xi]], writes=[rps])
                            yield
                        k.op("act", lambda e: e.activation(out=Wm, in_=ps[:, 0:128], func=AF.Identity), reads=[rps], writes=[rW])
                        yield
                        if not last:
                            k.op("act", lambda e: e.activation(out=WTm, in_=ps[:, 128:256], func=AF.Identity), reads=[rps], writes=[rWT])
                            yield
                        ps2, rps2 = hps()
                        k.op("pe", lambda e: e.matmul(ps2[:, 0:128], lhsT=XT[xi], rhs=Wm, start=True, stop=True), reads=[rXT[xi], rW], writes=[rps2])
                        yield
                        if not last:
                            k.op("pe", lambda e: e.matmul(ps2[:, 128:256], lhsT=X[xi], rhs=WTm, start=True, stop=True), reads=[rX[xi], rWT], writes=[rps2])
                            yield
                        k.op("dve", lambda e: e.tensor_tensor(out=X[1 - xi], in0=X[xi], in1=ps2[:, 0:128], op=ALU.subtract), reads=[rps2, rX[xi]], writes=[rX[1 - xi]])
                        yield
                        if not last:
                            k.op("dve", lambda e: e.tensor_tensor(out=XT[1 - xi], in0=XT[xi], in1=ps2[:, 128:256], op=ALU.subtract), reads=[rps2, rXT[xi]], writes=[rXT[1 - xi]])
                            yield
                        xi = 1 - xi
                    TT, rTT = X[xi], rX[xi]
                    ps, rps = hps()
                    k.op("pe", lambda e: e.matmul(ps[:, 0:64], lhsT=QR[hs_, 0:128], rhs=P[hs_, :], start=True, stop=False), reads=[Rs["QR"], RP], writes=[rps])
                    yield
                    k.op("pe", lambda e: e.matmul(ps[:, 0:64], lhsT=AkT[:, 0:128], rhs=Vh, start=False, stop=True), reads=[rr["AkT"], RVt[i]], writes=[rps])
                    yield
                    k.op("act", lambda e: e.activation(out=RH[:], in_=ps[:, 0:64], func=AF.Identity), reads=[rps], writes=[RRH])
                    yield
                    ps, rps = hps()
                    k.op("pe", lambda e: e.matmul(ps[:, 0:64], lhsT=TT, rhs=RH[:], start=True, stop=True), reads=[rTT, RRH], writes=[rps])
                    yield
                    k.op("act", lambda e: e.activation(out=U[:], in_=ps[:, 0:64], func=AF.Identity), reads=[rps], writes=[RU])
                    yield
                    k.op("pe", lambda e: e.matmul(pY[:, hs_], lhsT=QR[hs_, 128:256], rhs=P[hs_, :], start=True, stop=False), reads=[Rs["QR"], RP], writes=[rpY])
                    k.op("pe", lambda e: e.matmul(pY[:, hs_], lhsT=AkT[:, 128:256], rhs=Vh, start=False, stop=False), reads=[rr["AkT"], RVt[i]], writes=[rpY])
                    k.op("pe", lambda e: e.matmul(pY[:, hs_], lhsT=AcT[:, 128:256], rhs=U[:], start=False, stop=True), reads=[rr["AcT"], RU], writes=[rpY])
                    yield
                    k.op("pe", lambda e: e.matmul(pdl[hs_, 0:64], lhsT=Khat[:, hs_], rhs=Vh, start=True, stop=False), reads=[Rs["Khat"], RVt[i]], writes=[rpdl])
                    k.op("pe", lambda e: e.matmul(pdl[hs_, 0:64], lhsT=Chat[:, hs_], rhs=U[:], start=False, stop=True), reads=[Rs["Chat"], RU], writes=[rpdl])
                    yield

                freeb = [bi for bi in range(len(self.PS)) if bi not in self.pinned]
                gens = [head_gen(0), head_gen(1)]
                while gens:
                    for g_ in list(gens):
                        try:
                            next(g_)
                        except StopIteration:
                            gens.remove(g_)
                k.op("dve", lambda e: e.scalar_tensor_tensor(out=P[:], in0=P[:], scalar=Ep[:, endcol:endcol + 1], in1=pdl[:, 0:64], op0=ALU.mult, op1=ALU.add),
                     reads=[RP, Rs["Ep"], rpdl], writes=[RP])
                if d == 0:
                    k.op("act", lambda e: e.activation(out=Yacc[:, i, :], in_=pY[:, 0:128], func=AF.Identity), reads=[rpY], writes=[RYa[i]])
                else:
                    k.op("dve", lambda e: e.tensor_tensor(out=Ys[:], in0=pY[:, 0:128], in1=Yacc[:, i, :], op=ALU.add), reads=[rpY, RYa[i]], writes=[RYs])
                self.unpin(rpdl, rpY)
                if i in seg_last[d]:
                    ps, rps = self.next_ps()
                    k.op("pe", lambda e: e.transpose(ps[0:64, 0:128], P[:], self.identf[:]), reads=[RP, self.Rid], writes=[rps])
                    k.op("act", lambda e: e.activation(out=Z[:], in_=ps[0:64, 0:128], func=AF.Identity), reads=[rps], writes=[RRH])
                    for hl in range(2):
                        k.dma("sp", "rso%d" % hl, self.dout["o_rwkv"][j, sgi, d, 2 * hp + hl], Z[:, hl * 64:(hl + 1) * 64], reads=[RRH])
                if d == 1:
                    Y3 = Ys[:].rearrange("p (h v) -> p h v", h=2)
                    k.op("dve", lambda e: e.tensor_reduce(out=sm[:, 2:4], in_=Y3, axis=AX.X, op=ALU.add), reads=[RYs], writes=[Rsm])
                    k.op("dve", lambda e: e.tensor_scalar(out=sm[:, 2:4], in0=sm[:, 2:4], scalar1=-1.0 / 64, scalar2=None, op0=ALU.mult), reads=[Rsm], writes=[Rsm])
                    k.op("dve", lambda e: e.tensor_tensor(out=Y3, in0=Y3, in1=sm[:, 2:4].unsqueeze(2).to_broadcast([128, 2, 64]), op=ALU.add), reads=[RYs, Rsm], writes=[RYs])
                    yn3 = yn[:].rearrange("p (h v) -> p h v", h=2)
                    k.op("dve", lambda e: e.tensor_tensor(out=yn[:], in0=Ys[:], in1=Ys[:], op=ALU.mult), reads=[RYs], writes=[Ryn])
                    k.op("dve", lambda e: e.tensor_reduce(out=sm[:, 4:6], in_=yn3, axis=AX.X, op=ALU.add), reads=[Ryn], writes=[Rsm])
                    k.op("act", lambda e: e.activation(out=sm[:, 4:6], in_=sm[:, 4:6], func=AF.Ln, scale=1.0 / 64, bias=cst[:, 2:3]), reads=[Rsm, Rvec], writes=[Rsm])
                    k.op("act", lambda e: e.activation(out=sm[:, 4:6], in_=sm[:, 4:6], func=AF.Exp, scale=-0.5), reads=[Rsm], writes=[Rsm])
                    k.op("dve", lambda e: e.tensor_tensor(out=yn3, in0=Y3, in1=sm[:, 4:6].unsqueeze(2).to_broadcast([128, 2, 64]), op=ALU.mult), reads=[RYs, Rsm], writes=[Ryn])
                    ps, rps = self.next_ps()
                    k.op("pe", lambda e: e.transpose(ps[:, 0:128], yn[:], self.identf[:]), reads=[Ryn, self.Rid], writes=[rps])
                    k.op("pe", lambda e: e.matmul(ps[:, 128:256], lhsT=lw3[:, 2, :], rhs=sgT[:, ts_], start=True, stop=True), reads=[Rlw, Rlo], writes=[rps])
                    k.op("act", lambda e: e.activation(out=Ys[:], in_=ps[:, 0:128], func=AF.Identity, scale=vec[:, 56 + hp:57 + hp], bias=vec[:, 64 + hp:65 + hp]), reads=[rps, Rvec], writes=[RYs])
                    k.op("dve", lambda e: e.tensor_tensor(out=Ys[:], in0=Ys[:], in1=bonT[:, ts_], op=ALU.add), reads=[RYs, Rbon], writes=[RYs])
                    k.op("dve", lambda e: e.tensor_tensor(out=yT[:, 8 + hp, ts_], in0=Ys[:], in1=ps[:, 128:256], op=ALU.mult), reads=[RYs, rps], writes=[RyT[i]])
        k.barrier()


Prog.rwkv = _rwkv


RWQ = (1, 2, 3)
RWLEV = 3
PARTS = ("gla", "ml", "ssd", "rw")


def kernel(**inputs):
    inputs = {k: np.asarray(v) for k, v in inputs.items()}
    p = Prog(nl=4, do_mix=True, parts=PARTS)
    nc = p.build()
    in_maps = []
    for core in range(8):
        m = _prep_core_inputs(inputs, core)
        in_maps.append({k: np.ascontiguousarray(v, dtype=np.float32) for k, v in m.items() if k in p.din})
    res = run_bass_kernel_spmd(nc, in_maps, core_ids=list(range(8)))
    R = res.results
    y_sample = np.stack([R[c]["y"] for c in range(4)], axis=0).astype(np.float32)
    y_prompt = np.concatenate([R[c]["y"].reshape(4, 256, D) for c in range(4, 8)], axis=0).astype(np.float32)

    def states(name, shape):
        if name not in R[4]:
            return np.zeros((16, 2, 2) + shape, np.float32)
        out = np.zeros((16, 2, 2) + shape, np.float32)
        for c in range(4, 8):
            o = R[c][name]
            for g in range(4):
                out[4 * (c - 4) + g] = o[:, g]
        return out
    new_ssd = states("o_ssd", (16, 128, 64))
    new_rwkv = states("o_rwkv", (16, 64, 64))
    new_gla = states("o_gla", (4, 128, 256))
    new_mc = states("o_mc", (4, 128, 256))
    new_mn = states("o_mn", (4, 128))
    new_mm = states("o_mm", (4,))
    return (y_prompt, y_sample, new_ssd, new_rwkv, new_gla, new_mc, new_mn, new_mm)
```
